# Optimizing a Trainium2 kernel written in Bass

```python
import math
import jax, jax.numpy as jnp
from jax import lax
import numpy as np

D_MODEL = 2048
BATCH = 16
SEQ = 256
DEPTH = 2
DEC_BATCH = 8
DEC_SEQ = 1024
PAST_LEN = 256

GRID_W = 64
EPS = 1e-6
HY_CH = 512
HY_ORDER = 2
HY_SHORT = 3
HY_BANDS = 16
HY_EMB = 1 + 2 * HY_BANDS
HY_FF = 64
HY_FAST_DECAY = 0.3
HY_SLOW_DECAY = 1.5
HY_TARGET = 1e-2
HY_W3_STD = 0.1 * HY_FF ** -0.5
HY_IN = 3 * HY_CH
GLA_HEADS = 4
GLA_DK = 64
GLA_DV = 128
GLA_LOWRANK = 16
GLA_TAU = 16.0
GLA_CHUNK = 64
GLA_QK = GLA_HEADS * GLA_DK
GLA_VW = GLA_HEADS * GLA_DV
GLA_IN = 2 * GLA_QK + 2 * GLA_VW + 2 * GLA_LOWRANK
MLA_HEADS = 8
MLA_Q_RANK = 384
MLA_KV_RANK = 256
MLA_NOPE = 128
MLA_ROPE = 64
MLA_V = 128
MLA_IN = MLA_Q_RANK + MLA_KV_RANK + MLA_ROPE
ROPE_THETA = 10000.0
Q_BLOCK = 128
D_IN = HY_IN + GLA_IN + MLA_IN
MIX_W = HY_CH + GLA_VW + MLA_HEADS * MLA_V
D_FF = -(-8 * D_MODEL // (3 * 256)) * 256

kernel_name = 'hybrid_hyena_gla_mla_prefix_dit_step'


def _split(t, sizes):
    pts = [int(p) for p in np.cumsum(sizes)[:-1]]
    return jnp.split(t, pts, axis=-1)


def rms_norm(x, g):
    xf = x.astype(jnp.float32)
    y = xf * lax.rsqrt(jnp.mean(xf * xf, axis=-1, keepdims=True) + EPS)
    return (y * g.astype(jnp.float32)).astype(x.dtype)


def short_conv(u, w, b):
    L = u.shape[1]
    p = HY_SHORT // 2
    up = jnp.pad(u, ((0, 0), (p, p), (0, 0)))
    return sum(up[:, i:i + L] * w[i] for i in range(HY_SHORT)) + b


def hyena_filters(L, w1, b1, freq, w2, b2, w3):
    t = jnp.linspace(0.0, 1.0, L, dtype=jnp.float32)[:, None]
    w = 2.0 * math.pi * jnp.arange(L, dtype=jnp.float32)[:, None] / L
    f = jnp.linspace(1e-4, HY_BANDS - 1, HY_BANDS, dtype=jnp.float32)
    z = jnp.concatenate([t, jnp.cos(f * w), -jnp.sin(f * w)], axis=-1)
    h = jnp.sin(freq[0] * (z @ w1 + b1))
    h = jnp.sin(freq[1] * (h @ w2 + b2))
    h = (h @ w3).astype(jnp.float32).reshape(L, 2, HY_ORDER, HY_CH)
    min_decay = math.log(HY_TARGET) / HY_SLOW_DECAY
    max_decay = math.log(HY_TARGET) / HY_FAST_DECAY
    delta = jnp.abs(jnp.linspace(min_decay, max_decay, HY_CH, dtype=jnp.float32))
    h = h * jnp.exp(-t[:, :, None, None] * delta)
    k_fwd, k_bwd = h[:, 0], h[:, 1]
    zero = jnp.zeros((1, HY_ORDER, HY_CH), jnp.float32)
    return jnp.concatenate([k_fwd, zero, k_bwd[1:][::-1]], axis=0)


def hyena_mixer(u, conv_w, conv_b, w1, b1, freq, w2, b2, w3, skip):
    L = u.shape[1]
    uc = short_conv(u, conv_w, conv_b).astype(jnp.float32)
    v, x1, x2 = jnp.split(uc, 3, axis=-1)
    k_f = jnp.fft.rfft(hyena_filters(L, w1, b1, freq, w2, b2, w3), n=2 * L, axis=0)
    z = v
    for o, gate in enumerate((x1, x2)):
        z_f = jnp.fft.rfft(z, n=2 * L, axis=1)
        conv = jnp.fft.irfft(z_f * k_f[:, o], n=2 * L, axis=1)[:, :L]
        z = gate * (conv + z * skip[o].astype(jnp.float32))
    return z.astype(u.dtype)


def gla_chunked(q, k, v, log_a, s0):
    B, H, L, dk = q.shape
    dv = v.shape[-1]
    n = L // GLA_CHUNK
    def chunks(t):
        return t.reshape(B, H, n, GLA_CHUNK, t.shape[-1])
    q, k, v, log_a = chunks(q), chunks(k), chunks(v), chunks(log_a)
    cum = jnp.cumsum(log_a, axis=3)
    total = cum[:, :, :, -1]
    q_dec = q * jnp.exp(cum)
    k_inv = k * jnp.exp(-cum)
    k_end = k * jnp.exp(total[:, :, :, None] - cum)
    mask = jnp.tril(jnp.ones((GLA_CHUNK, GLA_CHUNK), dtype=bool))
    att = jnp.where(mask, jnp.einsum('bhncd,bhnsd->bhncs', q_dec, k_inv), 0.0)
    o_intra = jnp.einsum('bhncs,bhnse->bhnce', att, v)
    kv_chunk = jnp.einsum('bhncd,bhnce->bhnde', k_end, v)
    def step(s, inp):
        qd, dec, kvc = inp
        o = jnp.einsum('bhcd,bhde->bhce', qd, s)
        return dec[..., None] * s + kvc, o
    xs = (jnp.moveaxis(q_dec, 2, 0), jnp.moveaxis(jnp.exp(total), 2, 0), jnp.moveaxis(kv_chunk, 2, 0))
    s_fin, o_inter = lax.scan(step, s0, xs)
    o = o_intra + jnp.moveaxis(o_inter, 0, 2)
    return o.reshape(B, H, L, dv), s_fin


def gla_mixer(u, s0f, s0b, wa_f, ba_f, wa_b, ba_b, norm_g):
    B, L, _ = u.shape
    q, k, v, g, af, ab = _split(u, (GLA_QK, GLA_QK, GLA_VW, GLA_VW, GLA_LOWRANK, GLA_LOWRANK))
    def heads(t, d):
        return t.reshape(B, L, GLA_HEADS, d).transpose(0, 2, 1, 3).astype(jnp.float32)
    q = heads(q, GLA_DK) * GLA_DK ** -0.5
    k = heads(k, GLA_DK)
    v = heads(v, GLA_DV)
    la_f = heads(jax.nn.log_sigmoid((af @ wa_f + ba_f).astype(jnp.float32)), GLA_DK) / GLA_TAU
    la_b = heads(jax.nn.log_sigmoid((ab @ wa_b + ba_b).astype(jnp.float32)), GLA_DK) / GLA_TAU
    o_f, s_f = gla_chunked(q, k, v, la_f, s0f.astype(jnp.float32))
    def flip(t):
        return t[:, :, ::-1]
    o_b, s_b = gla_chunked(flip(q), flip(k), flip(v), flip(la_b), s0b.astype(jnp.float32))
    o = rms_norm(o_f + flip(o_b), norm_g)
    o = o.transpose(0, 2, 1, 3).reshape(B, L, GLA_VW) * jax.nn.silu(g.astype(jnp.float32))
    return o.astype(u.dtype), s_f, s_b


def axial_rope(x):
    L = x.shape[1]
    rows = L // GRID_W
    row = jnp.repeat(jnp.arange(rows), GRID_W)
    col = jnp.tile(jnp.arange(GRID_W), rows)
    half = MLA_ROPE // 2
    inv = ROPE_THETA ** (-jnp.arange(0, half, 2, dtype=jnp.float32) / half)
    extra = (1,) * (x.ndim - 3)
    def rot(xa, pos):
        ang = (pos.astype(jnp.float32)[:, None] * inv).reshape((L,) + extra + (half // 2,))
        cos, sin = jnp.cos(ang), jnp.sin(ang)
        x1, x2 = jnp.split(xa.astype(jnp.float32), 2, axis=-1)
        return jnp.concatenate([x1 * cos - x2 * sin, x1 * sin + x2 * cos], axis=-1)
    return jnp.concatenate([rot(x[..., :half], row), rot(x[..., half:], col)], axis=-1).astype(x.dtype)


def mla_attend(q_nope, q_rope, k_nope, k_rope, v):
    B, Lq, H, _ = q_nope.shape
    nb = Lq // Q_BLOCK
    scale = (MLA_NOPE + MLA_ROPE) ** -0.5
    def blocks(t):
        return jnp.moveaxis(t.reshape((B, nb, Q_BLOCK) + t.shape[2:]), 1, 0)
    def one_block(qs):
        qn, qr = qs
        s = jnp.einsum('bqhd,bkhd->bhqk', qn, k_nope) + jnp.einsum('bqhd,bkd->bhqk', qr, k_rope)
        p = jax.nn.softmax(s.astype(jnp.float32) * scale, axis=-1).astype(v.dtype)
        return jnp.einsum('bhqk,bkhd->bqhd', p, v)
    out = lax.map(one_block, (blocks(q_nope), blocks(q_rope)))
    return jnp.moveaxis(out, 0, 1).reshape(B, Lq, H * MLA_V)


def mla_mixer(u, q_norm, w_uq, kv_norm, w_ukv, ctx=None):
    B, L, _ = u.shape
    cq, ckv, krope = _split(u, (MLA_Q_RANK, MLA_KV_RANK, MLA_ROPE))
    q = (rms_norm(cq, q_norm) @ w_uq).reshape(B, L, MLA_HEADS, MLA_NOPE + MLA_ROPE)
    q_nope, q_rope = q[..., :MLA_NOPE], q[..., MLA_NOPE:]
    ckv = rms_norm(ckv, kv_norm)
    if ctx is None:
        keys_ckv, keys_rope = ckv, krope
    else:
        ctx_ckv, ctx_krope = ctx
        q_rope = axial_rope(q_rope)
        keys_ckv = jnp.concatenate([ckv, ctx_ckv.astype(ckv.dtype)], axis=1)
        keys_rope = jnp.concatenate([axial_rope(krope), ctx_krope.astype(krope.dtype)], axis=1)
    Lk = keys_ckv.shape[1]
    kv = (keys_ckv @ w_ukv).reshape(B, Lk, MLA_HEADS, MLA_NOPE + MLA_V)
    k_nope, v = kv[..., :MLA_NOPE], kv[..., MLA_NOPE:]
    return mla_attend(q_nope, q_rope, k_nope, keys_rope, v), ckv, krope


def trunk_layer(x, mod, lp, ctx=None):
    B = x.shape[0]
    sh1, sc1, g1, sh2, sc2, g2 = [m[:, None] for m in jnp.split(mod, 6, axis=-1)]
    h = rms_norm(x, lp['ln_mix']) * (1.0 + sc1) + sh1
    u_hy, u_gla, u_mla = _split(h @ lp['w_in'], (HY_IN, GLA_IN, MLA_IN))
    y_hy = hyena_mixer(u_hy, lp['hy_conv_w'], lp['hy_conv_b'], lp['hy_filt_w1'], lp['hy_filt_b1'],
                       lp['hy_filt_freq'], lp['hy_filt_w2'], lp['hy_filt_b2'], lp['hy_filt_w3'], lp['hy_skip'])
    if ctx is None:
        s0 = jnp.zeros((B, GLA_HEADS, GLA_DK, GLA_DV), jnp.float32)
        s0f, s0b, mla_ctx = s0, s0, None
    else:
        ctx_ckv, ctx_krope, s0f, s0b = ctx
        mla_ctx = (ctx_ckv, ctx_krope)
    y_gla, s_f, s_b = gla_mixer(u_gla, s0f, s0b, lp['gla_wa_f'], lp['gla_ba_f'], lp['gla_wa_b'],
                                lp['gla_ba_b'], lp['gla_norm'])
    y_mla, ckv, krope = mla_mixer(u_mla, lp['mla_q_norm'], lp['mla_w_uq'], lp['mla_kv_norm'],
                                  lp['mla_w_ukv'], mla_ctx)
    y = jnp.concatenate([y_hy, y_gla.astype(y_hy.dtype), y_mla.astype(y_hy.dtype)], axis=-1) @ lp['w_out']
    x = x + g1 * y
    h = rms_norm(x, lp['ln_ffn']) * (1.0 + sc2) + sh2
    gate, up = jnp.split(h @ lp['w_ffn_in'], 2, axis=-1)
    x = x + g2 * ((jax.nn.silu(gate) * up) @ lp['w_ffn_out'])
    return x, ((ckv, krope, s_f, s_b) if ctx is None else None)


def setup_inputs(seed: int = 0) -> dict:
    key = jax.random.key(seed)
    ks = iter(jax.random.split(key, 48))
    def nrm(shape, s):
        return jax.random.normal(next(ks), shape, jnp.float32) * s
    def gain(shape):
        return 1.0 + nrm(shape, 0.02)
    return {
        'x_prompt': nrm((BATCH, SEQ, D_MODEL), 1.0),
        'x_sample': nrm((DEC_BATCH, DEC_SEQ, D_MODEL), 1.0),
        'cache_mla_ckv': nrm((DEC_BATCH, DEPTH, PAST_LEN, MLA_KV_RANK), 1.0),
        'cache_mla_krope': nrm((DEC_BATCH, DEPTH, PAST_LEN, MLA_ROPE), 1.0),
        'state_gla_fwd': nrm((DEC_BATCH, DEPTH, GLA_HEADS, GLA_DK, GLA_DV), 2.0),
        'state_gla_bwd': nrm((DEC_BATCH, DEPTH, GLA_HEADS, GLA_DK, GLA_DV), 2.0),
        'c': nrm((DEC_BATCH, D_MODEL), 1.0),
        'c_ctx': nrm((D_MODEL,), 1.0),
        'w_mod': nrm((DEPTH, D_MODEL, 6 * D_MODEL), 0.5 * D_MODEL ** -0.5),
        'b_mod': nrm((DEPTH, 6 * D_MODEL), 0.02),
        'ln_mix': gain((DEPTH, D_MODEL)),
        'w_in': nrm((DEPTH, D_MODEL, D_IN), D_MODEL ** -0.5),
        'hy_conv_w': nrm((DEPTH, HY_SHORT, HY_IN), HY_SHORT ** -0.5),
        'hy_conv_b': nrm((DEPTH, HY_IN), 0.02),
        'hy_filt_w1': nrm((DEPTH, HY_EMB, HY_FF), HY_EMB ** -0.5),
        'hy_filt_b1': nrm((DEPTH, HY_FF), 0.1),
        'hy_filt_freq': gain((DEPTH, 2, HY_FF)),
        'hy_filt_w2': nrm((DEPTH, HY_FF, HY_FF), HY_FF ** -0.5),
        'hy_filt_b2': nrm((DEPTH, HY_FF), 0.1),
        'hy_filt_w3': nrm((DEPTH, HY_FF, 2 * HY_ORDER * HY_CH), HY_W3_STD),
        'hy_skip': nrm((DEPTH, HY_ORDER, HY_CH), 0.5),
        'gla_wa_f': nrm((DEPTH, GLA_LOWRANK, GLA_QK), GLA_LOWRANK ** -0.5),
        'gla_ba_f': nrm((DEPTH, GLA_QK), 0.02),
        'gla_wa_b': nrm((DEPTH, GLA_LOWRANK, GLA_QK), GLA_LOWRANK ** -0.5),
        'gla_ba_b': nrm((DEPTH, GLA_QK), 0.02),
        'gla_norm': gain((DEPTH, GLA_DV)),
        'mla_q_norm': gain((DEPTH, MLA_Q_RANK)),
        'mla_w_uq': nrm((DEPTH, MLA_Q_RANK, MLA_HEADS * (MLA_NOPE + MLA_ROPE)), MLA_Q_RANK ** -0.5),
        'mla_kv_norm': gain((DEPTH, MLA_KV_RANK)),
        'mla_w_ukv': nrm((DEPTH, MLA_KV_RANK, MLA_HEADS * (MLA_NOPE + MLA_V)), MLA_KV_RANK ** -0.5),
        'w_out': nrm((DEPTH, MIX_W, D_MODEL), MIX_W ** -0.5),
        'ln_ffn': gain((DEPTH, D_MODEL)),
        'w_ffn_in': nrm((DEPTH, D_MODEL, 2 * D_FF), D_MODEL ** -0.5),
        'w_ffn_out': nrm((DEPTH, D_FF, D_MODEL), D_FF ** -0.5),
        'ln_final': gain((D_MODEL,)),
    }


def reference(x_prompt, x_sample, cache_mla_ckv, cache_mla_krope, state_gla_fwd, state_gla_bwd, c, c_ctx,
              w_mod, b_mod, ln_mix, w_in, hy_conv_w, hy_conv_b, hy_filt_w1, hy_filt_b1, hy_filt_freq,
              hy_filt_w2, hy_filt_b2, hy_filt_w3, hy_skip, gla_wa_f, gla_ba_f, gla_wa_b, gla_ba_b, gla_norm,
              mla_q_norm, mla_w_uq, mla_kv_norm, mla_w_ukv, w_out, ln_ffn, w_ffn_in, w_ffn_out, ln_final):
    x_ctx, x_lat = x_prompt, x_sample
    ckv_l, krope_l, sf_l, sb_l = [], [], [], []
    for l in range(DEPTH):
        lp = {
            'ln_mix': ln_mix[l], 'w_in': w_in[l],
            'hy_conv_w': hy_conv_w[l], 'hy_conv_b': hy_conv_b[l],
            'hy_filt_w1': hy_filt_w1[l], 'hy_filt_b1': hy_filt_b1[l], 'hy_filt_freq': hy_filt_freq[l],
            'hy_filt_w2': hy_filt_w2[l], 'hy_filt_b2': hy_filt_b2[l], 'hy_filt_w3': hy_filt_w3[l],
            'hy_skip': hy_skip[l],
            'gla_wa_f': gla_wa_f[l], 'gla_ba_f': gla_ba_f[l], 'gla_wa_b': gla_wa_b[l], 'gla_ba_b': gla_ba_b[l],
            'gla_norm': gla_norm[l],
            'mla_q_norm': mla_q_norm[l], 'mla_w_uq': mla_w_uq[l], 'mla_kv_norm': mla_kv_norm[l],
            'mla_w_ukv': mla_w_ukv[l],
            'w_out': w_out[l], 'ln_ffn': ln_ffn[l], 'w_ffn_in': w_ffn_in[l], 'w_ffn_out': w_ffn_out[l],
        }
        mod_ctx = jax.nn.silu(c_ctx)[None] @ w_mod[l] + b_mod[l]
        mod_lat = jax.nn.silu(c) @ w_mod[l] + b_mod[l]
        x_ctx, (ckv, krope, s_f, s_b) = trunk_layer(x_ctx, mod_ctx, lp)
        ckv_l.append(ckv)
        krope_l.append(krope)
        sf_l.append(s_f)
        sb_l.append(s_b)
        ctx = (cache_mla_ckv[:, l], cache_mla_krope[:, l], state_gla_fwd[:, l], state_gla_bwd[:, l])
        x_lat, _ = trunk_layer(x_lat, mod_lat, lp, ctx)
    y_prompt = rms_norm(x_ctx, ln_final)
    y_sample = rms_norm(x_lat, ln_final)
    new_mla_ckv = jnp.stack(ckv_l, axis=1)
    new_mla_krope = jnp.stack(krope_l, axis=1)
    new_gla_fwd = jnp.stack(sf_l, axis=1)
    new_gla_bwd = jnp.stack(sb_l, axis=1)
    return (y_prompt, y_sample, new_mla_ckv, new_mla_krope, new_gla_fwd, new_gla_bwd)
```

```python
import math
import contextlib
import numpy as np
import ml_dtypes
import concourse.bass as bass
import concourse.mybir as mybir
from concourse.bass_utils import run_bass_kernel_spmd

F32 = mybir.dt.float32
BF16 = mybir.dt.bfloat16
ALU = mybir.AluOpType
AF = mybir.ActivationFunctionType

D = 2048
DFF = 5632
NL = 2
LP = 256
LS = 1024
TP = 512
T = 1536
PAST = 256
EPS = 1e-6
NCORE = 8
C_HY, C_Q, C_K, C_V, C_G, C_AF, C_AB, C_CQ, C_CKV, C_KR, C_KRSW = 0, 1536, 1792, 2048, 2560, 3072, 3088, 3104, 3488, 3744, 3808
WIN_EXT = 3872
NV = 200
V_BMOD, V_LNMIX, V_LNFFN, V_CW, V_CB, V_SKIP, V_GN, V_QN, V_KVN, V_FB1, V_FB2, V_FR0, V_FR1 = 0, 96, 112, 128, 164, 176, 184, 185, 188, 190, 191, 192, 193
MAGIC = 12582912.0
TWO_PI = 2.0 * math.pi
SBASE = 16512
SEND = 229376


class Buf:
    __slots__ = ("w", "r")

    def __init__(self):
        self.w = None
        self.r = {}


class Op:
    __slots__ = ("eng", "fn", "waits", "signal", "src", "seq", "vc", "val", "dma")


class Sched:
    ENG = ["pe", "act", "dve", "pool", "sp"]
    NSLOT = 8

    def __init__(self):
        self.q = {e: [] for e in self.ENG}
        self.S = {e: {} for e in self.ENG}
        self.ncomp = {e: 0 for e in self.ENG}
        self.ndma = {e: 0 for e in self.ENG}
        self.slot_last = {}
        self.slot_seq = {}

    def add(self, eng, fn, reads=(), writes=(), dma=False, after=()):
        op = Op()
        op.eng = eng
        op.fn = fn
        op.signal = False
        op.dma = dma
        deps = []
        deps.extend(after)
        for b in reads:
            if b.w is not None:
                deps.append(b.w)
        for b in writes:
            if b.w is not None:
                deps.append(b.w)
            deps.extend(b.r.values())
        if dma:
            j = self.ndma[eng]
            self.ndma[eng] += 1
            slot = (eng, j % self.NSLOT)
            prev = self.slot_last.get(slot)
            if prev is not None:
                deps.append(prev)
            op.src = ("dma",) + slot
            op.seq = self.slot_seq.get(slot, 0) + 1
            self.slot_seq[slot] = op.seq
            self.slot_last[slot] = op
            op.signal = True
        else:
            self.ncomp[eng] += 1
            op.src = eng
            op.seq = self.ncomp[eng]
        S = self.S[eng]
        waits = {}
        for d in deps:
            if eng == "pe" and d.src == "pe":
                continue
            if S.get(d.src, 0) >= d.seq:
                continue
            d.signal = True
            cur = waits.get(d.src)
            if cur is None or cur.seq < d.seq:
                waits[d.src] = d
            for k, v in d.vc.items():
                if S.get(k, 0) < v:
                    S[k] = v
        op.waits = list(waits.values())
        op.vc = dict(S)
        op.vc[op.src] = op.seq
        for b in writes:
            b.w = op
            b.r = {}
        for b in reads:
            if b.w is not op:
                b.r[op.src] = op
        self.q[eng].append(op)
        return op

    @staticmethod
    def deps_of(bufs):
        out = []
        for b in bufs:
            if b.w is not None:
                out.append(b.w)
            out.extend(b.r.values())
        return out

    def barrier(self):
        lasts = []
        for e in self.ENG:
            for op in reversed(self.q[e]):
                if (not op.dma) and op.fn is not None:
                    lasts.append(op)
                    break
        lasts.extend(self.slot_last.values())
        for e in self.ENG:
            S = self.S[e]
            waits = {}
            for d in lasts:
                if S.get(d.src, 0) >= d.seq:
                    continue
                d.signal = True
                waits[d.src] = d
            for d in lasts:
                for k, v in d.vc.items():
                    if S.get(k, 0) < v:
                        S[k] = v
            if waits:
                op = Op()
                op.eng = e
                op.fn = None
                op.signal = False
                op.dma = False
                op.waits = list(waits.values())
                op.src = None
                op.seq = 0
                op.vc = {}
                self.q[e].append(op)

    def emit(self, nc, stack):
        sems = {}
        for e in self.ENG:
            cnt = 0
            for op in self.q[e]:
                if op.fn is None:
                    continue
                if op.dma:
                    op.val = 16 * op.seq
                else:
                    if op.signal:
                        cnt += 1
                    op.val = cnt
                if op.signal and op.src not in sems:
                    nm = "s_" + ("_".join(str(x) for x in op.src) if isinstance(op.src, tuple) else op.src)
                    sems[op.src] = stack.enter_context(nc.semaphore(nm))
        block = stack.enter_context(nc.Block())
        q = self.q

        def run(e, eng):
            for op in q[e]:
                for d in op.waits:
                    eng.wait_ge(sems[d.src], d.val)
                if op.fn is None:
                    continue
                ins = op.fn(eng)
                if op.signal:
                    ins.then_inc(sems[op.src], 16 if op.dma else 1)

        @block.tensor
        def _(eng):
            run("pe", eng)

        @block.scalar
        def _(eng):
            run("act", eng)

        @block.vector
        def _(eng):
            run("dve", eng)

        @block.gpsimd
        def _(eng):
            run("pool", eng)

        @block.sync
        def _(eng):
            run("sp", eng)


def _bf(x):
    return np.ascontiguousarray(x.astype(ml_dtypes.bfloat16))


_CONST_CACHE = {}


def host_consts():
    if _CONST_CACHE:
        return _CONST_CACHE
    c = {}
    c["ident_f"] = np.eye(128, dtype=np.float32)
    c["ident_b"] = _bf(np.eye(128, dtype=np.float32))
    c["ones_b"] = _bf(np.ones((128, 128), np.float32))
    s_ = np.arange(128)[:, None]
    c_ = np.arange(128)[None, :]
    v = -1.0 / 16.0
    tri = np.stack([(s_ <= c_) * v, (s_ > c_) * v, (s_ >= c_) * v, (s_ < c_) * v]).astype(np.float32)
    c["tri"] = np.ascontiguousarray(tri.transpose(1, 0, 2).reshape(128, 4 * 128))
    mk = np.stack([np.tile((s_ <= c_).astype(np.float32), (1, 4)), np.tile((s_ >= c_).astype(np.float32), (1, 4))])
    c["mask"] = np.ascontiguousarray(mk.transpose(1, 0, 2).reshape(128, 2 * 512))
    t = np.arange(LS)
    row, col = t // 64, t % 64
    inv = (10000.0 ** (-np.arange(0, 32, 2, dtype=np.float32) / 32.0)).astype(np.float32)
    cos = np.zeros((64, LS), np.float32)
    sin = np.zeros((64, LS), np.float32)
    perm = np.zeros(64, np.int64)
    for r in range(64):
        pos = (row if r < 32 else col).astype(np.float32)
        j = r % 16
        ang = (pos * inv[j]).astype(np.float32)
        cos[r] = np.cos(ang)
        x2 = (r % 32) >= 16
        sin[r] = np.sin(ang) * (1.0 if x2 else -1.0)
        perm[r] = r - 16 if x2 else r + 16
    c["rope"] = np.ascontiguousarray(np.concatenate([cos, sin], axis=1))
    c["perm"] = perm
    for L in (LP, LS):
        tt = np.linspace(0.0, 1.0, L, dtype=np.float32)[:, None]
        w = (2.0 * math.pi * np.arange(L, dtype=np.float32)[:, None] / L).astype(np.float32)
        f = np.linspace(1e-4, 15, 16, dtype=np.float32)
        z = np.concatenate([tt, np.cos(f * w), -np.sin(f * w)], axis=-1).astype(np.float32)
        c[f"zemb{L}"] = np.ascontiguousarray(z.T)
        c[f"negt{L}"] = np.ascontiguousarray((-tt[:, 0]).reshape(L // 128, 128).T)
        n = np.arange(L, dtype=np.float64)[:, None]
        fq = np.arange(L, dtype=np.float64)[None, :]
        th = math.pi * (2 * fq + 1) * n / (2 * L)
        C = np.cos(th)
        S = np.sin(th)
        c[f"dftC{L}"] = _bf(C.astype(np.float32))
        c[f"dftS{L}"] = _bf(S.astype(np.float32))
        c[f"dftCT{L}"] = _bf(C.T.astype(np.float32))
        c[f"dftST{L}"] = _bf(S.T.astype(np.float32))
    mind = math.log(1e-2) / 1.5
    maxd = math.log(1e-2) / 0.3
    delta = np.abs(np.linspace(mind, maxd, 512, dtype=np.float32))
    c["delta"] = np.ascontiguousarray(np.tile(delta[None, :], (128, 1)).astype(np.float32))
    _CONST_CACHE.update(c)
    return c


CONST_DRAM = [("ident_f", [128, 128], F32), ("ident_b", [128, 128], BF16), ("ones_b", [128, 128], BF16),
              ("tri", [128, 512], F32), ("mask", [128, 1024], F32), ("rope", [64, 2048], F32),
              ("zemb256", [33, 256], F32), ("zemb1024", [33, 1024], F32),
              ("negt256", [128, 2], F32), ("negt1024", [128, 8], F32), ("delta", [128, 512], F32),
              ("dftC256", [256, 256], BF16), ("dftS256", [256, 256], BF16), ("dftCT256", [256, 256], BF16), ("dftST256", [256, 256], BF16),
              ("dftC1024", [1024, 1024], BF16), ("dftS1024", [1024, 1024], BF16), ("dftCT1024", [1024, 1024], BF16), ("dftST1024", [1024, 1024], BF16)]

IN_DRAM = [("x_in", [T, D]), ("cvec", [128, 32]), ("ckv_c", [NL, PAST, 256]), ("kr_c", [NL, PAST, 64]),
           ("sf_c", [NL, 256, 128]), ("sb_c", [NL, 256, 128]),
           ("w_mod", [NL, D, 6 * D]), ("w_in", [NL, D, WIN_EXT]), ("w_out", [NL, D, D]),
           ("w_ffn_in", [NL, D, 2 * DFF]), ("w_ffn_out", [NL, DFF, D]), ("w_uq", [NL, 384, 2048]), ("w_ukv", [NL, 256, 2048]),
           ("vecs", [NL, 128, NV]), ("lnf", [128, 16]), ("fw1", [NL, 33, 64]), ("fw2", [NL, 64, 64]), ("fw3", [NL, 64, 2048]),
           ("wa_aug", [NL, 2, 17, 256])]

OUT_DRAM = [("y", [T, D]), ("o_ckv", [2, NL, LP, 256]), ("o_kr", [2, NL, LP, 64]), ("o_sf", [2, NL, 256, 128]), ("o_sb", [2, NL, 256, 128])]


class _Stop(Exception):
    pass


def build(nlayers=NL, dbg=None, stop=None, knob=None):
    nc = bass.Bass("TRN2", target_bir_lowering=False)
    dr = {}
    for name, shape in IN_DRAM:
        dr[name] = nc.dram_tensor(name, list(shape), F32, kind="ExternalInput").ap()
    for name, shape, dt in CONST_DRAM:
        dr[name] = nc.dram_tensor(name, list(shape), dt, kind="ExternalInput").ap()
    for name, shape in OUT_DRAM:
        dr[name] = nc.dram_tensor(name, list(shape), F32, kind="ExternalOutput").ap()
    dbg_out = {}
    if dbg:
        for name, shape in dbg.items():
            dbg_out[name] = nc.dram_tensor("dbg_" + name, list(shape), F32, kind="ExternalOutput").ap()
    xT_d = nc.dram_tensor("xT_scr", [D, T], F32, kind="Internal").ap()
    xTv = xT_d.rearrange("(k p) t -> p k t", p=128)

    s = Sched()
    st = contextlib.ExitStack()
    uid = [0]
    stg_ctr = [0]

    def stage(name):
        stg_ctr[0] += 1
        if stop is not None and stg_ctr[0] == stop:
            print('STOP at stage', stop, name)
            raise _Stop()

    class Region:
        def __init__(self, name, start, size):
            self.name, self.start, self.size, self.cur = name, start, size, start

        def reset(self):
            self.cur = self.start

        def alloc(self, n, dt):
            sz = n * (4 if dt == F32 else 2)
            sz = (sz + 31) // 32 * 32
            assert self.cur + sz <= self.start + self.size, (self.name, self.cur - self.start, sz, self.size)
            uid[0] += 1
            t = nc.alloc_sbuf_tensor_at(f"{self.name}_{uid[0]}", [128, n], dt, offset=self.cur)
            self.cur += sz
            return t

    sizes = [("C", 20480), ("W", 36864), ("A", 32768), ("B", 32768), ("IN", 67584)]
    regs = {}
    cur = SBASE
    for nm, sz in sizes:
        regs[nm] = Region(nm, cur, sz)
        cur += sz
    regs["T"] = Region("T", cur, SEND - cur)
    RC, RW, RA, RB, RIN, RT = (regs[k] for k in ("C", "W", "A", "B", "IN", "T"))
    RF = Region("F", regs["W"].start, SEND - regs["W"].start)

    PS = [st.enter_context(nc.psum_tensor(f"ps{i}", [128, 1024], F32)) for i in range(4)]
    bPS = [Buf() for _ in range(8)]

    def bank(b, n=512, off=0):
        return PS[b // 2][:, (b % 2) * 512 + off:(b % 2) * 512 + off + n]

    def MM(out, lhsT, rhs, st_, sp_, R, W):
        s.add("pe", lambda e: e.matmul(out, lhsT=lhsT, rhs=rhs, start=st_, stop=sp_), R, W)

    def AC(out, in_, func, R, W, bias=None, scale=None, accum=None):
        kw = {}
        if bias is not None:
            kw["bias"] = bias
        if scale is not None:
            kw["scale"] = scale
        if accum is not None:
            kw["accum_out"] = accum
        s.add("act", lambda e: e.activation(out=out, in_=in_, func=func, **kw), R, W)

    def TT(eng, out, a, b, op, R, W):
        s.add(eng, lambda e: e.tensor_tensor(out=out, in0=a, in1=b, op=op), R, W)

    def TS(eng, out, a, s1, s2, op0, op1, R, W):
        if s2 is None:
            s.add(eng, lambda e: e.tensor_scalar(out=out, in0=a, scalar1=s1, scalar2=None, op0=op0), R, W)
        else:
            s.add(eng, lambda e: e.tensor_scalar(out=out, in0=a, scalar1=s1, scalar2=s2, op0=op0, op1=op1), R, W)

    def STT(eng, out, a, sc, b, op0, op1, R, W):
        s.add(eng, lambda e: e.scalar_tensor_tensor(out=out, in0=a, scalar=sc, in1=b, op0=op0, op1=op1), R, W)

    def CP(eng, out, in_, R, W, after=()):
        if eng == "act":
            s.add("act", lambda e: e.copy(out=out, in_=in_), R, W, after=after)
        else:
            s.add(eng, lambda e: e.tensor_copy(out=out, in_=in_), R, W, after=after)

    def MS(eng, ap, val, W):
        s.add(eng, lambda e: e.memset(ap, val), (), W)

    def RCP(out, in_, R, W):
        s.add("dve", lambda e: e.reciprocal(out=out, in_=in_), R, W)

    def DMA(out, in_, R, W):
        s.add("sp", lambda e: e.dma_start(out=out, in_=in_), R, W, dma=True)

    def v3(ap2, k):
        return ap2.rearrange("p (k c) -> p k c", k=k)

    bC = Buf()
    ident_f = RC.alloc(128, F32)
    ident_b = RC.alloc(128, BF16)
    ones_b = RC.alloc(128, BF16)
    tri = RC.alloc(512, F32)
    mask = RC.alloc(1024, F32)
    rope = RC.alloc(2048, F32)
    vecs = [RC.alloc(NV, F32) for _ in range(NL)]
    lnf = RC.alloc(16, F32)
    cvec = RC.alloc(32, F32)
    scT = RC.alloc(32, BF16)
    modTs = [RC.alloc(192, F32) for _ in range(2)]
    gsc = RC.alloc(64, F32)
    epsb = RC.alloc(1, F32)
    fb = RC.alloc(4, F32)
    for tile_, name in ((ident_f, "ident_f"), (ident_b, "ident_b"), (ones_b, "ones_b"), (tri, "tri"), (mask, "mask"), (lnf, "lnf"), (cvec, "cvec")):
        DMA(tile_[:], dr[name], [], [bC])
    DMA(rope[0:64, :], dr["rope"], [], [bC])
    for l in range(NL):
        DMA(vecs[l][:], dr["vecs"][l], [], [bC])
    MS("pool", epsb[:], EPS, [bC])
    AC(scT[:], cvec[:], AF.Silu, [bC], [bC])

    NB = 3
    stg = [RW.alloc(2048, F32) for _ in range(NB)]
    wbf = [RW.alloc(2048, BF16) for _ in range(NB)]
    bStg = [Buf() for _ in range(NB)]
    bWb = [[Buf(), Buf(), Buf()] for _ in range(NB)]
    wctr = [0, 0, 0]

    def split_cast(dst, sv, kc, bsrc, bdst3):
        if kc < 8:
            eng = "act" if wctr[2] % 2 == 0 else "dve"
            wctr[2] += 1
            CP(eng, dst, sv, [bsrc], bdst3)
            return lambda k: bdst3[0]
        ka = int(round(kc * 7 / 16.0))
        kb = int(round(kc * 13 / 16.0))
        prev = Sched.deps_of(bdst3)
        for p, (eng, a_, b_) in enumerate((("act", 0, ka), ("dve", ka, kb), ("pool", kb, kc))):
            if b_ > a_:
                CP(eng, dst[:, a_:b_, :], sv[:, a_:b_, :], [bsrc], [bdst3[p]], after=prev)
        return lambda k: bdst3[0 if k < ka else (1 if k < kb else 2)]

    def wload(src, kc, ncol, mode="cast", parts=128):
        n = kc * ncol
        assert n <= 2048
        if mode == "bf16":
            i = wctr[1] % NB
            wctr[1] += 1
            dst = v3(wbf[i][0:parts, 0:n], kc)
            DMA(dst, src, [], bWb[i])
            return dst, (lambda k, i=i: bWb[i][0])
        i = wctr[0] % NB
        wctr[0] += 1
        sv = v3(stg[i][0:parts, 0:n], kc)
        DMA(sv, src, [], [bStg[i]])
        if mode == "f32":
            return sv, (lambda k, i=i: bStg[i])
        j = wctr[1] % NB
        wctr[1] += 1
        dst = v3(wbf[j][0:parts, 0:n], kc)
        return dst, split_cast(dst, sv, kc, bStg[i], bWb[j])

    def make_ws2(RF, nstg=3):
        stg2 = [RF.alloc(4096, F32) for _ in range(nstg)]
        wb2 = [RF.alloc(4096, BF16) for _ in range(2)]
        bS2 = [Buf() for _ in range(nstg)]
        bW2 = [[Buf(), Buf(), Buf()] for _ in range(2)]
        c2 = [0, 0]

        def wload2(src, kc, ncol):
            n = kc * ncol
            assert n <= 4096
            i = c2[0] % nstg
            c2[0] += 1
            j = c2[1] % 2
            c2[1] += 1
            sv = v3(stg2[i][:, 0:n], kc)
            DMA(sv, src, [], [bS2[i]])
            dst = v3(wb2[j][:, 0:n], kc)
            return dst, split_cast(dst, sv, kc, bS2[i], bW2[j])
        return wload2

    def wsrc(w2d, k0, kc, c0, ncol):
        return w2d[k0 * 128:(k0 + kc) * 128, c0:c0 + ncol].rearrange("(k p) c -> p k c", p=128)

    try:
        bXd = [[Buf() for _ in range(16)] for _ in range(3)]
        xin_t = [RA.alloc(2048, F32) for _ in range(2)]
        xtr_t = [RB.alloc(2048, F32) for _ in range(2)]
        bxin = [Buf(), Buf()]
        bxtr = [Buf(), Buf()]
        for tc in range(12):
            xi, bi = xin_t[tc % 2], bxin[tc % 2]
            xo, bo = xtr_t[tc % 2], bxtr[tc % 2]
            DMA(xi[:], dr["x_in"][tc * 128:(tc + 1) * 128, :], [], [bi])
            for half in range(2):
                u = (tc * 2 + half) % 4
                for kk in range(8):
                    k = half * 8 + kk
                    MM(PS[u][:, kk * 128:(kk + 1) * 128], xi[:, k * 128:(k + 1) * 128], ident_f[:], True, True, [bi, bC], [bPS[2 * u + kk // 4]])
                CP("act" if half else "dve", xo[:, half * 1024:(half + 1) * 1024], PS[u][:, :], [bPS[2 * u], bPS[2 * u + 1]], [bo])
            DMA(xTv[:, :, tc * 128:(tc + 1) * 128], v3(xo[:], 16), [bo], bXd[tc // 4])
        s.barrier()
        stage('prepass')

        def rms_rstd(sq_views, nk, width, denom, psb, out_rstd, Rsq, Wout):
            for k in range(nk):
                MM(bank(psb, width), ones_b[:], sq_views(k), k == 0, k == nk - 1, [bC] + Rsq, [bPS[psb]])
            AC(out_rstd, bank(psb, width), AF.Sqrt, [bPS[psb], bC], Wout, bias=epsb[:], scale=1.0 / denom)
            RCP(out_rstd, out_rstd, Wout, Wout)

        for l in range(nlayers):
            vl = vecs[l]
            modT = modTs[l % 2]
            scv = v3(scT[:], 16)
            if l == 0:
                pv = PS[0][:, 0:192].rearrange("p (j m) -> p m j", j=2)
                RF.reset()
                wl2 = make_ws2(RF)
                for mp in range(48):
                    wt, bk = wl2(wsrc(dr["w_mod"][l], 0, 16, mp * 256, 256), 16, 256)
                    for c in range(2):
                        m = 2 * mp + c
                        for k in range(16):
                            MM(pv[:, m, :], wt[:, k, c * 128:(c + 1) * 128], scv[:, k, :], k == 0, k == 15, [bk(k), bC], [bPS[0]])
                for j in range(2):
                    TT("dve", modT[:, j * 96:(j + 1) * 96], PS[0][:, j * 96:(j + 1) * 96], vl[:, V_BMOD:V_BMOD + 96], ALU.add, [bPS[0], bC], [bC])
            for j in range(2):
                STT("dve", gsc[:, (j * 2) * 16:(j * 2) * 16 + 16], modT[:, j * 96 + 16:j * 96 + 32], 1.0, vl[:, V_LNMIX:V_LNMIX + 16], ALU.add, ALU.mult, [bC], [bC])
                STT("dve", gsc[:, (j * 2 + 1) * 16:(j * 2 + 1) * 16 + 16], modT[:, j * 96 + 64:j * 96 + 80], 1.0, vl[:, V_LNFFN:V_LNFFN + 16], ALU.add, ALU.mult, [bC], [bC])
            TT("dve", fb[0:64, 0:1], vl[0:64, V_FR0:V_FR0 + 1], vl[0:64, V_FB1:V_FB1 + 1], ALU.mult, [bC], [bC])
            TT("dve", fb[0:64, 1:2], vl[0:64, V_FR1:V_FR1 + 1], vl[0:64, V_FB2:V_FB2 + 1], ALU.mult, [bC], [bC])
            s.barrier()
            stage('mod')

            for g in range(2):
                j = g
                L = LP if g == 0 else LS
                nseq = 2 if g == 0 else 1
                nblk = 1 if g == 0 else 2
                Ltot = nseq * L
                tok0 = 0 if g == 0 else TP
                Lk = L if g == 0 else L + PAST
                nf = L // 128
                for r_ in (RA, RB, RIN, RT):
                    r_.reset()
                hT = RA.alloc(16 * Ltot, BF16)
                bH = [Buf() for _ in range(nblk)]
                xblk = RB.alloc(16 * 512, F32)
                bxb = Buf()
                sqt = RT.alloc(16 * 512, BF16)
                bsq = Buf()
                rstd = RT.alloc(512, F32)
                brs = Buf()
                hv = v3(hT[:], 16)
                ntmp = RIN.alloc(2 * 512, F32)
                bnt = [Buf(), Buf()]
                for b in range(nblk):
                    blk = (tok0 // 512) + b
                    DMA(v3(xblk[:], 16), xTv[:, :, blk * 512:(blk + 1) * 512], bXd[blk], [bxb])
                    AC(sqt[:], xblk[:], AF.Square, [bxb], [bsq])
                    rms_rstd(lambda k: sqt[:, k * 512:(k + 1) * 512], 16, 512, float(D), 7, rstd[:], [bsq], [brs])
                    for k in range(16):
                        nt_, bn_ = ntmp[:, (k % 2) * 512:(k % 2) * 512 + 512], bnt[k % 2]
                        TT("dve", nt_, xblk[:, k * 512:(k + 1) * 512], rstd[:], ALU.mult, [bxb, brs], [bn_])
                        AC(hv[:, k, b * 512:(b + 1) * 512], nt_, AF.Identity, [bn_, bC], [bH[b]],
                           bias=modT[:, j * 96 + k:j * 96 + k + 1], scale=gsc[:, (j * 2) * 16 + k:(j * 2) * 16 + k + 1])
                s.barrier()
                stage('stepA')
                RB.reset()
                RIN.reset()
                RT.reset()
                hyT = RIN.alloc(12 * Ltot, BF16)
                qT = RIN.alloc(2 * Ltot, BF16)
                kT = RIN.alloc(2 * Ltot, BF16)
                vT = RIN.alloc(4 * Ltot, BF16)
                sgT = RIN.alloc(4 * Ltot, BF16)
                afT = RIN.alloc(Ltot, BF16)
                abT = RIN.alloc(Ltot, BF16)
                cqnT = RIN.alloc(3 * Ltot, BF16)
                keysT = RIN.alloc(2 * nseq * Lk, BF16)
                krT = RIN.alloc(nseq * Lk, BF16)
                bHY = [Buf() for _ in range(12)]
                bQ, bK, bV, bSG, bAF, bAB, bCQN, bKEYS, bKR = (Buf() for _ in range(9))
                hyv = v3(hyT[:], 12)
                qv, kv_, vv, sgv = v3(qT[:], 2), v3(kT[:], 2), v3(vT[:], 4), v3(sgT[:], 4)
                cqnv = v3(cqnT[:], 3)
                keysv = v3(keysT[:], 2)
                ctmp = RT.alloc(Ltot, F32)
                bct = Buf()
                cqf = RT.alloc(3 * Ltot, F32)
                bcqf = Buf()
                sq3 = RT.alloc(3 * 512, BF16)
                bsq3 = Buf()
                rs2 = RT.alloc(512, F32)
                brs2 = Buf()
                cqfv = v3(cqf[:], 3)
                MS("pool", afT[0:32, :], 1.0, [bAF])
                MS("pool", abT[0:32, :], 1.0, [bAB])
                ckvf = RB.alloc(2 * Ltot, F32)
                bckvf = Buf()
                ckvfv = v3(ckvf[:], 2)
                krf = RB.alloc(Ltot, F32)
                bkrf = Buf()
                otr = RB.alloc(2 * 384, F32)
                krtmp = RB.alloc(Ltot, F32)
                bkrt = Buf()
                krtmp2 = RB.alloc(Ltot, F32)
                bkrt2 = Buf()
                botr = [Buf(), Buf()]

                chunks = []
                for i in range(12):
                    chunks.append(("hy", C_HY + i * 128, 128, i))
                for i in range(2):
                    chunks.append(("q", C_Q + i * 128, 128, i))
                for i in range(2):
                    chunks.append(("k", C_K + i * 128, 128, i))
                for i in range(4):
                    chunks.append(("v", C_V + i * 128, 128, i))
                for i in range(4):
                    chunks.append(("g", C_G + i * 128, 128, i))
                chunks.append(("af", C_AF, 16, 0))
                chunks.append(("ab", C_AB, 16, 0))
                for i in range(3):
                    chunks.append(("cq", C_CQ + i * 128, 128, i))
                for i in range(2):
                    chunks.append(("ckv", C_CKV + i * 128, 128, i))
                chunks.append(("kr", C_KR, 64, 0))
                if g == 1:
                    chunks.append(("krsw", C_KRSW, 64, 0))

                import os as _os
                for ci, (kind, c0, ncol, idx) in enumerate(chunks):
                    if g == 1 and ci >= int(_os.environ.get('KCH', '999')):
                        break
                    wt, bw = wload(wsrc(dr["w_in"][l], 0, 16, c0, ncol), 16, ncol)
                    u = ci % 4
                    for b in range(nblk):
                        for k in range(16):
                            MM(PS[u][0:ncol, b * 512:(b + 1) * 512], wt[:, k, :], hv[:, k, b * 512:(b + 1) * 512], k == 0, k == 15, [bw(k), bH[b]], [bPS[2 * u + b]])
                    pb = [bPS[2 * u + b] for b in range(nblk)]
                    psu = PS[u]
                    if kind == "hy":
                        cw = lambda tap: vl[:, V_CW + tap * 12 + idx:V_CW + tap * 12 + idx + 1]
                        for sq_ in range(nseq):
                            a, e_ = sq_ * L, (sq_ + 1) * L
                            AC(ctmp[:, a:e_], psu[:, a:e_], AF.Identity, pb + [bC], [bct], bias=vl[:, V_CB + idx:V_CB + idx + 1], scale=cw(1))
                            STT("dve", ctmp[:, a + 1:e_], psu[:, a:e_ - 1], cw(0), ctmp[:, a + 1:e_], ALU.mult, ALU.add, pb + [bC, bct], [bct])
                            STT("dve", hyv[:, idx, a:e_ - 1], psu[:, a + 1:e_], cw(2), ctmp[:, a:e_ - 1], ALU.mult, ALU.add, pb + [bC, bct], [bHY[idx]])
                            CP("pool", hyv[:, idx, e_ - 1:e_], ctmp[:, e_ - 1:e_], [bct], [bHY[idx]])
                    elif kind == "q":
                        AC(qv[:, idx, :], psu[:, 0:Ltot], AF.Identity, pb, [bQ], scale=0.125)
                    elif kind == "k":
                        CP("dve", kv_[:, idx, :], psu[:, 0:Ltot], pb, [bK])
                    elif kind == "v":
                        CP("dve", vv[:, idx, :], psu[:, 0:Ltot], pb, [bV])
                    elif kind == "g":
                        AC(sgv[:, idx, :], psu[:, 0:Ltot], AF.Silu, pb, [bSG])
                    elif kind == "af":
                        CP("dve", afT[0:16, :], psu[0:16, 0:Ltot], pb, [bAF])
                    elif kind == "ab":
                        CP("dve", abT[0:16, :], psu[0:16, 0:Ltot], pb, [bAB])
                    elif kind == "cq":
                        CP("dve", cqfv[:, idx, :], psu[:, 0:Ltot], pb, [bcqf])
                        if idx == 2:
                            for b in range(nblk):
                                sl = slice(b * 512, (b + 1) * 512)
                                for k in range(3):
                                    AC(sq3[:, k * 512:(k + 1) * 512], cqfv[:, k, sl], AF.Square, [bcqf], [bsq3])
                                rms_rstd(lambda k: sq3[:, k * 512:(k + 1) * 512], 3, 512, 384.0, 7, rs2[:], [bsq3], [brs2])
                                for k in range(3):
                                    STT("dve", cqnv[:, k, sl], cqfv[:, k, sl], vl[:, V_QN + k:V_QN + k + 1], rs2[:], ALU.mult, ALU.mult, [bcqf, brs2, bC], [bCQN])
                    elif kind == "ckv":
                        CP("dve", ckvfv[:, idx, :], psu[:, 0:Ltot], pb, [bckvf])
                        if idx == 1:
                            for b in range(nblk):
                                sl = slice(b * 512, (b + 1) * 512)
                                for k in range(2):
                                    AC(sq3[:, k * 512:(k + 1) * 512], ckvfv[:, k, sl], AF.Square, [bckvf], [bsq3])
                                rms_rstd(lambda k: sq3[:, k * 512:(k + 1) * 512], 2, 512, 256.0, 7, rs2[:], [bsq3], [brs2])
                                for k in range(2):
                                    STT("dve", ckvfv[:, k, sl], ckvfv[:, k, sl], vl[:, V_KVN + k:V_KVN + k + 1], rs2[:], ALU.mult, ALU.mult, [bckvf, brs2, bC], [bckvf])
                            for sq_ in range(nseq):
                                for k in range(2):
                                    CP("pool", keysv[:, k, sq_ * Lk:sq_ * Lk + L], ckvfv[:, k, sq_ * L:(sq_ + 1) * L], [bckvf], [bKEYS])
                    elif kind == "kr":
                        if g == 0:
                            CP("dve", krf[0:64, 0:Ltot], psu[0:64, 0:Ltot], pb, [bkrf])
                            CP("pool", krT[0:64, 0:Ltot], krf[0:64, 0:Ltot], [bkrf], [bKR])
                        else:
                            TT("dve", krtmp[0:64, :], psu[0:64, 0:Ltot], rope[0:64, 0:LS], ALU.mult, pb + [bC], [bkrt])
                    elif kind == "krsw":
                        TT("dve", krtmp2[0:64, :], psu[0:64, 0:Ltot], rope[0:64, LS:2 * LS], ALU.mult, pb + [bC], [bkrt2])
                        TT("pool", krT[0:64, 0:L], krtmp[0:64, :], krtmp2[0:64, :], ALU.add, [bkrt, bkrt2], [bKR])
                if g == 0:
                    for sq_ in range(2):
                        for tcn in range(2):
                            ot, bo = otr[:, ((sq_ * 2 + tcn) % 2) * 384:((sq_ * 2 + tcn) % 2) * 384 + 384], botr[(sq_ * 2 + tcn) % 2]
                            ts_ = slice(sq_ * L + tcn * 128, sq_ * L + (tcn + 1) * 128)
                            for k in range(2):
                                MM(bank(6, 128, k * 128), ckvfv[:, k, ts_], ident_f[:], True, True, [bckvf, bC], [bPS[6]])
                            MM(bank(6, 64, 256), krf[0:64, ts_], ident_f[0:64, 0:64], True, True, [bkrf, bC], [bPS[6]])
                            CP("act", ot[:, 0:320], bank(6, 320), [bPS[6]], [bo])
                            DMA(dr["o_ckv"][sq_, l, tcn * 128:(tcn + 1) * 128, :], ot[:, 0:256], [bo], [])
                            DMA(dr["o_kr"][sq_, l, tcn * 128:(tcn + 1) * 128, :], ot[:, 256:320], [bo], [])
                elif int(_os.environ.get('KCACHE', '9')) > 0:
                    cst = RB.alloc(2 * 384, F32)
                    bcst = [Buf(), Buf()]
                    kc2 = int(_os.environ.get('KCACHE', '9'))
                    for jc in range(2):
                        if kc2 < 9:
                            cs = cst[:, jc * 384:(jc + 1) * 384]
                            if kc2 >= 1:
                                MS("pool", cs[:, 320:384], 0.0, [bcst[jc]])
                                DMA(cs[:, 0:256], dr["ckv_c"][l, jc * 128:(jc + 1) * 128, :], [], [bcst[jc]])
                            if kc2 >= 2:
                                DMA(cs[:, 256:320], dr["kr_c"][l, jc * 128:(jc + 1) * 128, :], [], [bcst[jc]])
                            if kc2 >= 3:
                                for k in range(3):
                                    MM(bank(6, 128, k * 128), cs[:, k * 128:(k + 1) * 128], ident_f[:], True, True, [bcst[jc], bC], [bPS[6]])
                            if kc2 >= 4:
                                for k in range(2):
                                    CP("act", keysv[:, k, L + jc * 128:L + (jc + 1) * 128], bank(6, 128, k * 128), [bPS[6]], [bKEYS])
                            continue
                        cs = cst[:, jc * 384:(jc + 1) * 384]
                        MS("pool", cs[:, 320:384], 0.0, [bcst[jc]])
                        DMA(cs[:, 0:256], dr["ckv_c"][l, jc * 128:(jc + 1) * 128, :], [], [bcst[jc]])
                        DMA(cs[:, 256:320], dr["kr_c"][l, jc * 128:(jc + 1) * 128, :], [], [bcst[jc]])
                        for k in range(3):
                            MM(bank(6, 128, k * 128), cs[:, k * 128:(k + 1) * 128], ident_f[:], True, True, [bcst[jc], bC], [bPS[6]])
                        for k in range(2):
                            CP("act", keysv[:, k, L + jc * 128:L + (jc + 1) * 128], bank(6, 128, k * 128), [bPS[6]], [bKEYS])
                        CP("act", krT[0:64, L + jc * 128:L + (jc + 1) * 128], bank(6, 128, 256)[0:64, :], [bPS[6]], [bKR])
                s.barrier()
                stage('stepB')
                RA.reset()
                RB.reset()
                RT.reset()
                mixT = RB.alloc(16 * Ltot, BF16)
                mixv = v3(mixT[:], 16)
                bMIX = [Buf() for _ in range(16)]
                if dbg and ("hy_%d_%d" % (l, g)) in dbg_out:
                    for q4 in range(3):
                        dt_ = RT.alloc(4 * Ltot, F32)
                        bd = Buf()
                        CP("dve", v3(dt_[:], 4), hyv[:, q4 * 4:(q4 + 1) * 4, :], bHY, [bd])
                        DMA(dbg_out["hy_%d_%d" % (l, g)][q4 * 512:(q4 + 1) * 512, :].rearrange("(k p) t -> p k t", p=128), v3(dt_[:], 4), [bd], [])
                        s.barrier()
                        RT.reset()

                scale = 192.0 ** -0.5
                qn = [RA.alloc(Ltot, BF16) for _ in range(2)]
                qr = [RA.alloc(Ltot, BF16) for _ in range(2)]
                kn = [RA.alloc(nseq * Lk, BF16) for _ in range(2)]
                vh = [RA.alloc(nseq * Lk, BF16) for _ in range(2)]
                bqn, bqr, bkn, bvh = ([Buf(), Buf()] for _ in range(4))
                pt = [RT.alloc(512, BF16) for _ in range(3)]
                bpt = [Buf() for _ in range(3)]
                rec = [RT.alloc(512, F32) for _ in range(2)]
                brec = [Buf(), Buf()]
                rt1 = RT.alloc(512, F32)
                rt2 = RT.alloc(512, F32)
                brt1, brt2 = Buf(), Buf()
                nkc = Lk // 128
                qblocks = []
                for sq_ in range(nseq):
                    for qb in range(max(1, L // 512)):
                        qn_ = min(L, 512)
                        qblocks.append((sq_, sq_ * L + qb * qn_, qn_))
                ptc = 0
                acc = 0
                for h in range(8):
                    hb = h % 2
                    wq, bwq = wload(wsrc(dr["w_uq"][l], 0, 3, h * 192, 192), 3, 192)
                    if g == 1:
                        wqs, bwqs = wload(wsrc(dr["w_uq"][l], 0, 3, 1536 + h * 64, 64), 3, 64)
                    wkv, bwkv = wload(wsrc(dr["w_ukv"][l], 0, 2, h * 256, 256), 2, 256)
                    for b in range(nblk):
                        sl = slice(b * 512, (b + 1) * 512)
                        for k in range(3):
                            MM(bank(7), wq[:, k, 0:128], cqnv[:, k, sl], k == 0, k == 2, [bwq(0), bCQN], [bPS[7]])
                        CP("act", qn[hb][:, sl], bank(7), [bPS[7]], [bqn[hb]])
                        for k in range(3):
                            MM(PS[3][0:64, 512:1024], wq[:, k, 128:192], cqnv[:, k, sl], k == 0, k == 2, [bwq(0), bCQN], [bPS[7]])
                        if g == 0:
                            CP("dve", qr[hb][0:64, sl], PS[3][0:64, 512:1024], [bPS[7]], [bqr[hb]])
                        else:
                            TT("dve", rt1[0:64, :], PS[3][0:64, 512:1024], rope[0:64, b * 512:(b + 1) * 512], ALU.mult, [bPS[7], bC], [brt1])
                            for k in range(3):
                                MM(PS[3][0:64, 512:1024], wqs[:, k, :], cqnv[:, k, sl], k == 0, k == 2, [bwqs(0), bCQN], [bPS[7]])
                            TT("dve", rt2[0:64, :], PS[3][0:64, 512:1024], rope[0:64, LS + b * 512:LS + (b + 1) * 512], ALU.mult, [bPS[7], bC], [brt2])
                            TT("pool", qr[hb][0:64, sl], rt1[0:64, :], rt2[0:64, :], ALU.add, [brt1, brt2], [bqr[hb]])
                    ktot = nseq * Lk
                    for c0 in range(0, ktot, 512):
                        n_ = min(512, ktot - c0)
                        for k in range(2):
                            MM(bank(7, n_), wkv[:, k, 0:128], keysv[:, k, c0:c0 + n_], k == 0, k == 1, [bwkv(0), bKEYS], [bPS[7]])
                        CP("act", kn[hb][:, c0:c0 + n_], bank(7, n_), [bPS[7]], [bkn[hb]])
                    for c0 in range(0, ktot, 512):
                        n_ = min(512, ktot - c0)
                        for jj in range(n_ // 128):
                            for k in range(2):
                                MM(bank(7, 128, jj * 128), keysv[:, k, c0 + jj * 128:c0 + (jj + 1) * 128], wkv[:, k, 128:256], k == 0, k == 1, [bwkv(0), bKEYS], [bPS[7]])
                        CP("dve", vh[hb][:, c0:c0 + n_], bank(7, n_), [bPS[7]], [bvh[hb]])
                    for (sq_, q0, qn_) in qblocks:
                        ob, db = 3 + 2 * (acc % 2), 4 + 2 * (acc % 2)
                        acc += 1
                        for jk in range(nkc):
                            sb_ = ptc % 3
                            ptc += 1
                            kc0 = sq_ * Lk + jk * 128
                            MM(bank(sb_, qn_), kn[hb][:, kc0:kc0 + 128], qn[hb][:, q0:q0 + qn_], True, False, [bkn[hb], bqn[hb]], [bPS[sb_]])
                            MM(bank(sb_, qn_), krT[0:64, kc0:kc0 + 128], qr[hb][0:64, q0:q0 + qn_], False, True, [bKR, bqr[hb]], [bPS[sb_]])
                            AC(pt[sb_][:, 0:qn_], bank(sb_, qn_), AF.Exp, [bPS[sb_]], [bpt[sb_]], scale=scale)
                            MM(bank(ob, qn_), vh[hb][:, kc0:kc0 + 128], pt[sb_][:, 0:qn_], jk == 0, jk == nkc - 1, [bvh[hb], bpt[sb_]], [bPS[ob]])
                            MM(bank(db, qn_), ones_b[:], pt[sb_][:, 0:qn_], jk == 0, jk == nkc - 1, [bC, bpt[sb_]], [bPS[db]])
                        rc, brc = rec[acc % 2], brec[acc % 2]
                        RCP(rc[:, 0:qn_], bank(db, qn_), [bPS[db]], [brc])
                        TT("dve", mixv[:, 8 + h, q0:q0 + qn_], bank(ob, qn_), rc[:, 0:qn_], ALU.mult, [bPS[ob], brc], [bMIX[8 + h]])
                s.barrier()
                stage('mla')
                RA.reset()
                RT.reset()

                if dbg and ("glain_%d_%d" % (l, g)) in dbg_out:
                    for nm_, t_, nk_, off_ in (("q", qT, 2, 0), ("k", kT, 2, 256), ("v", vT, 4, 512), ("sg", sgT, 4, 1024)):
                        RT.reset()
                        dt_ = RT.alloc(4 * Ltot, F32)
                        bd = Buf()
                        CP("dve", dt_[:, 0:nk_ * Ltot], t_[:], [bQ, bK, bV, bSG], [bd])
                        DMA(dbg_out["glain_%d_%d" % (l, g)][off_:off_ + nk_ * 128, :].rearrange("(k p) t -> p k t", p=128), v3(dt_[:, 0:nk_ * Ltot], nk_), [bd], [])
                        s.barrier()
                    RT.reset()
                    dt_ = RT.alloc(2 * Ltot, F32)
                    bd = Buf()
                    CP("dve", dt_[0:32, 0:Ltot], afT[0:32, :], [bAF], [bd])
                    CP("dve", dt_[0:32, Ltot:2 * Ltot], abT[0:32, :], [bAB], [bd])
                    DMA(dbg_out["glain_%d_%d" % (l, g)][1536:1568, :], dt_[0:32, 0:Ltot], [bd], [])
                    DMA(dbg_out["glain_%d_%d" % (l, g)][1568:1600, :], dt_[0:32, Ltot:2 * Ltot], [bd], [])
                    s.barrier()
                    RT.reset()
                oacc = RA.alloc(4 * Ltot, F32)
                oav = v3(oacc[:], 4)
                bOA = Buf()
                NT = 2
                ltok = [RT.alloc(256, F32) for _ in range(NT)]
                ep = [RT.alloc(256, F32) for _ in range(NT)]
                em = [RT.alloc(256, F32) for _ in range(NT)]
                er = [RT.alloc(256, F32) for _ in range(NT)]
                qd = [RT.alloc(512, BF16) for _ in range(NT)]
                ki = [RT.alloc(256, BF16) for _ in range(NT)]
                ke = [RT.alloc(256, BF16) for _ in range(NT)]
                vt = [RT.alloc(512, BF16) for _ in range(NT)]
                at = [RT.alloc(512, BF16) for _ in range(NT)]
                bl, bep, bem, ber, bqd, bki, bke, bvt, bat = ([Buf() for _ in range(NT)] for _ in range(9))
                for i0 in range(NT):
                    MS("pool", qd[i0][:], 0.0, [bqd[i0]])
                Sst = RT.alloc(256, F32)
                Sbf = RT.alloc(256, BF16)
                bSst, bSbf = Buf(), Buf()
                waug = RA.alloc(512, F32)
                waub = RA.alloc(512, BF16)
                bwa = Buf()
                for dd in range(2):
                    DMA(waug[0:17, dd * 256:(dd + 1) * 256], dr["wa_aug"][l, dd], [], [bwa])
                CP("dve", waub[0:17, :], waug[0:17, :], [bwa], [bwa])
                it = 0
                for sq_ in range(nseq):
                    for dd in range(2):
                        if g == 0:
                            MS("pool", Sst[:], 0.0, [bSst])
                        else:
                            DMA(v3(Sst[:], 2), dr["sf_c" if dd == 0 else "sb_c"][l].rearrange("(hp p) v -> p hp v", p=128), [], [bSst])
                        CP("pool", Sbf[:], Sst[:], [bSst], [bSbf])
                        aT, bA_ = (afT, bAF) if dd == 0 else (abT, bAB)
                        order = range(nf) if dd == 0 else range(nf - 1, -1, -1)
                        for n_ in order:
                            i_ = it % NT
                            it += 1
                            if knob is not None and it > knob[0]:
                                raise _Stop()
                            ts_ = slice(sq_ * L + n_ * 128, sq_ * L + (n_ + 1) * 128)

                            def kc_(n):
                                if knob is not None and it == knob[0] and n >= knob[1]:
                                    raise _Stop()
                            MM(bank(0, 256), aT[0:17, ts_], waub[0:17, dd * 256:(dd + 1) * 256], True, True, [bA_, bwa], [bPS[0]])
                            AC(ltok[i_][:], bank(0, 256), AF.Exp, [bPS[0]], [bl[i_]], scale=-1.0)
                            AC(ltok[i_][:], ltok[i_][:], AF.Ln, [bl[i_]], [bl[i_]], bias=1.0)
                            kc_(1)
                            triI = tri[:, (2 * dd) * 128:(2 * dd + 1) * 128]
                            triE = tri[:, (2 * dd + 1) * 128:(2 * dd + 2) * 128]
                            for hp in range(2):
                                MM(bank(1, 128, hp * 128), ltok[i_][:, hp * 128:(hp + 1) * 128], triI, True, True, [bl[i_], bC], [bPS[1]])
                            MM(bank(1, 256, 256), triE, ltok[i_][:], True, True, [bl[i_], bC], [bPS[1]])
                            kc_(2)
                            AC(ep[i_][:], bank(1, 256), AF.Exp, [bPS[1]], [bep[i_]])
                            AC(em[i_][:], bank(1, 256), AF.Exp, [bPS[1]], [bem[i_]], scale=-1.0)
                            AC(er[i_][:], bank(1, 256, 256), AF.Exp, [bPS[1]], [ber[i_]])
                            for hp in range(2):
                                for h2 in range(2):
                                    r0 = h2 * 64
                                    TT("dve", qd[i_][r0:r0 + 64, (hp * 2 + h2) * 128:(hp * 2 + h2 + 1) * 128], qv[r0:r0 + 64, hp, ts_], ep[i_][r0:r0 + 64, hp * 128:(hp + 1) * 128], ALU.mult, [bQ, bep[i_], bqd[i_]], [bqd[i_]])
                                TT("dve", ki[i_][:, hp * 128:(hp + 1) * 128], kv_[:, hp, ts_], em[i_][:, hp * 128:(hp + 1) * 128], ALU.mult, [bK, bem[i_]], [bki[i_]])
                            kc_(3)
                            for hp in range(2):
                                MM(bank(2, 128, hp * 128), kv_[:, hp, ts_], ident_b[:], True, True, [bK, bC], [bPS[2]])
                            TT("dve", ke[i_][:], bank(2, 256), er[i_][:], ALU.mult, [bPS[2], ber[i_]], [bke[i_]])
                            for hh in range(4):
                                MM(bank(3, 128, hh * 128), vv[:, hh, ts_], ident_b[:], True, True, [bV, bC], [bPS[3]])
                            CP("act", vt[i_][:], bank(3), [bPS[3]], [bvt[i_]])
                            kc_(4)
                            for hp in range(2):
                                MM(bank(4, 256, hp * 256), ki[i_][:, hp * 128:(hp + 1) * 128], qd[i_][:, hp * 256:(hp + 1) * 256], True, True, [bki[i_], bqd[i_]], [bPS[4]])
                            TT("dve", at[i_][:], bank(4), mask[:, dd * 512:(dd + 1) * 512], ALU.mult, [bPS[4], bC], [bat[i_]])
                            for hh in range(4):
                                hp, h2 = hh // 2, hh % 2
                                r0 = h2 * 64
                                MM(bank(5, 128, hh * 128), vt[i_][:, hh * 128:(hh + 1) * 128], at[i_][:, hh * 128:(hh + 1) * 128], True, False, [bvt[i_], bat[i_]], [bPS[5]])
                                MM(bank(5, 128, hh * 128), Sbf[:, hp * 128:(hp + 1) * 128], qd[i_][:, hh * 128:(hh + 1) * 128], False, True, [bSbf, bqd[i_]], [bPS[5]])
                            kc_(5)
                            o4 = bank(5).rearrange("p (h c) -> p h c", h=4)
                            if dd == 0:
                                CP("act", oav[:, :, ts_], o4, [bPS[5]], [bOA])
                            else:
                                TT("dve", oav[:, :, ts_], o4, oav[:, :, ts_], ALU.add, [bPS[5], bOA], [bOA])
                            kc_(6)
                            for hp in range(2):
                                MM(bank(6 + hp, 256), ke[i_][:, hp * 128:(hp + 1) * 128], vt[i_][:, hp * 256:(hp + 1) * 256], True, True, [bke[i_], bvt[i_]], [bPS[6 + hp]])
                            kc_(7)
                            dcol = 127 if dd == 0 else 0
                            for hp in range(2):
                                for h2 in range(2):
                                    r0 = h2 * 64
                                    STT("dve", Sst[r0:r0 + 64, hp * 128:(hp + 1) * 128], Sst[r0:r0 + 64, hp * 128:(hp + 1) * 128],
                                        ep[i_][r0:r0 + 64, hp * 128 + dcol:hp * 128 + dcol + 1], bank(6 + hp, 128, h2 * 128)[r0:r0 + 64, :],
                                        ALU.mult, ALU.add, [bSst, bep[i_], bPS[6 + hp]], [bSst])
                            CP("pool", Sbf[:], Sst[:], [bSst], [bSbf])
                            if _os.environ.get('GLA_BAR'):
                                s.barrier()
                        if g == 0:
                            DMA(dr["o_sf" if dd == 0 else "o_sb"][sq_, l].rearrange("(hp p) v -> p hp v", p=128), v3(Sst[:], 2), [bSst], [])
                if dbg and ("oacc_%d_%d" % (l, g)) in dbg_out:
                    s.barrier()
                    DMA(dbg_out["oacc_%d_%d" % (l, g)].rearrange("(k p) t -> p k t", p=128), oav, [bOA], [])
                    s.barrier()
                gsq = RA.alloc(512, BF16)
                bgsq = Buf()
                grs = RA.alloc(512, F32)
                bgrs = Buf()
                gtm = RA.alloc(512, F32)
                bgtm = Buf()
                for hh in range(4):
                    for b in range(nblk):
                        sl = slice(b * 512, (b + 1) * 512)
                        AC(gsq[:], oav[:, hh, sl], AF.Square, [bOA], [bgsq])
                        rms_rstd(lambda k: gsq[:], 1, 512, 128.0, 0, grs[:], [bgsq], [bgrs])
                        STT("dve", gtm[:], oav[:, hh, sl], vl[:, V_GN:V_GN + 1], grs[:], ALU.mult, ALU.mult, [bOA, bC, bgrs], [bgtm])
                        TT("pool", mixv[:, 4 + hh, sl], gtm[:], sgv[:, hh, sl], ALU.mult, [bgtm, bSG], [bMIX[4 + hh]])
                s.barrier()
                stage('gla')
                RA.reset()
                RT.reset()

                RIN.reset()
                _ = RIN.alloc(12 * Ltot, BF16)
                ksd = RIN.alloc(2 * nf * 512, BF16)
                bKSD = Buf()
                ksdv = ksd[:].rearrange("p (a n c) -> p a n c", a=2, n=nf)
                h1T = RIN.alloc(L, F32)
                h2T = RIN.alloc(L, F32)
                bh1, bh2 = Buf(), Buf()
                zemb = RIN.alloc(L, F32)
                fw12 = RIN.alloc(128, F32)
                bfz = Buf()
                arg = RIN.alloc(512, F32)
                arg2 = RIN.alloc(512, F32)
                barg, barg2 = Buf(), Buf()
                dec_t = RIN.alloc(512, F32)
                bdec = Buf()
                bsb_t = RIN.alloc(512, F32)
                bbsb = Buf()
                sm_t = RIN.alloc(512, F32)
                df_t = RIN.alloc(512, F32)
                bsm, bdf = Buf(), Buf()
                DMA(zemb[0:33, :], dr["zemb%d" % L], [], [bfz])
                DMA(fw12[0:33, 0:64], dr["fw1"][l], [], [bfz])
                DMA(fw12[0:64, 64:128], dr["fw2"][l], [], [bfz])
                ztok = RA.alloc(nseq * nf * 512, BF16)
                ztv = ztok[:].rearrange("p (s n c) -> p s n c", s=nseq, n=nf)
                bZT = Buf()
                p1d = RA.alloc(2 * nseq * nf * 512, BF16)
                p1v = p1d[:].rearrange("p (a s n c) -> p a s n c", a=2, s=nseq, n=nf)
                bP1 = Buf()
                kab = [RT.alloc(1024, F32) for _ in range(2)]
                bkab = [Buf(), Buf()]
                tt_ = [RT.alloc(512, F32) for _ in range(4)]
                btt = [Buf() for _ in range(4)]
                utmp = RT.alloc(512, F32)
                butmp = Buf()
                delta = RT.alloc(512, F32)
                negt = RT.alloc(8, F32)
                DMA(delta[:], dr["delta"], [], [bfz])
                DMA(negt[:, 0:nf], dr["negt%d" % L], [], [bfz])

                def sin_layer(pre_ps, n_, fcol, fbcol, out_ap, Rps, Wout):
                    TS("dve", arg[0:64, 0:n_], pre_ps, vl[0:64, fcol:fcol + 1], fb[0:64, fbcol:fbcol + 1], ALU.mult, ALU.add, Rps + [bC], [barg])
                    TS("dve", arg2[0:64, 0:n_], arg[0:64, 0:n_], 1.0 / TWO_PI, MAGIC, ALU.mult, ALU.add, [barg], [barg2])
                    TS("dve", arg2[0:64, 0:n_], arg2[0:64, 0:n_], MAGIC, -TWO_PI, ALU.subtract, ALU.mult, [barg2], [barg2])
                    TT("dve", arg[0:64, 0:n_], arg[0:64, 0:n_], arg2[0:64, 0:n_], ALU.add, [barg, barg2], [barg])
                    AC(out_ap, arg[0:64, 0:n_], AF.Sin, [barg], Wout)

                for c0 in range(0, L, 512):
                    n_ = min(512, L - c0)
                    MM(PS[0][0:64, 0:n_], fw12[0:33, 0:64], zemb[0:33, c0:c0 + n_], True, True, [bfz], [bPS[0]])
                    sin_layer(PS[0][0:64, 0:n_], n_, V_FR0, 0, h1T[0:64, c0:c0 + n_], [bPS[0]], [bh1])
                for c0 in range(0, L, 512):
                    n_ = min(512, L - c0)
                    MM(PS[0][0:64, 0:n_], fw12[0:64, 64:128], h1T[0:64, c0:c0 + n_], True, True, [bfz, bh1], [bPS[0]])
                    sin_layer(PS[0][0:64, 0:n_], n_, V_FR1, 1, h2T[0:64, c0:c0 + n_], [bPS[0]], [bh2])

                zcur = [hyv[:, cc, :] for cc in range(4)]
                zbuf = [bHY[cc] for cc in range(4)]
                for o in range(2):
                    w3t, bw3 = wload(dr["fw3"][l].rearrange("p (k c) -> p k c", k=1), 1, 2048, mode="f32", parts=64)
                    for lc in range(nf):
                        AC(dec_t[:], delta[:], AF.Exp, [bfz], [bdec], scale=negt[:, lc:lc + 1])
                        for side in range(2):
                            cofs = (side * 2 + o) * 512
                            MM(bank(side), h2T[0:64, lc * 128:(lc + 1) * 128], w3t[0:64, 0, cofs:cofs + 512], True, True, [bh2, bw3(0)], [bPS[side]])
                        CP("act", bsb_t[:], bank(1), [bPS[1]], [bbsb])
                        if lc == 0:
                            MS("pool", bsb_t[0:1, :], 0.0, [bbsb])
                        TT("dve", sm_t[:], bank(0), bsb_t[:], ALU.add, [bPS[0], bbsb], [bsm])
                        TT("dve", df_t[:], bank(0), bsb_t[:], ALU.subtract, [bPS[0], bbsb], [bdf])
                        TT("pool", ksdv[:, 0, lc, :], sm_t[:], dec_t[:], ALU.mult, [bsm, bdec], [bKSD])
                        TT("pool", ksdv[:, 1, lc, :], df_t[:], dec_t[:], ALU.mult, [bdf, bdec], [bKSD])
                    for sq_ in range(nseq):
                        for tcn in range(nf):
                            ts_ = slice(sq_ * L + tcn * 128, sq_ * L + (tcn + 1) * 128)
                            pbk = 2 + (tcn % 2)
                            for cc in range(4):
                                MM(bank(pbk, 128, cc * 128), zcur[cc][:, ts_], ident_b[:], True, True, [zbuf[cc], bC], [bPS[pbk]])
                            CP("act", ztv[:, sq_, tcn, :], bank(pbk), [bPS[pbk]], [bZT])
                    for fb_ in range(nf // 2):
                        Cp, bCp = wload(wsrc(dr["dftC%d" % L], 0, nf, fb_ * 256, 256), nf, 256, mode="bf16")
                        Sp, bSp = wload(wsrc(dr["dftS%d" % L], 0, nf, fb_ * 256, 256), nf, 256, mode="bf16")
                        for f2 in range(2):
                            fc = fb_ * 2 + f2
                            fcs = slice(f2 * 128, (f2 + 1) * 128)
                            kk, bk_ = kab[fc % 2], bkab[fc % 2]
                            for n_ in range(nf):
                                MM(bank(0), Cp[:, n_, fcs], ksdv[:, 0, n_, :], n_ == 0, n_ == nf - 1, [bCp(0), bKSD], [bPS[0]])
                            for n_ in range(nf):
                                MM(bank(1), Sp[:, n_, fcs], ksdv[:, 1, n_, :], n_ == 0, n_ == nf - 1, [bSp(0), bKSD], [bPS[1]])
                            AC(kk[:, 0:512], bank(0), AF.Identity, [bPS[0]], [bk_], scale=1.0 / L)
                            AC(kk[:, 512:1024], bank(1), AF.Identity, [bPS[1]], [bk_], scale=1.0 / L)
                            for sq_ in range(nseq):
                                pa, pb_ = (4, 5) if (fc * nseq + sq_) % 2 == 0 else (6, 7)
                                for n_ in range(nf):
                                    MM(bank(pa), Cp[:, n_, fcs], ztv[:, sq_, n_, :], n_ == 0, n_ == nf - 1, [bCp(0), bZT], [bPS[pa]])
                                for n_ in range(nf):
                                    MM(bank(pb_), Sp[:, n_, fcs], ztv[:, sq_, n_, :], n_ == 0, n_ == nf - 1, [bSp(0), bZT], [bPS[pb_]])
                                TT("dve", tt_[0][:], bank(pa), kk[:, 0:512], ALU.mult, [bPS[pa], bk_], [btt[0]])
                                TT("dve", tt_[1][:], bank(pb_), kk[:, 512:1024], ALU.mult, [bPS[pb_], bk_], [btt[1]])
                                TT("pool", p1v[:, 0, sq_, fc, :], tt_[0][:], tt_[1][:], ALU.subtract, [btt[0], btt[1]], [bP1])
                                TT("dve", tt_[2][:], bank(pa), kk[:, 512:1024], ALU.mult, [bPS[pa], bk_], [btt[2]])
                                TT("dve", tt_[3][:], bank(pb_), kk[:, 0:512], ALU.mult, [bPS[pb_], bk_], [btt[3]])
                                TT("pool", p1v[:, 1, sq_, fc, :], tt_[2][:], tt_[3][:], ALU.add, [btt[2], btt[3]], [bP1])
                    xo = [hyv[:, 4 * (o + 1) + cc, :] for cc in range(4)]
                    xob = [bHY[4 * (o + 1) + cc] for cc in range(4)]
                    it2 = 0
                    for sq_ in range(nseq):
                        for tb in range(L // 256):
                            CTp, bCTp = wload(wsrc(dr["dftCT%d" % L], 0, nf, tb * 256, 256), nf, 256, mode="bf16")
                            STp, bSTp = wload(wsrc(dr["dftST%d" % L], 0, nf, tb * 256, 256), nf, 256, mode="bf16")
                            ts_ = slice(sq_ * L + tb * 256, sq_ * L + (tb + 1) * 256)
                            for cc in range(4):
                                pbk = it2 % 4
                                it2 += 1
                                ccs = slice(cc * 128, (cc + 1) * 128)
                                for n_ in range(nf):
                                    MM(bank(pbk, 256), p1v[:, 0, sq_, n_, ccs], CTp[:, n_, :], n_ == 0, False, [bP1, bCTp(0)], [bPS[pbk]])
                                for n_ in range(nf):
                                    MM(bank(pbk, 256), p1v[:, 1, sq_, n_, ccs], STp[:, n_, :], False, n_ == nf - 1, [bP1, bSTp(0)], [bPS[pbk]])
                                STT("dve", utmp[:, 0:256], zcur[cc][:, ts_], vl[:, V_SKIP + o * 4 + cc:V_SKIP + o * 4 + cc + 1], bank(pbk, 256), ALU.mult, ALU.add, [zbuf[cc], bC, bPS[pbk]], [butmp])
                                if o == 0:
                                    TT("pool", zcur[cc][:, ts_], utmp[:, 0:256], xo[cc][:, ts_], ALU.mult, [butmp, xob[cc]], [zbuf[cc]])
                                else:
                                    TT("pool", mixv[:, cc, ts_], utmp[:, 0:256], xo[cc][:, ts_], ALU.mult, [butmp, xob[cc]], [bMIX[cc]])
                if dbg and ("mix_%d_%d" % (l, g)) in dbg_out:
                    s.barrier()
                    for q4 in range(4):
                        RT.reset()
                        dt_ = RT.alloc(4 * Ltot, F32)
                        bd = Buf()
                        CP("dve", v3(dt_[:], 4), mixv[:, q4 * 4:(q4 + 1) * 4, :], bMIX, [bd])
                        DMA(dbg_out["mix_%d_%d" % (l, g)][q4 * 512:(q4 + 1) * 512, :].rearrange("(k p) t -> p k t", p=128), v3(dt_[:], 4), [bd], [])
                        s.barrier()
                s.barrier()
                stage('hyena')
                RA.reset()
                RT.reset()
                RIN.reset()

                xc = [RT.alloc(512, F32) for _ in range(3)]
                bxc = [Buf() for _ in range(3)]
                it3 = 0
                for i in range(16):
                    wt, bw = wload(wsrc(dr["w_out"][l], 0, 16, i * 128, 128), 16, 128)
                    u = i % 4
                    for b in range(nblk):
                        blk = (tok0 // 512) + b
                        xi_, bxi = xc[it3 % 3], bxc[it3 % 3]
                        it3 += 1
                        DMA(xi_[:], xTv[:, i, blk * 512:(blk + 1) * 512], [bXd[blk][i]], [bxi])
                        for k in range(16):
                            MM(bank(2 * u + b), wt[:, k, :], mixv[:, k, b * 512:(b + 1) * 512], k == 0, k == 15, [bw(k), bMIX[k]], [bPS[2 * u + b]])
                        STT("dve", xi_[:], bank(2 * u + b), modT[:, j * 96 + 32 + i:j * 96 + 32 + i + 1], xi_[:], ALU.mult, ALU.add, [bPS[2 * u + b], bC, bxi], [bxi])
                        DMA(xTv[:, i, blk * 512:(blk + 1) * 512], xi_[:], [bxi], [bXd[blk][i]])
                s.barrier()
                stage('outproj')

            last = (l == nlayers - 1)
            RF.reset()
            wl2 = make_ws2(RF, nstg=2)
            h2all = RF.alloc(16 * T, BF16)
            h2a = v3(h2all[:], 16)
            bh2 = [Buf() for _ in range(3)]
            rstd = RF.alloc(512, F32)
            brs = Buf()
            sgt = [RF.alloc(512, F32) for _ in range(2)]
            bsg = [Buf(), Buf()]
            ntmp = RF.alloc(2 * 512, F32)
            bnt = [Buf(), Buf()]
            xc = [RF.alloc(512, F32) for _ in range(3)]
            bxc = [Buf() for _ in range(3)]
            act_off = RF.cur
            xblk = RF.alloc(16 * 512, F32)
            bxb = Buf()
            sqt = RF.alloc(16 * 512, BF16)
            bsq = Buf()
            for blk in range(3):
                j = 0 if blk == 0 else 1
                DMA(v3(xblk[:], 16), xTv[:, :, blk * 512:(blk + 1) * 512], bXd[blk], [bxb])
                AC(sqt[:], xblk[:], AF.Square, [bxb], [bsq])
                rms_rstd(lambda k: sqt[:, k * 512:(k + 1) * 512], 16, 512, float(D), 7, rstd[:], [bsq], [brs])
                for k in range(16):
                    nt_, bn_ = ntmp[:, (k % 2) * 512:(k % 2) * 512 + 512], bnt[k % 2]
                    TT("dve", nt_, xblk[:, k * 512:(k + 1) * 512], rstd[:], ALU.mult, [bxb, brs], [bn_])
                    AC(h2a[:, k, blk * 512:(blk + 1) * 512], nt_, AF.Identity, [bn_, bC], [bh2[blk]],
                       bias=modT[:, j * 96 + 48 + k:j * 96 + 48 + k + 1], scale=gsc[:, (j * 2 + 1) * 16 + k:(j * 2 + 1) * 16 + k + 1])
            s.barrier()
            RF.cur = act_off
            actH = RF.alloc(22 * T, BF16)
            actv = v3(actH[:], 22)
            bact = [[Buf() for _ in range(3)] for _ in range(22)]
            xc6 = xc + [RF.alloc(512, F32) for _ in range(3)]
            bxc6 = bxc + [Buf() for _ in range(3)]
            nup = 0
            for half in range(2):
                for jp in range(11):
                    jf0 = half * 22 + 2 * jp
                    wg, bg = wl2(wsrc(dr["w_ffn_in"][l], 0, 16, jf0 * 128, 256), 16, 256)
                    wu, bu = wl2(wsrc(dr["w_ffn_in"][l], 0, 16, DFF + jf0 * 128, 256), 16, 256)
                    for c in range(2):
                        for blk in range(3):
                            pg = c * 3 + blk
                            sl = slice(blk * 512, (blk + 1) * 512)
                            for k in range(16):
                                MM(bank(pg), wg[:, k, c * 128:(c + 1) * 128], h2a[:, k, sl], k == 0, k == 15, [bg(k), bh2[blk]], [bPS[pg]])
                    for c in range(2):
                        jfl = 2 * jp + c
                        for blk in range(3):
                            pg = c * 3 + blk
                            pu = 6 + (nup % 2)
                            sg_, bsg_ = sgt[nup % 2], bsg[nup % 2]
                            nup += 1
                            sl = slice(blk * 512, (blk + 1) * 512)
                            AC(sg_[:], bank(pg), AF.Silu, [bPS[pg]], [bsg_])
                            for k in range(16):
                                MM(bank(pu), wu[:, k, c * 128:(c + 1) * 128], h2a[:, k, sl], k == 0, k == 15, [bu(k), bh2[blk]], [bPS[pu]])
                            TT("dve", actv[:, jfl, sl], bank(pu), sg_[:], ALU.mult, [bPS[pu], bsg_], [bact[jfl][blk]])
                def out_tiles(ip_):
                    res = []
                    for (k0, kc) in ((0, 16), (16, 6)):
                        wt, bk = wl2(wsrc(dr["w_ffn_out"][l], half * 22 + k0, kc, ip_ * 256, 256), kc, 256)
                        res.append((k0, kc, wt, bk))
                    return res
                tiles = out_tiles(0)
                pvn = bank(7, 192).rearrange("p (j m) -> p m j", j=2)
                for ip in range(8):
                    i0 = 2 * ip
                    for c in range(2):
                        for blk in range(3):
                            n6 = c * 3 + blk
                            DMA(xc6[n6][:], xTv[:, i0 + c, blk * 512:(blk + 1) * 512], [bXd[blk][i0 + c]], [bxc6[n6]])
                    for (k0, kc, wt, bk) in tiles:
                        for c in range(2):
                            for blk in range(3):
                                pb = c * 3 + blk
                                for k in range(kc):
                                    kk = k0 + k
                                    MM(bank(pb), wt[:, k, c * 128:(c + 1) * 128], actv[:, kk, blk * 512:(blk + 1) * 512], kk == 0, kk == 21, [bk(k), bact[kk][blk]], [bPS[pb]])
                    for c in range(2):
                        i = i0 + c
                        for blk in range(3):
                            j = 0 if blk == 0 else 1
                            n6 = c * 3 + blk
                            STT("dve", xc6[n6][:], bank(n6), modT[:, j * 96 + 80 + i:j * 96 + 80 + i + 1], xc6[n6][:], ALU.mult, ALU.add, [bPS[n6], bC, bxc6[n6]], [bxc6[n6]])
                    if l + 1 < nlayers:
                        for t_ in range(3):
                            mp = half * 24 + ip * 3 + t_
                            wtm, bkm = wl2(wsrc(dr["w_mod"][l + 1], 0, 16, mp * 256, 256), 16, 256)
                            for c in range(2):
                                m = 2 * mp + c
                                for k in range(16):
                                    MM(pvn[:, m, :], wtm[:, k, c * 128:(c + 1) * 128], scv[:, k, :], k == 0, k == 15, [bkm(k), bC], [bPS[7]])
                    if ip < 7:
                        tiles = out_tiles(ip + 1)
                    for c in range(2):
                        for blk in range(3):
                            n6 = c * 3 + blk
                            DMA(xTv[:, i0 + c, blk * 512:(blk + 1) * 512], xc6[n6][:], [bxc6[n6]], [bXd[blk][i0 + c]])
                if l + 1 < nlayers:
                    m0 = half * 48
                    for j in range(2):
                        TT("dve", modTs[(l + 1) % 2][:, j * 96 + m0:j * 96 + m0 + 48], bank(7, 48, j * 96 + m0), vecs[l + 1][:, V_BMOD + m0:V_BMOD + m0 + 48], ALU.add, [bPS[7], bC], [bC])
            s.barrier()
            if last:
                RF.cur = act_off
                xblk = RF.alloc(16 * 512, F32)
                bxk = [Buf() for _ in range(16)]
                xv = v3(xblk[:], 16)
                sqt = RF.alloc(16 * 512, BF16)
                bsq = Buf()
                yo = [RF.alloc(2048, F32) for _ in range(2)]
                byo = [Buf(), Buf()]
                for blk in range(3):
                    DMA(xv, xTv[:, :, blk * 512:(blk + 1) * 512], bXd[blk], bxk)
                    AC(sqt[:], xblk[:], AF.Square, bxk, [bsq])
                    rms_rstd(lambda k: sqt[:, k * 512:(k + 1) * 512], 16, 512, float(D), 7, rstd[:], [bsq], [brs])
                    for k in range(16):
                        STT("dve", xv[:, k, :], xv[:, k, :], lnf[:, k:k + 1], rstd[:], ALU.mult, ALU.mult, [bxk[k], bC, brs], [bxk[k]])
                    for tcn in range(4):
                        yt, byt = yo[tcn % 2], byo[tcn % 2]
                        for half in range(2):
                            u = (tcn * 2 + half) % 3
                            for kk in range(8):
                                k = half * 8 + kk
                                MM(PS[u][:, kk * 128:(kk + 1) * 128], xv[:, k, tcn * 128:(tcn + 1) * 128], ident_f[:], True, True, [bxk[k], bC], [bPS[2 * u + kk // 4]])
                            CP("act" if half else "dve", yt[:, half * 1024:(half + 1) * 1024], PS[u][:, :], [bPS[2 * u], bPS[2 * u + 1]], [byt])
                        DMA(dr["y"][blk * 512 + tcn * 128:blk * 512 + (tcn + 1) * 128, :], yt[:], [byt], [])
                s.barrier()
    except _Stop:
        pass
    s.barrier()
    s.emit(nc, st)
    st.close()
    return nc


_NC_CACHE = {}


def prep_inputs(inp, core):
    c = host_consts()
    f32 = lambda a: np.ascontiguousarray(np.asarray(a, dtype=np.float32))
    m = {}
    xp = f32(inp["x_prompt"])[2 * core:2 * core + 2].reshape(TP, D)
    xs = f32(inp["x_sample"])[core]
    m["x_in"] = np.ascontiguousarray(np.concatenate([xp, xs], axis=0))
    cv = np.stack([f32(inp["c_ctx"]), f32(inp["c"])[core]], axis=0)
    m["cvec"] = np.ascontiguousarray(cv.reshape(2, 16, 128).transpose(2, 1, 0).reshape(128, 32))
    m["ckv_c"] = f32(inp["cache_mla_ckv"])[core]
    m["kr_c"] = f32(inp["cache_mla_krope"])[core]
    m["sf_c"] = np.ascontiguousarray(f32(inp["state_gla_fwd"])[core].reshape(NL, 256, 128))
    m["sb_c"] = np.ascontiguousarray(f32(inp["state_gla_bwd"])[core].reshape(NL, 256, 128))
    return m


def prep_shared(inp):
    c = host_consts()
    f32 = lambda a: np.ascontiguousarray(np.asarray(a, dtype=np.float32))
    m = {}
    m["w_mod"] = f32(inp["w_mod"])
    w_in = f32(inp["w_in"])
    m["w_in"] = np.ascontiguousarray(np.concatenate([w_in, w_in[:, :, C_KR + c["perm"]]], axis=2))
    m["w_out"] = f32(inp["w_out"])
    m["w_ffn_in"] = f32(inp["w_ffn_in"])
    m["w_ffn_out"] = f32(inp["w_ffn_out"])
    wuq = f32(inp["mla_w_uq"])
    sw = np.concatenate([wuq[:, :, h * 192 + 128 + c["perm"]] for h in range(8)], axis=2)
    m["w_uq"] = np.ascontiguousarray(np.concatenate([wuq, sw], axis=2))
    m["w_ukv"] = f32(inp["mla_w_ukv"])
    vecs = np.zeros((NL, 128, NV), np.float32)
    pm = lambda a, n: f32(a).reshape(NL, n, 128).transpose(0, 2, 1)
    vecs[:, :, V_BMOD:V_BMOD + 96] = pm(inp["b_mod"], 96)
    vecs[:, :, V_LNMIX:V_LNMIX + 16] = pm(inp["ln_mix"], 16)
    vecs[:, :, V_LNFFN:V_LNFFN + 16] = pm(inp["ln_ffn"], 16)
    vecs[:, :, V_CW:V_CW + 36] = f32(inp["hy_conv_w"]).reshape(NL, 3, 12, 128).transpose(0, 3, 1, 2).reshape(NL, 128, 36)
    vecs[:, :, V_CB:V_CB + 12] = pm(inp["hy_conv_b"], 12)
    vecs[:, :, V_SKIP:V_SKIP + 8] = f32(inp["hy_skip"]).reshape(NL, 2, 4, 128).transpose(0, 3, 1, 2).reshape(NL, 128, 8)
    vecs[:, :, V_GN] = f32(inp["gla_norm"])
    vecs[:, :, V_QN:V_QN + 3] = pm(inp["mla_q_norm"], 3)
    vecs[:, :, V_KVN:V_KVN + 2] = pm(inp["mla_kv_norm"], 2)
    vecs[:, 0:64, V_FB1] = f32(inp["hy_filt_b1"])
    vecs[:, 0:64, V_FB2] = f32(inp["hy_filt_b2"])
    vecs[:, 0:64, V_FR0] = f32(inp["hy_filt_freq"])[:, 0]
    vecs[:, 0:64, V_FR1] = f32(inp["hy_filt_freq"])[:, 1]
    m["vecs"] = vecs
    m["lnf"] = np.ascontiguousarray(f32(inp["ln_final"]).reshape(16, 128).T)
    m["fw1"] = f32(inp["hy_filt_w1"])
    m["fw2"] = f32(inp["hy_filt_w2"])
    m["fw3"] = f32(inp["hy_filt_w3"])
    wa = np.zeros((NL, 2, 17, 256), np.float32)
    wa[:, 0, 0:16] = f32(inp["gla_wa_f"])
    wa[:, 0, 16] = f32(inp["gla_ba_f"])
    wa[:, 1, 0:16] = f32(inp["gla_wa_b"])
    wa[:, 1, 16] = f32(inp["gla_ba_b"])
    m["wa_aug"] = wa
    for name, shape, dt in CONST_DRAM:
        m[name] = c[name]
    return m


def kernel(**inputs):
    if "nc" not in _NC_CACHE:
        _NC_CACHE["nc"] = build()
    nc = _NC_CACHE["nc"]
    shared = prep_shared(inputs)
    in_maps = []
    for core in range(NCORE):
        m = dict(shared)
        m.update(prep_inputs(inputs, core))
        in_maps.append(m)
    res = run_bass_kernel_spmd(nc, in_maps, core_ids=list(range(NCORE)))
    R = res.results
    y_prompt = np.concatenate([r["y"][0:TP].reshape(2, LP, D) for r in R], axis=0)
    y_sample = np.stack([r["y"][TP:T] for r in R], axis=0)
    o_ckv = np.concatenate([r["o_ckv"] for r in R], axis=0)
    o_kr = np.concatenate([r["o_kr"] for r in R], axis=0)
    o_sf = np.concatenate([r["o_sf"].reshape(2, NL, 4, 64, 128) for r in R], axis=0)
    o_sb = np.concatenate([r["o_sb"].reshape(2, NL, 4, 64, 128) for r in R], axis=0)
    f = lambda a: np.ascontiguousarray(a.astype(np.float32))
    return (f(y_prompt), f(y_sample), f(o_ckv), f(o_kr), f(o_sf), f(o_sb))
```

```python
import math
import contextlib
import numpy as np
import ml_dtypes
import concourse.bass as bass
import concourse.mybir as mybir
from concourse.bass_utils import run_bass_kernel_spmd

F32 = mybir.dt.float32
BF16 = mybir.dt.bfloat16
ALU = mybir.AluOpType
AF = mybir.ActivationFunctionType

D = 2048
DFF = 5632
NL = 2
LP = 256
LS = 1024
TP = 512
T = 1536
PAST = 256
EPS = 1e-6
NCORE = 8
C_HY, C_Q, C_K, C_V, C_G, C_AF, C_AB, C_CQ, C_CKV, C_KR, C_KRSW = 0, 1536, 1792, 2048, 2560, 3072, 3088, 3104, 3488, 3744, 3808
WIN_EXT = 3872
NV = 200
V_BMOD, V_LNMIX, V_LNFFN, V_CW, V_CB, V_SKIP, V_GN, V_QN, V_KVN, V_FB1, V_FB2, V_FR0, V_FR1 = 0, 96, 112, 128, 164, 176, 184, 185, 188, 190, 191, 192, 193
MAGIC = 12582912.0
TWO_PI = 2.0 * math.pi
SBASE = 16512
SEND = 229376


class Buf:
    __slots__ = ("w", "r")

    def __init__(self):
        self.w = None
        self.r = {}


class Op:
    __slots__ = ("eng", "fn", "waits", "signal", "src", "seq", "vc", "val", "dma")


class Sched:
    ENG = ["pe", "act", "dve", "pool", "sp"]
    NSLOT = 8

    def __init__(self):
        self.q = {e: [] for e in self.ENG}
        self.S = {e: {} for e in self.ENG}
        self.ncomp = {e: 0 for e in self.ENG}
        self.ndma = {e: 0 for e in self.ENG}
        self.slot_last = {}
        self.slot_seq = {}

    def add(self, eng, fn, reads=(), writes=(), dma=False, after=()):
        op = Op()
        op.eng = eng
        op.fn = fn
        op.signal = False
        op.dma = dma
        deps = []
        deps.extend(after)
        for b in reads:
            if b.w is not None:
                deps.append(b.w)
        for b in writes:
            if b.w is not None:
                deps.append(b.w)
            deps.extend(b.r.values())
        if dma:
            j = self.ndma[eng]
            self.ndma[eng] += 1
            slot = (eng, j % self.NSLOT)
            prev = self.slot_last.get(slot)
            if prev is not None:
                deps.append(prev)
            op.src = ("dma",) + slot
            op.seq = self.slot_seq.get(slot, 0) + 1
            self.slot_seq[slot] = op.seq
            self.slot_last[slot] = op
            op.signal = True
        else:
            self.ncomp[eng] += 1
            op.src = eng
            op.seq = self.ncomp[eng]
        S = self.S[eng]
        waits = {}
        for d in deps:
            if eng == "pe" and d.src == "pe":
                continue
            if S.get(d.src, 0) >= d.seq:
                continue
            d.signal = True
            cur = waits.get(d.src)
            if cur is None or cur.seq < d.seq:
                waits[d.src] = d
            for k, v in d.vc.items():
                if S.get(k, 0) < v:
                    S[k] = v
        op.waits = list(waits.values())
        op.vc = dict(S)
        op.vc[op.src] = op.seq
        for b in writes:
            b.w = op
            b.r = {}
        for b in reads:
            if b.w is not op:
                b.r[op.src] = op
        self.q[eng].append(op)
        return op

    @staticmethod
    def deps_of(bufs):
        out = []
        for b in bufs:
            if b.w is not None:
                out.append(b.w)
            out.extend(b.r.values())
        return out

    def barrier(self):
        lasts = []
        for e in self.ENG:
            for op in reversed(self.q[e]):
                if (not op.dma) and op.fn is not None:
                    lasts.append(op)
                    break
        lasts.extend(self.slot_last.values())
        for e in self.ENG:
            S = self.S[e]
            waits = {}
            for d in lasts:
                if S.get(d.src, 0) >= d.seq:
                    continue
                d.signal = True
                waits[d.src] = d
            for d in lasts:
                for k, v in d.vc.items():
                    if S.get(k, 0) < v:
                        S[k] = v
            if waits:
                op = Op()
                op.eng = e
                op.fn = None
                op.signal = False
                op.dma = False
                op.waits = list(waits.values())
                op.src = None
                op.seq = 0
                op.vc = {}
                self.q[e].append(op)

    def emit(self, nc, stack):
        sems = {}
        for e in self.ENG:
            cnt = 0
            for op in self.q[e]:
                if op.fn is None:
                    continue
                if op.dma:
                    op.val = 16 * op.seq
                else:
                    if op.signal:
                        cnt += 1
                    op.val = cnt
                if op.signal and op.src not in sems:
                    nm = "s_" + ("_".join(str(x) for x in op.src) if isinstance(op.src, tuple) else op.src)
                    sems[op.src] = stack.enter_context(nc.semaphore(nm))
        block = stack.enter_context(nc.Block())
        q = self.q

        def run(e, eng):
            for op in q[e]:
                for d in op.waits:
                    eng.wait_ge(sems[d.src], d.val)
                if op.fn is None:
                    continue
                ins = op.fn(eng)
                if op.signal:
                    ins.then_inc(sems[op.src], 16 if op.dma else 1)

        @block.tensor
        def _(eng):
            run("pe", eng)

        @block.scalar
        def _(eng):
            run("act", eng)

        @block.vector
        def _(eng):
            run("dve", eng)

        @block.gpsimd
        def _(eng):
            run("pool", eng)

        @block.sync
        def _(eng):
            run("sp", eng)


def _bf(x):
    return np.ascontiguousarray(x.astype(ml_dtypes.bfloat16))


_CONST_CACHE = {}


def host_consts():
    if _CONST_CACHE:
        return _CONST_CACHE
    c = {}
    c["ident_f"] = np.eye(128, dtype=np.float32)
    c["ident_b"] = _bf(np.eye(128, dtype=np.float32))
    c["ones_b"] = _bf(np.ones((128, 128), np.float32))
    s_ = np.arange(128)[:, None]
    c_ = np.arange(128)[None, :]
    v = -1.0 / 16.0
    tri = np.stack([(s_ <= c_) * v, (s_ > c_) * v, (s_ >= c_) * v, (s_ < c_) * v]).astype(np.float32)
    c["tri"] = np.ascontiguousarray(tri.transpose(1, 0, 2).reshape(128, 4 * 128))
    mk = np.stack([np.tile((s_ <= c_).astype(np.float32), (1, 4)), np.tile((s_ >= c_).astype(np.float32), (1, 4))])
    c["mask"] = np.ascontiguousarray(mk.transpose(1, 0, 2).reshape(128, 2 * 512))
    t = np.arange(LS)
    row, col = t // 64, t % 64
    inv = (10000.0 ** (-np.arange(0, 32, 2, dtype=np.float32) / 32.0)).astype(np.float32)
    cos = np.zeros((64, LS), np.float32)
    sin = np.zeros((64, LS), np.float32)
    perm = np.zeros(64, np.int64)
    for r in range(64):
        pos = (row if r < 32 else col).astype(np.float32)
        j = r % 16
        ang = (pos * inv[j]).astype(np.float32)
        cos[r] = np.cos(ang)
        x2 = (r % 32) >= 16
        sin[r] = np.sin(ang) * (1.0 if x2 else -1.0)
        perm[r] = r - 16 if x2 else r + 16
    c["rope"] = np.ascontiguousarray(np.concatenate([cos, sin], axis=1))
    c["perm"] = perm
    for L in (LP, LS):
        tt = np.linspace(0.0, 1.0, L, dtype=np.float32)[:, None]
        w = (2.0 * math.pi * np.arange(L, dtype=np.float32)[:, None] / L).astype(np.float32)
        f = np.linspace(1e-4, 15, 16, dtype=np.float32)
        z = np.concatenate([tt, np.cos(f * w), -np.sin(f * w)], axis=-1).astype(np.float32)
        c[f"zemb{L}"] = np.ascontiguousarray(z.T)
        c[f"negt{L}"] = np.ascontiguousarray((-tt[:, 0]).reshape(L // 128, 128).T)
        n = np.arange(L, dtype=np.float64)[:, None]
        fq = np.arange(L, dtype=np.float64)[None, :]
        th = math.pi * (2 * fq + 1) * n / (2 * L)
        C = np.cos(th)
        S = np.sin(th)
        c[f"dftC{L}"] = _bf(C.astype(np.float32))
        c[f"dftS{L}"] = _bf(S.astype(np.float32))
        c[f"dftCT{L}"] = _bf(C.T.astype(np.float32))
        c[f"dftST{L}"] = _bf(S.T.astype(np.float32))
    mind = math.log(1e-2) / 1.5
    maxd = math.log(1e-2) / 0.3
    delta = np.abs(np.linspace(mind, maxd, 512, dtype=np.float32))
    c["delta"] = np.ascontiguousarray(np.tile(delta[None, :], (128, 1)).astype(np.float32))
    _CONST_CACHE.update(c)
    return c


CONST_DRAM = [("ident_f", [128, 128], F32), ("ident_b", [128, 128], BF16), ("ones_b", [128, 128], BF16),
              ("tri", [128, 512], F32), ("mask", [128, 1024], F32), ("rope", [64, 2048], F32),
              ("zemb256", [33, 256], F32), ("zemb1024", [33, 1024], F32),
              ("negt256", [128, 2], F32), ("negt1024", [128, 8], F32), ("delta", [128, 512], F32),
              ("dftC256", [256, 256], BF16), ("dftS256", [256, 256], BF16), ("dftCT256", [256, 256], BF16), ("dftST256", [256, 256], BF16),
              ("dftC1024", [1024, 1024], BF16), ("dftS1024", [1024, 1024], BF16), ("dftCT1024", [1024, 1024], BF16), ("dftST1024", [1024, 1024], BF16)]

IN_DRAM = [("x_in", [T, D]), ("cvec", [128, 32]), ("ckv_c", [NL, PAST, 256]), ("kr_c", [NL, PAST, 64]),
           ("sf_c", [NL, 256, 128]), ("sb_c", [NL, 256, 128]),
           ("w_mod", [NL, D, 6 * D]), ("w_in", [NL, D, WIN_EXT]), ("w_out", [NL, D, D]),
           ("w_ffn_in", [NL, D, 2 * DFF]), ("w_ffn_out", [NL, DFF, D]), ("w_uq", [NL, 384, 2048]), ("w_ukv", [NL, 256, 2048]),
           ("vecs", [NL, 128, NV]), ("lnf", [128, 16]), ("fw1", [NL, 33, 64]), ("fw2", [NL, 64, 64]), ("fw3", [NL, 64, 2048]),
           ("wa_aug", [NL, 2, 17, 256])]

OUT_DRAM = [("y", [T, D]), ("o_ckv", [2, NL, LP, 256]), ("o_kr", [2, NL, LP, 64]), ("o_sf", [2, NL, 256, 128]), ("o_sb", [2, NL, 256, 128])]


class _Stop(Exception):
    pass


def build(nlayers=NL, dbg=None, stop=None, knob=None):
    nc = bass.Bass("TRN2", target_bir_lowering=False)
    dr = {}
    for name, shape in IN_DRAM:
        dr[name] = nc.dram_tensor(name, list(shape), F32, kind="ExternalInput").ap()
    for name, shape, dt in CONST_DRAM:
        dr[name] = nc.dram_tensor(name, list(shape), dt, kind="ExternalInput").ap()
    for name, shape in OUT_DRAM:
        dr[name] = nc.dram_tensor(name, list(shape), F32, kind="ExternalOutput").ap()
    dbg_out = {}
    if dbg:
        for name, shape in dbg.items():
            dbg_out[name] = nc.dram_tensor("dbg_" + name, list(shape), F32, kind="ExternalOutput").ap()
    xT_d = nc.dram_tensor("xT_scr", [D, T], F32, kind="Internal").ap()
    xTv = xT_d.rearrange("(k p) t -> p k t", p=128)

    s = Sched()
    st = contextlib.ExitStack()
    uid = [0]
    stg_ctr = [0]

    def stage(name):
        stg_ctr[0] += 1
        if stop is not None and stg_ctr[0] == stop:
            print('STOP at stage', stop, name)
            raise _Stop()

    class Region:
        def __init__(self, name, start, size):
            self.name, self.start, self.size, self.cur = name, start, size, start

        def reset(self):
            self.cur = self.start

        def alloc(self, n, dt):
            sz = n * (4 if dt == F32 else 2)
            sz = (sz + 31) // 32 * 32
            assert self.cur + sz <= self.start + self.size, (self.name, self.cur - self.start, sz, self.size)
            uid[0] += 1
            t = nc.alloc_sbuf_tensor_at(f"{self.name}_{uid[0]}", [128, n], dt, offset=self.cur)
            self.cur += sz
            return t

    sizes = [("C", 20480), ("W", 36864), ("A", 32768), ("B", 32768), ("IN", 67584)]
    regs = {}
    cur = SBASE
    for nm, sz in sizes:
        regs[nm] = Region(nm, cur, sz)
        cur += sz
    regs["T"] = Region("T", cur, SEND - cur)
    RC, RW, RA, RB, RIN, RT = (regs[k] for k in ("C", "W", "A", "B", "IN", "T"))
    RF = Region("F", regs["W"].start, SEND - regs["W"].start)

    PS = [st.enter_context(nc.psum_tensor(f"ps{i}", [128, 1024], F32)) for i in range(4)]
    bPS = [Buf() for _ in range(8)]

    def bank(b, n=512, off=0):
        return PS[b // 2][:, (b % 2) * 512 + off:(b % 2) * 512 + off + n]

    def MM(out, lhsT, rhs, st_, sp_, R, W):
        s.add("pe", lambda e: e.matmul(out, lhsT=lhsT, rhs=rhs, start=st_, stop=sp_), R, W)

    def AC(out, in_, func, R, W, bias=None, scale=None, accum=None):
        kw = {}
        if bias is not None:
            kw["bias"] = bias
        if scale is not None:
            kw["scale"] = scale
        if accum is not None:
            kw["accum_out"] = accum
        s.add("act", lambda e: e.activation(out=out, in_=in_, func=func, **kw), R, W)

    def TT(eng, out, a, b, op, R, W):
        s.add(eng, lambda e: e.tensor_tensor(out=out, in0=a, in1=b, op=op), R, W)

    def TS(eng, out, a, s1, s2, op0, op1, R, W):
        if s2 is None:
            s.add(eng, lambda e: e.tensor_scalar(out=out, in0=a, scalar1=s1, scalar2=None, op0=op0), R, W)
        else:
            s.add(eng, lambda e: e.tensor_scalar(out=out, in0=a, scalar1=s1, scalar2=s2, op0=op0, op1=op1), R, W)

    def STT(eng, out, a, sc, b, op0, op1, R, W):
        s.add(eng, lambda e: e.scalar_tensor_tensor(out=out, in0=a, scalar=sc, in1=b, op0=op0, op1=op1), R, W)

    def CP(eng, out, in_, R, W, after=()):
        if eng == "act":
            s.add("act", lambda e: e.copy(out=out, in_=in_), R, W, after=after)
        else:
            s.add(eng, lambda e: e.tensor_copy(out=out, in_=in_), R, W, after=after)

    def MS(eng, ap, val, W):
        s.add(eng, lambda e: e.memset(ap, val), (), W)

    def RCP(out, in_, R, W):
        s.add("dve", lambda e: e.reciprocal(out=out, in_=in_), R, W)

    def DMA(out, in_, R, W):
        s.add("sp", lambda e: e.dma_start(out=out, in_=in_), R, W, dma=True)

    def v3(ap2, k):
        return ap2.rearrange("p (k c) -> p k c", k=k)

    bC = Buf()
    ident_f = RC.alloc(128, F32)
    ident_b = RC.alloc(128, BF16)
    ones_b = RC.alloc(128, BF16)
    tri = RC.alloc(512, F32)
    mask = RC.alloc(1024, F32)
    rope = RC.alloc(2048, F32)
    vecs = [RC.alloc(NV, F32) for _ in range(NL)]
    lnf = RC.alloc(16, F32)
    cvec = RC.alloc(32, F32)
    scT = RC.alloc(32, BF16)
    modTs = [RC.alloc(192, F32) for _ in range(2)]
    gsc = RC.alloc(64, F32)
    epsb = RC.alloc(1, F32)
    fb = RC.alloc(4, F32)
    for tile_, name in ((ident_f, "ident_f"), (ident_b, "ident_b"), (ones_b, "ones_b"), (tri, "tri"), (mask, "mask"), (lnf, "lnf"), (cvec, "cvec")):
        DMA(tile_[:], dr[name], [], [bC])
    DMA(rope[0:64, :], dr["rope"], [], [bC])
    for l in range(NL):
        DMA(vecs[l][:], dr["vecs"][l], [], [bC])
    MS("pool", epsb[:], EPS, [bC])
    AC(scT[:], cvec[:], AF.Silu, [bC], [bC])

    NB = 3
    stg = [RW.alloc(2048, F32) for _ in range(NB)]
    wbf = [RW.alloc(2048, BF16) for _ in range(NB)]
    bStg = [Buf() for _ in range(NB)]
    bWb = [[Buf(), Buf(), Buf()] for _ in range(NB)]
    wctr = [0, 0, 0]

    def split_cast(dst, sv, kc, bsrc, bdst3):
        if kc < 8:
            eng = "act" if wctr[2] % 2 == 0 else "dve"
            wctr[2] += 1
            CP(eng, dst, sv, [bsrc], bdst3)
            return lambda k: bdst3[0]
        ka = int(round(kc * 7 / 16.0))
        kb = int(round(kc * 13 / 16.0))
        prev = Sched.deps_of(bdst3)
        for p, (eng, a_, b_) in enumerate((("act", 0, ka), ("dve", ka, kb), ("pool", kb, kc))):
            if b_ > a_:
                CP(eng, dst[:, a_:b_, :], sv[:, a_:b_, :], [bsrc], [bdst3[p]], after=prev)
        return lambda k: bdst3[0 if k < ka else (1 if k < kb else 2)]

    def wload(src, kc, ncol, mode="cast", parts=128):
        n = kc * ncol
        assert n <= 2048
        if mode == "bf16":
            i = wctr[1] % NB
            wctr[1] += 1
            dst = v3(wbf[i][0:parts, 0:n], kc)
            DMA(dst, src, [], bWb[i])
            return dst, (lambda k, i=i: bWb[i][0])
        i = wctr[0] % NB
        wctr[0] += 1
        sv = v3(stg[i][0:parts, 0:n], kc)
        DMA(sv, src, [], [bStg[i]])
        if mode == "f32":
            return sv, (lambda k, i=i: bStg[i])
        j = wctr[1] % NB
        wctr[1] += 1
        dst = v3(wbf[j][0:parts, 0:n], kc)
        return dst, split_cast(dst, sv, kc, bStg[i], bWb[j])

    def make_ws2(RF, nstg=3):
        stg2 = [RF.alloc(4096, F32) for _ in range(nstg)]
        wb2 = [RF.alloc(4096, BF16) for _ in range(2)]
        bS2 = [Buf() for _ in range(nstg)]
        bW2 = [[Buf(), Buf(), Buf()] for _ in range(2)]
        c2 = [0, 0]

        def wload2(src, kc, ncol):
            n = kc * ncol
            assert n <= 4096
            i = c2[0] % nstg
            c2[0] += 1
            j = c2[1] % 2
            c2[1] += 1
            sv = v3(stg2[i][:, 0:n], kc)
            DMA(sv, src, [], [bS2[i]])
            dst = v3(wb2[j][:, 0:n], kc)
            return dst, split_cast(dst, sv, kc, bS2[i], bW2[j])
        return wload2

    def wsrc(w2d, k0, kc, c0, ncol):
        return w2d[k0 * 128:(k0 + kc) * 128, c0:c0 + ncol].rearrange("(k p) c -> p k c", p=128)

    try:
        bXd = [[Buf() for _ in range(16)] for _ in range(3)]
        xin_t = [RA.alloc(2048, F32) for _ in range(2)]
        xtr_t = [RB.alloc(2048, F32) for _ in range(2)]
        bxin = [Buf(), Buf()]
        bxtr = [Buf(), Buf()]
        for tc in range(12):
            xi, bi = xin_t[tc % 2], bxin[tc % 2]
            xo, bo = xtr_t[tc % 2], bxtr[tc % 2]
            DMA(xi[:], dr["x_in"][tc * 128:(tc + 1) * 128, :], [], [bi])
            for half in range(2):
                u = (tc * 2 + half) % 4
                for kk in range(8):
                    k = half * 8 + kk
                    MM(PS[u][:, kk * 128:(kk + 1) * 128], xi[:, k * 128:(k + 1) * 128], ident_f[:], True, True, [bi, bC], [bPS[2 * u + kk // 4]])
                CP("act" if half else "dve", xo[:, half * 1024:(half + 1) * 1024], PS[u][:, :], [bPS[2 * u], bPS[2 * u + 1]], [bo])
            DMA(xTv[:, :, tc * 128:(tc + 1) * 128], v3(xo[:], 16), [bo], bXd[tc // 4])
        s.barrier()
        stage('prepass')

        def rms_rstd(sq_views, nk, width, denom, psb, out_rstd, Rsq, Wout):
            for k in range(nk):
                MM(bank(psb, width), ones_b[:], sq_views(k), k == 0, k == nk - 1, [bC] + Rsq, [bPS[psb]])
            AC(out_rstd, bank(psb, width), AF.Sqrt, [bPS[psb], bC], Wout, bias=epsb[:], scale=1.0 / denom)
            RCP(out_rstd, out_rstd, Wout, Wout)

        for l in range(nlayers):
            vl = vecs[l]
            modT = modTs[l % 2]
            scv = v3(scT[:], 16)
            if l == 0:
                pv = PS[0][:, 0:192].rearrange("p (j m) -> p m j", j=2)
                RF.reset()
                wl2 = make_ws2(RF)
                for mp in range(48):
                    wt, bk = wl2(wsrc(dr["w_mod"][l], 0, 16, mp * 256, 256), 16, 256)
                    for c in range(2):
                        m = 2 * mp + c
                        for k in range(16):
                            MM(pv[:, m, :], wt[:, k, c * 128:(c + 1) * 128], scv[:, k, :], k == 0, k == 15, [bk(k), bC], [bPS[0]])
                for j in range(2):
                    TT("dve", modT[:, j * 96:(j + 1) * 96], PS[0][:, j * 96:(j + 1) * 96], vl[:, V_BMOD:V_BMOD + 96], ALU.add, [bPS[0], bC], [bC])
            for j in range(2):
                STT("dve", gsc[:, (j * 2) * 16:(j * 2) * 16 + 16], modT[:, j * 96 + 16:j * 96 + 32], 1.0, vl[:, V_LNMIX:V_LNMIX + 16], ALU.add, ALU.mult, [bC], [bC])
                STT("dve", gsc[:, (j * 2 + 1) * 16:(j * 2 + 1) * 16 + 16], modT[:, j * 96 + 64:j * 96 + 80], 1.0, vl[:, V_LNFFN:V_LNFFN + 16], ALU.add, ALU.mult, [bC], [bC])
            TT("dve", fb[0:64, 0:1], vl[0:64, V_FR0:V_FR0 + 1], vl[0:64, V_FB1:V_FB1 + 1], ALU.mult, [bC], [bC])
            TT("dve", fb[0:64, 1:2], vl[0:64, V_FR1:V_FR1 + 1], vl[0:64, V_FB2:V_FB2 + 1], ALU.mult, [bC], [bC])
            s.barrier()
            stage('mod')

            for g in range(2):
                j = g
                L = LP if g == 0 else LS
                nseq = 2 if g == 0 else 1
                nblk = 1 if g == 0 else 2
                Ltot = nseq * L
                tok0 = 0 if g == 0 else TP
                Lk = L if g == 0 else L + PAST
                nf = L // 128
                for r_ in (RA, RB, RIN, RT):
                    r_.reset()
                hT = RA.alloc(16 * Ltot, BF16)
                bH = [Buf() for _ in range(nblk)]
                xblk = RB.alloc(16 * 512, F32)
                bxb = Buf()
                sqt = RT.alloc(16 * 512, BF16)
                bsq = Buf()
                rstd = RT.alloc(512, F32)
                brs = Buf()
                hv = v3(hT[:], 16)
                ntmp = RIN.alloc(2 * 512, F32)
                bnt = [Buf(), Buf()]
                for b in range(nblk):
                    blk = (tok0 // 512) + b
                    DMA(v3(xblk[:], 16), xTv[:, :, blk * 512:(blk + 1) * 512], bXd[blk], [bxb])
                    AC(sqt[:], xblk[:], AF.Square, [bxb], [bsq])
                    rms_rstd(lambda k: sqt[:, k * 512:(k + 1) * 512], 16, 512, float(D), 7, rstd[:], [bsq], [brs])
                    for k in range(16):
                        nt_, bn_ = ntmp[:, (k % 2) * 512:(k % 2) * 512 + 512], bnt[k % 2]
                        TT("dve", nt_, xblk[:, k * 512:(k + 1) * 512], rstd[:], ALU.mult, [bxb, brs], [bn_])
                        AC(hv[:, k, b * 512:(b + 1) * 512], nt_, AF.Identity, [bn_, bC], [bH[b]],
                           bias=modT[:, j * 96 + k:j * 96 + k + 1], scale=gsc[:, (j * 2) * 16 + k:(j * 2) * 16 + k + 1])
                s.barrier()
                stage('stepA')
                RB.reset()
                RIN.reset()
                RT.reset()
                hyT = RIN.alloc(12 * Ltot, BF16)
                qT = RIN.alloc(2 * Ltot, BF16)
                kT = RIN.alloc(2 * Ltot, BF16)
                vT = RIN.alloc(4 * Ltot, BF16)
                sgT = RIN.alloc(4 * Ltot, BF16)
                afT = RIN.alloc(Ltot, BF16)
                abT = RIN.alloc(Ltot, BF16)
                cqnT = RIN.alloc(3 * Ltot, BF16)
                keysT = RIN.alloc(2 * nseq * Lk, BF16)
                krT = RIN.alloc(nseq * Lk, BF16)
                bHY = [Buf() for _ in range(12)]
                bQ, bK, bV, bSG, bAF, bAB, bCQN, bKEYS, bKR = (Buf() for _ in range(9))
                hyv = v3(hyT[:], 12)
                qv, kv_, vv, sgv = v3(qT[:], 2), v3(kT[:], 2), v3(vT[:], 4), v3(sgT[:], 4)
                cqnv = v3(cqnT[:], 3)
                keysv = v3(keysT[:], 2)
                ctmp = RT.alloc(Ltot, F32)
                bct = Buf()
                cqf = RT.alloc(3 * Ltot, F32)
                bcqf = Buf()
                sq3 = RT.alloc(3 * 512, BF16)
                bsq3 = Buf()
                rs2 = RT.alloc(512, F32)
                brs2 = Buf()
                cqfv = v3(cqf[:], 3)
                MS("pool", afT[0:32, :], 1.0, [bAF])
                MS("pool", abT[0:32, :], 1.0, [bAB])
                ckvf = RB.alloc(2 * Ltot, F32)
                bckvf = Buf()
                ckvfv = v3(ckvf[:], 2)
                krf = RB.alloc(Ltot, F32)
                bkrf = Buf()
                otr = RB.alloc(2 * 384, F32)
                krtmp = RB.alloc(Ltot, F32)
                bkrt = Buf()
                krtmp2 = RB.alloc(Ltot, F32)
                bkrt2 = Buf()
                botr = [Buf(), Buf()]

                chunks = []
                for i in range(12):
                    chunks.append(("hy", C_HY + i * 128, 128, i))
                for i in range(2):
                    chunks.append(("q", C_Q + i * 128, 128, i))
                for i in range(2):
                    chunks.append(("k", C_K + i * 128, 128, i))
                for i in range(4):
                    chunks.append(("v", C_V + i * 128, 128, i))
                for i in range(4):
                    chunks.append(("g", C_G + i * 128, 128, i))
                chunks.append(("af", C_AF, 16, 0))
                chunks.append(("ab", C_AB, 16, 0))
                for i in range(3):
                    chunks.append(("cq", C_CQ + i * 128, 128, i))
                for i in range(2):
                    chunks.append(("ckv", C_CKV + i * 128, 128, i))
                chunks.append(("kr", C_KR, 64, 0))
                if g == 1:
                    chunks.append(("krsw", C_KRSW, 64, 0))

                import os as _os
                for ci, (kind, c0, ncol, idx) in enumerate(chunks):
                    if g == 1 and ci >= int(_os.environ.get('KCH', '999')):
                        break
                    wt, bw = wload(wsrc(dr["w_in"][l], 0, 16, c0, ncol), 16, ncol)
                    u = ci % 4
                    for b in range(nblk):
                        for k in range(16):
                            MM(PS[u][0:ncol, b * 512:(b + 1) * 512], wt[:, k, :], hv[:, k, b * 512:(b + 1) * 512], k == 0, k == 15, [bw(k), bH[b]], [bPS[2 * u + b]])
                    pb = [bPS[2 * u + b] for b in range(nblk)]
                    psu = PS[u]
                    if kind == "hy":
                        cw = lambda tap: vl[:, V_CW + tap * 12 + idx:V_CW + tap * 12 + idx + 1]
                        for sq_ in range(nseq):
                            a, e_ = sq_ * L, (sq_ + 1) * L
                            AC(ctmp[:, a:e_], psu[:, a:e_], AF.Identity, pb + [bC], [bct], bias=vl[:, V_CB + idx:V_CB + idx + 1], scale=cw(1))
                            STT("dve", ctmp[:, a + 1:e_], psu[:, a:e_ - 1], cw(0), ctmp[:, a + 1:e_], ALU.mult, ALU.add, pb + [bC, bct], [bct])
                            STT("dve", hyv[:, idx, a:e_ - 1], psu[:, a + 1:e_], cw(2), ctmp[:, a:e_ - 1], ALU.mult, ALU.add, pb + [bC, bct], [bHY[idx]])
                            CP("pool", hyv[:, idx, e_ - 1:e_], ctmp[:, e_ - 1:e_], [bct], [bHY[idx]])
                    elif kind == "q":
                        AC(qv[:, idx, :], psu[:, 0:Ltot], AF.Identity, pb, [bQ], scale=0.125)
                    elif kind == "k":
                        CP("dve", kv_[:, idx, :], psu[:, 0:Ltot], pb, [bK])
                    elif kind == "v":
                        CP("dve", vv[:, idx, :], psu[:, 0:Ltot], pb, [bV])
                    elif kind == "g":
                        AC(sgv[:, idx, :], psu[:, 0:Ltot], AF.Silu, pb, [bSG])
                    elif kind == "af":
                        CP("dve", afT[0:16, :], psu[0:16, 0:Ltot], pb, [bAF])
                    elif kind == "ab":
                        CP("dve", abT[0:16, :], psu[0:16, 0:Ltot], pb, [bAB])
                    elif kind == "cq":
                        CP("dve", cqfv[:, idx, :], psu[:, 0:Ltot], pb, [bcqf])
                        if idx == 2:
                            for b in range(nblk):
                                sl = slice(b * 512, (b + 1) * 512)
                                for k in range(3):
                                    AC(sq3[:, k * 512:(k + 1) * 512], cqfv[:, k, sl], AF.Square, [bcqf], [bsq3])
                                rms_rstd(lambda k: sq3[:, k * 512:(k + 1) * 512], 3, 512, 384.0, 7, rs2[:], [bsq3], [brs2])
                                for k in range(3):
                                    STT("dve", cqnv[:, k, sl], cqfv[:, k, sl], vl[:, V_QN + k:V_QN + k + 1], rs2[:], ALU.mult, ALU.mult, [bcqf, brs2, bC], [bCQN])
                    elif kind == "ckv":
                        CP("dve", ckvfv[:, idx, :], psu[:, 0:Ltot], pb, [bckvf])
                        if idx == 1:
                            for b in range(nblk):
                                sl = slice(b * 512, (b + 1) * 512)
                                for k in range(2):
                                    AC(sq3[:, k * 512:(k + 1) * 512], ckvfv[:, k, sl], AF.Square, [bckvf], [bsq3])
                                rms_rstd(lambda k: sq3[:, k * 512:(k + 1) * 512], 2, 512, 256.0, 7, rs2[:], [bsq3], [brs2])
                                for k in range(2):
                                    STT("dve", ckvfv[:, k, sl], ckvfv[:, k, sl], vl[:, V_KVN + k:V_KVN + k + 1], rs2[:], ALU.mult, ALU.mult, [bckvf, brs2, bC], [bckvf])
                            for sq_ in range(nseq):
                                for k in range(2):
                                    CP("pool", keysv[:, k, sq_ * Lk:sq_ * Lk + L], ckvfv[:, k, sq_ * L:(sq_ + 1) * L], [bckvf], [bKEYS])
                    elif kind == "kr":
                        if g == 0:
                            CP("dve", krf[0:64, 0:Ltot], psu[0:64, 0:Ltot], pb, [bkrf])
                            CP("pool", krT[0:64, 0:Ltot], krf[0:64, 0:Ltot], [bkrf], [bKR])
                        else:
                            TT("dve", krtmp[0:64, :], psu[0:64, 0:Ltot], rope[0:64, 0:LS], ALU.mult, pb + [bC], [bkrt])
                    elif kind == "krsw":
                        TT("dve", krtmp2[0:64, :], psu[0:64, 0:Ltot], rope[0:64, LS:2 * LS], ALU.mult, pb + [bC], [bkrt2])
                        TT("pool", krT[0:64, 0:L], krtmp[0:64, :], krtmp2[0:64, :], ALU.add, [bkrt, bkrt2], [bKR])
                if g == 0:
                    for sq_ in range(2):
                        for tcn in range(2):
                            ot, bo = otr[:, ((sq_ * 2 + tcn) % 2) * 384:((sq_ * 2 + tcn) % 2) * 384 + 384], botr[(sq_ * 2 + tcn) % 2]
                            ts_ = slice(sq_ * L + tcn * 128, sq_ * L + (tcn + 1) * 128)
                            for k in range(2):
                                MM(bank(6, 128, k * 128), ckvfv[:, k, ts_], ident_f[:], True, True, [bckvf, bC], [bPS[6]])
                            MM(bank(6, 64, 256), krf[0:64, ts_], ident_f[0:64, 0:64], True, True, [bkrf, bC], [bPS[6]])
                            CP("act", ot[:, 0:320], bank(6, 320), [bPS[6]], [bo])
                            DMA(dr["o_ckv"][sq_, l, tcn * 128:(tcn + 1) * 128, :], ot[:, 0:256], [bo], [])
                            DMA(dr["o_kr"][sq_, l, tcn * 128:(tcn + 1) * 128, :], ot[:, 256:320], [bo], [])
                elif int(_os.environ.get('KCACHE', '9')) > 0:
                    cst = RB.alloc(2 * 384, F32)
                    bcst = [Buf(), Buf()]
                    kc2 = int(_os.environ.get('KCACHE', '9'))
                    for jc in range(2):
                        if kc2 < 9:
                            cs = cst[:, jc * 384:(jc + 1) * 384]
                            if kc2 >= 1:
                                MS("pool", cs[:, 320:384], 0.0, [bcst[jc]])
                                DMA(cs[:, 0:256], dr["ckv_c"][l, jc * 128:(jc + 1) * 128, :], [], [bcst[jc]])
                            if kc2 >= 2:
                                DMA(cs[:, 256:320], dr["kr_c"][l, jc * 128:(jc + 1) * 128, :], [], [bcst[jc]])
                            if kc2 >= 3:
                                for k in range(3):
                                    MM(bank(6, 128, k * 128), cs[:, k * 128:(k + 1) * 128], ident_f[:], True, True, [bcst[jc], bC], [bPS[6]])
                            if kc2 >= 4:
                                for k in range(2):
                                    CP("act", keysv[:, k, L + jc * 128:L + (jc + 1) * 128], bank(6, 128, k * 128), [bPS[6]], [bKEYS])
                            continue
                        cs = cst[:, jc * 384:(jc + 1) * 384]
                        MS("pool", cs[:, 320:384], 0.0, [bcst[jc]])
                        DMA(cs[:, 0:256], dr["ckv_c"][l, jc * 128:(jc + 1) * 128, :], [], [bcst[jc]])
                        DMA(cs[:, 256:320], dr["kr_c"][l, jc * 128:(jc + 1) * 128, :], [], [bcst[jc]])
                        for k in range(3):
                            MM(bank(6, 128, k * 128), cs[:, k * 128:(k + 1) * 128], ident_f[:], True, True, [bcst[jc], bC], [bPS[6]])
                        for k in range(2):
                            CP("act", keysv[:, k, L + jc * 128:L + (jc + 1) * 128], bank(6, 128, k * 128), [bPS[6]], [bKEYS])
                        CP("act", krT[0:64, L + jc * 128:L + (jc + 1) * 128], bank(6, 128, 256)[0:64, :], [bPS[6]], [bKR])
                s.barrier()
                stage('stepB')
                RA.reset()
                RB.reset()
                RT.reset()
                mixT = RB.alloc(16 * Ltot, BF16)
                mixv = v3(mixT[:], 16)
                bMIX = [Buf() for _ in range(16)]
                if dbg and ("hy_%d_%d" % (l, g)) in dbg_out:
                    for q4 in range(3):
                        dt_ = RT.alloc(4 * Ltot, F32)
                        bd = Buf()
                        CP("dve", v3(dt_[:], 4), hyv[:, q4 * 4:(q4 + 1) * 4, :], bHY, [bd])
                        DMA(dbg_out["hy_%d_%d" % (l, g)][q4 * 512:(q4 + 1) * 512, :].rearrange("(k p) t -> p k t", p=128), v3(dt_[:], 4), [bd], [])
                        s.barrier()
                        RT.reset()

                scale = 192.0 ** -0.5
                qn = [RA.alloc(Ltot, BF16) for _ in range(2)]
                qr = [RA.alloc(Ltot, BF16) for _ in range(2)]
                kn = [RA.alloc(nseq * Lk, BF16) for _ in range(2)]
                vh = [RA.alloc(nseq * Lk, BF16) for _ in range(2)]
                bqn, bqr, bkn, bvh = ([Buf(), Buf()] for _ in range(4))
                pt = [RT.alloc(512, BF16) for _ in range(3)]
                bpt = [Buf() for _ in range(3)]
                rec = [RT.alloc(512, F32) for _ in range(2)]
                brec = [Buf(), Buf()]
                rt1 = RT.alloc(512, F32)
                rt2 = RT.alloc(512, F32)
                brt1, brt2 = Buf(), Buf()
                nkc = Lk // 128
                qblocks = []
                for sq_ in range(nseq):
                    for qb in range(max(1, L // 512)):
                        qn_ = min(L, 512)
                        qblocks.append((sq_, sq_ * L + qb * qn_, qn_))
                ptc = 0
                acc = 0
                for h in range(8):
                    hb = h % 2
                    wq, bwq = wload(wsrc(dr["w_uq"][l], 0, 3, h * 192, 192), 3, 192)
                    if g == 1:
                        wqs, bwqs = wload(wsrc(dr["w_uq"][l], 0, 3, 1536 + h * 64, 64), 3, 64)
                    wkv, bwkv = wload(wsrc(dr["w_ukv"][l], 0, 2, h * 256, 256), 2, 256)
                    for b in range(nblk):
                        sl = slice(b * 512, (b + 1) * 512)
                        for k in range(3):
                            MM(bank(7), wq[:, k, 0:128], cqnv[:, k, sl], k == 0, k == 2, [bwq(0), bCQN], [bPS[7]])
                        CP("act", qn[hb][:, sl], bank(7), [bPS[7]], [bqn[hb]])
                        for k in range(3):
                            MM(PS[3][0:64, 512:1024], wq[:, k, 128:192], cqnv[:, k, sl], k == 0, k == 2, [bwq(0), bCQN], [bPS[7]])
                        if g == 0:
                            CP("dve", qr[hb][0:64, sl], PS[3][0:64, 512:1024], [bPS[7]], [bqr[hb]])
                        else:
                            TT("dve", rt1[0:64, :], PS[3][0:64, 512:1024], rope[0:64, b * 512:(b + 1) * 512], ALU.mult, [bPS[7], bC], [brt1])
                            for k in range(3):
                                MM(PS[3][0:64, 512:1024], wqs[:, k, :], cqnv[:, k, sl], k == 0, k == 2, [bwqs(0), bCQN], [bPS[7]])
                            TT("dve", rt2[0:64, :], PS[3][0:64, 512:1024], rope[0:64, LS + b * 512:LS + (b + 1) * 512], ALU.mult, [bPS[7], bC], [brt2])
                            TT("pool", qr[hb][0:64, sl], rt1[0:64, :], rt2[0:64, :], ALU.add, [brt1, brt2], [bqr[hb]])
                    ktot = nseq * Lk
                    for c0 in range(0, ktot, 512):
                        n_ = min(512, ktot - c0)
                        for k in range(2):
                            MM(bank(7, n_), wkv[:, k, 0:128], keysv[:, k, c0:c0 + n_], k == 0, k == 1, [bwkv(0), bKEYS], [bPS[7]])
                        CP("act", kn[hb][:, c0:c0 + n_], bank(7, n_), [bPS[7]], [bkn[hb]])
                    for c0 in range(0, ktot, 512):
                        n_ = min(512, ktot - c0)
                        for jj in range(n_ // 128):
                            for k in range(2):
                                MM(bank(7, 128, jj * 128), keysv[:, k, c0 + jj * 128:c0 + (jj + 1) * 128], wkv[:, k, 128:256], k == 0, k == 1, [bwkv(0), bKEYS], [bPS[7]])
                        CP("dve", vh[hb][:, c0:c0 + n_], bank(7, n_), [bPS[7]], [bvh[hb]])
                    for (sq_, q0, qn_) in qblocks:
                        ob, db = 3 + 2 * (acc % 2), 4 + 2 * (acc % 2)
                        acc += 1
                        def emit_sc(jk_, ptc0=ptc, sq_=sq_, q0=q0, qn_=qn_, hb=hb):
                            sb_ = (ptc0 + jk_) % 3
                            kc0 = sq_ * Lk + jk_ * 128
                            MM(bank(sb_, qn_), kn[hb][:, kc0:kc0 + 128], qn[hb][:, q0:q0 + qn_], True, False, [bkn[hb], bqn[hb]], [bPS[sb_]])
                            MM(bank(sb_, qn_), krT[0:64, kc0:kc0 + 128], qr[hb][0:64, q0:q0 + qn_], False, True, [bKR, bqr[hb]], [bPS[sb_]])
                            AC(pt[sb_][:, 0:qn_], bank(sb_, qn_), AF.Exp, [bPS[sb_]], [bpt[sb_]], scale=scale)
                        for jk in range(min(2, nkc)):
                            emit_sc(jk)
                        for jk in range(nkc):
                            sb_ = (ptc + jk) % 3
                            kc0 = sq_ * Lk + jk * 128
                            MM(bank(ob, qn_), vh[hb][:, kc0:kc0 + 128], pt[sb_][:, 0:qn_], jk == 0, jk == nkc - 1, [bvh[hb], bpt[sb_]], [bPS[ob]])
                            MM(bank(db, qn_), ones_b[:], pt[sb_][:, 0:qn_], jk == 0, jk == nkc - 1, [bC, bpt[sb_]], [bPS[db]])
                            if jk + 2 < nkc:
                                emit_sc(jk + 2)
                        ptc += nkc
                        rc, brc = rec[acc % 2], brec[acc % 2]
                        RCP(rc[:, 0:qn_], bank(db, qn_), [bPS[db]], [brc])
                        TT("dve", mixv[:, 8 + h, q0:q0 + qn_], bank(ob, qn_), rc[:, 0:qn_], ALU.mult, [bPS[ob], brc], [bMIX[8 + h]])
                s.barrier()
                stage('mla')
                RA.reset()
                RT.reset()

                if dbg and ("glain_%d_%d" % (l, g)) in dbg_out:
                    for nm_, t_, nk_, off_ in (("q", qT, 2, 0), ("k", kT, 2, 256), ("v", vT, 4, 512), ("sg", sgT, 4, 1024)):
                        RT.reset()
                        dt_ = RT.alloc(4 * Ltot, F32)
                        bd = Buf()
                        CP("dve", dt_[:, 0:nk_ * Ltot], t_[:], [bQ, bK, bV, bSG], [bd])
                        DMA(dbg_out["glain_%d_%d" % (l, g)][off_:off_ + nk_ * 128, :].rearrange("(k p) t -> p k t", p=128), v3(dt_[:, 0:nk_ * Ltot], nk_), [bd], [])
                        s.barrier()
                    RT.reset()
                    dt_ = RT.alloc(2 * Ltot, F32)
                    bd = Buf()
                    CP("dve", dt_[0:32, 0:Ltot], afT[0:32, :], [bAF], [bd])
                    CP("dve", dt_[0:32, Ltot:2 * Ltot], abT[0:32, :], [bAB], [bd])
                    DMA(dbg_out["glain_%d_%d" % (l, g)][1536:1568, :], dt_[0:32, 0:Ltot], [bd], [])
                    DMA(dbg_out["glain_%d_%d" % (l, g)][1568:1600, :], dt_[0:32, Ltot:2 * Ltot], [bd], [])
                    s.barrier()
                    RT.reset()
                oacc = RA.alloc(4 * Ltot, F32)
                oav = v3(oacc[:], 4)
                bOA = Buf()
                NT = 2
                ltok = [RT.alloc(256, F32) for _ in range(NT)]
                ep = [RT.alloc(256, F32) for _ in range(NT)]
                em = [RT.alloc(256, F32) for _ in range(NT)]
                er = [RT.alloc(256, F32) for _ in range(NT)]
                qd = [RT.alloc(512, BF16) for _ in range(NT)]
                ki = [RT.alloc(256, BF16) for _ in range(NT)]
                ke = [RT.alloc(256, BF16) for _ in range(NT)]
                vt = [RT.alloc(512, BF16) for _ in range(NT)]
                at = [RT.alloc(512, BF16) for _ in range(NT)]
                bl, bep, bem, ber, bqd, bki, bke, bvt, bat = ([Buf() for _ in range(NT)] for _ in range(9))
                for i0 in range(NT):
                    MS("pool", qd[i0][:], 0.0, [bqd[i0]])
                Sst = RT.alloc(256, F32)
                Sbf = RT.alloc(256, BF16)
                bSst, bSbf = Buf(), Buf()
                waug = RA.alloc(512, F32)
                waub = RA.alloc(512, BF16)
                bwa = Buf()
                for dd in range(2):
                    DMA(waug[0:17, dd * 256:(dd + 1) * 256], dr["wa_aug"][l, dd], [], [bwa])
                CP("dve", waub[0:17, :], waug[0:17, :], [bwa], [bwa])
                it = 0
                for sq_ in range(nseq):
                    for dd in range(2):
                        if g == 0:
                            MS("pool", Sst[:], 0.0, [bSst])
                        else:
                            DMA(v3(Sst[:], 2), dr["sf_c" if dd == 0 else "sb_c"][l].rearrange("(hp p) v -> p hp v", p=128), [], [bSst])
                        CP("pool", Sbf[:], Sst[:], [bSst], [bSbf])
                        aT, bA_ = (afT, bAF) if dd == 0 else (abT, bAB)
                        order = range(nf) if dd == 0 else range(nf - 1, -1, -1)
                        for n_ in order:
                            i_ = it % NT
                            it += 1
                            if knob is not None and it > knob[0]:
                                raise _Stop()
                            ts_ = slice(sq_ * L + n_ * 128, sq_ * L + (n_ + 1) * 128)

                            def kc_(n):
                                if knob is not None and it == knob[0] and n >= knob[1]:
                                    raise _Stop()
                            MM(bank(0, 256), aT[0:17, ts_], waub[0:17, dd * 256:(dd + 1) * 256], True, True, [bA_, bwa], [bPS[0]])
                            AC(ltok[i_][:], bank(0, 256), AF.Exp, [bPS[0]], [bl[i_]], scale=-1.0)
                            AC(ltok[i_][:], ltok[i_][:], AF.Ln, [bl[i_]], [bl[i_]], bias=1.0)
                            kc_(1)
                            triI = tri[:, (2 * dd) * 128:(2 * dd + 1) * 128]
                            triE = tri[:, (2 * dd + 1) * 128:(2 * dd + 2) * 128]
                            for hp in range(2):
                                MM(bank(1, 128, hp * 128), ltok[i_][:, hp * 128:(hp + 1) * 128], triI, True, True, [bl[i_], bC], [bPS[1]])
                            MM(bank(1, 256, 256), triE, ltok[i_][:], True, True, [bl[i_], bC], [bPS[1]])
                            kc_(2)
                            AC(ep[i_][:], bank(1, 256), AF.Exp, [bPS[1]], [bep[i_]])
                            AC(em[i_][:], bank(1, 256), AF.Exp, [bPS[1]], [bem[i_]], scale=-1.0)
                            AC(er[i_][:], bank(1, 256, 256), AF.Exp, [bPS[1]], [ber[i_]])
                            for hp in range(2):
                                for h2 in range(2):
                                    r0 = h2 * 64
                                    TT("dve", qd[i_][r0:r0 + 64, (hp * 2 + h2) * 128:(hp * 2 + h2 + 1) * 128], qv[r0:r0 + 64, hp, ts_], ep[i_][r0:r0 + 64, hp * 128:(hp + 1) * 128], ALU.mult, [bQ, bep[i_], bqd[i_]], [bqd[i_]])
                                TT("dve", ki[i_][:, hp * 128:(hp + 1) * 128], kv_[:, hp, ts_], em[i_][:, hp * 128:(hp + 1) * 128], ALU.mult, [bK, bem[i_]], [bki[i_]])
                            kc_(3)
                            for hp in range(2):
                                MM(bank(2, 128, hp * 128), kv_[:, hp, ts_], ident_b[:], True, True, [bK, bC], [bPS[2]])
                            TT("dve", ke[i_][:], bank(2, 256), er[i_][:], ALU.mult, [bPS[2], ber[i_]], [bke[i_]])
                            for hh in range(4):
                                MM(bank(3, 128, hh * 128), vv[:, hh, ts_], ident_b[:], True, True, [bV, bC], [bPS[3]])
                            CP("act", vt[i_][:], bank(3), [bPS[3]], [bvt[i_]])
                            kc_(4)
                            for hp in range(2):
                                MM(bank(4, 256, hp * 256), ki[i_][:, hp * 128:(hp + 1) * 128], qd[i_][:, hp * 256:(hp + 1) * 256], True, True, [bki[i_], bqd[i_]], [bPS[4]])
                            TT("dve", at[i_][:], bank(4), mask[:, dd * 512:(dd + 1) * 512], ALU.mult, [bPS[4], bC], [bat[i_]])
                            for hh in range(4):
                                hp, h2 = hh // 2, hh % 2
                                r0 = h2 * 64
                                MM(bank(5, 128, hh * 128), vt[i_][:, hh * 128:(hh + 1) * 128], at[i_][:, hh * 128:(hh + 1) * 128], True, False, [bvt[i_], bat[i_]], [bPS[5]])
                                MM(bank(5, 128, hh * 128), Sbf[:, hp * 128:(hp + 1) * 128], qd[i_][:, hh * 128:(hh + 1) * 128], False, True, [bSbf, bqd[i_]], [bPS[5]])
                            kc_(5)
                            o4 = bank(5).rearrange("p (h c) -> p h c", h=4)
                            if dd == 0:
                                CP("act", oav[:, :, ts_], o4, [bPS[5]], [bOA])
                            else:
                                TT("dve", oav[:, :, ts_], o4, oav[:, :, ts_], ALU.add, [bPS[5], bOA], [bOA])
                            kc_(6)
                            for hp in range(2):
                                MM(bank(6 + hp, 256), ke[i_][:, hp * 128:(hp + 1) * 128], vt[i_][:, hp * 256:(hp + 1) * 256], True, True, [bke[i_], bvt[i_]], [bPS[6 + hp]])
                            kc_(7)
                            dcol = 127 if dd == 0 else 0
                            for hp in range(2):
                                for h2 in range(2):
                                    r0 = h2 * 64
                                    STT("dve", Sst[r0:r0 + 64, hp * 128:(hp + 1) * 128], Sst[r0:r0 + 64, hp * 128:(hp + 1) * 128],
                                        ep[i_][r0:r0 + 64, hp * 128 + dcol:hp * 128 + dcol + 1], bank(6 + hp, 128, h2 * 128)[r0:r0 + 64, :],
                                        ALU.mult, ALU.add, [bSst, bep[i_], bPS[6 + hp]], [bSst])
                            CP("pool", Sbf[:], Sst[:], [bSst], [bSbf])
                            if _os.environ.get('GLA_BAR'):
                                s.barrier()
                        if g == 0:
                            DMA(dr["o_sf" if dd == 0 else "o_sb"][sq_, l].rearrange("(hp p) v -> p hp v", p=128), v3(Sst[:], 2), [bSst], [])
                if dbg and ("oacc_%d_%d" % (l, g)) in dbg_out:
                    s.barrier()
                    DMA(dbg_out["oacc_%d_%d" % (l, g)].rearrange("(k p) t -> p k t", p=128), oav, [bOA], [])
                    s.barrier()
                gsq = RA.alloc(512, BF16)
                bgsq = Buf()
                grs = RA.alloc(512, F32)
                bgrs = Buf()
                gtm = RA.alloc(512, F32)
                bgtm = Buf()
                for hh in range(4):
                    for b in range(nblk):
                        sl = slice(b * 512, (b + 1) * 512)
                        AC(gsq[:], oav[:, hh, sl], AF.Square, [bOA], [bgsq])
                        rms_rstd(lambda k: gsq[:], 1, 512, 128.0, 0, grs[:], [bgsq], [bgrs])
                        STT("dve", gtm[:], oav[:, hh, sl], vl[:, V_GN:V_GN + 1], grs[:], ALU.mult, ALU.mult, [bOA, bC, bgrs], [bgtm])
                        TT("pool", mixv[:, 4 + hh, sl], gtm[:], sgv[:, hh, sl], ALU.mult, [bgtm, bSG], [bMIX[4 + hh]])
                s.barrier()
                stage('gla')
                RA.reset()
                RT.reset()

                RIN.reset()
                _ = RIN.alloc(12 * Ltot, BF16)
                ksd = RIN.alloc(2 * nf * 512, BF16)
                bKSD = Buf()
                ksdv = ksd[:].rearrange("p (a n c) -> p a n c", a=2, n=nf)
                h1T = RIN.alloc(L, F32)
                h2T = RIN.alloc(L, F32)
                bh1, bh2 = Buf(), Buf()
                zemb = RIN.alloc(L, F32)
                fw12 = RIN.alloc(128, F32)
                bfz = Buf()
                arg = RIN.alloc(512, F32)
                arg2 = RIN.alloc(512, F32)
                barg, barg2 = Buf(), Buf()
                dec_t = RIN.alloc(512, F32)
                bdec = Buf()
                bsb_t = RIN.alloc(512, F32)
                bbsb = Buf()
                sm_t = RIN.alloc(512, F32)
                df_t = RIN.alloc(512, F32)
                bsm, bdf = Buf(), Buf()
                DMA(zemb[0:33, :], dr["zemb%d" % L], [], [bfz])
                DMA(fw12[0:33, 0:64], dr["fw1"][l], [], [bfz])
                DMA(fw12[0:64, 64:128], dr["fw2"][l], [], [bfz])
                ztok = RA.alloc(nseq * nf * 512, BF16)
                ztv = ztok[:].rearrange("p (s n c) -> p s n c", s=nseq, n=nf)
                bZT = Buf()
                p1d = RA.alloc(2 * nseq * nf * 512, BF16)
                p1v = p1d[:].rearrange("p (a s n c) -> p a s n c", a=2, s=nseq, n=nf)
                bP1 = Buf()
                kab = [RT.alloc(1024, F32) for _ in range(2)]
                bkab = [Buf(), Buf()]
                tt_ = [RT.alloc(512, F32) for _ in range(4)]
                btt = [Buf() for _ in range(4)]
                utmp = RT.alloc(512, F32)
                butmp = Buf()
                delta = RT.alloc(512, F32)
                negt = RT.alloc(8, F32)
                DMA(delta[:], dr["delta"], [], [bfz])
                DMA(negt[:, 0:nf], dr["negt%d" % L], [], [bfz])

                def sin_layer(pre_ps, n_, fcol, fbcol, out_ap, Rps, Wout):
                    TS("dve", arg[0:64, 0:n_], pre_ps, vl[0:64, fcol:fcol + 1], fb[0:64, fbcol:fbcol + 1], ALU.mult, ALU.add, Rps + [bC], [barg])
                    TS("dve", arg2[0:64, 0:n_], arg[0:64, 0:n_], 1.0 / TWO_PI, MAGIC, ALU.mult, ALU.add, [barg], [barg2])
                    TS("dve", arg2[0:64, 0:n_], arg2[0:64, 0:n_], MAGIC, -TWO_PI, ALU.subtract, ALU.mult, [barg2], [barg2])
                    TT("dve", arg[0:64, 0:n_], arg[0:64, 0:n_], arg2[0:64, 0:n_], ALU.add, [barg, barg2], [barg])
                    AC(out_ap, arg[0:64, 0:n_], AF.Sin, [barg], Wout)

                for c0 in range(0, L, 512):
                    n_ = min(512, L - c0)
                    MM(PS[0][0:64, 0:n_], fw12[0:33, 0:64], zemb[0:33, c0:c0 + n_], True, True, [bfz], [bPS[0]])
                    sin_layer(PS[0][0:64, 0:n_], n_, V_FR0, 0, h1T[0:64, c0:c0 + n_], [bPS[0]], [bh1])
                for c0 in range(0, L, 512):
                    n_ = min(512, L - c0)
                    MM(PS[0][0:64, 0:n_], fw12[0:64, 64:128], h1T[0:64, c0:c0 + n_], True, True, [bfz, bh1], [bPS[0]])
                    sin_layer(PS[0][0:64, 0:n_], n_, V_FR1, 1, h2T[0:64, c0:c0 + n_], [bPS[0]], [bh2])

                zcur = [hyv[:, cc, :] for cc in range(4)]
                zbuf = [bHY[cc] for cc in range(4)]
                for o in range(2):
                    w3t, bw3 = wload(dr["fw3"][l].rearrange("p (k c) -> p k c", k=1), 1, 2048, mode="f32", parts=64)
                    for lc in range(nf):
                        AC(dec_t[:], delta[:], AF.Exp, [bfz], [bdec], scale=negt[:, lc:lc + 1])
                        for side in range(2):
                            cofs = (side * 2 + o) * 512
                            MM(bank(side), h2T[0:64, lc * 128:(lc + 1) * 128], w3t[0:64, 0, cofs:cofs + 512], True, True, [bh2, bw3(0)], [bPS[side]])
                        CP("act", bsb_t[:], bank(1), [bPS[1]], [bbsb])
                        if lc == 0:
                            MS("pool", bsb_t[0:1, :], 0.0, [bbsb])
                        TT("dve", sm_t[:], bank(0), bsb_t[:], ALU.add, [bPS[0], bbsb], [bsm])
                        TT("dve", df_t[:], bank(0), bsb_t[:], ALU.subtract, [bPS[0], bbsb], [bdf])
                        TT("pool", ksdv[:, 0, lc, :], sm_t[:], dec_t[:], ALU.mult, [bsm, bdec], [bKSD])
                        TT("pool", ksdv[:, 1, lc, :], df_t[:], dec_t[:], ALU.mult, [bdf, bdec], [bKSD])
                    for sq_ in range(nseq):
                        for tcn in range(nf):
                            ts_ = slice(sq_ * L + tcn * 128, sq_ * L + (tcn + 1) * 128)
                            pbk = 2 + (tcn % 2)
                            for cc in range(4):
                                MM(bank(pbk, 128, cc * 128), zcur[cc][:, ts_], ident_b[:], True, True, [zbuf[cc], bC], [bPS[pbk]])
                            CP("act", ztv[:, sq_, tcn, :], bank(pbk), [bPS[pbk]], [bZT])
                    for fb_ in range(nf // 2):
                        Cp, bCp = wload(wsrc(dr["dftC%d" % L], 0, nf, fb_ * 256, 256), nf, 256, mode="bf16")
                        Sp, bSp = wload(wsrc(dr["dftS%d" % L], 0, nf, fb_ * 256, 256), nf, 256, mode="bf16")
                        for f2 in range(2):
                            fc = fb_ * 2 + f2
                            fcs = slice(f2 * 128, (f2 + 1) * 128)
                            kk, bk_ = kab[fc % 2], bkab[fc % 2]
                            for n_ in range(nf):
                                MM(bank(0), Cp[:, n_, fcs], ksdv[:, 0, n_, :], n_ == 0, n_ == nf - 1, [bCp(0), bKSD], [bPS[0]])
                            for n_ in range(nf):
                                MM(bank(1), Sp[:, n_, fcs], ksdv[:, 1, n_, :], n_ == 0, n_ == nf - 1, [bSp(0), bKSD], [bPS[1]])
                            AC(kk[:, 0:512], bank(0), AF.Identity, [bPS[0]], [bk_], scale=1.0 / L)
                            AC(kk[:, 512:1024], bank(1), AF.Identity, [bPS[1]], [bk_], scale=1.0 / L)
                            for sq_ in range(nseq):
                                pa, pb_ = (4, 5) if (fc * nseq + sq_) % 2 == 0 else (6, 7)
                                for n_ in range(nf):
                                    MM(bank(pa), Cp[:, n_, fcs], ztv[:, sq_, n_, :], n_ == 0, n_ == nf - 1, [bCp(0), bZT], [bPS[pa]])
                                for n_ in range(nf):
                                    MM(bank(pb_), Sp[:, n_, fcs], ztv[:, sq_, n_, :], n_ == 0, n_ == nf - 1, [bSp(0), bZT], [bPS[pb_]])
                                TT("dve", tt_[0][:], bank(pa), kk[:, 0:512], ALU.mult, [bPS[pa], bk_], [btt[0]])
                                TT("dve", tt_[1][:], bank(pb_), kk[:, 512:1024], ALU.mult, [bPS[pb_], bk_], [btt[1]])
                                TT("pool", p1v[:, 0, sq_, fc, :], tt_[0][:], tt_[1][:], ALU.subtract, [btt[0], btt[1]], [bP1])
                                TT("dve", tt_[2][:], bank(pa), kk[:, 512:1024], ALU.mult, [bPS[pa], bk_], [btt[2]])
                                TT("dve", tt_[3][:], bank(pb_), kk[:, 0:512], ALU.mult, [bPS[pb_], bk_], [btt[3]])
                                TT("pool", p1v[:, 1, sq_, fc, :], tt_[2][:], tt_[3][:], ALU.add, [btt[2], btt[3]], [bP1])
                    xo = [hyv[:, 4 * (o + 1) + cc, :] for cc in range(4)]
                    xob = [bHY[4 * (o + 1) + cc] for cc in range(4)]
                    it2 = 0
                    for sq_ in range(nseq):
                        for tb in range(L // 256):
                            CTp, bCTp = wload(wsrc(dr["dftCT%d" % L], 0, nf, tb * 256, 256), nf, 256, mode="bf16")
                            STp, bSTp = wload(wsrc(dr["dftST%d" % L], 0, nf, tb * 256, 256), nf, 256, mode="bf16")
                            ts_ = slice(sq_ * L + tb * 256, sq_ * L + (tb + 1) * 256)
                            for cc in range(4):
                                pbk = it2 % 4
                                it2 += 1
                                ccs = slice(cc * 128, (cc + 1) * 128)
                                for n_ in range(nf):
                                    MM(bank(pbk, 256), p1v[:, 0, sq_, n_, ccs], CTp[:, n_, :], n_ == 0, False, [bP1, bCTp(0)], [bPS[pbk]])
                                for n_ in range(nf):
                                    MM(bank(pbk, 256), p1v[:, 1, sq_, n_, ccs], STp[:, n_, :], False, n_ == nf - 1, [bP1, bSTp(0)], [bPS[pbk]])
                                STT("dve", utmp[:, 0:256], zcur[cc][:, ts_], vl[:, V_SKIP + o * 4 + cc:V_SKIP + o * 4 + cc + 1], bank(pbk, 256), ALU.mult, ALU.add, [zbuf[cc], bC, bPS[pbk]], [butmp])
                                if o == 0:
                                    TT("pool", zcur[cc][:, ts_], utmp[:, 0:256], xo[cc][:, ts_], ALU.mult, [butmp, xob[cc]], [zbuf[cc]])
                                else:
                                    TT("pool", mixv[:, cc, ts_], utmp[:, 0:256], xo[cc][:, ts_], ALU.mult, [butmp, xob[cc]], [bMIX[cc]])
                if dbg and ("mix_%d_%d" % (l, g)) in dbg_out:
                    s.barrier()
                    for q4 in range(4):
                        RT.reset()
                        dt_ = RT.alloc(4 * Ltot, F32)
                        bd = Buf()
                        CP("dve", v3(dt_[:], 4), mixv[:, q4 * 4:(q4 + 1) * 4, :], bMIX, [bd])
                        DMA(dbg_out["mix_%d_%d" % (l, g)][q4 * 512:(q4 + 1) * 512, :].rearrange("(k p) t -> p k t", p=128), v3(dt_[:], 4), [bd], [])
                        s.barrier()
                s.barrier()
                stage('hyena')
                RA.reset()
                RT.reset()
                RIN.reset()

                xc = [RT.alloc(512, F32) for _ in range(3)]
                bxc = [Buf() for _ in range(3)]
                it3 = 0
                for i in range(16):
                    wt, bw = wload(wsrc(dr["w_out"][l], 0, 16, i * 128, 128), 16, 128)
                    u = i % 4
                    for b in range(nblk):
                        blk = (tok0 // 512) + b
                        xi_, bxi = xc[it3 % 3], bxc[it3 % 3]
                        it3 += 1
                        DMA(xi_[:], xTv[:, i, blk * 512:(blk + 1) * 512], [bXd[blk][i]], [bxi])
                        for k in range(16):
                            MM(bank(2 * u + b), wt[:, k, :], mixv[:, k, b * 512:(b + 1) * 512], k == 0, k == 15, [bw(k), bMIX[k]], [bPS[2 * u + b]])
                        STT("dve", xi_[:], bank(2 * u + b), modT[:, j * 96 + 32 + i:j * 96 + 32 + i + 1], xi_[:], ALU.mult, ALU.add, [bPS[2 * u + b], bC, bxi], [bxi])
                        DMA(xTv[:, i, blk * 512:(blk + 1) * 512], xi_[:], [bxi], [bXd[blk][i]])
                s.barrier()
                stage('outproj')

            last = (l == nlayers - 1)
            RF.reset()
            wl2 = make_ws2(RF, nstg=2)
            h2all = RF.alloc(16 * T, BF16)
            h2a = v3(h2all[:], 16)
            bh2 = [Buf() for _ in range(3)]
            rstd = RF.alloc(512, F32)
            brs = Buf()
            sgt = [RF.alloc(512, F32) for _ in range(2)]
            bsg = [Buf(), Buf()]
            ntmp = RF.alloc(2 * 512, F32)
            bnt = [Buf(), Buf()]
            xc = [RF.alloc(512, F32) for _ in range(3)]
            bxc = [Buf() for _ in range(3)]
            act_off = RF.cur
            xblk = RF.alloc(16 * 512, F32)
            bxb = Buf()
            sqt = RF.alloc(16 * 512, BF16)
            bsq = Buf()
            for blk in range(3):
                j = 0 if blk == 0 else 1
                DMA(v3(xblk[:], 16), xTv[:, :, blk * 512:(blk + 1) * 512], bXd[blk], [bxb])
                AC(sqt[:], xblk[:], AF.Square, [bxb], [bsq])
                rms_rstd(lambda k: sqt[:, k * 512:(k + 1) * 512], 16, 512, float(D), 7, rstd[:], [bsq], [brs])
                for k in range(16):
                    nt_, bn_ = ntmp[:, (k % 2) * 512:(k % 2) * 512 + 512], bnt[k % 2]
                    TT("dve", nt_, xblk[:, k * 512:(k + 1) * 512], rstd[:], ALU.mult, [bxb, brs], [bn_])
                    AC(h2a[:, k, blk * 512:(blk + 1) * 512], nt_, AF.Identity, [bn_, bC], [bh2[blk]],
                       bias=modT[:, j * 96 + 48 + k:j * 96 + 48 + k + 1], scale=gsc[:, (j * 2 + 1) * 16 + k:(j * 2 + 1) * 16 + k + 1])
            s.barrier()
            RF.cur = act_off
            actH = RF.alloc(22 * T, BF16)
            actv = v3(actH[:], 22)
            bact = [[Buf() for _ in range(3)] for _ in range(22)]
            xc6 = xc + [RF.alloc(512, F32) for _ in range(3)]
            bxc6 = bxc + [Buf() for _ in range(3)]
            nup = 0
            for half in range(2):
                for jp in range(11):
                    jf0 = half * 22 + 2 * jp
                    wg, bg = wl2(wsrc(dr["w_ffn_in"][l], 0, 16, jf0 * 128, 256), 16, 256)
                    wu, bu = wl2(wsrc(dr["w_ffn_in"][l], 0, 16, DFF + jf0 * 128, 256), 16, 256)
                    for c in range(2):
                        for blk in range(3):
                            pg = c * 3 + blk
                            sl = slice(blk * 512, (blk + 1) * 512)
                            for k in range(16):
                                MM(bank(pg), wg[:, k, c * 128:(c + 1) * 128], h2a[:, k, sl], k == 0, k == 15, [bg(k), bh2[blk]], [bPS[pg]])
                    for c in range(2):
                        jfl = 2 * jp + c
                        for blk in range(3):
                            pg = c * 3 + blk
                            pu = 6 + (nup % 2)
                            sg_, bsg_ = sgt[nup % 2], bsg[nup % 2]
                            nup += 1
                            sl = slice(blk * 512, (blk + 1) * 512)
                            AC(sg_[:], bank(pg), AF.Silu, [bPS[pg]], [bsg_])
                            for k in range(16):
                                MM(bank(pu), wu[:, k, c * 128:(c + 1) * 128], h2a[:, k, sl], k == 0, k == 15, [bu(k), bh2[blk]], [bPS[pu]])
                            TT("dve", actv[:, jfl, sl], bank(pu), sg_[:], ALU.mult, [bPS[pu], bsg_], [bact[jfl][blk]])
                def out_tiles(ip_):
                    res = []
                    for (k0, kc) in ((0, 16), (16, 6)):
                        wt, bk = wl2(wsrc(dr["w_ffn_out"][l], half * 22 + k0, kc, ip_ * 256, 256), kc, 256)
                        res.append((k0, kc, wt, bk))
                    return res
                tiles = out_tiles(0)
                pvn = bank(7, 192).rearrange("p (j m) -> p m j", j=2)
                for ip in range(8):
                    i0 = 2 * ip
                    for c in range(2):
                        for blk in range(3):
                            n6 = c * 3 + blk
                            DMA(xc6[n6][:], xTv[:, i0 + c, blk * 512:(blk + 1) * 512], [bXd[blk][i0 + c]], [bxc6[n6]])
                    for (k0, kc, wt, bk) in tiles:
                        for c in range(2):
                            for blk in range(3):
                                pb = c * 3 + blk
                                for k in range(kc):
                                    kk = k0 + k
                                    MM(bank(pb), wt[:, k, c * 128:(c + 1) * 128], actv[:, kk, blk * 512:(blk + 1) * 512], kk == 0, kk == 21, [bk(k), bact[kk][blk]], [bPS[pb]])
                    for c in range(2):
                        i = i0 + c
                        for blk in range(3):
                            j = 0 if blk == 0 else 1
                            n6 = c * 3 + blk
                            STT("dve", xc6[n6][:], bank(n6), modT[:, j * 96 + 80 + i:j * 96 + 80 + i + 1], xc6[n6][:], ALU.mult, ALU.add, [bPS[n6], bC, bxc6[n6]], [bxc6[n6]])
                    if l + 1 < nlayers:
                        for t_ in range(3):
                            mp = half * 24 + ip * 3 + t_
                            wtm, bkm = wl2(wsrc(dr["w_mod"][l + 1], 0, 16, mp * 256, 256), 16, 256)
                            for c in range(2):
                                m = 2 * mp + c
                                for k in range(16):
                                    MM(pvn[:, m, :], wtm[:, k, c * 128:(c + 1) * 128], scv[:, k, :], k == 0, k == 15, [bkm(k), bC], [bPS[7]])
                    if ip < 7:
                        tiles = out_tiles(ip + 1)
                    for c in range(2):
                        for blk in range(3):
                            n6 = c * 3 + blk
                            DMA(xTv[:, i0 + c, blk * 512:(blk + 1) * 512], xc6[n6][:], [bxc6[n6]], [bXd[blk][i0 + c]])
                if l + 1 < nlayers:
                    m0 = half * 48
                    for j in range(2):
                        TT("dve", modTs[(l + 1) % 2][:, j * 96 + m0:j * 96 + m0 + 48], bank(7, 48, j * 96 + m0), vecs[l + 1][:, V_BMOD + m0:V_BMOD + m0 + 48], ALU.add, [bPS[7], bC], [bC])
            s.barrier()
            if last:
                RF.cur = act_off
                xblk = RF.alloc(16 * 512, F32)
                bxk = [Buf() for _ in range(16)]
                xv = v3(xblk[:], 16)
                sqt = RF.alloc(16 * 512, BF16)
                bsq = Buf()
                yo = [RF.alloc(2048, F32) for _ in range(2)]
                byo = [Buf(), Buf()]
                for blk in range(3):
                    DMA(xv, xTv[:, :, blk * 512:(blk + 1) * 512], bXd[blk], bxk)
                    AC(sqt[:], xblk[:], AF.Square, bxk, [bsq])
                    rms_rstd(lambda k: sqt[:, k * 512:(k + 1) * 512], 16, 512, float(D), 7, rstd[:], [bsq], [brs])
                    for k in range(16):
                        STT("dve", xv[:, k, :], xv[:, k, :], lnf[:, k:k + 1], rstd[:], ALU.mult, ALU.mult, [bxk[k], bC, brs], [bxk[k]])
                    for tcn in range(4):
                        yt, byt = yo[tcn % 2], byo[tcn % 2]
                        for half in range(2):
                            u = (tcn * 2 + half) % 3
                            for kk in range(8):
                                k = half * 8 + kk
                                MM(PS[u][:, kk * 128:(kk + 1) * 128], xv[:, k, tcn * 128:(tcn + 1) * 128], ident_f[:], True, True, [bxk[k], bC], [bPS[2 * u + kk // 4]])
                            CP("act" if half else "dve", yt[:, half * 1024:(half + 1) * 1024], PS[u][:, :], [bPS[2 * u], bPS[2 * u + 1]], [byt])
                        DMA(dr["y"][blk * 512 + tcn * 128:blk * 512 + (tcn + 1) * 128, :], yt[:], [byt], [])
                s.barrier()
    except _Stop:
        pass
    s.barrier()
    s.emit(nc, st)
    st.close()
    return nc


_NC_CACHE = {}


def prep_inputs(inp, core):
    c = host_consts()
    f32 = lambda a: np.ascontiguousarray(np.asarray(a, dtype=np.float32))
    m = {}
    xp = f32(inp["x_prompt"])[2 * core:2 * core + 2].reshape(TP, D)
    xs = f32(inp["x_sample"])[core]
    m["x_in"] = np.ascontiguousarray(np.concatenate([xp, xs], axis=0))
    cv = np.stack([f32(inp["c_ctx"]), f32(inp["c"])[core]], axis=0)
    m["cvec"] = np.ascontiguousarray(cv.reshape(2, 16, 128).transpose(2, 1, 0).reshape(128, 32))
    m["ckv_c"] = f32(inp["cache_mla_ckv"])[core]
    m["kr_c"] = f32(inp["cache_mla_krope"])[core]
    m["sf_c"] = np.ascontiguousarray(f32(inp["state_gla_fwd"])[core].reshape(NL, 256, 128))
    m["sb_c"] = np.ascontiguousarray(f32(inp["state_gla_bwd"])[core].reshape(NL, 256, 128))
    return m


def prep_shared(inp):
    c = host_consts()
    f32 = lambda a: np.ascontiguousarray(np.asarray(a, dtype=np.float32))
    m = {}
    m["w_mod"] = f32(inp["w_mod"])
    w_in = f32(inp["w_in"])
    m["w_in"] = np.ascontiguousarray(np.concatenate([w_in, w_in[:, :, C_KR + c["perm"]]], axis=2))
    m["w_out"] = f32(inp["w_out"])
    m["w_ffn_in"] = f32(inp["w_ffn_in"])
    m["w_ffn_out"] = f32(inp["w_ffn_out"])
    wuq = f32(inp["mla_w_uq"])
    sw = np.concatenate([wuq[:, :, h * 192 + 128 + c["perm"]] for h in range(8)], axis=2)
    m["w_uq"] = np.ascontiguousarray(np.concatenate([wuq, sw], axis=2))
    m["w_ukv"] = f32(inp["mla_w_ukv"])
    vecs = np.zeros((NL, 128, NV), np.float32)
    pm = lambda a, n: f32(a).reshape(NL, n, 128).transpose(0, 2, 1)
    vecs[:, :, V_BMOD:V_BMOD + 96] = pm(inp["b_mod"], 96)
    vecs[:, :, V_LNMIX:V_LNMIX + 16] = pm(inp["ln_mix"], 16)
    vecs[:, :, V_LNFFN:V_LNFFN + 16] = pm(inp["ln_ffn"], 16)
    vecs[:, :, V_CW:V_CW + 36] = f32(inp["hy_conv_w"]).reshape(NL, 3, 12, 128).transpose(0, 3, 1, 2).reshape(NL, 128, 36)
    vecs[:, :, V_CB:V_CB + 12] = pm(inp["hy_conv_b"], 12)
    vecs[:, :, V_SKIP:V_SKIP + 8] = f32(inp["hy_skip"]).reshape(NL, 2, 4, 128).transpose(0, 3, 1, 2).reshape(NL, 128, 8)
    vecs[:, :, V_GN] = f32(inp["gla_norm"])
    vecs[:, :, V_QN:V_QN + 3] = pm(inp["mla_q_norm"], 3)
    vecs[:, :, V_KVN:V_KVN + 2] = pm(inp["mla_kv_norm"], 2)
    vecs[:, 0:64, V_FB1] = f32(inp["hy_filt_b1"])
    vecs[:, 0:64, V_FB2] = f32(inp["hy_filt_b2"])
    vecs[:, 0:64, V_FR0] = f32(inp["hy_filt_freq"])[:, 0]
    vecs[:, 0:64, V_FR1] = f32(inp["hy_filt_freq"])[:, 1]
    m["vecs"] = vecs
    m["lnf"] = np.ascontiguousarray(f32(inp["ln_final"]).reshape(16, 128).T)
    m["fw1"] = f32(inp["hy_filt_w1"])
    m["fw2"] = f32(inp["hy_filt_w2"])
    m["fw3"] = f32(inp["hy_filt_w3"])
    wa = np.zeros((NL, 2, 17, 256), np.float32)
    wa[:, 0, 0:16] = f32(inp["gla_wa_f"])
    wa[:, 0, 16] = f32(inp["gla_ba_f"])
    wa[:, 1, 0:16] = f32(inp["gla_wa_b"])
    wa[:, 1, 16] = f32(inp["gla_ba_b"])
    m["wa_aug"] = wa
    for name, shape, dt in CONST_DRAM:
        m[name] = c[name]
    return m


def kernel(**inputs):
    if "nc" not in _NC_CACHE:
        _NC_CACHE["nc"] = build()
    nc = _NC_CACHE["nc"]
    shared = prep_shared(inputs)
    in_maps = []
    for core in range(NCORE):
        m = dict(shared)
        m.update(prep_inputs(inputs, core))
        in_maps.append(m)
    res = run_bass_kernel_spmd(nc, in_maps, core_ids=list(range(NCORE)))
    R = res.results
    y_prompt = np.concatenate([r["y"][0:TP].reshape(2, LP, D) for r in R], axis=0)
    y_sample = np.stack([r["y"][TP:T] for r in R], axis=0)
    o_ckv = np.concatenate([r["o_ckv"] for r in R], axis=0)
    o_kr = np.concatenate([r["o_kr"] for r in R], axis=0)
    o_sf = np.concatenate([r["o_sf"].reshape(2, NL, 4, 64, 128) for r in R], axis=0)
    o_sb = np.concatenate([r["o_sb"].reshape(2, NL, 4, 64, 128) for r in R], axis=0)
    f = lambda a: np.ascontiguousarray(a.astype(np.float32))
    return (f(y_prompt), f(y_sample), f(o_ckv), f(o_kr), f(o_sf), f(o_sb))
```

```python
import math
import contextlib
import numpy as np
import ml_dtypes
import concourse.bass as bass
import concourse.mybir as mybir
from concourse.bass_utils import run_bass_kernel_spmd

F32 = mybir.dt.float32
BF16 = mybir.dt.bfloat16
ALU = mybir.AluOpType
AF = mybir.ActivationFunctionType

D = 2048
DFF = 5632
NL = 2
LP = 256
LS = 1024
TP = 512
T = 1536
PAST = 256
EPS = 1e-6
NCORE = 8
C_HY, C_Q, C_K, C_V, C_G, C_AF, C_AB, C_CQ, C_CKV, C_KR, C_KRSW = 0, 1536, 1792, 2048, 2560, 3072, 3088, 3104, 3488, 3744, 3808
WIN_EXT = 3872
NV = 200
V_BMOD, V_LNMIX, V_LNFFN, V_CW, V_CB, V_SKIP, V_GN, V_QN, V_KVN, V_FB1, V_FB2, V_FR0, V_FR1 = 0, 96, 112, 128, 164, 176, 184, 185, 188, 190, 191, 192, 193
MAGIC = 12582912.0
TWO_PI = 2.0 * math.pi
SBASE = 16512
SEND = 229376


class Buf:
    __slots__ = ("w", "r")

    def __init__(self):
        self.w = None
        self.r = {}


class Op:
    __slots__ = ("eng", "fn", "waits", "signal", "src", "seq", "vc", "val", "dma")


class Sched:
    ENG = ["pe", "act", "dve", "pool", "sp"]
    NSLOT = 8

    def __init__(self):
        self.q = {e: [] for e in self.ENG}
        self.S = {e: {} for e in self.ENG}
        self.ncomp = {e: 0 for e in self.ENG}
        self.ndma = {e: 0 for e in self.ENG}
        self.slot_last = {}
        self.slot_seq = {}

    def add(self, eng, fn, reads=(), writes=(), dma=False, after=()):
        op = Op()
        op.eng = eng
        op.fn = fn
        op.signal = False
        op.dma = dma
        deps = []
        deps.extend(after)
        for b in reads:
            if b.w is not None:
                deps.append(b.w)
        for b in writes:
            if b.w is not None:
                deps.append(b.w)
            deps.extend(b.r.values())
        if dma:
            j = self.ndma[eng]
            self.ndma[eng] += 1
            slot = (eng, j % self.NSLOT)
            prev = self.slot_last.get(slot)
            if prev is not None:
                deps.append(prev)
            op.src = ("dma",) + slot
            op.seq = self.slot_seq.get(slot, 0) + 1
            self.slot_seq[slot] = op.seq
            self.slot_last[slot] = op
            op.signal = True
        else:
            self.ncomp[eng] += 1
            op.src = eng
            op.seq = self.ncomp[eng]
        S = self.S[eng]
        waits = {}
        for d in deps:
            if eng == "pe" and d.src == "pe":
                continue
            if S.get(d.src, 0) >= d.seq:
                continue
            d.signal = True
            cur = waits.get(d.src)
            if cur is None or cur.seq < d.seq:
                waits[d.src] = d
            for k, v in d.vc.items():
                if S.get(k, 0) < v:
                    S[k] = v
        op.waits = list(waits.values())
        op.vc = dict(S)
        op.vc[op.src] = op.seq
        for b in writes:
            b.w = op
            b.r = {}
        for b in reads:
            if b.w is not op:
                b.r[op.src] = op
        self.q[eng].append(op)
        return op

    @staticmethod
    def deps_of(bufs):
        out = []
        for b in bufs:
            if b.w is not None:
                out.append(b.w)
            out.extend(b.r.values())
        return out

    def barrier(self):
        lasts = []
        for e in self.ENG:
            for op in reversed(self.q[e]):
                if (not op.dma) and op.fn is not None:
                    lasts.append(op)
                    break
        lasts.extend(self.slot_last.values())
        for e in self.ENG:
            S = self.S[e]
            waits = {}
            for d in lasts:
                if S.get(d.src, 0) >= d.seq:
                    continue
                d.signal = True
                waits[d.src] = d
            for d in lasts:
                for k, v in d.vc.items():
                    if S.get(k, 0) < v:
                        S[k] = v
            if waits:
                op = Op()
                op.eng = e
                op.fn = None
                op.signal = False
                op.dma = False
                op.waits = list(waits.values())
                op.src = None
                op.seq = 0
                op.vc = {}
                self.q[e].append(op)

    def emit(self, nc, stack):
        sems = {}
        for e in self.ENG:
            cnt = 0
            for op in self.q[e]:
                if op.fn is None:
                    continue
                if op.dma:
                    op.val = 16 * op.seq
                else:
                    if op.signal:
                        cnt += 1
                    op.val = cnt
                if op.signal and op.src not in sems:
                    nm = "s_" + ("_".join(str(x) for x in op.src) if isinstance(op.src, tuple) else op.src)
                    sems[op.src] = stack.enter_context(nc.semaphore(nm))
        block = stack.enter_context(nc.Block())
        q = self.q

        def run(e, eng):
            for op in q[e]:
                for d in op.waits:
                    eng.wait_ge(sems[d.src], d.val)
                if op.fn is None:
                    continue
                ins = op.fn(eng)
                if op.signal:
                    ins.then_inc(sems[op.src], 16 if op.dma else 1)

        @block.tensor
        def _(eng):
            run("pe", eng)

        @block.scalar
        def _(eng):
            run("act", eng)

        @block.vector
        def _(eng):
            run("dve", eng)

        @block.gpsimd
        def _(eng):
            run("pool", eng)

        @block.sync
        def _(eng):
            run("sp", eng)


def _bf(x):
    return np.ascontiguousarray(x.astype(ml_dtypes.bfloat16))


_CONST_CACHE = {}


def host_consts():
    if _CONST_CACHE:
        return _CONST_CACHE
    c = {}
    c["ident_f"] = np.eye(128, dtype=np.float32)
    c["ident_b"] = _bf(np.eye(128, dtype=np.float32))
    c["ones_b"] = _bf(np.ones((128, 128), np.float32))
    s_ = np.arange(128)[:, None]
    c_ = np.arange(128)[None, :]
    v = -1.0 / 16.0
    tri = np.stack([(s_ <= c_) * v, (s_ > c_) * v, (s_ >= c_) * v, (s_ < c_) * v]).astype(np.float32)
    c["tri"] = np.ascontiguousarray(tri.transpose(1, 0, 2).reshape(128, 4 * 128))
    mk = np.stack([np.tile((s_ <= c_).astype(np.float32), (1, 4)), np.tile((s_ >= c_).astype(np.float32), (1, 4))])
    c["mask"] = np.ascontiguousarray(mk.transpose(1, 0, 2).reshape(128, 2 * 512))
    t = np.arange(LS)
    row, col = t // 64, t % 64
    inv = (10000.0 ** (-np.arange(0, 32, 2, dtype=np.float32) / 32.0)).astype(np.float32)
    cos = np.zeros((64, LS), np.float32)
    sin = np.zeros((64, LS), np.float32)
    perm = np.zeros(64, np.int64)
    for r in range(64):
        pos = (row if r < 32 else col).astype(np.float32)
        j = r % 16
        ang = (pos * inv[j]).astype(np.float32)
        cos[r] = np.cos(ang)
        x2 = (r % 32) >= 16
        sin[r] = np.sin(ang) * (1.0 if x2 else -1.0)
        perm[r] = r - 16 if x2 else r + 16
    c["rope"] = np.ascontiguousarray(np.concatenate([cos, sin], axis=1))
    c["perm"] = perm
    for L in (LP, LS):
        tt = np.linspace(0.0, 1.0, L, dtype=np.float32)[:, None]
        w = (2.0 * math.pi * np.arange(L, dtype=np.float32)[:, None] / L).astype(np.float32)
        f = np.linspace(1e-4, 15, 16, dtype=np.float32)
        z = np.concatenate([tt, np.cos(f * w), -np.sin(f * w)], axis=-1).astype(np.float32)
        c[f"zemb{L}"] = np.ascontiguousarray(z.T)
        c[f"negt{L}"] = np.ascontiguousarray((-tt[:, 0]).reshape(L // 128, 128).T)
        n = np.arange(L, dtype=np.float64)[:, None]
        fq = np.arange(L, dtype=np.float64)[None, :]
        th = math.pi * (2 * fq + 1) * n / (2 * L)
        C = np.cos(th)
        S = np.sin(th)
        c[f"dftC{L}"] = _bf(C.astype(np.float32))
        c[f"dftS{L}"] = _bf(S.astype(np.float32))
        c[f"dftCT{L}"] = _bf(C.T.astype(np.float32))
        c[f"dftST{L}"] = _bf(S.T.astype(np.float32))
    mind = math.log(1e-2) / 1.5
    maxd = math.log(1e-2) / 0.3
    delta = np.abs(np.linspace(mind, maxd, 512, dtype=np.float32))
    c["delta"] = np.ascontiguousarray(np.tile(delta[None, :], (128, 1)).astype(np.float32))
    _CONST_CACHE.update(c)
    return c


CONST_DRAM = [("ident_f", [128, 128], F32), ("ident_b", [128, 128], BF16), ("ones_b", [128, 128], BF16),
              ("tri", [128, 512], F32), ("mask", [128, 1024], F32), ("rope", [64, 2048], F32),
              ("zemb256", [33, 256], F32), ("zemb1024", [33, 1024], F32),
              ("negt256", [128, 2], F32), ("negt1024", [128, 8], F32), ("delta", [128, 512], F32),
              ("dftC256", [256, 256], BF16), ("dftS256", [256, 256], BF16), ("dftCT256", [256, 256], BF16), ("dftST256", [256, 256], BF16),
              ("dftC1024", [1024, 1024], BF16), ("dftS1024", [1024, 1024], BF16), ("dftCT1024", [1024, 1024], BF16), ("dftST1024", [1024, 1024], BF16)]

IN_DRAM = [("x_in", [T, D]), ("cvec", [128, 32]), ("ckv_c", [NL, PAST, 256]), ("kr_c", [NL, PAST, 64]),
           ("sf_c", [NL, 256, 128]), ("sb_c", [NL, 256, 128]),
           ("w_mod", [NL, D, 6 * D]), ("w_in", [NL, D, WIN_EXT]), ("w_out", [NL, D, D]),
           ("w_ffn_in", [NL, D, 2 * DFF]), ("w_ffn_out", [NL, DFF, D]), ("w_uq", [NL, 384, 2048]), ("w_ukv", [NL, 256, 2048]),
           ("vecs", [NL, 128, NV]), ("lnf", [128, 16]), ("fw1", [NL, 33, 64]), ("fw2", [NL, 64, 64]), ("fw3", [NL, 64, 2048]),
           ("wa_aug", [NL, 2, 17, 256])]

OUT_DRAM = [("y", [T, D]), ("o_ckv", [2, NL, LP, 256]), ("o_kr", [2, NL, LP, 64]), ("o_sf", [2, NL, 256, 128]), ("o_sb", [2, NL, 256, 128])]


class _Stop(Exception):
    pass


def build(nlayers=NL, dbg=None, stop=None, knob=None):
    nc = bass.Bass("TRN2", target_bir_lowering=False)
    dr = {}
    for name, shape in IN_DRAM:
        dr[name] = nc.dram_tensor(name, list(shape), F32, kind="ExternalInput").ap()
    for name, shape, dt in CONST_DRAM:
        dr[name] = nc.dram_tensor(name, list(shape), dt, kind="ExternalInput").ap()
    for name, shape in OUT_DRAM:
        dr[name] = nc.dram_tensor(name, list(shape), F32, kind="ExternalOutput").ap()
    dbg_out = {}
    if dbg:
        for name, shape in dbg.items():
            dbg_out[name] = nc.dram_tensor("dbg_" + name, list(shape), F32, kind="ExternalOutput").ap()
    xT_d = nc.dram_tensor("xT_scr", [D, T], F32, kind="Internal").ap()
    xTv = xT_d.rearrange("(k p) t -> p k t", p=128)

    s = Sched()
    st = contextlib.ExitStack()
    uid = [0]
    stg_ctr = [0]

    def stage(name):
        stg_ctr[0] += 1
        if stop is not None and stg_ctr[0] == stop:
            print('STOP at stage', stop, name)
            raise _Stop()

    class Region:
        def __init__(self, name, start, size):
            self.name, self.start, self.size, self.cur = name, start, size, start

        def reset(self):
            self.cur = self.start

        def alloc(self, n, dt):
            sz = n * (4 if dt == F32 else 2)
            sz = (sz + 31) // 32 * 32
            assert self.cur + sz <= self.start + self.size, (self.name, self.cur - self.start, sz, self.size)
            uid[0] += 1
            t = nc.alloc_sbuf_tensor_at(f"{self.name}_{uid[0]}", [128, n], dt, offset=self.cur)
            self.cur += sz
            return t

    sizes = [("C", 20480), ("W", 36864), ("A", 32768), ("B", 32768), ("IN", 67584)]
    regs = {}
    cur = SBASE
    for nm, sz in sizes:
        regs[nm] = Region(nm, cur, sz)
        cur += sz
    regs["T"] = Region("T", cur, SEND - cur)
    RC, RW, RA, RB, RIN, RT = (regs[k] for k in ("C", "W", "A", "B", "IN", "T"))
    RF = Region("F", regs["W"].start, SEND - regs["W"].start)

    PS = [st.enter_context(nc.psum_tensor(f"ps{i}", [128, 1024], F32)) for i in range(4)]
    bPS = [Buf() for _ in range(8)]

    def bank(b, n=512, off=0):
        return PS[b // 2][:, (b % 2) * 512 + off:(b % 2) * 512 + off + n]

    def MM(out, lhsT, rhs, st_, sp_, R, W):
        s.add("pe", lambda e: e.matmul(out, lhsT=lhsT, rhs=rhs, start=st_, stop=sp_), R, W)

    def AC(out, in_, func, R, W, bias=None, scale=None, accum=None):
        kw = {}
        if bias is not None:
            kw["bias"] = bias
        if scale is not None:
            kw["scale"] = scale
        if accum is not None:
            kw["accum_out"] = accum
        s.add("act", lambda e: e.activation(out=out, in_=in_, func=func, **kw), R, W)

    def TT(eng, out, a, b, op, R, W):
        s.add(eng, lambda e: e.tensor_tensor(out=out, in0=a, in1=b, op=op), R, W)

    def TS(eng, out, a, s1, s2, op0, op1, R, W):
        if s2 is None:
            s.add(eng, lambda e: e.tensor_scalar(out=out, in0=a, scalar1=s1, scalar2=None, op0=op0), R, W)
        else:
            s.add(eng, lambda e: e.tensor_scalar(out=out, in0=a, scalar1=s1, scalar2=s2, op0=op0, op1=op1), R, W)

    def STT(eng, out, a, sc, b, op0, op1, R, W):
        s.add(eng, lambda e: e.scalar_tensor_tensor(out=out, in0=a, scalar=sc, in1=b, op0=op0, op1=op1), R, W)

    def CP(eng, out, in_, R, W, after=()):
        if eng == "act":
            s.add("act", lambda e: e.copy(out=out, in_=in_), R, W, after=after)
        else:
            s.add(eng, lambda e: e.tensor_copy(out=out, in_=in_), R, W, after=after)

    def MS(eng, ap, val, W):
        s.add(eng, lambda e: e.memset(ap, val), (), W)

    def RCP(out, in_, R, W):
        s.add("dve", lambda e: e.reciprocal(out=out, in_=in_), R, W)

    def DMA(out, in_, R, W):
        s.add("sp", lambda e: e.dma_start(out=out, in_=in_), R, W, dma=True)

    def v3(ap2, k):
        return ap2.rearrange("p (k c) -> p k c", k=k)

    bC = Buf()
    ident_f = RC.alloc(128, F32)
    ident_b = RC.alloc(128, BF16)
    ones_b = RC.alloc(128, BF16)
    tri = RC.alloc(512, F32)
    mask = RC.alloc(1024, F32)
    rope = RC.alloc(2048, F32)
    vecs = [RC.alloc(NV, F32) for _ in range(NL)]
    lnf = RC.alloc(16, F32)
    cvec = RC.alloc(32, F32)
    scT = RC.alloc(32, BF16)
    modTs = [RC.alloc(192, F32) for _ in range(2)]
    gsc = RC.alloc(64, F32)
    epsb = RC.alloc(1, F32)
    fb = RC.alloc(4, F32)
    for tile_, name in ((ident_f, "ident_f"), (ident_b, "ident_b"), (ones_b, "ones_b"), (tri, "tri"), (mask, "mask"), (lnf, "lnf"), (cvec, "cvec")):
        DMA(tile_[:], dr[name], [], [bC])
    DMA(rope[0:64, :], dr["rope"], [], [bC])
    for l in range(NL):
        DMA(vecs[l][:], dr["vecs"][l], [], [bC])
    MS("pool", epsb[:], EPS, [bC])
    AC(scT[:], cvec[:], AF.Silu, [bC], [bC])

    NB = 3
    stg = [RW.alloc(2048, F32) for _ in range(NB)]
    wbf = [RW.alloc(2048, BF16) for _ in range(NB)]
    bStg = [Buf() for _ in range(NB)]
    bWb = [[Buf(), Buf(), Buf()] for _ in range(NB)]
    wctr = [0, 0, 0]

    def split_cast(dst, sv, kc, bsrc, bdst3):
        if kc < 8:
            eng = "act" if wctr[2] % 2 == 0 else "dve"
            wctr[2] += 1
            CP(eng, dst, sv, [bsrc], bdst3)
            return lambda k: bdst3[0]
        ka = int(round(kc * 7 / 16.0))
        kb = int(round(kc * 13 / 16.0))
        prev = Sched.deps_of(bdst3)
        for p, (eng, a_, b_) in enumerate((("act", 0, ka), ("dve", ka, kb), ("pool", kb, kc))):
            if b_ > a_:
                CP(eng, dst[:, a_:b_, :], sv[:, a_:b_, :], [bsrc], [bdst3[p]], after=prev)
        return lambda k: bdst3[0 if k < ka else (1 if k < kb else 2)]

    def wload(src, kc, ncol, mode="cast", parts=128):
        n = kc * ncol
        assert n <= 2048
        if mode == "bf16":
            i = wctr[1] % NB
            wctr[1] += 1
            dst = v3(wbf[i][0:parts, 0:n], kc)
            DMA(dst, src, [], bWb[i])
            return dst, (lambda k, i=i: bWb[i][0])
        i = wctr[0] % NB
        wctr[0] += 1
        sv = v3(stg[i][0:parts, 0:n], kc)
        DMA(sv, src, [], [bStg[i]])
        if mode == "f32":
            return sv, (lambda k, i=i: bStg[i])
        j = wctr[1] % NB
        wctr[1] += 1
        dst = v3(wbf[j][0:parts, 0:n], kc)
        return dst, split_cast(dst, sv, kc, bStg[i], bWb[j])

    def make_ws2(RF, nstg=3):
        stg2 = [RF.alloc(4096, F32) for _ in range(nstg)]
        wb2 = [RF.alloc(4096, BF16) for _ in range(2)]
        bS2 = [Buf() for _ in range(nstg)]
        bW2 = [[Buf(), Buf(), Buf()] for _ in range(2)]
        c2 = [0, 0]

        def wload2(src, kc, ncol):
            n = kc * ncol
            assert n <= 4096
            i = c2[0] % nstg
            c2[0] += 1
            j = c2[1] % 2
            c2[1] += 1
            sv = v3(stg2[i][:, 0:n], kc)
            DMA(sv, src, [], [bS2[i]])
            dst = v3(wb2[j][:, 0:n], kc)
            return dst, split_cast(dst, sv, kc, bS2[i], bW2[j])
        return wload2

    def wsrc(w2d, k0, kc, c0, ncol):
        return w2d[k0 * 128:(k0 + kc) * 128, c0:c0 + ncol].rearrange("(k p) c -> p k c", p=128)

    try:
        bXd = [[Buf() for _ in range(16)] for _ in range(3)]
        xin_t = [RA.alloc(2048, F32) for _ in range(2)]
        xtr_t = [RB.alloc(2048, F32) for _ in range(2)]
        bxin = [Buf(), Buf()]
        bxtr = [Buf(), Buf()]
        for tc in range(12):
            xi, bi = xin_t[tc % 2], bxin[tc % 2]
            xo, bo = xtr_t[tc % 2], bxtr[tc % 2]
            DMA(xi[:], dr["x_in"][tc * 128:(tc + 1) * 128, :], [], [bi])
            for half in range(2):
                u = (tc * 2 + half) % 4
                for kk in range(8):
                    k = half * 8 + kk
                    MM(PS[u][:, kk * 128:(kk + 1) * 128], xi[:, k * 128:(k + 1) * 128], ident_f[:], True, True, [bi, bC], [bPS[2 * u + kk // 4]])
                CP("act" if half else "dve", xo[:, half * 1024:(half + 1) * 1024], PS[u][:, :], [bPS[2 * u], bPS[2 * u + 1]], [bo])
            DMA(xTv[:, :, tc * 128:(tc + 1) * 128], v3(xo[:], 16), [bo], bXd[tc // 4])
        s.barrier()
        stage('prepass')

        def rms_rstd(sq_views, nk, width, denom, psb, out_rstd, Rsq, Wout):
            for k in range(nk):
                MM(bank(psb, width), ones_b[:], sq_views(k), k == 0, k == nk - 1, [bC] + Rsq, [bPS[psb]])
            AC(out_rstd, bank(psb, width), AF.Sqrt, [bPS[psb], bC], Wout, bias=epsb[:], scale=1.0 / denom)
            RCP(out_rstd, out_rstd, Wout, Wout)

        for l in range(nlayers):
            vl = vecs[l]
            modT = modTs[l % 2]
            scv = v3(scT[:], 16)
            if l == 0:
                pv = PS[0][:, 0:192].rearrange("p (j m) -> p m j", j=2)
                RF.reset()
                wl2 = make_ws2(RF)
                for mp in range(48):
                    wt, bk = wl2(wsrc(dr["w_mod"][l], 0, 16, mp * 256, 256), 16, 256)
                    for c in range(2):
                        m = 2 * mp + c
                        for k in range(16):
                            MM(pv[:, m, :], wt[:, k, c * 128:(c + 1) * 128], scv[:, k, :], k == 0, k == 15, [bk(k), bC], [bPS[0]])
                for j in range(2):
                    TT("dve", modT[:, j * 96:(j + 1) * 96], PS[0][:, j * 96:(j + 1) * 96], vl[:, V_BMOD:V_BMOD + 96], ALU.add, [bPS[0], bC], [bC])
            for j in range(2):
                STT("dve", gsc[:, (j * 2) * 16:(j * 2) * 16 + 16], modT[:, j * 96 + 16:j * 96 + 32], 1.0, vl[:, V_LNMIX:V_LNMIX + 16], ALU.add, ALU.mult, [bC], [bC])
                STT("dve", gsc[:, (j * 2 + 1) * 16:(j * 2 + 1) * 16 + 16], modT[:, j * 96 + 64:j * 96 + 80], 1.0, vl[:, V_LNFFN:V_LNFFN + 16], ALU.add, ALU.mult, [bC], [bC])
            TT("dve", fb[0:64, 0:1], vl[0:64, V_FR0:V_FR0 + 1], vl[0:64, V_FB1:V_FB1 + 1], ALU.mult, [bC], [bC])
            TT("dve", fb[0:64, 1:2], vl[0:64, V_FR1:V_FR1 + 1], vl[0:64, V_FB2:V_FB2 + 1], ALU.mult, [bC], [bC])
            s.barrier()
            stage('mod')

            for g in range(2):
                j = g
                L = LP if g == 0 else LS
                nseq = 2 if g == 0 else 1
                nblk = 1 if g == 0 else 2
                Ltot = nseq * L
                tok0 = 0 if g == 0 else TP
                Lk = L if g == 0 else L + PAST
                nf = L // 128
                for r_ in (RA, RB, RIN, RT):
                    r_.reset()
                hT = RA.alloc(16 * Ltot, BF16)
                bH = [Buf() for _ in range(nblk)]
                xblk = RB.alloc(16 * 512, F32)
                bxb = Buf()
                sqt = RT.alloc(16 * 512, BF16)
                bsq = Buf()
                rstd = RT.alloc(512, F32)
                brs = Buf()
                hv = v3(hT[:], 16)
                ntmp = RIN.alloc(2 * 512, F32)
                bnt = [Buf(), Buf()]
                for b in range(nblk):
                    blk = (tok0 // 512) + b
                    DMA(v3(xblk[:], 16), xTv[:, :, blk * 512:(blk + 1) * 512], bXd[blk], [bxb])
                    AC(sqt[:], xblk[:], AF.Square, [bxb], [bsq])
                    rms_rstd(lambda k: sqt[:, k * 512:(k + 1) * 512], 16, 512, float(D), 7, rstd[:], [bsq], [brs])
                    for k in range(16):
                        nt_, bn_ = ntmp[:, (k % 2) * 512:(k % 2) * 512 + 512], bnt[k % 2]
                        TT("dve", nt_, xblk[:, k * 512:(k + 1) * 512], rstd[:], ALU.mult, [bxb, brs], [bn_])
                        AC(hv[:, k, b * 512:(b + 1) * 512], nt_, AF.Identity, [bn_, bC], [bH[b]],
                           bias=modT[:, j * 96 + k:j * 96 + k + 1], scale=gsc[:, (j * 2) * 16 + k:(j * 2) * 16 + k + 1])
                s.barrier()
                stage('stepA')
                RB.reset()
                RIN.reset()
                RT.reset()
                hyT = RIN.alloc(12 * Ltot, BF16)
                qT = RIN.alloc(2 * Ltot, BF16)
                kT = RIN.alloc(2 * Ltot, BF16)
                vT = RIN.alloc(4 * Ltot, BF16)
                sgT = RIN.alloc(4 * Ltot, BF16)
                afT = RIN.alloc(Ltot, BF16)
                abT = RIN.alloc(Ltot, BF16)
                cqnT = RIN.alloc(3 * Ltot, BF16)
                keysT = RIN.alloc(2 * nseq * Lk, BF16)
                krT = RIN.alloc(nseq * Lk, BF16)
                bHY = [Buf() for _ in range(12)]
                bQ, bK, bV, bSG, bAF, bAB, bCQN, bKEYS, bKR = (Buf() for _ in range(9))
                hyv = v3(hyT[:], 12)
                qv, kv_, vv, sgv = v3(qT[:], 2), v3(kT[:], 2), v3(vT[:], 4), v3(sgT[:], 4)
                cqnv = v3(cqnT[:], 3)
                keysv = v3(keysT[:], 2)
                ctmp = RT.alloc(Ltot, F32)
                bct = Buf()
                cqf = RT.alloc(3 * Ltot, F32)
                bcqf = Buf()
                sq3 = RT.alloc(3 * 512, BF16)
                bsq3 = Buf()
                rs2 = RT.alloc(512, F32)
                brs2 = Buf()
                cqfv = v3(cqf[:], 3)
                MS("pool", afT[0:32, :], 1.0, [bAF])
                MS("pool", abT[0:32, :], 1.0, [bAB])
                ckvf = RB.alloc(2 * Ltot, F32)
                bckvf = Buf()
                ckvfv = v3(ckvf[:], 2)
                krf = RB.alloc(Ltot, F32)
                bkrf = Buf()
                otr = RB.alloc(2 * 384, F32)
                krtmp = RB.alloc(Ltot, F32)
                bkrt = Buf()
                krtmp2 = RB.alloc(Ltot, F32)
                bkrt2 = Buf()
                botr = [Buf(), Buf()]

                chunks = []
                for i in range(12):
                    chunks.append(("hy", C_HY + i * 128, 128, i))
                for i in range(2):
                    chunks.append(("q", C_Q + i * 128, 128, i))
                for i in range(2):
                    chunks.append(("k", C_K + i * 128, 128, i))
                for i in range(4):
                    chunks.append(("v", C_V + i * 128, 128, i))
                for i in range(4):
                    chunks.append(("g", C_G + i * 128, 128, i))
                chunks.append(("af", C_AF, 16, 0))
                chunks.append(("ab", C_AB, 16, 0))
                for i in range(3):
                    chunks.append(("cq", C_CQ + i * 128, 128, i))
                for i in range(2):
                    chunks.append(("ckv", C_CKV + i * 128, 128, i))
                chunks.append(("kr", C_KR, 64, 0))
                if g == 1:
                    chunks.append(("krsw", C_KRSW, 64, 0))

                import os as _os
                for ci, (kind, c0, ncol, idx) in enumerate(chunks):
                    if g == 1 and ci >= int(_os.environ.get('KCH', '999')):
                        break
                    wt, bw = wload(wsrc(dr["w_in"][l], 0, 16, c0, ncol), 16, ncol)
                    u = ci % 4
                    for b in range(nblk):
                        for k in range(16):
                            MM(PS[u][0:ncol, b * 512:(b + 1) * 512], wt[:, k, :], hv[:, k, b * 512:(b + 1) * 512], k == 0, k == 15, [bw(k), bH[b]], [bPS[2 * u + b]])
                    pb = [bPS[2 * u + b] for b in range(nblk)]
                    psu = PS[u]
                    if kind == "hy":
                        cw = lambda tap: vl[:, V_CW + tap * 12 + idx:V_CW + tap * 12 + idx + 1]
                        for sq_ in range(nseq):
                            a, e_ = sq_ * L, (sq_ + 1) * L
                            AC(ctmp[:, a:e_], psu[:, a:e_], AF.Identity, pb + [bC], [bct], bias=vl[:, V_CB + idx:V_CB + idx + 1], scale=cw(1))
                            STT("dve", ctmp[:, a + 1:e_], psu[:, a:e_ - 1], cw(0), ctmp[:, a + 1:e_], ALU.mult, ALU.add, pb + [bC, bct], [bct])
                            STT("dve", hyv[:, idx, a:e_ - 1], psu[:, a + 1:e_], cw(2), ctmp[:, a:e_ - 1], ALU.mult, ALU.add, pb + [bC, bct], [bHY[idx]])
                            CP("pool", hyv[:, idx, e_ - 1:e_], ctmp[:, e_ - 1:e_], [bct], [bHY[idx]])
                    elif kind == "q":
                        AC(qv[:, idx, :], psu[:, 0:Ltot], AF.Identity, pb, [bQ], scale=0.125)
                    elif kind == "k":
                        CP("dve", kv_[:, idx, :], psu[:, 0:Ltot], pb, [bK])
                    elif kind == "v":
                        CP("dve", vv[:, idx, :], psu[:, 0:Ltot], pb, [bV])
                    elif kind == "g":
                        AC(sgv[:, idx, :], psu[:, 0:Ltot], AF.Silu, pb, [bSG])
                    elif kind == "af":
                        CP("dve", afT[0:16, :], psu[0:16, 0:Ltot], pb, [bAF])
                    elif kind == "ab":
                        CP("dve", abT[0:16, :], psu[0:16, 0:Ltot], pb, [bAB])
                    elif kind == "cq":
                        CP("dve", cqfv[:, idx, :], psu[:, 0:Ltot], pb, [bcqf])
                        if idx == 2:
                            for b in range(nblk):
                                sl = slice(b * 512, (b + 1) * 512)
                                for k in range(3):
                                    AC(sq3[:, k * 512:(k + 1) * 512], cqfv[:, k, sl], AF.Square, [bcqf], [bsq3])
                                rms_rstd(lambda k: sq3[:, k * 512:(k + 1) * 512], 3, 512, 384.0, 7, rs2[:], [bsq3], [brs2])
                                for k in range(3):
                                    STT("dve", cqnv[:, k, sl], cqfv[:, k, sl], vl[:, V_QN + k:V_QN + k + 1], rs2[:], ALU.mult, ALU.mult, [bcqf, brs2, bC], [bCQN])
                    elif kind == "ckv":
                        CP("dve", ckvfv[:, idx, :], psu[:, 0:Ltot], pb, [bckvf])
                        if idx == 1:
                            for b in range(nblk):
                                sl = slice(b * 512, (b + 1) * 512)
                                for k in range(2):
                                    AC(sq3[:, k * 512:(k + 1) * 512], ckvfv[:, k, sl], AF.Square, [bckvf], [bsq3])
                                rms_rstd(lambda k: sq3[:, k * 512:(k + 1) * 512], 2, 512, 256.0, 7, rs2[:], [bsq3], [brs2])
                                for k in range(2):
                                    STT("dve", ckvfv[:, k, sl], ckvfv[:, k, sl], vl[:, V_KVN + k:V_KVN + k + 1], rs2[:], ALU.mult, ALU.mult, [bckvf, brs2, bC], [bckvf])
                            for sq_ in range(nseq):
                                for k in range(2):
                                    CP("pool", keysv[:, k, sq_ * Lk:sq_ * Lk + L], ckvfv[:, k, sq_ * L:(sq_ + 1) * L], [bckvf], [bKEYS])
                    elif kind == "kr":
                        if g == 0:
                            CP("dve", krf[0:64, 0:Ltot], psu[0:64, 0:Ltot], pb, [bkrf])
                            CP("pool", krT[0:64, 0:Ltot], krf[0:64, 0:Ltot], [bkrf], [bKR])
                        else:
                            TT("dve", krtmp[0:64, :], psu[0:64, 0:Ltot], rope[0:64, 0:LS], ALU.mult, pb + [bC], [bkrt])
                    elif kind == "krsw":
                        TT("dve", krtmp2[0:64, :], psu[0:64, 0:Ltot], rope[0:64, LS:2 * LS], ALU.mult, pb + [bC], [bkrt2])
                        TT("pool", krT[0:64, 0:L], krtmp[0:64, :], krtmp2[0:64, :], ALU.add, [bkrt, bkrt2], [bKR])
                if g == 0:
                    for sq_ in range(2):
                        for tcn in range(2):
                            ot, bo = otr[:, ((sq_ * 2 + tcn) % 2) * 384:((sq_ * 2 + tcn) % 2) * 384 + 384], botr[(sq_ * 2 + tcn) % 2]
                            ts_ = slice(sq_ * L + tcn * 128, sq_ * L + (tcn + 1) * 128)
                            for k in range(2):
                                MM(bank(6, 128, k * 128), ckvfv[:, k, ts_], ident_f[:], True, True, [bckvf, bC], [bPS[6]])
                            MM(bank(6, 64, 256), krf[0:64, ts_], ident_f[0:64, 0:64], True, True, [bkrf, bC], [bPS[6]])
                            CP("act", ot[:, 0:320], bank(6, 320), [bPS[6]], [bo])
                            DMA(dr["o_ckv"][sq_, l, tcn * 128:(tcn + 1) * 128, :], ot[:, 0:256], [bo], [])
                            DMA(dr["o_kr"][sq_, l, tcn * 128:(tcn + 1) * 128, :], ot[:, 256:320], [bo], [])
                elif int(_os.environ.get('KCACHE', '9')) > 0:
                    cst = RB.alloc(2 * 384, F32)
                    bcst = [Buf(), Buf()]
                    kc2 = int(_os.environ.get('KCACHE', '9'))
                    for jc in range(2):
                        if kc2 < 9:
                            cs = cst[:, jc * 384:(jc + 1) * 384]
                            if kc2 >= 1:
                                MS("pool", cs[:, 320:384], 0.0, [bcst[jc]])
                                DMA(cs[:, 0:256], dr["ckv_c"][l, jc * 128:(jc + 1) * 128, :], [], [bcst[jc]])
                            if kc2 >= 2:
                                DMA(cs[:, 256:320], dr["kr_c"][l, jc * 128:(jc + 1) * 128, :], [], [bcst[jc]])
                            if kc2 >= 3:
                                for k in range(3):
                                    MM(bank(6, 128, k * 128), cs[:, k * 128:(k + 1) * 128], ident_f[:], True, True, [bcst[jc], bC], [bPS[6]])
                            if kc2 >= 4:
                                for k in range(2):
                                    CP("act", keysv[:, k, L + jc * 128:L + (jc + 1) * 128], bank(6, 128, k * 128), [bPS[6]], [bKEYS])
                            continue
                        cs = cst[:, jc * 384:(jc + 1) * 384]
                        MS("pool", cs[:, 320:384], 0.0, [bcst[jc]])
                        DMA(cs[:, 0:256], dr["ckv_c"][l, jc * 128:(jc + 1) * 128, :], [], [bcst[jc]])
                        DMA(cs[:, 256:320], dr["kr_c"][l, jc * 128:(jc + 1) * 128, :], [], [bcst[jc]])
                        for k in range(3):
                            MM(bank(6, 128, k * 128), cs[:, k * 128:(k + 1) * 128], ident_f[:], True, True, [bcst[jc], bC], [bPS[6]])
                        for k in range(2):
                            CP("act", keysv[:, k, L + jc * 128:L + (jc + 1) * 128], bank(6, 128, k * 128), [bPS[6]], [bKEYS])
                        CP("act", krT[0:64, L + jc * 128:L + (jc + 1) * 128], bank(6, 128, 256)[0:64, :], [bPS[6]], [bKR])
                s.barrier()
                stage('stepB')
                RA.reset()
                RB.reset()
                RT.reset()
                mixT = RB.alloc(16 * Ltot, BF16)
                mixv = v3(mixT[:], 16)
                bMIX = [Buf() for _ in range(16)]
                if dbg and ("hy_%d_%d" % (l, g)) in dbg_out:
                    for q4 in range(3):
                        dt_ = RT.alloc(4 * Ltot, F32)
                        bd = Buf()
                        CP("dve", v3(dt_[:], 4), hyv[:, q4 * 4:(q4 + 1) * 4, :], bHY, [bd])
                        DMA(dbg_out["hy_%d_%d" % (l, g)][q4 * 512:(q4 + 1) * 512, :].rearrange("(k p) t -> p k t", p=128), v3(dt_[:], 4), [bd], [])
                        s.barrier()
                        RT.reset()

                scale = 192.0 ** -0.5
                qn = [RA.alloc(Ltot, BF16) for _ in range(2)]
                qr = [RA.alloc(Ltot, BF16) for _ in range(2)]
                kn = [RA.alloc(nseq * Lk, BF16) for _ in range(2)]
                vh = [RA.alloc(nseq * Lk, BF16) for _ in range(2)]
                bqn, bqr, bkn, bvh = ([Buf(), Buf()] for _ in range(4))
                pt = [RT.alloc(512, BF16) for _ in range(3)]
                bpt = [Buf() for _ in range(3)]
                rec = [RT.alloc(512, F32) for _ in range(2)]
                brec = [Buf(), Buf()]
                rt1 = RT.alloc(512, F32)
                rt2 = RT.alloc(512, F32)
                brt1, brt2 = Buf(), Buf()
                nkc = Lk // 128
                qblocks = []
                for sq_ in range(nseq):
                    for qb in range(max(1, L // 512)):
                        qn_ = min(L, 512)
                        qblocks.append((sq_, sq_ * L + qb * qn_, qn_))
                ptc = 0
                acc = 0
                pjc = [0]

                def pjn():
                    pjc[0] += 1
                    return 5 + pjc[0] % 3

                def mla_pro(h):
                    hb = h % 2
                    hb = h % 2
                    wq, bwq = wload(wsrc(dr["w_uq"][l], 0, 3, h * 192, 192), 3, 192)
                    if g == 1:
                        wqs, bwqs = wload(wsrc(dr["w_uq"][l], 0, 3, 1536 + h * 64, 64), 3, 64)
                    wkv, bwkv = wload(wsrc(dr["w_ukv"][l], 0, 2, h * 256, 256), 2, 256)
                    for b in range(nblk):
                        sl = slice(b * 512, (b + 1) * 512)
                        pj = pjn()
                        for k in range(3):
                            MM(bank(pj), wq[:, k, 0:128], cqnv[:, k, sl], k == 0, k == 2, [bwq(0), bCQN], [bPS[pj]])
                        CP("act", qn[hb][:, sl], bank(pj), [bPS[pj]], [bqn[hb]])
                        pj = pjn()
                        for k in range(3):
                            MM(bank(pj)[0:64, :], wq[:, k, 128:192], cqnv[:, k, sl], k == 0, k == 2, [bwq(0), bCQN], [bPS[pj]])
                        if g == 0:
                            CP("dve", qr[hb][0:64, sl], bank(pj)[0:64, :], [bPS[pj]], [bqr[hb]])
                        else:
                            TT("dve", rt1[0:64, :], bank(pj)[0:64, :], rope[0:64, b * 512:(b + 1) * 512], ALU.mult, [bPS[pj], bC], [brt1])
                            pj = pjn()
                            for k in range(3):
                                MM(bank(pj)[0:64, :], wqs[:, k, :], cqnv[:, k, sl], k == 0, k == 2, [bwqs(0), bCQN], [bPS[pj]])
                            TT("dve", rt2[0:64, :], bank(pj)[0:64, :], rope[0:64, LS + b * 512:LS + (b + 1) * 512], ALU.mult, [bPS[pj], bC], [brt2])
                            TT("pool", qr[hb][0:64, sl], rt1[0:64, :], rt2[0:64, :], ALU.add, [brt1, brt2], [bqr[hb]])
                    ktot = nseq * Lk
                    for c0 in range(0, ktot, 512):
                        n_ = min(512, ktot - c0)
                        pj = pjn()
                        for k in range(2):
                            MM(bank(pj, n_), wkv[:, k, 0:128], keysv[:, k, c0:c0 + n_], k == 0, k == 1, [bwkv(0), bKEYS], [bPS[pj]])
                        CP("act", kn[hb][:, c0:c0 + n_], bank(pj, n_), [bPS[pj]], [bkn[hb]])
                    for c0 in range(0, ktot, 512):
                        n_ = min(512, ktot - c0)
                        pj = pjn()
                        for jj in range(n_ // 128):
                            for k in range(2):
                                MM(bank(pj, 128, jj * 128), keysv[:, k, c0 + jj * 128:c0 + (jj + 1) * 128], wkv[:, k, 128:256], k == 0, k == 1, [bwkv(0), bKEYS], [bPS[pj]])
                        CP("dve", vh[hb][:, c0:c0 + n_], bank(pj, n_), [bPS[pj]], [bvh[hb]])

                def mla_att(h):
                    nonlocal ptc, acc
                    hb = h % 2
                    for (sq_, q0, qn_) in qblocks:
                        ob, db = 3, 4
                        acc += 1
                        def emit_sc(jk_, ptc0=ptc, sq_=sq_, q0=q0, qn_=qn_, hb=hb):
                            sb_ = (ptc0 + jk_) % 3
                            kc0 = sq_ * Lk + jk_ * 128
                            MM(bank(sb_, qn_), kn[hb][:, kc0:kc0 + 128], qn[hb][:, q0:q0 + qn_], True, False, [bkn[hb], bqn[hb]], [bPS[sb_]])
                            MM(bank(sb_, qn_), krT[0:64, kc0:kc0 + 128], qr[hb][0:64, q0:q0 + qn_], False, True, [bKR, bqr[hb]], [bPS[sb_]])
                            AC(pt[sb_][:, 0:qn_], bank(sb_, qn_), AF.Exp, [bPS[sb_]], [bpt[sb_]], scale=scale)
                        for jk in range(min(2, nkc)):
                            emit_sc(jk)
                        for jk in range(nkc):
                            sb_ = (ptc + jk) % 3
                            kc0 = sq_ * Lk + jk * 128
                            MM(bank(ob, qn_), vh[hb][:, kc0:kc0 + 128], pt[sb_][:, 0:qn_], jk == 0, jk == nkc - 1, [bvh[hb], bpt[sb_]], [bPS[ob]])
                            MM(bank(db, qn_), ones_b[:], pt[sb_][:, 0:qn_], jk == 0, jk == nkc - 1, [bC, bpt[sb_]], [bPS[db]])
                            if jk + 2 < nkc:
                                emit_sc(jk + 2)
                        ptc += nkc
                        rc, brc = rec[acc % 2], brec[acc % 2]
                        RCP(rc[:, 0:qn_], bank(db, qn_), [bPS[db]], [brc])
                        TT("dve", mixv[:, 8 + h, q0:q0 + qn_], bank(ob, qn_), rc[:, 0:qn_], ALU.mult, [bPS[ob], brc], [bMIX[8 + h]])

                mla_pro(0)
                for h in range(8):
                    if h + 1 < 8:
                        mla_pro(h + 1)
                    mla_att(h)
                s.barrier()
                stage('mla')
                RA.reset()
                RT.reset()

                if dbg and ("glain_%d_%d" % (l, g)) in dbg_out:
                    for nm_, t_, nk_, off_ in (("q", qT, 2, 0), ("k", kT, 2, 256), ("v", vT, 4, 512), ("sg", sgT, 4, 1024)):
                        RT.reset()
                        dt_ = RT.alloc(4 * Ltot, F32)
                        bd = Buf()
                        CP("dve", dt_[:, 0:nk_ * Ltot], t_[:], [bQ, bK, bV, bSG], [bd])
                        DMA(dbg_out["glain_%d_%d" % (l, g)][off_:off_ + nk_ * 128, :].rearrange("(k p) t -> p k t", p=128), v3(dt_[:, 0:nk_ * Ltot], nk_), [bd], [])
                        s.barrier()
                    RT.reset()
                    dt_ = RT.alloc(2 * Ltot, F32)
                    bd = Buf()
                    CP("dve", dt_[0:32, 0:Ltot], afT[0:32, :], [bAF], [bd])
                    CP("dve", dt_[0:32, Ltot:2 * Ltot], abT[0:32, :], [bAB], [bd])
                    DMA(dbg_out["glain_%d_%d" % (l, g)][1536:1568, :], dt_[0:32, 0:Ltot], [bd], [])
                    DMA(dbg_out["glain_%d_%d" % (l, g)][1568:1600, :], dt_[0:32, Ltot:2 * Ltot], [bd], [])
                    s.barrier()
                    RT.reset()
                oacc = RA.alloc(4 * Ltot, F32)
                oav = v3(oacc[:], 4)
                bOA = Buf()
                NT = 2
                ltok = [RT.alloc(256, F32) for _ in range(NT)]
                ep = [RT.alloc(256, F32) for _ in range(NT)]
                em = [RT.alloc(256, F32) for _ in range(NT)]
                er = [RT.alloc(256, F32) for _ in range(NT)]
                qd = [RT.alloc(512, BF16) for _ in range(NT)]
                ki = [RT.alloc(256, BF16) for _ in range(NT)]
                ke = [RT.alloc(256, BF16) for _ in range(NT)]
                vt = [RT.alloc(512, BF16) for _ in range(NT)]
                at = [RT.alloc(512, BF16) for _ in range(NT)]
                bl, bep, bem, ber, bqd, bki, bke, bvt, bat = ([Buf() for _ in range(NT)] for _ in range(9))
                for i0 in range(NT):
                    MS("pool", qd[i0][:], 0.0, [bqd[i0]])
                Sst = RT.alloc(256, F32)
                Sbf = RT.alloc(256, BF16)
                bSst, bSbf = Buf(), Buf()
                waug = RA.alloc(512, F32)
                waub = RA.alloc(512, BF16)
                bwa = Buf()
                for dd in range(2):
                    DMA(waug[0:17, dd * 256:(dd + 1) * 256], dr["wa_aug"][l, dd], [], [bwa])
                CP("dve", waub[0:17, :], waug[0:17, :], [bwa], [bwa])
                it = 0
                for sq_ in range(nseq):
                    for dd in range(2):
                        if g == 0:
                            MS("pool", Sst[:], 0.0, [bSst])
                        else:
                            DMA(v3(Sst[:], 2), dr["sf_c" if dd == 0 else "sb_c"][l].rearrange("(hp p) v -> p hp v", p=128), [], [bSst])
                        CP("pool", Sbf[:], Sst[:], [bSst], [bSbf])
                        aT, bA_ = (afT, bAF) if dd == 0 else (abT, bAB)
                        order = range(nf) if dd == 0 else range(nf - 1, -1, -1)
                        for n_ in order:
                            i_ = it % NT
                            it += 1
                            if knob is not None and it > knob[0]:
                                raise _Stop()
                            ts_ = slice(sq_ * L + n_ * 128, sq_ * L + (n_ + 1) * 128)

                            def kc_(n):
                                if knob is not None and it == knob[0] and n >= knob[1]:
                                    raise _Stop()
                            MM(bank(0, 256), aT[0:17, ts_], waub[0:17, dd * 256:(dd + 1) * 256], True, True, [bA_, bwa], [bPS[0]])
                            AC(ltok[i_][:], bank(0, 256), AF.Exp, [bPS[0]], [bl[i_]], scale=-1.0)
                            AC(ltok[i_][:], ltok[i_][:], AF.Ln, [bl[i_]], [bl[i_]], bias=1.0)
                            kc_(1)
                            triI = tri[:, (2 * dd) * 128:(2 * dd + 1) * 128]
                            triE = tri[:, (2 * dd + 1) * 128:(2 * dd + 2) * 128]
                            for hp in range(2):
                                MM(bank(1, 128, hp * 128), ltok[i_][:, hp * 128:(hp + 1) * 128], triI, True, True, [bl[i_], bC], [bPS[1]])
                            MM(bank(1, 256, 256), triE, ltok[i_][:], True, True, [bl[i_], bC], [bPS[1]])
                            kc_(2)
                            AC(ep[i_][:], bank(1, 256), AF.Exp, [bPS[1]], [bep[i_]])
                            AC(em[i_][:], bank(1, 256), AF.Exp, [bPS[1]], [bem[i_]], scale=-1.0)
                            AC(er[i_][:], bank(1, 256, 256), AF.Exp, [bPS[1]], [ber[i_]])
                            for hp in range(2):
                                for h2 in range(2):
                                    r0 = h2 * 64
                                    TT("dve", qd[i_][r0:r0 + 64, (hp * 2 + h2) * 128:(hp * 2 + h2 + 1) * 128], qv[r0:r0 + 64, hp, ts_], ep[i_][r0:r0 + 64, hp * 128:(hp + 1) * 128], ALU.mult, [bQ, bep[i_], bqd[i_]], [bqd[i_]])
                                TT("dve", ki[i_][:, hp * 128:(hp + 1) * 128], kv_[:, hp, ts_], em[i_][:, hp * 128:(hp + 1) * 128], ALU.mult, [bK, bem[i_]], [bki[i_]])
                            kc_(3)
                            for hp in range(2):
                                MM(bank(2, 128, hp * 128), kv_[:, hp, ts_], ident_b[:], True, True, [bK, bC], [bPS[2]])
                            TT("dve", ke[i_][:], bank(2, 256), er[i_][:], ALU.mult, [bPS[2], ber[i_]], [bke[i_]])
                            for hh in range(4):
                                MM(bank(3, 128, hh * 128), vv[:, hh, ts_], ident_b[:], True, True, [bV, bC], [bPS[3]])
                            CP("act", vt[i_][:], bank(3), [bPS[3]], [bvt[i_]])
                            kc_(4)
                            for hp in range(2):
                                MM(bank(4, 256, hp * 256), ki[i_][:, hp * 128:(hp + 1) * 128], qd[i_][:, hp * 256:(hp + 1) * 256], True, True, [bki[i_], bqd[i_]], [bPS[4]])
                            TT("dve", at[i_][:], bank(4), mask[:, dd * 512:(dd + 1) * 512], ALU.mult, [bPS[4], bC], [bat[i_]])
                            for hh in range(4):
                                hp, h2 = hh // 2, hh % 2
                                r0 = h2 * 64
                                MM(bank(5, 128, hh * 128), vt[i_][:, hh * 128:(hh + 1) * 128], at[i_][:, hh * 128:(hh + 1) * 128], True, False, [bvt[i_], bat[i_]], [bPS[5]])
                                MM(bank(5, 128, hh * 128), Sbf[:, hp * 128:(hp + 1) * 128], qd[i_][:, hh * 128:(hh + 1) * 128], False, True, [bSbf, bqd[i_]], [bPS[5]])
                            kc_(5)
                            o4 = bank(5).rearrange("p (h c) -> p h c", h=4)
                            if dd == 0:
                                CP("act", oav[:, :, ts_], o4, [bPS[5]], [bOA])
                            else:
                                TT("dve", oav[:, :, ts_], o4, oav[:, :, ts_], ALU.add, [bPS[5], bOA], [bOA])
                            kc_(6)
                            for hp in range(2):
                                MM(bank(6 + hp, 256), ke[i_][:, hp * 128:(hp + 1) * 128], vt[i_][:, hp * 256:(hp + 1) * 256], True, True, [bke[i_], bvt[i_]], [bPS[6 + hp]])
                            kc_(7)
                            dcol = 127 if dd == 0 else 0
                            for hp in range(2):
                                for h2 in range(2):
                                    r0 = h2 * 64
                                    STT("dve", Sst[r0:r0 + 64, hp * 128:(hp + 1) * 128], Sst[r0:r0 + 64, hp * 128:(hp + 1) * 128],
                                        ep[i_][r0:r0 + 64, hp * 128 + dcol:hp * 128 + dcol + 1], bank(6 + hp, 128, h2 * 128)[r0:r0 + 64, :],
                                        ALU.mult, ALU.add, [bSst, bep[i_], bPS[6 + hp]], [bSst])
                            CP("pool", Sbf[:], Sst[:], [bSst], [bSbf])
                            if _os.environ.get('GLA_BAR'):
                                s.barrier()
                        if g == 0:
                            DMA(dr["o_sf" if dd == 0 else "o_sb"][sq_, l].rearrange("(hp p) v -> p hp v", p=128), v3(Sst[:], 2), [bSst], [])
                if dbg and ("oacc_%d_%d" % (l, g)) in dbg_out:
                    s.barrier()
                    DMA(dbg_out["oacc_%d_%d" % (l, g)].rearrange("(k p) t -> p k t", p=128), oav, [bOA], [])
                    s.barrier()
                gsq = RA.alloc(512, BF16)
                bgsq = Buf()
                grs = RA.alloc(512, F32)
                bgrs = Buf()
                gtm = RA.alloc(512, F32)
                bgtm = Buf()
                for hh in range(4):
                    for b in range(nblk):
                        sl = slice(b * 512, (b + 1) * 512)
                        AC(gsq[:], oav[:, hh, sl], AF.Square, [bOA], [bgsq])
                        rms_rstd(lambda k: gsq[:], 1, 512, 128.0, 0, grs[:], [bgsq], [bgrs])
                        STT("dve", gtm[:], oav[:, hh, sl], vl[:, V_GN:V_GN + 1], grs[:], ALU.mult, ALU.mult, [bOA, bC, bgrs], [bgtm])
                        TT("pool", mixv[:, 4 + hh, sl], gtm[:], sgv[:, hh, sl], ALU.mult, [bgtm, bSG], [bMIX[4 + hh]])
                s.barrier()
                stage('gla')
                RA.reset()
                RT.reset()

                RIN.reset()
                _ = RIN.alloc(12 * Ltot, BF16)
                ksd = RIN.alloc(2 * nf * 512, BF16)
                bKSD = Buf()
                ksdv = ksd[:].rearrange("p (a n c) -> p a n c", a=2, n=nf)
                h1T = RIN.alloc(L, F32)
                h2T = RIN.alloc(L, F32)
                bh1, bh2 = Buf(), Buf()
                zemb = RIN.alloc(L, F32)
                fw12 = RIN.alloc(128, F32)
                bfz = Buf()
                arg = RIN.alloc(512, F32)
                arg2 = RIN.alloc(512, F32)
                barg, barg2 = Buf(), Buf()
                dec_t = RIN.alloc(512, F32)
                bdec = Buf()
                bsb_t = RIN.alloc(512, F32)
                bbsb = Buf()
                sm_t = RIN.alloc(512, F32)
                df_t = RIN.alloc(512, F32)
                bsm, bdf = Buf(), Buf()
                DMA(zemb[0:33, :], dr["zemb%d" % L], [], [bfz])
                DMA(fw12[0:33, 0:64], dr["fw1"][l], [], [bfz])
                DMA(fw12[0:64, 64:128], dr["fw2"][l], [], [bfz])
                ztok = RA.alloc(nseq * nf * 512, BF16)
                ztv = ztok[:].rearrange("p (s n c) -> p s n c", s=nseq, n=nf)
                bZT = Buf()
                p1d = RA.alloc(2 * nseq * nf * 512, BF16)
                p1v = p1d[:].rearrange("p (a s n c) -> p a s n c", a=2, s=nseq, n=nf)
                bP1 = Buf()
                kab = [RT.alloc(1024, F32) for _ in range(2)]
                bkab = [Buf(), Buf()]
                tt_ = [RT.alloc(512, F32) for _ in range(4)]
                btt = [Buf() for _ in range(4)]
                utmp = RT.alloc(512, F32)
                butmp = Buf()
                delta = RT.alloc(512, F32)
                negt = RT.alloc(8, F32)
                DMA(delta[:], dr["delta"], [], [bfz])
                DMA(negt[:, 0:nf], dr["negt%d" % L], [], [bfz])

                def sin_layer(pre_ps, n_, fcol, fbcol, out_ap, Rps, Wout):
                    TS("dve", arg[0:64, 0:n_], pre_ps, vl[0:64, fcol:fcol + 1], fb[0:64, fbcol:fbcol + 1], ALU.mult, ALU.add, Rps + [bC], [barg])
                    TS("dve", arg2[0:64, 0:n_], arg[0:64, 0:n_], 1.0 / TWO_PI, MAGIC, ALU.mult, ALU.add, [barg], [barg2])
                    TS("dve", arg2[0:64, 0:n_], arg2[0:64, 0:n_], MAGIC, -TWO_PI, ALU.subtract, ALU.mult, [barg2], [barg2])
                    TT("dve", arg[0:64, 0:n_], arg[0:64, 0:n_], arg2[0:64, 0:n_], ALU.add, [barg, barg2], [barg])
                    AC(out_ap, arg[0:64, 0:n_], AF.Sin, [barg], Wout)

                for c0 in range(0, L, 512):
                    n_ = min(512, L - c0)
                    MM(PS[0][0:64, 0:n_], fw12[0:33, 0:64], zemb[0:33, c0:c0 + n_], True, True, [bfz], [bPS[0]])
                    sin_layer(PS[0][0:64, 0:n_], n_, V_FR0, 0, h1T[0:64, c0:c0 + n_], [bPS[0]], [bh1])
                for c0 in range(0, L, 512):
                    n_ = min(512, L - c0)
                    MM(PS[0][0:64, 0:n_], fw12[0:64, 64:128], h1T[0:64, c0:c0 + n_], True, True, [bfz, bh1], [bPS[0]])
                    sin_layer(PS[0][0:64, 0:n_], n_, V_FR1, 1, h2T[0:64, c0:c0 + n_], [bPS[0]], [bh2])

                zcur = [hyv[:, cc, :] for cc in range(4)]
                zbuf = [bHY[cc] for cc in range(4)]
                for o in range(2):
                    w3t, bw3 = wload(dr["fw3"][l].rearrange("p (k c) -> p k c", k=1), 1, 2048, mode="f32", parts=64)
                    for lc in range(nf):
                        AC(dec_t[:], delta[:], AF.Exp, [bfz], [bdec], scale=negt[:, lc:lc + 1])
                        for side in range(2):
                            cofs = (side * 2 + o) * 512
                            MM(bank(side), h2T[0:64, lc * 128:(lc + 1) * 128], w3t[0:64, 0, cofs:cofs + 512], True, True, [bh2, bw3(0)], [bPS[side]])
                        CP("act", bsb_t[:], bank(1), [bPS[1]], [bbsb])
                        if lc == 0:
                            MS("pool", bsb_t[0:1, :], 0.0, [bbsb])
                        TT("dve", sm_t[:], bank(0), bsb_t[:], ALU.add, [bPS[0], bbsb], [bsm])
                        TT("dve", df_t[:], bank(0), bsb_t[:], ALU.subtract, [bPS[0], bbsb], [bdf])
                        TT("pool", ksdv[:, 0, lc, :], sm_t[:], dec_t[:], ALU.mult, [bsm, bdec], [bKSD])
                        TT("pool", ksdv[:, 1, lc, :], df_t[:], dec_t[:], ALU.mult, [bdf, bdec], [bKSD])
                    for sq_ in range(nseq):
                        for tcn in range(nf):
                            ts_ = slice(sq_ * L + tcn * 128, sq_ * L + (tcn + 1) * 128)
                            pbk = 2 + (tcn % 2)
                            for cc in range(4):
                                MM(bank(pbk, 128, cc * 128), zcur[cc][:, ts_], ident_b[:], True, True, [zbuf[cc], bC], [bPS[pbk]])
                            CP("act", ztv[:, sq_, tcn, :], bank(pbk), [bPS[pbk]], [bZT])
                    for fb_ in range(nf // 2):
                        Cp, bCp = wload(wsrc(dr["dftC%d" % L], 0, nf, fb_ * 256, 256), nf, 256, mode="bf16")
                        Sp, bSp = wload(wsrc(dr["dftS%d" % L], 0, nf, fb_ * 256, 256), nf, 256, mode="bf16")
                        for f2 in range(2):
                            fc = fb_ * 2 + f2
                            fcs = slice(f2 * 128, (f2 + 1) * 128)
                            kk, bk_ = kab[fc % 2], bkab[fc % 2]
                            for n_ in range(nf):
                                MM(bank(0), Cp[:, n_, fcs], ksdv[:, 0, n_, :], n_ == 0, n_ == nf - 1, [bCp(0), bKSD], [bPS[0]])
                            for n_ in range(nf):
                                MM(bank(1), Sp[:, n_, fcs], ksdv[:, 1, n_, :], n_ == 0, n_ == nf - 1, [bSp(0), bKSD], [bPS[1]])
                            AC(kk[:, 0:512], bank(0), AF.Identity, [bPS[0]], [bk_], scale=1.0 / L)
                            AC(kk[:, 512:1024], bank(1), AF.Identity, [bPS[1]], [bk_], scale=1.0 / L)
                            for sq_ in range(nseq):
                                pa, pb_ = (4, 5) if (fc * nseq + sq_) % 2 == 0 else (6, 7)
                                for n_ in range(nf):
                                    MM(bank(pa), Cp[:, n_, fcs], ztv[:, sq_, n_, :], n_ == 0, n_ == nf - 1, [bCp(0), bZT], [bPS[pa]])
                                for n_ in range(nf):
                                    MM(bank(pb_), Sp[:, n_, fcs], ztv[:, sq_, n_, :], n_ == 0, n_ == nf - 1, [bSp(0), bZT], [bPS[pb_]])
                                TT("dve", tt_[0][:], bank(pa), kk[:, 0:512], ALU.mult, [bPS[pa], bk_], [btt[0]])
                                TT("dve", tt_[1][:], bank(pb_), kk[:, 512:1024], ALU.mult, [bPS[pb_], bk_], [btt[1]])
                                TT("pool", p1v[:, 0, sq_, fc, :], tt_[0][:], tt_[1][:], ALU.subtract, [btt[0], btt[1]], [bP1])
                                TT("dve", tt_[2][:], bank(pa), kk[:, 512:1024], ALU.mult, [bPS[pa], bk_], [btt[2]])
                                TT("dve", tt_[3][:], bank(pb_), kk[:, 0:512], ALU.mult, [bPS[pb_], bk_], [btt[3]])
                                TT("pool", p1v[:, 1, sq_, fc, :], tt_[2][:], tt_[3][:], ALU.add, [btt[2], btt[3]], [bP1])
                    xo = [hyv[:, 4 * (o + 1) + cc, :] for cc in range(4)]
                    xob = [bHY[4 * (o + 1) + cc] for cc in range(4)]
                    it2 = 0
                    for sq_ in range(nseq):
                        for tb in range(L // 256):
                            CTp, bCTp = wload(wsrc(dr["dftCT%d" % L], 0, nf, tb * 256, 256), nf, 256, mode="bf16")
                            STp, bSTp = wload(wsrc(dr["dftST%d" % L], 0, nf, tb * 256, 256), nf, 256, mode="bf16")
                            ts_ = slice(sq_ * L + tb * 256, sq_ * L + (tb + 1) * 256)
                            for cc in range(4):
                                pbk = it2 % 4
                                it2 += 1
                                ccs = slice(cc * 128, (cc + 1) * 128)
                                for n_ in range(nf):
                                    MM(bank(pbk, 256), p1v[:, 0, sq_, n_, ccs], CTp[:, n_, :], n_ == 0, False, [bP1, bCTp(0)], [bPS[pbk]])
                                for n_ in range(nf):
                                    MM(bank(pbk, 256), p1v[:, 1, sq_, n_, ccs], STp[:, n_, :], False, n_ == nf - 1, [bP1, bSTp(0)], [bPS[pbk]])
                                STT("dve", utmp[:, 0:256], zcur[cc][:, ts_], vl[:, V_SKIP + o * 4 + cc:V_SKIP + o * 4 + cc + 1], bank(pbk, 256), ALU.mult, ALU.add, [zbuf[cc], bC, bPS[pbk]], [butmp])
                                if o == 0:
                                    TT("pool", zcur[cc][:, ts_], utmp[:, 0:256], xo[cc][:, ts_], ALU.mult, [butmp, xob[cc]], [zbuf[cc]])
                                else:
                                    TT("pool", mixv[:, cc, ts_], utmp[:, 0:256], xo[cc][:, ts_], ALU.mult, [butmp, xob[cc]], [bMIX[cc]])
                if dbg and ("mix_%d_%d" % (l, g)) in dbg_out:
                    s.barrier()
                    for q4 in range(4):
                        RT.reset()
                        dt_ = RT.alloc(4 * Ltot, F32)
                        bd = Buf()
                        CP("dve", v3(dt_[:], 4), mixv[:, q4 * 4:(q4 + 1) * 4, :], bMIX, [bd])
                        DMA(dbg_out["mix_%d_%d" % (l, g)][q4 * 512:(q4 + 1) * 512, :].rearrange("(k p) t -> p k t", p=128), v3(dt_[:], 4), [bd], [])
                        s.barrier()
                s.barrier()
                stage('hyena')
                RA.reset()
                RT.reset()
                RIN.reset()

                xc = [RT.alloc(512, F32) for _ in range(3)]
                bxc = [Buf() for _ in range(3)]
                it3 = 0
                for i in range(16):
                    wt, bw = wload(wsrc(dr["w_out"][l], 0, 16, i * 128, 128), 16, 128)
                    u = i % 4
                    for b in range(nblk):
                        blk = (tok0 // 512) + b
                        xi_, bxi = xc[it3 % 3], bxc[it3 % 3]
                        it3 += 1
                        DMA(xi_[:], xTv[:, i, blk * 512:(blk + 1) * 512], [bXd[blk][i]], [bxi])
                        for k in range(16):
                            MM(bank(2 * u + b), wt[:, k, :], mixv[:, k, b * 512:(b + 1) * 512], k == 0, k == 15, [bw(k), bMIX[k]], [bPS[2 * u + b]])
                        STT("dve", xi_[:], bank(2 * u + b), modT[:, j * 96 + 32 + i:j * 96 + 32 + i + 1], xi_[:], ALU.mult, ALU.add, [bPS[2 * u + b], bC, bxi], [bxi])
                        DMA(xTv[:, i, blk * 512:(blk + 1) * 512], xi_[:], [bxi], [bXd[blk][i]])
                s.barrier()
                stage('outproj')

            last = (l == nlayers - 1)
            RF.reset()
            wl2 = make_ws2(RF, nstg=2)
            h2all = RF.alloc(16 * T, BF16)
            h2a = v3(h2all[:], 16)
            bh2 = [Buf() for _ in range(3)]
            rstd = RF.alloc(512, F32)
            brs = Buf()
            sgt = [RF.alloc(512, F32) for _ in range(2)]
            bsg = [Buf(), Buf()]
            ntmp = RF.alloc(2 * 512, F32)
            bnt = [Buf(), Buf()]
            xc = [RF.alloc(512, F32) for _ in range(3)]
            bxc = [Buf() for _ in range(3)]
            act_off = RF.cur
            xblk = RF.alloc(16 * 512, F32)
            bxb = Buf()
            sqt = RF.alloc(16 * 512, BF16)
            bsq = Buf()
            for blk in range(3):
                j = 0 if blk == 0 else 1
                DMA(v3(xblk[:], 16), xTv[:, :, blk * 512:(blk + 1) * 512], bXd[blk], [bxb])
                AC(sqt[:], xblk[:], AF.Square, [bxb], [bsq])
                rms_rstd(lambda k: sqt[:, k * 512:(k + 1) * 512], 16, 512, float(D), 7, rstd[:], [bsq], [brs])
                for k in range(16):
                    nt_, bn_ = ntmp[:, (k % 2) * 512:(k % 2) * 512 + 512], bnt[k % 2]
                    TT("dve", nt_, xblk[:, k * 512:(k + 1) * 512], rstd[:], ALU.mult, [bxb, brs], [bn_])
                    AC(h2a[:, k, blk * 512:(blk + 1) * 512], nt_, AF.Identity, [bn_, bC], [bh2[blk]],
                       bias=modT[:, j * 96 + 48 + k:j * 96 + 48 + k + 1], scale=gsc[:, (j * 2 + 1) * 16 + k:(j * 2 + 1) * 16 + k + 1])
            s.barrier()
            RF.cur = act_off
            actH = RF.alloc(22 * T, BF16)
            actv = v3(actH[:], 22)
            bact = [[Buf() for _ in range(3)] for _ in range(22)]
            xc6 = xc + [RF.alloc(512, F32) for _ in range(3)]
            bxc6 = bxc + [Buf() for _ in range(3)]
            nup = 0
            for half in range(2):
                for jp in range(11):
                    jf0 = half * 22 + 2 * jp
                    wg, bg = wl2(wsrc(dr["w_ffn_in"][l], 0, 16, jf0 * 128, 256), 16, 256)
                    wu, bu = wl2(wsrc(dr["w_ffn_in"][l], 0, 16, DFF + jf0 * 128, 256), 16, 256)
                    for c in range(2):
                        for blk in range(3):
                            pg = c * 3 + blk
                            sl = slice(blk * 512, (blk + 1) * 512)
                            for k in range(16):
                                MM(bank(pg), wg[:, k, c * 128:(c + 1) * 128], h2a[:, k, sl], k == 0, k == 15, [bg(k), bh2[blk]], [bPS[pg]])
                    for c in range(2):
                        jfl = 2 * jp + c
                        for blk in range(3):
                            pg = c * 3 + blk
                            pu = 6 + (nup % 2)
                            sg_, bsg_ = sgt[nup % 2], bsg[nup % 2]
                            nup += 1
                            sl = slice(blk * 512, (blk + 1) * 512)
                            AC(sg_[:], bank(pg), AF.Silu, [bPS[pg]], [bsg_])
                            for k in range(16):
                                MM(bank(pu), wu[:, k, c * 128:(c + 1) * 128], h2a[:, k, sl], k == 0, k == 15, [bu(k), bh2[blk]], [bPS[pu]])
                            TT("dve", actv[:, jfl, sl], bank(pu), sg_[:], ALU.mult, [bPS[pu], bsg_], [bact[jfl][blk]])
                def out_tiles(ip_):
                    res = []
                    for (k0, kc) in ((0, 16), (16, 6)):
                        wt, bk = wl2(wsrc(dr["w_ffn_out"][l], half * 22 + k0, kc, ip_ * 256, 256), kc, 256)
                        res.append((k0, kc, wt, bk))
                    return res
                tiles = out_tiles(0)
                pvn = bank(7, 192).rearrange("p (j m) -> p m j", j=2)
                for ip in range(8):
                    i0 = 2 * ip
                    for c in range(2):
                        for blk in range(3):
                            n6 = c * 3 + blk
                            DMA(xc6[n6][:], xTv[:, i0 + c, blk * 512:(blk + 1) * 512], [bXd[blk][i0 + c]], [bxc6[n6]])
                    for (k0, kc, wt, bk) in tiles:
                        for c in range(2):
                            for blk in range(3):
                                pb = c * 3 + blk
                                for k in range(kc):
                                    kk = k0 + k
                                    MM(bank(pb), wt[:, k, c * 128:(c + 1) * 128], actv[:, kk, blk * 512:(blk + 1) * 512], kk == 0, kk == 21, [bk(k), bact[kk][blk]], [bPS[pb]])
                    for c in range(2):
                        i = i0 + c
                        for blk in range(3):
                            j = 0 if blk == 0 else 1
                            n6 = c * 3 + blk
                            STT("dve", xc6[n6][:], bank(n6), modT[:, j * 96 + 80 + i:j * 96 + 80 + i + 1], xc6[n6][:], ALU.mult, ALU.add, [bPS[n6], bC, bxc6[n6]], [bxc6[n6]])
                    if l + 1 < nlayers:
                        for t_ in range(3):
                            mp = half * 24 + ip * 3 + t_
                            wtm, bkm = wl2(wsrc(dr["w_mod"][l + 1], 0, 16, mp * 256, 256), 16, 256)
                            for c in range(2):
                                m = 2 * mp + c
                                for k in range(16):
                                    MM(pvn[:, m, :], wtm[:, k, c * 128:(c + 1) * 128], scv[:, k, :], k == 0, k == 15, [bkm(k), bC], [bPS[7]])
                    if ip < 7:
                        tiles = out_tiles(ip + 1)
                    for c in range(2):
                        for blk in range(3):
                            n6 = c * 3 + blk
                            DMA(xTv[:, i0 + c, blk * 512:(blk + 1) * 512], xc6[n6][:], [bxc6[n6]], [bXd[blk][i0 + c]])
                if l + 1 < nlayers:
                    m0 = half * 48
                    for j in range(2):
                        TT("dve", modTs[(l + 1) % 2][:, j * 96 + m0:j * 96 + m0 + 48], bank(7, 48, j * 96 + m0), vecs[l + 1][:, V_BMOD + m0:V_BMOD + m0 + 48], ALU.add, [bPS[7], bC], [bC])
            s.barrier()
            if last:
                RF.cur = act_off
                xblk = RF.alloc(16 * 512, F32)
                bxk = [Buf() for _ in range(16)]
                xv = v3(xblk[:], 16)
                sqt = RF.alloc(16 * 512, BF16)
                bsq = Buf()
                yo = [RF.alloc(2048, F32) for _ in range(2)]
                byo = [Buf(), Buf()]
                for blk in range(3):
                    DMA(xv, xTv[:, :, blk * 512:(blk + 1) * 512], bXd[blk], bxk)
                    AC(sqt[:], xblk[:], AF.Square, bxk, [bsq])
                    rms_rstd(lambda k: sqt[:, k * 512:(k + 1) * 512], 16, 512, float(D), 7, rstd[:], [bsq], [brs])
                    for k in range(16):
                        STT("dve", xv[:, k, :], xv[:, k, :], lnf[:, k:k + 1], rstd[:], ALU.mult, ALU.mult, [bxk[k], bC, brs], [bxk[k]])
                    for tcn in range(4):
                        yt, byt = yo[tcn % 2], byo[tcn % 2]
                        for half in range(2):
                            u = (tcn * 2 + half) % 3
                            for kk in range(8):
                                k = half * 8 + kk
                                MM(PS[u][:, kk * 128:(kk + 1) * 128], xv[:, k, tcn * 128:(tcn + 1) * 128], ident_f[:], True, True, [bxk[k], bC], [bPS[2 * u + kk // 4]])
                            CP("act" if half else "dve", yt[:, half * 1024:(half + 1) * 1024], PS[u][:, :], [bPS[2 * u], bPS[2 * u + 1]], [byt])
                        DMA(dr["y"][blk * 512 + tcn * 128:blk * 512 + (tcn + 1) * 128, :], yt[:], [byt], [])
                s.barrier()
    except _Stop:
        pass
    s.barrier()
    s.emit(nc, st)
    st.close()
    return nc


_NC_CACHE = {}


def prep_inputs(inp, core):
    c = host_consts()
    f32 = lambda a: np.ascontiguousarray(np.asarray(a, dtype=np.float32))
    m = {}
    xp = f32(inp["x_prompt"])[2 * core:2 * core + 2].reshape(TP, D)
    xs = f32(inp["x_sample"])[core]
    m["x_in"] = np.ascontiguousarray(np.concatenate([xp, xs], axis=0))
    cv = np.stack([f32(inp["c_ctx"]), f32(inp["c"])[core]], axis=0)
    m["cvec"] = np.ascontiguousarray(cv.reshape(2, 16, 128).transpose(2, 1, 0).reshape(128, 32))
    m["ckv_c"] = f32(inp["cache_mla_ckv"])[core]
    m["kr_c"] = f32(inp["cache_mla_krope"])[core]
    m["sf_c"] = np.ascontiguousarray(f32(inp["state_gla_fwd"])[core].reshape(NL, 256, 128))
    m["sb_c"] = np.ascontiguousarray(f32(inp["state_gla_bwd"])[core].reshape(NL, 256, 128))
    return m


def prep_shared(inp):
    c = host_consts()
    f32 = lambda a: np.ascontiguousarray(np.asarray(a, dtype=np.float32))
    m = {}
    m["w_mod"] = f32(inp["w_mod"])
    w_in = f32(inp["w_in"])
    m["w_in"] = np.ascontiguousarray(np.concatenate([w_in, w_in[:, :, C_KR + c["perm"]]], axis=2))
    m["w_out"] = f32(inp["w_out"])
    m["w_ffn_in"] = f32(inp["w_ffn_in"])
    m["w_ffn_out"] = f32(inp["w_ffn_out"])
    wuq = f32(inp["mla_w_uq"])
    sw = np.concatenate([wuq[:, :, h * 192 + 128 + c["perm"]] for h in range(8)], axis=2)
    m["w_uq"] = np.ascontiguousarray(np.concatenate([wuq, sw], axis=2))
    m["w_ukv"] = f32(inp["mla_w_ukv"])
    vecs = np.zeros((NL, 128, NV), np.float32)
    pm = lambda a, n: f32(a).reshape(NL, n, 128).transpose(0, 2, 1)
    vecs[:, :, V_BMOD:V_BMOD + 96] = pm(inp["b_mod"], 96)
    vecs[:, :, V_LNMIX:V_LNMIX + 16] = pm(inp["ln_mix"], 16)
    vecs[:, :, V_LNFFN:V_LNFFN + 16] = pm(inp["ln_ffn"], 16)
    vecs[:, :, V_CW:V_CW + 36] = f32(inp["hy_conv_w"]).reshape(NL, 3, 12, 128).transpose(0, 3, 1, 2).reshape(NL, 128, 36)
    vecs[:, :, V_CB:V_CB + 12] = pm(inp["hy_conv_b"], 12)
    vecs[:, :, V_SKIP:V_SKIP + 8] = f32(inp["hy_skip"]).reshape(NL, 2, 4, 128).transpose(0, 3, 1, 2).reshape(NL, 128, 8)
    vecs[:, :, V_GN] = f32(inp["gla_norm"])
    vecs[:, :, V_QN:V_QN + 3] = pm(inp["mla_q_norm"], 3)
    vecs[:, :, V_KVN:V_KVN + 2] = pm(inp["mla_kv_norm"], 2)
    vecs[:, 0:64, V_FB1] = f32(inp["hy_filt_b1"])
    vecs[:, 0:64, V_FB2] = f32(inp["hy_filt_b2"])
    vecs[:, 0:64, V_FR0] = f32(inp["hy_filt_freq"])[:, 0]
    vecs[:, 0:64, V_FR1] = f32(inp["hy_filt_freq"])[:, 1]
    m["vecs"] = vecs
    m["lnf"] = np.ascontiguousarray(f32(inp["ln_final"]).reshape(16, 128).T)
    m["fw1"] = f32(inp["hy_filt_w1"])
    m["fw2"] = f32(inp["hy_filt_w2"])
    m["fw3"] = f32(inp["hy_filt_w3"])
    wa = np.zeros((NL, 2, 17, 256), np.float32)
    wa[:, 0, 0:16] = f32(inp["gla_wa_f"])
    wa[:, 0, 16] = f32(inp["gla_ba_f"])
    wa[:, 1, 0:16] = f32(inp["gla_wa_b"])
    wa[:, 1, 16] = f32(inp["gla_ba_b"])
    m["wa_aug"] = wa
    for name, shape, dt in CONST_DRAM:
        m[name] = c[name]
    return m


def kernel(**inputs):
    if "nc" not in _NC_CACHE:
        _NC_CACHE["nc"] = build()
    nc = _NC_CACHE["nc"]
    shared = prep_shared(inputs)
    in_maps = []
    for core in range(NCORE):
        m = dict(shared)
        m.update(prep_inputs(inputs, core))
        in_maps.append(m)
    res = run_bass_kernel_spmd(nc, in_maps, core_ids=list(range(NCORE)))
    R = res.results
    y_prompt = np.concatenate([r["y"][0:TP].reshape(2, LP, D) for r in R], axis=0)
    y_sample = np.stack([r["y"][TP:T] for r in R], axis=0)
    o_ckv = np.concatenate([r["o_ckv"] for r in R], axis=0)
    o_kr = np.concatenate([r["o_kr"] for r in R], axis=0)
    o_sf = np.concatenate([r["o_sf"].reshape(2, NL, 4, 64, 128) for r in R], axis=0)
    o_sb = np.concatenate([r["o_sb"].reshape(2, NL, 4, 64, 128) for r in R], axis=0)
    f = lambda a: np.ascontiguousarray(a.astype(np.float32))
    return (f(y_prompt), f(y_sample), f(o_ckv), f(o_kr), f(o_sf), f(o_sb))
```

```python
import math
import contextlib
import numpy as np
import ml_dtypes
import concourse.bass as bass
import concourse.mybir as mybir
from concourse.bass_utils import run_bass_kernel_spmd

F32 = mybir.dt.float32
BF16 = mybir.dt.bfloat16
ALU = mybir.AluOpType
AF = mybir.ActivationFunctionType

D = 2048
DFF = 5632
NL = 2
LP = 256
LS = 1024
TP = 512
T = 1536
PAST = 256
EPS = 1e-6
NCORE = 8
C_HY, C_Q, C_K, C_V, C_G, C_AF, C_AB, C_CQ, C_CKV, C_KR, C_KRSW = 0, 1536, 1792, 2048, 2560, 3072, 3088, 3104, 3488, 3744, 3808
WIN_EXT = 3872
NV = 200
V_BMOD, V_LNMIX, V_LNFFN, V_CW, V_CB, V_SKIP, V_GN, V_QN, V_KVN, V_FB1, V_FB2, V_FR0, V_FR1 = 0, 96, 112, 128, 164, 176, 184, 185, 188, 190, 191, 192, 193
MAGIC = 12582912.0
TWO_PI = 2.0 * math.pi
SBASE = 16512
SEND = 229376


class Buf:
    __slots__ = ("w", "r")

    def __init__(self):
        self.w = None
        self.r = {}


class Op:
    __slots__ = ("eng", "fn", "waits", "signal", "src", "seq", "vc", "val", "dma")


class Sched:
    ENG = ["pe", "act", "dve", "pool", "sp"]
    NSLOT = 8

    def __init__(self):
        self.q = {e: [] for e in self.ENG}
        self.S = {e: {} for e in self.ENG}
        self.ncomp = {e: 0 for e in self.ENG}
        self.ndma = {e: 0 for e in self.ENG}
        self.slot_last = {}
        self.slot_seq = {}

    def add(self, eng, fn, reads=(), writes=(), dma=False, after=()):
        op = Op()
        op.eng = eng
        op.fn = fn
        op.signal = False
        op.dma = dma
        deps = []
        deps.extend(after)
        for b in reads:
            if b.w is not None:
                deps.append(b.w)
        for b in writes:
            if b.w is not None:
                deps.append(b.w)
            deps.extend(b.r.values())
        if dma:
            j = self.ndma[eng]
            self.ndma[eng] += 1
            slot = (eng, j % self.NSLOT)
            prev = self.slot_last.get(slot)
            if prev is not None:
                deps.append(prev)
            op.src = ("dma",) + slot
            op.seq = self.slot_seq.get(slot, 0) + 1
            self.slot_seq[slot] = op.seq
            self.slot_last[slot] = op
            op.signal = True
        else:
            self.ncomp[eng] += 1
            op.src = eng
            op.seq = self.ncomp[eng]
        S = self.S[eng]
        waits = {}
        for d in deps:
            if eng == "pe" and d.src == "pe":
                continue
            if S.get(d.src, 0) >= d.seq:
                continue
            d.signal = True
            cur = waits.get(d.src)
            if cur is None or cur.seq < d.seq:
                waits[d.src] = d
            for k, v in d.vc.items():
                if S.get(k, 0) < v:
                    S[k] = v
        op.waits = list(waits.values())
        op.vc = dict(S)
        op.vc[op.src] = op.seq
        for b in writes:
            b.w = op
            b.r = {}
        for b in reads:
            if b.w is not op:
                b.r[op.src] = op
        self.q[eng].append(op)
        return op

    @staticmethod
    def deps_of(bufs):
        out = []
        for b in bufs:
            if b.w is not None:
                out.append(b.w)
            out.extend(b.r.values())
        return out

    def barrier(self):
        lasts = []
        for e in self.ENG:
            for op in reversed(self.q[e]):
                if (not op.dma) and op.fn is not None:
                    lasts.append(op)
                    break
        lasts.extend(self.slot_last.values())
        for e in self.ENG:
            S = self.S[e]
            waits = {}
            for d in lasts:
                if S.get(d.src, 0) >= d.seq:
                    continue
                d.signal = True
                waits[d.src] = d
            for d in lasts:
                for k, v in d.vc.items():
                    if S.get(k, 0) < v:
                        S[k] = v
            if waits:
                op = Op()
                op.eng = e
                op.fn = None
                op.signal = False
                op.dma = False
                op.waits = list(waits.values())
                op.src = None
                op.seq = 0
                op.vc = {}
                self.q[e].append(op)

    def emit(self, nc, stack):
        sems = {}
        for e in self.ENG:
            cnt = 0
            for op in self.q[e]:
                if op.fn is None:
                    continue
                if op.dma:
                    op.val = 16 * op.seq
                else:
                    if op.signal:
                        cnt += 1
                    op.val = cnt
                if op.signal and op.src not in sems:
                    nm = "s_" + ("_".join(str(x) for x in op.src) if isinstance(op.src, tuple) else op.src)
                    sems[op.src] = stack.enter_context(nc.semaphore(nm))
        block = stack.enter_context(nc.Block())
        q = self.q

        def run(e, eng):
            for op in q[e]:
                for d in op.waits:
                    eng.wait_ge(sems[d.src], d.val)
                if op.fn is None:
                    continue
                ins = op.fn(eng)
                if op.signal:
                    ins.then_inc(sems[op.src], 16 if op.dma else 1)

        @block.tensor
        def _(eng):
            run("pe", eng)

        @block.scalar
        def _(eng):
            run("act", eng)

        @block.vector
        def _(eng):
            run("dve", eng)

        @block.gpsimd
        def _(eng):
            run("pool", eng)

        @block.sync
        def _(eng):
            run("sp", eng)


def _bf(x):
    return np.ascontiguousarray(x.astype(ml_dtypes.bfloat16))


_CONST_CACHE = {}


def host_consts():
    if _CONST_CACHE:
        return _CONST_CACHE
    c = {}
    c["ident_f"] = np.eye(128, dtype=np.float32)
    c["ident_b"] = _bf(np.eye(128, dtype=np.float32))
    c["ones_b"] = _bf(np.ones((128, 128), np.float32))
    s_ = np.arange(128)[:, None]
    c_ = np.arange(128)[None, :]
    v = -1.0 / 16.0
    tri = np.stack([(s_ <= c_) * v, (s_ > c_) * v, (s_ >= c_) * v, (s_ < c_) * v]).astype(np.float32)
    c["tri"] = np.ascontiguousarray(tri.transpose(1, 0, 2).reshape(128, 4 * 128))
    mk = np.stack([np.tile((s_ <= c_).astype(np.float32), (1, 4)), np.tile((s_ >= c_).astype(np.float32), (1, 4))])
    c["mask"] = np.ascontiguousarray(mk.transpose(1, 0, 2).reshape(128, 2 * 512))
    t = np.arange(LS)
    row, col = t // 64, t % 64
    inv = (10000.0 ** (-np.arange(0, 32, 2, dtype=np.float32) / 32.0)).astype(np.float32)
    cos = np.zeros((64, LS), np.float32)
    sin = np.zeros((64, LS), np.float32)
    perm = np.zeros(64, np.int64)
    for r in range(64):
        pos = (row if r < 32 else col).astype(np.float32)
        j = r % 16
        ang = (pos * inv[j]).astype(np.float32)
        cos[r] = np.cos(ang)
        x2 = (r % 32) >= 16
        sin[r] = np.sin(ang) * (1.0 if x2 else -1.0)
        perm[r] = r - 16 if x2 else r + 16
    c["rope"] = np.ascontiguousarray(np.concatenate([cos, sin], axis=1))
    c["perm"] = perm
    for L in (LP, LS):
        tt = np.linspace(0.0, 1.0, L, dtype=np.float32)[:, None]
        w = (2.0 * math.pi * np.arange(L, dtype=np.float32)[:, None] / L).astype(np.float32)
        f = np.linspace(1e-4, 15, 16, dtype=np.float32)
        z = np.concatenate([tt, np.cos(f * w), -np.sin(f * w)], axis=-1).astype(np.float32)
        c[f"zemb{L}"] = np.ascontiguousarray(z.T)
        c[f"negt{L}"] = np.ascontiguousarray((-tt[:, 0]).reshape(L // 128, 128).T)
        n = np.arange(L, dtype=np.float64)[:, None]
        fq = np.arange(L, dtype=np.float64)[None, :]
        th = math.pi * (2 * fq + 1) * n / (2 * L)
        C = np.cos(th)
        S = np.sin(th)
        c[f"dftC{L}"] = _bf(C.astype(np.float32))
        c[f"dftS{L}"] = _bf(S.astype(np.float32))
        c[f"dftCT{L}"] = _bf(C.T.astype(np.float32))
        c[f"dftST{L}"] = _bf(S.T.astype(np.float32))
    mind = math.log(1e-2) / 1.5
    maxd = math.log(1e-2) / 0.3
    delta = np.abs(np.linspace(mind, maxd, 512, dtype=np.float32))
    c["delta"] = np.ascontiguousarray(np.tile(delta[None, :], (128, 1)).astype(np.float32))
    _CONST_CACHE.update(c)
    return c


CONST_DRAM = [("ident_f", [128, 128], F32), ("ident_b", [128, 128], BF16), ("ones_b", [128, 128], BF16),
              ("tri", [128, 512], F32), ("mask", [128, 1024], F32), ("rope", [64, 2048], F32),
              ("zemb256", [33, 256], F32), ("zemb1024", [33, 1024], F32),
              ("negt256", [128, 2], F32), ("negt1024", [128, 8], F32), ("delta", [128, 512], F32),
              ("dftC256", [256, 256], BF16), ("dftS256", [256, 256], BF16), ("dftCT256", [256, 256], BF16), ("dftST256", [256, 256], BF16),
              ("dftC1024", [1024, 1024], BF16), ("dftS1024", [1024, 1024], BF16), ("dftCT1024", [1024, 1024], BF16), ("dftST1024", [1024, 1024], BF16)]

IN_DRAM = [("x_in", [T, D]), ("cvec", [128, 32]), ("ckv_c", [NL, PAST, 256]), ("kr_c", [NL, PAST, 64]),
           ("sf_c", [NL, 256, 128]), ("sb_c", [NL, 256, 128]),
           ("w_mod", [NL, D, 6 * D]), ("w_in", [NL, D, WIN_EXT]), ("w_out", [NL, D, D]),
           ("w_ffn_in", [NL, D, 2 * DFF]), ("w_ffn_out", [NL, DFF, D]), ("w_uq", [NL, 384, 2048]), ("w_ukv", [NL, 256, 2048]),
           ("vecs", [NL, 128, NV]), ("lnf", [128, 16]), ("fw1", [NL, 33, 64]), ("fw2", [NL, 64, 64]), ("fw3", [NL, 64, 2048]),
           ("wa_aug", [NL, 2, 17, 256])]

OUT_DRAM = [("y", [T, D]), ("o_ckv", [2, NL, LP, 256]), ("o_kr", [2, NL, LP, 64]), ("o_sf", [2, NL, 256, 128]), ("o_sb", [2, NL, 256, 128])]


class _Stop(Exception):
    pass


def build(nlayers=NL, dbg=None, stop=None, knob=None):
    nc = bass.Bass("TRN2", target_bir_lowering=False)
    dr = {}
    for name, shape in IN_DRAM:
        dr[name] = nc.dram_tensor(name, list(shape), F32, kind="ExternalInput").ap()
    for name, shape, dt in CONST_DRAM:
        dr[name] = nc.dram_tensor(name, list(shape), dt, kind="ExternalInput").ap()
    for name, shape in OUT_DRAM:
        dr[name] = nc.dram_tensor(name, list(shape), F32, kind="ExternalOutput").ap()
    dbg_out = {}
    if dbg:
        for name, shape in dbg.items():
            dbg_out[name] = nc.dram_tensor("dbg_" + name, list(shape), F32, kind="ExternalOutput").ap()
    xT_d = nc.dram_tensor("xT_scr", [D, T], F32, kind="Internal").ap()
    xTv = xT_d.rearrange("(k p) t -> p k t", p=128)

    s = Sched()
    st = contextlib.ExitStack()
    uid = [0]
    stg_ctr = [0]

    def stage(name):
        stg_ctr[0] += 1
        if stop is not None and stg_ctr[0] == stop:
            print('STOP at stage', stop, name)
            raise _Stop()

    class Region:
        def __init__(self, name, start, size):
            self.name, self.start, self.size, self.cur = name, start, size, start

        def reset(self):
            self.cur = self.start

        def alloc(self, n, dt):
            sz = n * (4 if dt == F32 else 2)
            sz = (sz + 31) // 32 * 32
            assert self.cur + sz <= self.start + self.size, (self.name, self.cur - self.start, sz, self.size)
            uid[0] += 1
            t = nc.alloc_sbuf_tensor_at(f"{self.name}_{uid[0]}", [128, n], dt, offset=self.cur)
            self.cur += sz
            return t

    sizes = [("C", 20480), ("W", 36864), ("A", 32768), ("B", 32768), ("IN", 67584)]
    regs = {}
    cur = SBASE
    for nm, sz in sizes:
        regs[nm] = Region(nm, cur, sz)
        cur += sz
    regs["T"] = Region("T", cur, SEND - cur)
    RC, RW, RA, RB, RIN, RT = (regs[k] for k in ("C", "W", "A", "B", "IN", "T"))
    RF = Region("F", regs["W"].start, SEND - regs["W"].start)

    PS = [st.enter_context(nc.psum_tensor(f"ps{i}", [128, 1024], F32)) for i in range(4)]
    bPS = [Buf() for _ in range(8)]

    def bank(b, n=512, off=0):
        return PS[b // 2][:, (b % 2) * 512 + off:(b % 2) * 512 + off + n]

    def MM(out, lhsT, rhs, st_, sp_, R, W):
        s.add("pe", lambda e: e.matmul(out, lhsT=lhsT, rhs=rhs, start=st_, stop=sp_), R, W)

    def AC(out, in_, func, R, W, bias=None, scale=None, accum=None):
        kw = {}
        if bias is not None:
            kw["bias"] = bias
        if scale is not None:
            kw["scale"] = scale
        if accum is not None:
            kw["accum_out"] = accum
        s.add("act", lambda e: e.activation(out=out, in_=in_, func=func, **kw), R, W)

    def TT(eng, out, a, b, op, R, W):
        s.add(eng, lambda e: e.tensor_tensor(out=out, in0=a, in1=b, op=op), R, W)

    def TS(eng, out, a, s1, s2, op0, op1, R, W):
        if s2 is None:
            s.add(eng, lambda e: e.tensor_scalar(out=out, in0=a, scalar1=s1, scalar2=None, op0=op0), R, W)
        else:
            s.add(eng, lambda e: e.tensor_scalar(out=out, in0=a, scalar1=s1, scalar2=s2, op0=op0, op1=op1), R, W)

    def STT(eng, out, a, sc, b, op0, op1, R, W):
        s.add(eng, lambda e: e.scalar_tensor_tensor(out=out, in0=a, scalar=sc, in1=b, op0=op0, op1=op1), R, W)

    def CP(eng, out, in_, R, W, after=()):
        if eng == "act":
            s.add("act", lambda e: e.copy(out=out, in_=in_), R, W, after=after)
        else:
            s.add(eng, lambda e: e.tensor_copy(out=out, in_=in_), R, W, after=after)

    def MS(eng, ap, val, W):
        s.add(eng, lambda e: e.memset(ap, val), (), W)

    def RCP(out, in_, R, W):
        s.add("dve", lambda e: e.reciprocal(out=out, in_=in_), R, W)

    def DMA(out, in_, R, W):
        s.add("sp", lambda e: e.dma_start(out=out, in_=in_), R, W, dma=True)

    def v3(ap2, k):
        return ap2.rearrange("p (k c) -> p k c", k=k)

    bC = Buf()
    ident_f = RC.alloc(128, F32)
    ident_b = RC.alloc(128, BF16)
    ones_b = RC.alloc(128, BF16)
    tri = RC.alloc(512, F32)
    mask = RC.alloc(1024, F32)
    rope = RC.alloc(2048, F32)
    vecs = [RC.alloc(NV, F32) for _ in range(NL)]
    lnf = RC.alloc(16, F32)
    cvec = RC.alloc(32, F32)
    scT = RC.alloc(32, BF16)
    modTs = [RC.alloc(192, F32) for _ in range(2)]
    gsc = RC.alloc(64, F32)
    epsb = RC.alloc(1, F32)
    fb = RC.alloc(4, F32)
    for tile_, name in ((ident_f, "ident_f"), (ident_b, "ident_b"), (ones_b, "ones_b"), (tri, "tri"), (mask, "mask"), (lnf, "lnf"), (cvec, "cvec")):
        DMA(tile_[:], dr[name], [], [bC])
    DMA(rope[0:64, :], dr["rope"], [], [bC])
    for l in range(NL):
        DMA(vecs[l][:], dr["vecs"][l], [], [bC])
    MS("pool", epsb[:], EPS, [bC])
    AC(scT[:], cvec[:], AF.Silu, [bC], [bC])

    NB = 3
    stg = [RW.alloc(2048, F32) for _ in range(NB)]
    wbf = [RW.alloc(2048, BF16) for _ in range(NB)]
    bStg = [Buf() for _ in range(NB)]
    bWb = [[Buf(), Buf(), Buf()] for _ in range(NB)]
    wctr = [0, 0, 0]

    def split_cast(dst, sv, kc, bsrc, bdst3):
        if kc < 8:
            eng = "act" if wctr[2] % 2 == 0 else "dve"
            wctr[2] += 1
            CP(eng, dst, sv, [bsrc], bdst3)
            return lambda k: bdst3[0]
        ka = int(round(kc * 7 / 16.0))
        kb = int(round(kc * 13 / 16.0))
        prev = Sched.deps_of(bdst3)
        for p, (eng, a_, b_) in enumerate((("act", 0, ka), ("dve", ka, kb), ("pool", kb, kc))):
            if b_ > a_:
                CP(eng, dst[:, a_:b_, :], sv[:, a_:b_, :], [bsrc], [bdst3[p]], after=prev)
        return lambda k: bdst3[0 if k < ka else (1 if k < kb else 2)]

    def wload(src, kc, ncol, mode="cast", parts=128):
        n = kc * ncol
        assert n <= 2048
        if mode == "bf16":
            i = wctr[1] % NB
            wctr[1] += 1
            dst = v3(wbf[i][0:parts, 0:n], kc)
            DMA(dst, src, [], bWb[i])
            return dst, (lambda k, i=i: bWb[i][0])
        i = wctr[0] % NB
        wctr[0] += 1
        sv = v3(stg[i][0:parts, 0:n], kc)
        DMA(sv, src, [], [bStg[i]])
        if mode == "f32":
            return sv, (lambda k, i=i: bStg[i])
        j = wctr[1] % NB
        wctr[1] += 1
        dst = v3(wbf[j][0:parts, 0:n], kc)
        return dst, split_cast(dst, sv, kc, bStg[i], bWb[j])

    def make_ws2(RF, nstg=3):
        stg2 = [RF.alloc(4096, F32) for _ in range(nstg)]
        wb2 = [RF.alloc(4096, BF16) for _ in range(2)]
        bS2 = [Buf() for _ in range(nstg)]
        bW2 = [[Buf(), Buf(), Buf()] for _ in range(2)]
        c2 = [0, 0]

        def wload2(src, kc, ncol):
            n = kc * ncol
            assert n <= 4096
            i = c2[0] % nstg
            c2[0] += 1
            j = c2[1] % 2
            c2[1] += 1
            sv = v3(stg2[i][:, 0:n], kc)
            DMA(sv, src, [], [bS2[i]])
            dst = v3(wb2[j][:, 0:n], kc)
            return dst, split_cast(dst, sv, kc, bS2[i], bW2[j])
        return wload2

    def wsrc(w2d, k0, kc, c0, ncol):
        return w2d[k0 * 128:(k0 + kc) * 128, c0:c0 + ncol].rearrange("(k p) c -> p k c", p=128)

    try:
        bXd = [[Buf() for _ in range(16)] for _ in range(3)]
        xin_t = [RA.alloc(2048, F32) for _ in range(2)]
        xtr_t = [RB.alloc(2048, F32) for _ in range(2)]
        bxin = [Buf(), Buf()]
        bxtr = [Buf(), Buf()]
        for tc in range(12):
            xi, bi = xin_t[tc % 2], bxin[tc % 2]
            xo, bo = xtr_t[tc % 2], bxtr[tc % 2]
            DMA(xi[:], dr["x_in"][tc * 128:(tc + 1) * 128, :], [], [bi])
            for half in range(2):
                u = (tc * 2 + half) % 4
                for kk in range(8):
                    k = half * 8 + kk
                    MM(PS[u][:, kk * 128:(kk + 1) * 128], xi[:, k * 128:(k + 1) * 128], ident_f[:], True, True, [bi, bC], [bPS[2 * u + kk // 4]])
                CP("act" if half else "dve", xo[:, half * 1024:(half + 1) * 1024], PS[u][:, :], [bPS[2 * u], bPS[2 * u + 1]], [bo])
            DMA(xTv[:, :, tc * 128:(tc + 1) * 128], v3(xo[:], 16), [bo], bXd[tc // 4])
        s.barrier()
        stage('prepass')

        def rms_rstd(sq_views, nk, width, denom, psb, out_rstd, Rsq, Wout):
            for k in range(nk):
                MM(bank(psb, width), ones_b[:], sq_views(k), k == 0, k == nk - 1, [bC] + Rsq, [bPS[psb]])
            AC(out_rstd, bank(psb, width), AF.Sqrt, [bPS[psb], bC], Wout, bias=epsb[:], scale=1.0 / denom)
            RCP(out_rstd, out_rstd, Wout, Wout)

        for l in range(nlayers):
            vl = vecs[l]
            modT = modTs[l % 2]
            scv = v3(scT[:], 16)
            if l == 0:
                pv = PS[0][:, 0:192].rearrange("p (j m) -> p m j", j=2)
                RF.reset()
                wl2 = make_ws2(RF)
                for mp in range(48):
                    wt, bk = wl2(wsrc(dr["w_mod"][l], 0, 16, mp * 256, 256), 16, 256)
                    for c in range(2):
                        m = 2 * mp + c
                        for k in range(16):
                            MM(pv[:, m, :], wt[:, k, c * 128:(c + 1) * 128], scv[:, k, :], k == 0, k == 15, [bk(k), bC], [bPS[0]])
                for j in range(2):
                    TT("dve", modT[:, j * 96:(j + 1) * 96], PS[0][:, j * 96:(j + 1) * 96], vl[:, V_BMOD:V_BMOD + 96], ALU.add, [bPS[0], bC], [bC])
            for j in range(2):
                STT("dve", gsc[:, (j * 2) * 16:(j * 2) * 16 + 16], modT[:, j * 96 + 16:j * 96 + 32], 1.0, vl[:, V_LNMIX:V_LNMIX + 16], ALU.add, ALU.mult, [bC], [bC])
                STT("dve", gsc[:, (j * 2 + 1) * 16:(j * 2 + 1) * 16 + 16], modT[:, j * 96 + 64:j * 96 + 80], 1.0, vl[:, V_LNFFN:V_LNFFN + 16], ALU.add, ALU.mult, [bC], [bC])
            TT("dve", fb[0:64, 0:1], vl[0:64, V_FR0:V_FR0 + 1], vl[0:64, V_FB1:V_FB1 + 1], ALU.mult, [bC], [bC])
            TT("dve", fb[0:64, 1:2], vl[0:64, V_FR1:V_FR1 + 1], vl[0:64, V_FB2:V_FB2 + 1], ALU.mult, [bC], [bC])
            s.barrier()
            stage('mod')

            for g in range(2):
                j = g
                L = LP if g == 0 else LS
                nseq = 2 if g == 0 else 1
                nblk = 1 if g == 0 else 2
                Ltot = nseq * L
                tok0 = 0 if g == 0 else TP
                Lk = L if g == 0 else L + PAST
                nf = L // 128
                for r_ in (RA, RB, RIN, RT):
                    r_.reset()
                hT = RA.alloc(16 * Ltot, BF16)
                bH = [Buf() for _ in range(nblk)]
                xblk = RB.alloc(16 * 512, F32)
                bxb = Buf()
                sqt = RT.alloc(16 * 512, BF16)
                bsq = Buf()
                rstd = RT.alloc(512, F32)
                brs = Buf()
                hv = v3(hT[:], 16)
                ntmp = RIN.alloc(2 * 512, F32)
                bnt = [Buf(), Buf()]
                for b in range(nblk):
                    blk = (tok0 // 512) + b
                    DMA(v3(xblk[:], 16), xTv[:, :, blk * 512:(blk + 1) * 512], bXd[blk], [bxb])
                    AC(sqt[:], xblk[:], AF.Square, [bxb], [bsq])
                    rms_rstd(lambda k: sqt[:, k * 512:(k + 1) * 512], 16, 512, float(D), 7, rstd[:], [bsq], [brs])
                    for k in range(16):
                        nt_, bn_ = ntmp[:, (k % 2) * 512:(k % 2) * 512 + 512], bnt[k % 2]
                        TT("dve", nt_, xblk[:, k * 512:(k + 1) * 512], rstd[:], ALU.mult, [bxb, brs], [bn_])
                        AC(hv[:, k, b * 512:(b + 1) * 512], nt_, AF.Identity, [bn_, bC], [bH[b]],
                           bias=modT[:, j * 96 + k:j * 96 + k + 1], scale=gsc[:, (j * 2) * 16 + k:(j * 2) * 16 + k + 1])
                s.barrier()
                stage('stepA')
                RB.reset()
                RIN.reset()
                RT.reset()
                hyT = RIN.alloc(12 * Ltot, BF16)
                qT = RIN.alloc(2 * Ltot, BF16)
                kT = RIN.alloc(2 * Ltot, BF16)
                vT = RIN.alloc(4 * Ltot, BF16)
                sgT = RIN.alloc(4 * Ltot, BF16)
                afT = RIN.alloc(Ltot, BF16)
                abT = RIN.alloc(Ltot, BF16)
                cqnT = RIN.alloc(3 * Ltot, BF16)
                keysT = RIN.alloc(2 * nseq * Lk, BF16)
                krT = RIN.alloc(nseq * Lk, BF16)
                bHY = [Buf() for _ in range(12)]
                bQ, bK, bV, bSG, bAF, bAB, bCQN, bKEYS, bKR = (Buf() for _ in range(9))
                hyv = v3(hyT[:], 12)
                qv, kv_, vv, sgv = v3(qT[:], 2), v3(kT[:], 2), v3(vT[:], 4), v3(sgT[:], 4)
                cqnv = v3(cqnT[:], 3)
                keysv = v3(keysT[:], 2)
                ctmp = RT.alloc(Ltot, F32)
                bct = Buf()
                cqf = RT.alloc(3 * Ltot, F32)
                bcqf = Buf()
                sq3 = RT.alloc(3 * 512, BF16)
                bsq3 = Buf()
                rs2 = RT.alloc(512, F32)
                brs2 = Buf()
                cqfv = v3(cqf[:], 3)
                MS("pool", afT[0:32, :], 1.0, [bAF])
                MS("pool", abT[0:32, :], 1.0, [bAB])
                ckvf = RB.alloc(2 * Ltot, F32)
                bckvf = Buf()
                ckvfv = v3(ckvf[:], 2)
                krf = RB.alloc(Ltot, F32)
                bkrf = Buf()
                otr = RB.alloc(2 * 384, F32)
                krtmp = RB.alloc(Ltot, F32)
                bkrt = Buf()
                krtmp2 = RB.alloc(Ltot, F32)
                bkrt2 = Buf()
                botr = [Buf(), Buf()]

                chunks = []
                for i in range(12):
                    chunks.append(("hy", C_HY + i * 128, 128, i))
                for i in range(2):
                    chunks.append(("q", C_Q + i * 128, 128, i))
                for i in range(2):
                    chunks.append(("k", C_K + i * 128, 128, i))
                for i in range(4):
                    chunks.append(("v", C_V + i * 128, 128, i))
                for i in range(4):
                    chunks.append(("g", C_G + i * 128, 128, i))
                chunks.append(("af", C_AF, 16, 0))
                chunks.append(("ab", C_AB, 16, 0))
                for i in range(3):
                    chunks.append(("cq", C_CQ + i * 128, 128, i))
                for i in range(2):
                    chunks.append(("ckv", C_CKV + i * 128, 128, i))
                chunks.append(("kr", C_KR, 64, 0))
                if g == 1:
                    chunks.append(("krsw", C_KRSW, 64, 0))

                import os as _os
                for ci, (kind, c0, ncol, idx) in enumerate(chunks):
                    if g == 1 and ci >= int(_os.environ.get('KCH', '999')):
                        break
                    wt, bw = wload(wsrc(dr["w_in"][l], 0, 16, c0, ncol), 16, ncol)
                    u = ci % 4
                    for b in range(nblk):
                        for k in range(16):
                            MM(PS[u][0:ncol, b * 512:(b + 1) * 512], wt[:, k, :], hv[:, k, b * 512:(b + 1) * 512], k == 0, k == 15, [bw(k), bH[b]], [bPS[2 * u + b]])
                    pb = [bPS[2 * u + b] for b in range(nblk)]
                    psu = PS[u]
                    if kind == "hy":
                        cw = lambda tap: vl[:, V_CW + tap * 12 + idx:V_CW + tap * 12 + idx + 1]
                        for sq_ in range(nseq):
                            a, e_ = sq_ * L, (sq_ + 1) * L
                            AC(ctmp[:, a:e_], psu[:, a:e_], AF.Identity, pb + [bC], [bct], bias=vl[:, V_CB + idx:V_CB + idx + 1], scale=cw(1))
                            STT("dve", ctmp[:, a + 1:e_], psu[:, a:e_ - 1], cw(0), ctmp[:, a + 1:e_], ALU.mult, ALU.add, pb + [bC, bct], [bct])
                            STT("dve", hyv[:, idx, a:e_ - 1], psu[:, a + 1:e_], cw(2), ctmp[:, a:e_ - 1], ALU.mult, ALU.add, pb + [bC, bct], [bHY[idx]])
                            CP("pool", hyv[:, idx, e_ - 1:e_], ctmp[:, e_ - 1:e_], [bct], [bHY[idx]])
                    elif kind == "q":
                        AC(qv[:, idx, :], psu[:, 0:Ltot], AF.Identity, pb, [bQ], scale=0.125)
                    elif kind == "k":
                        CP("dve", kv_[:, idx, :], psu[:, 0:Ltot], pb, [bK])
                    elif kind == "v":
                        CP("dve", vv[:, idx, :], psu[:, 0:Ltot], pb, [bV])
                    elif kind == "g":
                        AC(sgv[:, idx, :], psu[:, 0:Ltot], AF.Silu, pb, [bSG])
                    elif kind == "af":
                        CP("dve", afT[0:16, :], psu[0:16, 0:Ltot], pb, [bAF])
                    elif kind == "ab":
                        CP("dve", abT[0:16, :], psu[0:16, 0:Ltot], pb, [bAB])
                    elif kind == "cq":
                        CP("dve", cqfv[:, idx, :], psu[:, 0:Ltot], pb, [bcqf])
                        if idx == 2:
                            for b in range(nblk):
                                sl = slice(b * 512, (b + 1) * 512)
                                for k in range(3):
                                    AC(sq3[:, k * 512:(k + 1) * 512], cqfv[:, k, sl], AF.Square, [bcqf], [bsq3])
                                rms_rstd(lambda k: sq3[:, k * 512:(k + 1) * 512], 3, 512, 384.0, 7, rs2[:], [bsq3], [brs2])
                                for k in range(3):
                                    STT("dve", cqnv[:, k, sl], cqfv[:, k, sl], vl[:, V_QN + k:V_QN + k + 1], rs2[:], ALU.mult, ALU.mult, [bcqf, brs2, bC], [bCQN])
                    elif kind == "ckv":
                        CP("dve", ckvfv[:, idx, :], psu[:, 0:Ltot], pb, [bckvf])
                        if idx == 1:
                            for b in range(nblk):
                                sl = slice(b * 512, (b + 1) * 512)
                                for k in range(2):
                                    AC(sq3[:, k * 512:(k + 1) * 512], ckvfv[:, k, sl], AF.Square, [bckvf], [bsq3])
                                rms_rstd(lambda k: sq3[:, k * 512:(k + 1) * 512], 2, 512, 256.0, 7, rs2[:], [bsq3], [brs2])
                                for k in range(2):
                                    STT("dve", ckvfv[:, k, sl], ckvfv[:, k, sl], vl[:, V_KVN + k:V_KVN + k + 1], rs2[:], ALU.mult, ALU.mult, [bckvf, brs2, bC], [bckvf])
                            for sq_ in range(nseq):
                                for k in range(2):
                                    CP("pool", keysv[:, k, sq_ * Lk:sq_ * Lk + L], ckvfv[:, k, sq_ * L:(sq_ + 1) * L], [bckvf], [bKEYS])
                    elif kind == "kr":
                        if g == 0:
                            CP("dve", krf[0:64, 0:Ltot], psu[0:64, 0:Ltot], pb, [bkrf])
                            CP("pool", krT[0:64, 0:Ltot], krf[0:64, 0:Ltot], [bkrf], [bKR])
                        else:
                            TT("dve", krtmp[0:64, :], psu[0:64, 0:Ltot], rope[0:64, 0:LS], ALU.mult, pb + [bC], [bkrt])
                    elif kind == "krsw":
                        TT("dve", krtmp2[0:64, :], psu[0:64, 0:Ltot], rope[0:64, LS:2 * LS], ALU.mult, pb + [bC], [bkrt2])
                        TT("pool", krT[0:64, 0:L], krtmp[0:64, :], krtmp2[0:64, :], ALU.add, [bkrt, bkrt2], [bKR])
                if g == 0:
                    for sq_ in range(2):
                        for tcn in range(2):
                            ot, bo = otr[:, ((sq_ * 2 + tcn) % 2) * 384:((sq_ * 2 + tcn) % 2) * 384 + 384], botr[(sq_ * 2 + tcn) % 2]
                            ts_ = slice(sq_ * L + tcn * 128, sq_ * L + (tcn + 1) * 128)
                            for k in range(2):
                                MM(bank(6, 128, k * 128), ckvfv[:, k, ts_], ident_f[:], True, True, [bckvf, bC], [bPS[6]])
                            MM(bank(6, 64, 256), krf[0:64, ts_], ident_f[0:64, 0:64], True, True, [bkrf, bC], [bPS[6]])
                            CP("act", ot[:, 0:320], bank(6, 320), [bPS[6]], [bo])
                            DMA(dr["o_ckv"][sq_, l, tcn * 128:(tcn + 1) * 128, :], ot[:, 0:256], [bo], [])
                            DMA(dr["o_kr"][sq_, l, tcn * 128:(tcn + 1) * 128, :], ot[:, 256:320], [bo], [])
                elif int(_os.environ.get('KCACHE', '9')) > 0:
                    cst = RB.alloc(2 * 384, F32)
                    bcst = [Buf(), Buf()]
                    kc2 = int(_os.environ.get('KCACHE', '9'))
                    for jc in range(2):
                        if kc2 < 9:
                            cs = cst[:, jc * 384:(jc + 1) * 384]
                            if kc2 >= 1:
                                MS("pool", cs[:, 320:384], 0.0, [bcst[jc]])
                                DMA(cs[:, 0:256], dr["ckv_c"][l, jc * 128:(jc + 1) * 128, :], [], [bcst[jc]])
                            if kc2 >= 2:
                                DMA(cs[:, 256:320], dr["kr_c"][l, jc * 128:(jc + 1) * 128, :], [], [bcst[jc]])
                            if kc2 >= 3:
                                for k in range(3):
                                    MM(bank(6, 128, k * 128), cs[:, k * 128:(k + 1) * 128], ident_f[:], True, True, [bcst[jc], bC], [bPS[6]])
                            if kc2 >= 4:
                                for k in range(2):
                                    CP("act", keysv[:, k, L + jc * 128:L + (jc + 1) * 128], bank(6, 128, k * 128), [bPS[6]], [bKEYS])
                            continue
                        cs = cst[:, jc * 384:(jc + 1) * 384]
                        MS("pool", cs[:, 320:384], 0.0, [bcst[jc]])
                        DMA(cs[:, 0:256], dr["ckv_c"][l, jc * 128:(jc + 1) * 128, :], [], [bcst[jc]])
                        DMA(cs[:, 256:320], dr["kr_c"][l, jc * 128:(jc + 1) * 128, :], [], [bcst[jc]])
                        for k in range(3):
                            MM(bank(6, 128, k * 128), cs[:, k * 128:(k + 1) * 128], ident_f[:], True, True, [bcst[jc], bC], [bPS[6]])
                        for k in range(2):
                            CP("act", keysv[:, k, L + jc * 128:L + (jc + 1) * 128], bank(6, 128, k * 128), [bPS[6]], [bKEYS])
                        CP("act", krT[0:64, L + jc * 128:L + (jc + 1) * 128], bank(6, 128, 256)[0:64, :], [bPS[6]], [bKR])
                s.barrier()
                stage('stepB')
                RA.reset()
                RB.reset()
                RT.reset()
                mixT = RB.alloc(16 * Ltot, BF16)
                mixv = v3(mixT[:], 16)
                bMIX = [Buf() for _ in range(16)]
                if dbg and ("hy_%d_%d" % (l, g)) in dbg_out:
                    for q4 in range(3):
                        dt_ = RT.alloc(4 * Ltot, F32)
                        bd = Buf()
                        CP("dve", v3(dt_[:], 4), hyv[:, q4 * 4:(q4 + 1) * 4, :], bHY, [bd])
                        DMA(dbg_out["hy_%d_%d" % (l, g)][q4 * 512:(q4 + 1) * 512, :].rearrange("(k p) t -> p k t", p=128), v3(dt_[:], 4), [bd], [])
                        s.barrier()
                        RT.reset()

                scale = 192.0 ** -0.5
                qn = [RA.alloc(Ltot, BF16) for _ in range(2)]
                qr = [RA.alloc(Ltot, BF16) for _ in range(2)]
                kn = [RA.alloc(nseq * Lk, BF16) for _ in range(2)]
                vh = [RA.alloc(nseq * Lk, BF16) for _ in range(2)]
                bqn, bqr, bkn, bvh = ([Buf(), Buf()] for _ in range(4))
                pt = [RT.alloc(512, BF16) for _ in range(3)]
                bpt = [Buf() for _ in range(3)]
                rec = [RT.alloc(512, F32) for _ in range(2)]
                brec = [Buf(), Buf()]
                rt1 = RT.alloc(512, F32)
                rt2 = RT.alloc(512, F32)
                brt1, brt2 = Buf(), Buf()
                nkc = Lk // 128
                qblocks = []
                for sq_ in range(nseq):
                    for qb in range(max(1, L // 512)):
                        qn_ = min(L, 512)
                        qblocks.append((sq_, sq_ * L + qb * qn_, qn_))
                ptc = 0
                acc = 0
                pjc = [0]

                def pjn():
                    pjc[0] += 1
                    return 5 + pjc[0] % 3

                def mla_pro(h):
                    hb = h % 2
                    hb = h % 2
                    wq, bwq = wload(wsrc(dr["w_uq"][l], 0, 3, h * 192, 192), 3, 192)
                    if g == 1:
                        wqs, bwqs = wload(wsrc(dr["w_uq"][l], 0, 3, 1536 + h * 64, 64), 3, 64)
                    wkv, bwkv = wload(wsrc(dr["w_ukv"][l], 0, 2, h * 256, 256), 2, 256)
                    for b in range(nblk):
                        sl = slice(b * 512, (b + 1) * 512)
                        pj = pjn()
                        for k in range(3):
                            MM(bank(pj), wq[:, k, 0:128], cqnv[:, k, sl], k == 0, k == 2, [bwq(0), bCQN], [bPS[pj]])
                        CP("act", qn[hb][:, sl], bank(pj), [bPS[pj]], [bqn[hb]])
                        pj = pjn()
                        for k in range(3):
                            MM(bank(pj)[0:64, :], wq[:, k, 128:192], cqnv[:, k, sl], k == 0, k == 2, [bwq(0), bCQN], [bPS[pj]])
                        if g == 0:
                            CP("dve", qr[hb][0:64, sl], bank(pj)[0:64, :], [bPS[pj]], [bqr[hb]])
                        else:
                            TT("dve", rt1[0:64, :], bank(pj)[0:64, :], rope[0:64, b * 512:(b + 1) * 512], ALU.mult, [bPS[pj], bC], [brt1])
                            pj = pjn()
                            for k in range(3):
                                MM(bank(pj)[0:64, :], wqs[:, k, :], cqnv[:, k, sl], k == 0, k == 2, [bwqs(0), bCQN], [bPS[pj]])
                            TT("dve", rt2[0:64, :], bank(pj)[0:64, :], rope[0:64, LS + b * 512:LS + (b + 1) * 512], ALU.mult, [bPS[pj], bC], [brt2])
                            TT("pool", qr[hb][0:64, sl], rt1[0:64, :], rt2[0:64, :], ALU.add, [brt1, brt2], [bqr[hb]])
                    ktot = nseq * Lk
                    for c0 in range(0, ktot, 512):
                        n_ = min(512, ktot - c0)
                        pj = pjn()
                        for k in range(2):
                            MM(bank(pj, n_), wkv[:, k, 0:128], keysv[:, k, c0:c0 + n_], k == 0, k == 1, [bwkv(0), bKEYS], [bPS[pj]])
                        CP("act", kn[hb][:, c0:c0 + n_], bank(pj, n_), [bPS[pj]], [bkn[hb]])
                    for c0 in range(0, ktot, 512):
                        n_ = min(512, ktot - c0)
                        pj = pjn()
                        for jj in range(n_ // 128):
                            for k in range(2):
                                MM(bank(pj, 128, jj * 128), keysv[:, k, c0 + jj * 128:c0 + (jj + 1) * 128], wkv[:, k, 128:256], k == 0, k == 1, [bwkv(0), bKEYS], [bPS[pj]])
                        CP("dve", vh[hb][:, c0:c0 + n_], bank(pj, n_), [bPS[pj]], [bvh[hb]])

                def mla_att(h):
                    nonlocal ptc, acc
                    hb = h % 2
                    for (sq_, q0, qn_) in qblocks:
                        ob, db = 3, 4
                        acc += 1
                        def emit_sc(jk_, ptc0=ptc, sq_=sq_, q0=q0, qn_=qn_, hb=hb):
                            sb_ = (ptc0 + jk_) % 3
                            kc0 = sq_ * Lk + jk_ * 128
                            MM(bank(sb_, qn_), kn[hb][:, kc0:kc0 + 128], qn[hb][:, q0:q0 + qn_], True, False, [bkn[hb], bqn[hb]], [bPS[sb_]])
                            MM(bank(sb_, qn_), krT[0:64, kc0:kc0 + 128], qr[hb][0:64, q0:q0 + qn_], False, True, [bKR, bqr[hb]], [bPS[sb_]])
                            AC(pt[sb_][:, 0:qn_], bank(sb_, qn_), AF.Exp, [bPS[sb_]], [bpt[sb_]], scale=scale)
                        for jk in range(min(2, nkc)):
                            emit_sc(jk)
                        for jk in range(nkc):
                            sb_ = (ptc + jk) % 3
                            kc0 = sq_ * Lk + jk * 128
                            MM(bank(ob, qn_), vh[hb][:, kc0:kc0 + 128], pt[sb_][:, 0:qn_], jk == 0, jk == nkc - 1, [bvh[hb], bpt[sb_]], [bPS[ob]])
                            MM(bank(db, qn_), ones_b[:], pt[sb_][:, 0:qn_], jk == 0, jk == nkc - 1, [bC, bpt[sb_]], [bPS[db]])
                            if jk + 2 < nkc:
                                emit_sc(jk + 2)
                        ptc += nkc
                        rc, brc = rec[acc % 2], brec[acc % 2]
                        RCP(rc[:, 0:qn_], bank(db, qn_), [bPS[db]], [brc])
                        TT("dve", mixv[:, 8 + h, q0:q0 + qn_], bank(ob, qn_), rc[:, 0:qn_], ALU.mult, [bPS[ob], brc], [bMIX[8 + h]])

                mla_pro(0)
                for h in range(8):
                    if h + 1 < 8:
                        mla_pro(h + 1)
                    mla_att(h)
                s.barrier()
                stage('mla')
                RA.reset()
                RT.reset()

                if dbg and ("glain_%d_%d" % (l, g)) in dbg_out:
                    for nm_, t_, nk_, off_ in (("q", qT, 2, 0), ("k", kT, 2, 256), ("v", vT, 4, 512), ("sg", sgT, 4, 1024)):
                        RT.reset()
                        dt_ = RT.alloc(4 * Ltot, F32)
                        bd = Buf()
                        CP("dve", dt_[:, 0:nk_ * Ltot], t_[:], [bQ, bK, bV, bSG], [bd])
                        DMA(dbg_out["glain_%d_%d" % (l, g)][off_:off_ + nk_ * 128, :].rearrange("(k p) t -> p k t", p=128), v3(dt_[:, 0:nk_ * Ltot], nk_), [bd], [])
                        s.barrier()
                    RT.reset()
                    dt_ = RT.alloc(2 * Ltot, F32)
                    bd = Buf()
                    CP("dve", dt_[0:32, 0:Ltot], afT[0:32, :], [bAF], [bd])
                    CP("dve", dt_[0:32, Ltot:2 * Ltot], abT[0:32, :], [bAB], [bd])
                    DMA(dbg_out["glain_%d_%d" % (l, g)][1536:1568, :], dt_[0:32, 0:Ltot], [bd], [])
                    DMA(dbg_out["glain_%d_%d" % (l, g)][1568:1600, :], dt_[0:32, Ltot:2 * Ltot], [bd], [])
                    s.barrier()
                    RT.reset()
                oacc = RA.alloc(4 * Ltot, F32)
                oav = v3(oacc[:], 4)
                bOA = Buf()
                NT = 2
                ltok = [RT.alloc(256, F32) for _ in range(NT)]
                ep = [RT.alloc(256, F32) for _ in range(NT)]
                em = [RT.alloc(256, F32) for _ in range(NT)]
                er = [RT.alloc(256, F32) for _ in range(NT)]
                qd = [RT.alloc(512, BF16) for _ in range(NT)]
                ki = [RT.alloc(256, BF16) for _ in range(NT)]
                ke = [RT.alloc(256, BF16) for _ in range(NT)]
                vt = [RT.alloc(512, BF16) for _ in range(NT)]
                at = [RT.alloc(512, BF16) for _ in range(NT)]
                bl, bep, bem, ber, bqd, bki, bke, bvt, bat = ([Buf() for _ in range(NT)] for _ in range(9))
                for i0 in range(NT):
                    MS("pool", qd[i0][:], 0.0, [bqd[i0]])
                Sst = RT.alloc(256, F32)
                Sbf = RT.alloc(256, BF16)
                bSst, bSbf = Buf(), Buf()
                waug = RA.alloc(512, F32)
                waub = RA.alloc(512, BF16)
                bwa = Buf()
                for dd in range(2):
                    DMA(waug[0:17, dd * 256:(dd + 1) * 256], dr["wa_aug"][l, dd], [], [bwa])
                CP("dve", waub[0:17, :], waug[0:17, :], [bwa], [bwa])
                it = 0
                for sq_ in range(nseq):
                    for dd in range(2):
                        if g == 0:
                            MS("pool", Sst[:], 0.0, [bSst])
                        else:
                            DMA(v3(Sst[:], 2), dr["sf_c" if dd == 0 else "sb_c"][l].rearrange("(hp p) v -> p hp v", p=128), [], [bSst])
                        CP("pool", Sbf[:], Sst[:], [bSst], [bSbf])
                        aT, bA_ = (afT, bAF) if dd == 0 else (abT, bAB)
                        order = range(nf) if dd == 0 else range(nf - 1, -1, -1)
                        for n_ in order:
                            i_ = it % NT
                            it += 1
                            if knob is not None and it > knob[0]:
                                raise _Stop()
                            ts_ = slice(sq_ * L + n_ * 128, sq_ * L + (n_ + 1) * 128)

                            def kc_(n):
                                if knob is not None and it == knob[0] and n >= knob[1]:
                                    raise _Stop()
                            MM(bank(0, 256), aT[0:17, ts_], waub[0:17, dd * 256:(dd + 1) * 256], True, True, [bA_, bwa], [bPS[0]])
                            AC(ltok[i_][:], bank(0, 256), AF.Exp, [bPS[0]], [bl[i_]], scale=-1.0)
                            AC(ltok[i_][:], ltok[i_][:], AF.Ln, [bl[i_]], [bl[i_]], bias=1.0)
                            kc_(1)
                            triI = tri[:, (2 * dd) * 128:(2 * dd + 1) * 128]
                            triE = tri[:, (2 * dd + 1) * 128:(2 * dd + 2) * 128]
                            for hp in range(2):
                                MM(bank(1, 128, hp * 128), ltok[i_][:, hp * 128:(hp + 1) * 128], triI, True, True, [bl[i_], bC], [bPS[1]])
                            MM(bank(1, 256, 256), triE, ltok[i_][:], True, True, [bl[i_], bC], [bPS[1]])
                            kc_(2)
                            AC(ep[i_][:], bank(1, 256), AF.Exp, [bPS[1]], [bep[i_]])
                            AC(em[i_][:], bank(1, 256), AF.Exp, [bPS[1]], [bem[i_]], scale=-1.0)
                            AC(er[i_][:], bank(1, 256, 256), AF.Exp, [bPS[1]], [ber[i_]])
                            for hp in range(2):
                                for h2 in range(2):
                                    r0 = h2 * 64
                                    TT("dve", qd[i_][r0:r0 + 64, (hp * 2 + h2) * 128:(hp * 2 + h2 + 1) * 128], qv[r0:r0 + 64, hp, ts_], ep[i_][r0:r0 + 64, hp * 128:(hp + 1) * 128], ALU.mult, [bQ, bep[i_], bqd[i_]], [bqd[i_]])
                                TT("dve", ki[i_][:, hp * 128:(hp + 1) * 128], kv_[:, hp, ts_], em[i_][:, hp * 128:(hp + 1) * 128], ALU.mult, [bK, bem[i_]], [bki[i_]])
                            kc_(3)
                            for hp in range(2):
                                MM(bank(2, 128, hp * 128), kv_[:, hp, ts_], ident_b[:], True, True, [bK, bC], [bPS[2]])
                            TT("dve", ke[i_][:], bank(2, 256), er[i_][:], ALU.mult, [bPS[2], ber[i_]], [bke[i_]])
                            for hh in range(4):
                                MM(bank(3, 128, hh * 128), vv[:, hh, ts_], ident_b[:], True, True, [bV, bC], [bPS[3]])
                            CP("act", vt[i_][:], bank(3), [bPS[3]], [bvt[i_]])
                            kc_(4)
                            for hp in range(2):
                                MM(bank(4, 256, hp * 256), ki[i_][:, hp * 128:(hp + 1) * 128], qd[i_][:, hp * 256:(hp + 1) * 256], True, True, [bki[i_], bqd[i_]], [bPS[4]])
                            TT("dve", at[i_][:], bank(4), mask[:, dd * 512:(dd + 1) * 512], ALU.mult, [bPS[4], bC], [bat[i_]])
                            for hh in range(4):
                                hp, h2 = hh // 2, hh % 2
                                r0 = h2 * 64
                                MM(bank(5, 128, hh * 128), vt[i_][:, hh * 128:(hh + 1) * 128], at[i_][:, hh * 128:(hh + 1) * 128], True, False, [bvt[i_], bat[i_]], [bPS[5]])
                                MM(bank(5, 128, hh * 128), Sbf[:, hp * 128:(hp + 1) * 128], qd[i_][:, hh * 128:(hh + 1) * 128], False, True, [bSbf, bqd[i_]], [bPS[5]])
                            kc_(5)
                            o4 = bank(5).rearrange("p (h c) -> p h c", h=4)
                            if dd == 0:
                                CP("act", oav[:, :, ts_], o4, [bPS[5]], [bOA])
                            else:
                                TT("dve", oav[:, :, ts_], o4, oav[:, :, ts_], ALU.add, [bPS[5], bOA], [bOA])
                            kc_(6)
                            for hp in range(2):
                                MM(bank(6 + hp, 256), ke[i_][:, hp * 128:(hp + 1) * 128], vt[i_][:, hp * 256:(hp + 1) * 256], True, True, [bke[i_], bvt[i_]], [bPS[6 + hp]])
                            kc_(7)
                            dcol = 127 if dd == 0 else 0
                            for hp in range(2):
                                for h2 in range(2):
                                    r0 = h2 * 64
                                    STT("dve", Sst[r0:r0 + 64, hp * 128:(hp + 1) * 128], Sst[r0:r0 + 64, hp * 128:(hp + 1) * 128],
                                        ep[i_][r0:r0 + 64, hp * 128 + dcol:hp * 128 + dcol + 1], bank(6 + hp, 128, h2 * 128)[r0:r0 + 64, :],
                                        ALU.mult, ALU.add, [bSst, bep[i_], bPS[6 + hp]], [bSst])
                            CP("pool", Sbf[:], Sst[:], [bSst], [bSbf])
                            if _os.environ.get('GLA_BAR'):
                                s.barrier()
                        if g == 0:
                            DMA(dr["o_sf" if dd == 0 else "o_sb"][sq_, l].rearrange("(hp p) v -> p hp v", p=128), v3(Sst[:], 2), [bSst], [])
                if dbg and ("oacc_%d_%d" % (l, g)) in dbg_out:
                    s.barrier()
                    DMA(dbg_out["oacc_%d_%d" % (l, g)].rearrange("(k p) t -> p k t", p=128), oav, [bOA], [])
                    s.barrier()
                gsq = RA.alloc(512, BF16)
                bgsq = Buf()
                grs = RA.alloc(512, F32)
                bgrs = Buf()
                gtm = RA.alloc(512, F32)
                bgtm = Buf()
                for hh in range(4):
                    for b in range(nblk):
                        sl = slice(b * 512, (b + 1) * 512)
                        AC(gsq[:], oav[:, hh, sl], AF.Square, [bOA], [bgsq])
                        rms_rstd(lambda k: gsq[:], 1, 512, 128.0, 0, grs[:], [bgsq], [bgrs])
                        STT("dve", gtm[:], oav[:, hh, sl], vl[:, V_GN:V_GN + 1], grs[:], ALU.mult, ALU.mult, [bOA, bC, bgrs], [bgtm])
                        TT("pool", mixv[:, 4 + hh, sl], gtm[:], sgv[:, hh, sl], ALU.mult, [bgtm, bSG], [bMIX[4 + hh]])
                s.barrier()
                stage('gla')
                RA.reset()
                RT.reset()

                RIN.reset()
                _ = RIN.alloc(12 * Ltot, BF16)
                ksd = RIN.alloc(2 * nf * 512, BF16)
                bKSD = Buf()
                ksdv = ksd[:].rearrange("p (a n c) -> p a n c", a=2, n=nf)
                h1T = RIN.alloc(L, F32)
                h2T = RIN.alloc(L, F32)
                bh1, bh2 = Buf(), Buf()
                zemb = RIN.alloc(L, F32)
                fw12 = RIN.alloc(128, F32)
                bfz = Buf()
                arg = RIN.alloc(512, F32)
                arg2 = RIN.alloc(512, F32)
                barg, barg2 = Buf(), Buf()
                dec_t = RIN.alloc(512, F32)
                bdec = Buf()
                bsb_t = RIN.alloc(512, F32)
                bbsb = Buf()
                sm_t = RIN.alloc(512, F32)
                df_t = RIN.alloc(512, F32)
                bsm, bdf = Buf(), Buf()
                DMA(zemb[0:33, :], dr["zemb%d" % L], [], [bfz])
                DMA(fw12[0:33, 0:64], dr["fw1"][l], [], [bfz])
                DMA(fw12[0:64, 64:128], dr["fw2"][l], [], [bfz])
                ztok = RA.alloc(nseq * nf * 512, BF16)
                ztv = ztok[:].rearrange("p (s n c) -> p s n c", s=nseq, n=nf)
                bZT = Buf()
                p1d = RA.alloc(2 * nseq * nf * 512, BF16)
                p1v = p1d[:].rearrange("p (a s n c) -> p a s n c", a=2, s=nseq, n=nf)
                bP1 = Buf()
                kab = [RT.alloc(1024, F32) for _ in range(2)]
                bkab = [Buf(), Buf()]
                tt_ = [RT.alloc(512, F32) for _ in range(4)]
                btt = [Buf() for _ in range(4)]
                utmp = RT.alloc(512, F32)
                butmp = Buf()
                delta = RT.alloc(512, F32)
                negt = RT.alloc(8, F32)
                DMA(delta[:], dr["delta"], [], [bfz])
                DMA(negt[:, 0:nf], dr["negt%d" % L], [], [bfz])

                def sin_layer(pre_ps, n_, fcol, fbcol, out_ap, Rps, Wout):
                    TS("dve", arg[0:64, 0:n_], pre_ps, vl[0:64, fcol:fcol + 1], fb[0:64, fbcol:fbcol + 1], ALU.mult, ALU.add, Rps + [bC], [barg])
                    TS("dve", arg2[0:64, 0:n_], arg[0:64, 0:n_], 1.0 / TWO_PI, MAGIC, ALU.mult, ALU.add, [barg], [barg2])
                    TS("dve", arg2[0:64, 0:n_], arg2[0:64, 0:n_], MAGIC, -TWO_PI, ALU.subtract, ALU.mult, [barg2], [barg2])
                    TT("dve", arg[0:64, 0:n_], arg[0:64, 0:n_], arg2[0:64, 0:n_], ALU.add, [barg, barg2], [barg])
                    AC(out_ap, arg[0:64, 0:n_], AF.Sin, [barg], Wout)

                for c0 in range(0, L, 512):
                    n_ = min(512, L - c0)
                    MM(PS[0][0:64, 0:n_], fw12[0:33, 0:64], zemb[0:33, c0:c0 + n_], True, True, [bfz], [bPS[0]])
                    sin_layer(PS[0][0:64, 0:n_], n_, V_FR0, 0, h1T[0:64, c0:c0 + n_], [bPS[0]], [bh1])
                for c0 in range(0, L, 512):
                    n_ = min(512, L - c0)
                    MM(PS[0][0:64, 0:n_], fw12[0:64, 64:128], h1T[0:64, c0:c0 + n_], True, True, [bfz, bh1], [bPS[0]])
                    sin_layer(PS[0][0:64, 0:n_], n_, V_FR1, 1, h2T[0:64, c0:c0 + n_], [bPS[0]], [bh2])

                zcur = [hyv[:, cc, :] for cc in range(4)]
                zbuf = [bHY[cc] for cc in range(4)]
                for o in range(2):
                    w3t, bw3 = wload(dr["fw3"][l].rearrange("p (k c) -> p k c", k=1), 1, 2048, mode="f32", parts=64)
                    for lc in range(nf):
                        AC(dec_t[:], delta[:], AF.Exp, [bfz], [bdec], scale=negt[:, lc:lc + 1])
                        for side in range(2):
                            cofs = (side * 2 + o) * 512
                            MM(bank(side), h2T[0:64, lc * 128:(lc + 1) * 128], w3t[0:64, 0, cofs:cofs + 512], True, True, [bh2, bw3(0)], [bPS[side]])
                        CP("act", bsb_t[:], bank(1), [bPS[1]], [bbsb])
                        if lc == 0:
                            MS("pool", bsb_t[0:1, :], 0.0, [bbsb])
                        TT("dve", sm_t[:], bank(0), bsb_t[:], ALU.add, [bPS[0], bbsb], [bsm])
                        TT("dve", df_t[:], bank(0), bsb_t[:], ALU.subtract, [bPS[0], bbsb], [bdf])
                        TT("pool", ksdv[:, 0, lc, :], sm_t[:], dec_t[:], ALU.mult, [bsm, bdec], [bKSD])
                        TT("pool", ksdv[:, 1, lc, :], df_t[:], dec_t[:], ALU.mult, [bdf, bdec], [bKSD])
                    for sq_ in range(nseq):
                        for tcn in range(nf):
                            ts_ = slice(sq_ * L + tcn * 128, sq_ * L + (tcn + 1) * 128)
                            pbk = 2 + (tcn % 2)
                            for cc in range(4):
                                MM(bank(pbk, 128, cc * 128), zcur[cc][:, ts_], ident_b[:], True, True, [zbuf[cc], bC], [bPS[pbk]])
                            CP("act", ztv[:, sq_, tcn, :], bank(pbk), [bPS[pbk]], [bZT])
                    for fb_ in range(nf // 2):
                        Cp, bCp = wload(wsrc(dr["dftC%d" % L], 0, nf, fb_ * 256, 256), nf, 256, mode="bf16")
                        Sp, bSp = wload(wsrc(dr["dftS%d" % L], 0, nf, fb_ * 256, 256), nf, 256, mode="bf16")
                        for f2 in range(2):
                            fc = fb_ * 2 + f2
                            fcs = slice(f2 * 128, (f2 + 1) * 128)
                            kk, bk_ = kab[fc % 2], bkab[fc % 2]
                            ka_b, kb_b = 2 * (fc % 2), 2 * (fc % 2) + 1
                            for n_ in range(nf):
                                MM(bank(ka_b), Cp[:, n_, fcs], ksdv[:, 0, n_, :], n_ == 0, n_ == nf - 1, [bCp(0), bKSD], [bPS[ka_b]])
                            for n_ in range(nf):
                                MM(bank(kb_b), Sp[:, n_, fcs], ksdv[:, 1, n_, :], n_ == 0, n_ == nf - 1, [bSp(0), bKSD], [bPS[kb_b]])
                            AC(kk[:, 0:512], bank(ka_b), AF.Identity, [bPS[ka_b]], [bk_], scale=1.0 / L)
                            AC(kk[:, 512:1024], bank(kb_b), AF.Identity, [bPS[kb_b]], [bk_], scale=1.0 / L)
                            for sq_ in range(nseq):
                                pa, pb_ = (4, 5) if (fc * nseq + sq_) % 2 == 0 else (6, 7)
                                for n_ in range(nf):
                                    MM(bank(pa), Cp[:, n_, fcs], ztv[:, sq_, n_, :], n_ == 0, n_ == nf - 1, [bCp(0), bZT], [bPS[pa]])
                                for n_ in range(nf):
                                    MM(bank(pb_), Sp[:, n_, fcs], ztv[:, sq_, n_, :], n_ == 0, n_ == nf - 1, [bSp(0), bZT], [bPS[pb_]])
                                TT("dve", tt_[0][:], bank(pa), kk[:, 0:512], ALU.mult, [bPS[pa], bk_], [btt[0]])
                                TT("dve", tt_[1][:], bank(pb_), kk[:, 512:1024], ALU.mult, [bPS[pb_], bk_], [btt[1]])
                                TT("pool", p1v[:, 0, sq_, fc, :], tt_[0][:], tt_[1][:], ALU.subtract, [btt[0], btt[1]], [bP1])
                                TT("dve", tt_[2][:], bank(pa), kk[:, 512:1024], ALU.mult, [bPS[pa], bk_], [btt[2]])
                                TT("dve", tt_[3][:], bank(pb_), kk[:, 0:512], ALU.mult, [bPS[pb_], bk_], [btt[3]])
                                TT("pool", p1v[:, 1, sq_, fc, :], tt_[2][:], tt_[3][:], ALU.add, [btt[2], btt[3]], [bP1])
                    xo = [hyv[:, 4 * (o + 1) + cc, :] for cc in range(4)]
                    xob = [bHY[4 * (o + 1) + cc] for cc in range(4)]
                    it2 = 0
                    for sq_ in range(nseq):
                        for tb in range(L // 256):
                            CTp, bCTp = wload(wsrc(dr["dftCT%d" % L], 0, nf, tb * 256, 256), nf, 256, mode="bf16")
                            STp, bSTp = wload(wsrc(dr["dftST%d" % L], 0, nf, tb * 256, 256), nf, 256, mode="bf16")
                            ts_ = slice(sq_ * L + tb * 256, sq_ * L + (tb + 1) * 256)
                            for cc in range(4):
                                pbk = it2 % 4
                                it2 += 1
                                ccs = slice(cc * 128, (cc + 1) * 128)
                                for n_ in range(nf):
                                    MM(bank(pbk, 256), p1v[:, 0, sq_, n_, ccs], CTp[:, n_, :], n_ == 0, False, [bP1, bCTp(0)], [bPS[pbk]])
                                for n_ in range(nf):
                                    MM(bank(pbk, 256), p1v[:, 1, sq_, n_, ccs], STp[:, n_, :], False, n_ == nf - 1, [bP1, bSTp(0)], [bPS[pbk]])
                                STT("dve", utmp[:, 0:256], zcur[cc][:, ts_], vl[:, V_SKIP + o * 4 + cc:V_SKIP + o * 4 + cc + 1], bank(pbk, 256), ALU.mult, ALU.add, [zbuf[cc], bC, bPS[pbk]], [butmp])
                                if o == 0:
                                    TT("pool", zcur[cc][:, ts_], utmp[:, 0:256], xo[cc][:, ts_], ALU.mult, [butmp, xob[cc]], [zbuf[cc]])
                                else:
                                    TT("pool", mixv[:, cc, ts_], utmp[:, 0:256], xo[cc][:, ts_], ALU.mult, [butmp, xob[cc]], [bMIX[cc]])
                if dbg and ("mix_%d_%d" % (l, g)) in dbg_out:
                    s.barrier()
                    for q4 in range(4):
                        RT.reset()
                        dt_ = RT.alloc(4 * Ltot, F32)
                        bd = Buf()
                        CP("dve", v3(dt_[:], 4), mixv[:, q4 * 4:(q4 + 1) * 4, :], bMIX, [bd])
                        DMA(dbg_out["mix_%d_%d" % (l, g)][q4 * 512:(q4 + 1) * 512, :].rearrange("(k p) t -> p k t", p=128), v3(dt_[:], 4), [bd], [])
                        s.barrier()
                s.barrier()
                stage('hyena')
                RA.reset()
                RT.reset()
                RIN.reset()

                xc = [RT.alloc(512, F32) for _ in range(3)]
                bxc = [Buf() for _ in range(3)]
                it3 = 0
                for i in range(16):
                    wt, bw = wload(wsrc(dr["w_out"][l], 0, 16, i * 128, 128), 16, 128)
                    u = i % 4
                    for b in range(nblk):
                        blk = (tok0 // 512) + b
                        xi_, bxi = xc[it3 % 3], bxc[it3 % 3]
                        it3 += 1
                        DMA(xi_[:], xTv[:, i, blk * 512:(blk + 1) * 512], [bXd[blk][i]], [bxi])
                        for k in range(16):
                            MM(bank(2 * u + b), wt[:, k, :], mixv[:, k, b * 512:(b + 1) * 512], k == 0, k == 15, [bw(k), bMIX[k]], [bPS[2 * u + b]])
                        STT("dve", xi_[:], bank(2 * u + b), modT[:, j * 96 + 32 + i:j * 96 + 32 + i + 1], xi_[:], ALU.mult, ALU.add, [bPS[2 * u + b], bC, bxi], [bxi])
                        DMA(xTv[:, i, blk * 512:(blk + 1) * 512], xi_[:], [bxi], [bXd[blk][i]])
                s.barrier()
                stage('outproj')

            last = (l == nlayers - 1)
            RF.reset()
            wl2 = make_ws2(RF, nstg=2)
            h2all = RF.alloc(16 * T, BF16)
            h2a = v3(h2all[:], 16)
            bh2 = [Buf() for _ in range(3)]
            rstd = RF.alloc(512, F32)
            brs = Buf()
            sgt = [RF.alloc(512, F32) for _ in range(2)]
            bsg = [Buf(), Buf()]
            ntmp = RF.alloc(2 * 512, F32)
            bnt = [Buf(), Buf()]
            xc = [RF.alloc(512, F32) for _ in range(3)]
            bxc = [Buf() for _ in range(3)]
            act_off = RF.cur
            xblk = RF.alloc(16 * 512, F32)
            bxb = Buf()
            sqt = RF.alloc(16 * 512, BF16)
            bsq = Buf()
            for blk in range(3):
                j = 0 if blk == 0 else 1
                DMA(v3(xblk[:], 16), xTv[:, :, blk * 512:(blk + 1) * 512], bXd[blk], [bxb])
                AC(sqt[:], xblk[:], AF.Square, [bxb], [bsq])
                rms_rstd(lambda k: sqt[:, k * 512:(k + 1) * 512], 16, 512, float(D), 7, rstd[:], [bsq], [brs])
                for k in range(16):
                    nt_, bn_ = ntmp[:, (k % 2) * 512:(k % 2) * 512 + 512], bnt[k % 2]
                    TT("dve", nt_, xblk[:, k * 512:(k + 1) * 512], rstd[:], ALU.mult, [bxb, brs], [bn_])
                    AC(h2a[:, k, blk * 512:(blk + 1) * 512], nt_, AF.Identity, [bn_, bC], [bh2[blk]],
                       bias=modT[:, j * 96 + 48 + k:j * 96 + 48 + k + 1], scale=gsc[:, (j * 2 + 1) * 16 + k:(j * 2 + 1) * 16 + k + 1])
            s.barrier()
            RF.cur = act_off
            actH = RF.alloc(22 * T, BF16)
            actv = v3(actH[:], 22)
            bact = [[Buf() for _ in range(3)] for _ in range(22)]
            xc6 = xc + [RF.alloc(512, F32) for _ in range(3)]
            bxc6 = bxc + [Buf() for _ in range(3)]
            nup = 0
            for half in range(2):
                for jp in range(11):
                    jf0 = half * 22 + 2 * jp
                    wg, bg = wl2(wsrc(dr["w_ffn_in"][l], 0, 16, jf0 * 128, 256), 16, 256)
                    wu, bu = wl2(wsrc(dr["w_ffn_in"][l], 0, 16, DFF + jf0 * 128, 256), 16, 256)
                    for c in range(2):
                        for blk in range(3):
                            pg = c * 3 + blk
                            sl = slice(blk * 512, (blk + 1) * 512)
                            for k in range(16):
                                MM(bank(pg), wg[:, k, c * 128:(c + 1) * 128], h2a[:, k, sl], k == 0, k == 15, [bg(k), bh2[blk]], [bPS[pg]])
                    for c in range(2):
                        jfl = 2 * jp + c
                        for blk in range(3):
                            pg = c * 3 + blk
                            pu = 6 + (nup % 2)
                            sg_, bsg_ = sgt[nup % 2], bsg[nup % 2]
                            nup += 1
                            sl = slice(blk * 512, (blk + 1) * 512)
                            AC(sg_[:], bank(pg), AF.Silu, [bPS[pg]], [bsg_])
                            for k in range(16):
                                MM(bank(pu), wu[:, k, c * 128:(c + 1) * 128], h2a[:, k, sl], k == 0, k == 15, [bu(k), bh2[blk]], [bPS[pu]])
                            TT("dve", actv[:, jfl, sl], bank(pu), sg_[:], ALU.mult, [bPS[pu], bsg_], [bact[jfl][blk]])
                def out_tiles(ip_):
                    res = []
                    for (k0, kc) in ((0, 16), (16, 6)):
                        wt, bk = wl2(wsrc(dr["w_ffn_out"][l], half * 22 + k0, kc, ip_ * 256, 256), kc, 256)
                        res.append((k0, kc, wt, bk))
                    return res
                tiles = out_tiles(0)
                pvn = bank(7, 192).rearrange("p (j m) -> p m j", j=2)
                for ip in range(8):
                    i0 = 2 * ip
                    for c in range(2):
                        for blk in range(3):
                            n6 = c * 3 + blk
                            DMA(xc6[n6][:], xTv[:, i0 + c, blk * 512:(blk + 1) * 512], [bXd[blk][i0 + c]], [bxc6[n6]])
                    for (k0, kc, wt, bk) in tiles:
                        for c in range(2):
                            for blk in range(3):
                                pb = c * 3 + blk
                                for k in range(kc):
                                    kk = k0 + k
                                    MM(bank(pb), wt[:, k, c * 128:(c + 1) * 128], actv[:, kk, blk * 512:(blk + 1) * 512], kk == 0, kk == 21, [bk(k), bact[kk][blk]], [bPS[pb]])
                    for c in range(2):
                        i = i0 + c
                        for blk in range(3):
                            j = 0 if blk == 0 else 1
                            n6 = c * 3 + blk
                            STT("dve", xc6[n6][:], bank(n6), modT[:, j * 96 + 80 + i:j * 96 + 80 + i + 1], xc6[n6][:], ALU.mult, ALU.add, [bPS[n6], bC, bxc6[n6]], [bxc6[n6]])
                    if l + 1 < nlayers:
                        for t_ in range(3):
                            mp = half * 24 + ip * 3 + t_
                            wtm, bkm = wl2(wsrc(dr["w_mod"][l + 1], 0, 16, mp * 256, 256), 16, 256)
                            for c in range(2):
                                m = 2 * mp + c
                                for k in range(16):
                                    MM(pvn[:, m, :], wtm[:, k, c * 128:(c + 1) * 128], scv[:, k, :], k == 0, k == 15, [bkm(k), bC], [bPS[7]])
                    if ip < 7:
                        tiles = out_tiles(ip + 1)
                    for c in range(2):
                        for blk in range(3):
                            n6 = c * 3 + blk
                            DMA(xTv[:, i0 + c, blk * 512:(blk + 1) * 512], xc6[n6][:], [bxc6[n6]], [bXd[blk][i0 + c]])
                if l + 1 < nlayers:
                    m0 = half * 48
                    for j in range(2):
                        TT("dve", modTs[(l + 1) % 2][:, j * 96 + m0:j * 96 + m0 + 48], bank(7, 48, j * 96 + m0), vecs[l + 1][:, V_BMOD + m0:V_BMOD + m0 + 48], ALU.add, [bPS[7], bC], [bC])
            s.barrier()
            if last:
                RF.cur = act_off
                xblk = RF.alloc(16 * 512, F32)
                bxk = [Buf() for _ in range(16)]
                xv = v3(xblk[:], 16)
                sqt = RF.alloc(16 * 512, BF16)
                bsq = Buf()
                yo = [RF.alloc(2048, F32) for _ in range(2)]
                byo = [Buf(), Buf()]
                for blk in range(3):
                    DMA(xv, xTv[:, :, blk * 512:(blk + 1) * 512], bXd[blk], bxk)
                    AC(sqt[:], xblk[:], AF.Square, bxk, [bsq])
                    rms_rstd(lambda k: sqt[:, k * 512:(k + 1) * 512], 16, 512, float(D), 7, rstd[:], [bsq], [brs])
                    for k in range(16):
                        STT("dve", xv[:, k, :], xv[:, k, :], lnf[:, k:k + 1], rstd[:], ALU.mult, ALU.mult, [bxk[k], bC, brs], [bxk[k]])
                    for tcn in range(4):
                        yt, byt = yo[tcn % 2], byo[tcn % 2]
                        for half in range(2):
                            u = (tcn * 2 + half) % 3
                            for kk in range(8):
                                k = half * 8 + kk
                                MM(PS[u][:, kk * 128:(kk + 1) * 128], xv[:, k, tcn * 128:(tcn + 1) * 128], ident_f[:], True, True, [bxk[k], bC], [bPS[2 * u + kk // 4]])
                            CP("act" if half else "dve", yt[:, half * 1024:(half + 1) * 1024], PS[u][:, :], [bPS[2 * u], bPS[2 * u + 1]], [byt])
                        DMA(dr["y"][blk * 512 + tcn * 128:blk * 512 + (tcn + 1) * 128, :], yt[:], [byt], [])
                s.barrier()
    except _Stop:
        pass
    s.barrier()
    s.emit(nc, st)
    st.close()
    return nc


_NC_CACHE = {}


def prep_inputs(inp, core):
    c = host_consts()
    f32 = lambda a: np.ascontiguousarray(np.asarray(a, dtype=np.float32))
    m = {}
    xp = f32(inp["x_prompt"])[2 * core:2 * core + 2].reshape(TP, D)
    xs = f32(inp["x_sample"])[core]
    m["x_in"] = np.ascontiguousarray(np.concatenate([xp, xs], axis=0))
    cv = np.stack([f32(inp["c_ctx"]), f32(inp["c"])[core]], axis=0)
    m["cvec"] = np.ascontiguousarray(cv.reshape(2, 16, 128).transpose(2, 1, 0).reshape(128, 32))
    m["ckv_c"] = f32(inp["cache_mla_ckv"])[core]
    m["kr_c"] = f32(inp["cache_mla_krope"])[core]
    m["sf_c"] = np.ascontiguousarray(f32(inp["state_gla_fwd"])[core].reshape(NL, 256, 128))
    m["sb_c"] = np.ascontiguousarray(f32(inp["state_gla_bwd"])[core].reshape(NL, 256, 128))
    return m


def prep_shared(inp):
    c = host_consts()
    f32 = lambda a: np.ascontiguousarray(np.asarray(a, dtype=np.float32))
    m = {}
    m["w_mod"] = f32(inp["w_mod"])
    w_in = f32(inp["w_in"])
    m["w_in"] = np.ascontiguousarray(np.concatenate([w_in, w_in[:, :, C_KR + c["perm"]]], axis=2))
    m["w_out"] = f32(inp["w_out"])
    m["w_ffn_in"] = f32(inp["w_ffn_in"])
    m["w_ffn_out"] = f32(inp["w_ffn_out"])
    wuq = f32(inp["mla_w_uq"])
    sw = np.concatenate([wuq[:, :, h * 192 + 128 + c["perm"]] for h in range(8)], axis=2)
    m["w_uq"] = np.ascontiguousarray(np.concatenate([wuq, sw], axis=2))
    m["w_ukv"] = f32(inp["mla_w_ukv"])
    vecs = np.zeros((NL, 128, NV), np.float32)
    pm = lambda a, n: f32(a).reshape(NL, n, 128).transpose(0, 2, 1)
    vecs[:, :, V_BMOD:V_BMOD + 96] = pm(inp["b_mod"], 96)
    vecs[:, :, V_LNMIX:V_LNMIX + 16] = pm(inp["ln_mix"], 16)
    vecs[:, :, V_LNFFN:V_LNFFN + 16] = pm(inp["ln_ffn"], 16)
    vecs[:, :, V_CW:V_CW + 36] = f32(inp["hy_conv_w"]).reshape(NL, 3, 12, 128).transpose(0, 3, 1, 2).reshape(NL, 128, 36)
    vecs[:, :, V_CB:V_CB + 12] = pm(inp["hy_conv_b"], 12)
    vecs[:, :, V_SKIP:V_SKIP + 8] = f32(inp["hy_skip"]).reshape(NL, 2, 4, 128).transpose(0, 3, 1, 2).reshape(NL, 128, 8)
    vecs[:, :, V_GN] = f32(inp["gla_norm"])
    vecs[:, :, V_QN:V_QN + 3] = pm(inp["mla_q_norm"], 3)
    vecs[:, :, V_KVN:V_KVN + 2] = pm(inp["mla_kv_norm"], 2)
    vecs[:, 0:64, V_FB1] = f32(inp["hy_filt_b1"])
    vecs[:, 0:64, V_FB2] = f32(inp["hy_filt_b2"])
    vecs[:, 0:64, V_FR0] = f32(inp["hy_filt_freq"])[:, 0]
    vecs[:, 0:64, V_FR1] = f32(inp["hy_filt_freq"])[:, 1]
    m["vecs"] = vecs
    m["lnf"] = np.ascontiguousarray(f32(inp["ln_final"]).reshape(16, 128).T)
    m["fw1"] = f32(inp["hy_filt_w1"])
    m["fw2"] = f32(inp["hy_filt_w2"])
    m["fw3"] = f32(inp["hy_filt_w3"])
    wa = np.zeros((NL, 2, 17, 256), np.float32)
    wa[:, 0, 0:16] = f32(inp["gla_wa_f"])
    wa[:, 0, 16] = f32(inp["gla_ba_f"])
    wa[:, 1, 0:16] = f32(inp["gla_wa_b"])
    wa[:, 1, 16] = f32(inp["gla_ba_b"])
    m["wa_aug"] = wa
    for name, shape, dt in CONST_DRAM:
        m[name] = c[name]
    return m


def kernel(**inputs):
    if "nc" not in _NC_CACHE:
        _NC_CACHE["nc"] = build()
    nc = _NC_CACHE["nc"]
    shared = prep_shared(inputs)
    in_maps = []
    for core in range(NCORE):
        m = dict(shared)
        m.update(prep_inputs(inputs, core))
        in_maps.append(m)
    res = run_bass_kernel_spmd(nc, in_maps, core_ids=list(range(NCORE)))
    R = res.results
    y_prompt = np.concatenate([r["y"][0:TP].reshape(2, LP, D) for r in R], axis=0)
    y_sample = np.stack([r["y"][TP:T] for r in R], axis=0)
    o_ckv = np.concatenate([r["o_ckv"] for r in R], axis=0)
    o_kr = np.concatenate([r["o_kr"] for r in R], axis=0)
    o_sf = np.concatenate([r["o_sf"].reshape(2, NL, 4, 64, 128) for r in R], axis=0)
    o_sb = np.concatenate([r["o_sb"].reshape(2, NL, 4, 64, 128) for r in R], axis=0)
    f = lambda a: np.ascontiguousarray(a.astype(np.float32))
    return (f(y_prompt), f(y_sample), f(o_ckv), f(o_kr), f(o_sf), f(o_sb))
```

```python
import math
import contextlib
import numpy as np
import ml_dtypes
import concourse.bass as bass
import concourse.mybir as mybir
from concourse.bass_utils import run_bass_kernel_spmd

F32 = mybir.dt.float32
BF16 = mybir.dt.bfloat16
ALU = mybir.AluOpType
AF = mybir.ActivationFunctionType

D = 2048
DFF = 5632
NL = 2
LP = 256
LS = 1024
TP = 512
T = 1536
PAST = 256
EPS = 1e-6
NCORE = 8
C_HY, C_Q, C_K, C_V, C_G, C_AF, C_AB, C_CQ, C_CKV, C_KR, C_KRSW = 0, 1536, 1792, 2048, 2560, 3072, 3088, 3104, 3488, 3744, 3808
WIN_EXT = 3872
NV = 200
V_BMOD, V_LNMIX, V_LNFFN, V_CW, V_CB, V_SKIP, V_GN, V_QN, V_KVN, V_FB1, V_FB2, V_FR0, V_FR1 = 0, 96, 112, 128, 164, 176, 184, 185, 188, 190, 191, 192, 193
MAGIC = 12582912.0
TWO_PI = 2.0 * math.pi
SBASE = 16512
SEND = 229376


class Buf:
    __slots__ = ("w", "r")

    def __init__(self):
        self.w = None
        self.r = {}


class Op:
    __slots__ = ("eng", "fn", "waits", "signal", "src", "seq", "vc", "val", "dma")


class Sched:
    ENG = ["pe", "act", "dve", "pool", "sp"]
    NSLOT = 8

    def __init__(self):
        self.q = {e: [] for e in self.ENG}
        self.S = {e: {} for e in self.ENG}
        self.ncomp = {e: 0 for e in self.ENG}
        self.ndma = {e: 0 for e in self.ENG}
        self.slot_last = {}
        self.slot_seq = {}

    def add(self, eng, fn, reads=(), writes=(), dma=False, after=()):
        op = Op()
        op.eng = eng
        op.fn = fn
        op.signal = False
        op.dma = dma
        deps = []
        deps.extend(after)
        for b in reads:
            if b.w is not None:
                deps.append(b.w)
        for b in writes:
            if b.w is not None:
                deps.append(b.w)
            deps.extend(b.r.values())
        if dma:
            j = self.ndma[eng]
            self.ndma[eng] += 1
            slot = (eng, j % self.NSLOT)
            prev = self.slot_last.get(slot)
            if prev is not None:
                deps.append(prev)
            op.src = ("dma",) + slot
            op.seq = self.slot_seq.get(slot, 0) + 1
            self.slot_seq[slot] = op.seq
            self.slot_last[slot] = op
            op.signal = True
        else:
            self.ncomp[eng] += 1
            op.src = eng
            op.seq = self.ncomp[eng]
        S = self.S[eng]
        waits = {}
        for d in deps:
            if eng == "pe" and d.src == "pe":
                continue
            if S.get(d.src, 0) >= d.seq:
                continue
            d.signal = True
            cur = waits.get(d.src)
            if cur is None or cur.seq < d.seq:
                waits[d.src] = d
            for k, v in d.vc.items():
                if S.get(k, 0) < v:
                    S[k] = v
        op.waits = list(waits.values())
        op.vc = dict(S)
        op.vc[op.src] = op.seq
        for b in writes:
            b.w = op
            b.r = {}
        for b in reads:
            if b.w is not op:
                b.r[op.src] = op
        self.q[eng].append(op)
        return op

    @staticmethod
    def deps_of(bufs):
        out = []
        for b in bufs:
            if b.w is not None:
                out.append(b.w)
            out.extend(b.r.values())
        return out

    def barrier(self):
        lasts = []
        for e in self.ENG:
            for op in reversed(self.q[e]):
                if (not op.dma) and op.fn is not None:
                    lasts.append(op)
                    break
        lasts.extend(self.slot_last.values())
        for e in self.ENG:
            S = self.S[e]
            waits = {}
            for d in lasts:
                if S.get(d.src, 0) >= d.seq:
                    continue
                d.signal = True
                waits[d.src] = d
            for d in lasts:
                for k, v in d.vc.items():
                    if S.get(k, 0) < v:
                        S[k] = v
            if waits:
                op = Op()
                op.eng = e
                op.fn = None
                op.signal = False
                op.dma = False
                op.waits = list(waits.values())
                op.src = None
                op.seq = 0
                op.vc = {}
                self.q[e].append(op)

    def emit(self, nc, stack):
        sems = {}
        for e in self.ENG:
            cnt = 0
            for op in self.q[e]:
                if op.fn is None:
                    continue
                if op.dma:
                    op.val = 16 * op.seq
                else:
                    if op.signal:
                        cnt += 1
                    op.val = cnt
                if op.signal and op.src not in sems:
                    nm = "s_" + ("_".join(str(x) for x in op.src) if isinstance(op.src, tuple) else op.src)
                    sems[op.src] = stack.enter_context(nc.semaphore(nm))
        block = stack.enter_context(nc.Block())
        q = self.q

        def run(e, eng):
            for op in q[e]:
                for d in op.waits:
                    eng.wait_ge(sems[d.src], d.val)
                if op.fn is None:
                    continue
                ins = op.fn(eng)
                if op.signal:
                    ins.then_inc(sems[op.src], 16 if op.dma else 1)

        @block.tensor
        def _(eng):
            run("pe", eng)

        @block.scalar
        def _(eng):
            run("act", eng)

        @block.vector
        def _(eng):
            run("dve", eng)

        @block.gpsimd
        def _(eng):
            run("pool", eng)

        @block.sync
        def _(eng):
            run("sp", eng)


def _bf(x):
    return np.ascontiguousarray(x.astype(ml_dtypes.bfloat16))


_CONST_CACHE = {}


def host_consts():
    if _CONST_CACHE:
        return _CONST_CACHE
    c = {}
    c["ident_f"] = np.eye(128, dtype=np.float32)
    c["ident_b"] = _bf(np.eye(128, dtype=np.float32))
    c["ones_b"] = _bf(np.ones((128, 128), np.float32))
    s_ = np.arange(128)[:, None]
    c_ = np.arange(128)[None, :]
    v = -1.0 / 16.0
    tri = np.stack([(s_ <= c_) * v, (s_ > c_) * v, (s_ >= c_) * v, (s_ < c_) * v]).astype(np.float32)
    c["tri"] = np.ascontiguousarray(tri.transpose(1, 0, 2).reshape(128, 4 * 128))
    mk = np.stack([np.tile((s_ <= c_).astype(np.float32), (1, 4)), np.tile((s_ >= c_).astype(np.float32), (1, 4))])
    c["mask"] = np.ascontiguousarray(mk.transpose(1, 0, 2).reshape(128, 2 * 512))
    t = np.arange(LS)
    row, col = t // 64, t % 64
    inv = (10000.0 ** (-np.arange(0, 32, 2, dtype=np.float32) / 32.0)).astype(np.float32)
    cos = np.zeros((64, LS), np.float32)
    sin = np.zeros((64, LS), np.float32)
    perm = np.zeros(64, np.int64)
    for r in range(64):
        pos = (row if r < 32 else col).astype(np.float32)
        j = r % 16
        ang = (pos * inv[j]).astype(np.float32)
        cos[r] = np.cos(ang)
        x2 = (r % 32) >= 16
        sin[r] = np.sin(ang) * (1.0 if x2 else -1.0)
        perm[r] = r - 16 if x2 else r + 16
    c["rope"] = np.ascontiguousarray(np.concatenate([cos, sin], axis=1))
    c["perm"] = perm
    for L in (LP, LS):
        tt = np.linspace(0.0, 1.0, L, dtype=np.float32)[:, None]
        w = (2.0 * math.pi * np.arange(L, dtype=np.float32)[:, None] / L).astype(np.float32)
        f = np.linspace(1e-4, 15, 16, dtype=np.float32)
        z = np.concatenate([tt, np.cos(f * w), -np.sin(f * w)], axis=-1).astype(np.float32)
        c[f"zemb{L}"] = np.ascontiguousarray(z.T)
        c[f"negt{L}"] = np.ascontiguousarray((-tt[:, 0]).reshape(L // 128, 128).T)
        n = np.arange(L, dtype=np.float64)[:, None]
        fq = np.arange(L, dtype=np.float64)[None, :]
        th = math.pi * (2 * fq + 1) * n / (2 * L)
        C = np.cos(th)
        S = np.sin(th)
        c[f"dftC{L}"] = _bf(C.astype(np.float32))
        c[f"dftS{L}"] = _bf(S.astype(np.float32))
        c[f"dftCT{L}"] = _bf(C.T.astype(np.float32))
        c[f"dftST{L}"] = _bf(S.T.astype(np.float32))
    mind = math.log(1e-2) / 1.5
    maxd = math.log(1e-2) / 0.3
    delta = np.abs(np.linspace(mind, maxd, 512, dtype=np.float32))
    c["delta"] = np.ascontiguousarray(np.tile(delta[None, :], (128, 1)).astype(np.float32))
    _CONST_CACHE.update(c)
    return c


CONST_DRAM = [("ident_f", [128, 128], F32), ("ident_b", [128, 128], BF16), ("ones_b", [128, 128], BF16),
              ("tri", [128, 512], F32), ("mask", [128, 1024], F32), ("rope", [64, 2048], F32),
              ("zemb256", [33, 256], F32), ("zemb1024", [33, 1024], F32),
              ("negt256", [128, 2], F32), ("negt1024", [128, 8], F32), ("delta", [128, 512], F32),
              ("dftC256", [256, 256], BF16), ("dftS256", [256, 256], BF16), ("dftCT256", [256, 256], BF16), ("dftST256", [256, 256], BF16),
              ("dftC1024", [1024, 1024], BF16), ("dftS1024", [1024, 1024], BF16), ("dftCT1024", [1024, 1024], BF16), ("dftST1024", [1024, 1024], BF16)]

IN_DRAM = [("x_in", [T, D]), ("cvec", [128, 32]), ("ckv_c", [NL, PAST, 256]), ("kr_c", [NL, PAST, 64]),
           ("sf_c", [NL, 256, 128]), ("sb_c", [NL, 256, 128]),
           ("w_mod", [NL, D, 6 * D]), ("w_in", [NL, D, WIN_EXT]), ("w_out", [NL, D, D]),
           ("w_ffn_in", [NL, D, 2 * DFF]), ("w_ffn_out", [NL, DFF, D]), ("w_uq", [NL, 384, 2048]), ("w_ukv", [NL, 256, 2048]),
           ("vecs", [NL, 128, NV]), ("lnf", [128, 16]), ("fw1", [NL, 33, 64]), ("fw2", [NL, 64, 64]), ("fw3", [NL, 64, 2048]),
           ("wa_aug", [NL, 2, 17, 256])]

OUT_DRAM = [("y", [T, D]), ("o_ckv", [2, NL, LP, 256]), ("o_kr", [2, NL, LP, 64]), ("o_sf", [2, NL, 256, 128]), ("o_sb", [2, NL, 256, 128])]


class _Stop(Exception):
    pass


def build(nlayers=NL, dbg=None, stop=None, knob=None):
    nc = bass.Bass("TRN2", target_bir_lowering=False)
    dr = {}
    for name, shape in IN_DRAM:
        dr[name] = nc.dram_tensor(name, list(shape), F32, kind="ExternalInput").ap()
    for name, shape, dt in CONST_DRAM:
        dr[name] = nc.dram_tensor(name, list(shape), dt, kind="ExternalInput").ap()
    for name, shape in OUT_DRAM:
        dr[name] = nc.dram_tensor(name, list(shape), F32, kind="ExternalOutput").ap()
    dbg_out = {}
    if dbg:
        for name, shape in dbg.items():
            dbg_out[name] = nc.dram_tensor("dbg_" + name, list(shape), F32, kind="ExternalOutput").ap()
    xT_d = nc.dram_tensor("xT_scr", [D, T], F32, kind="Internal").ap()
    xTv = xT_d.rearrange("(k p) t -> p k t", p=128)

    s = Sched()
    st = contextlib.ExitStack()
    uid = [0]
    stg_ctr = [0]

    def stage(name):
        stg_ctr[0] += 1
        if stop is not None and stg_ctr[0] == stop:
            print('STOP at stage', stop, name)
            raise _Stop()

    class Region:
        def __init__(self, name, start, size):
            self.name, self.start, self.size, self.cur = name, start, size, start

        def reset(self):
            self.cur = self.start

        def alloc(self, n, dt):
            sz = n * (4 if dt == F32 else 2)
            sz = (sz + 31) // 32 * 32
            assert self.cur + sz <= self.start + self.size, (self.name, self.cur - self.start, sz, self.size)
            uid[0] += 1
            t = nc.alloc_sbuf_tensor_at(f"{self.name}_{uid[0]}", [128, n], dt, offset=self.cur)
            self.cur += sz
            return t

    sizes = [("C", 20480), ("W", 36864), ("A", 32768), ("B", 32768), ("IN", 67584)]
    regs = {}
    cur = SBASE
    for nm, sz in sizes:
        regs[nm] = Region(nm, cur, sz)
        cur += sz
    regs["T"] = Region("T", cur, SEND - cur)
    RC, RW, RA, RB, RIN, RT = (regs[k] for k in ("C", "W", "A", "B", "IN", "T"))
    RF = Region("F", regs["W"].start, SEND - regs["W"].start)

    PS = [st.enter_context(nc.psum_tensor(f"ps{i}", [128, 1024], F32)) for i in range(4)]
    bPS = [Buf() for _ in range(8)]

    def bank(b, n=512, off=0):
        return PS[b // 2][:, (b % 2) * 512 + off:(b % 2) * 512 + off + n]

    def MM(out, lhsT, rhs, st_, sp_, R, W):
        s.add("pe", lambda e: e.matmul(out, lhsT=lhsT, rhs=rhs, start=st_, stop=sp_), R, W)

    def AC(out, in_, func, R, W, bias=None, scale=None, accum=None):
        kw = {}
        if bias is not None:
            kw["bias"] = bias
        if scale is not None:
            kw["scale"] = scale
        if accum is not None:
            kw["accum_out"] = accum
        s.add("act", lambda e: e.activation(out=out, in_=in_, func=func, **kw), R, W)

    def TT(eng, out, a, b, op, R, W):
        s.add(eng, lambda e: e.tensor_tensor(out=out, in0=a, in1=b, op=op), R, W)

    def TS(eng, out, a, s1, s2, op0, op1, R, W):
        if s2 is None:
            s.add(eng, lambda e: e.tensor_scalar(out=out, in0=a, scalar1=s1, scalar2=None, op0=op0), R, W)
        else:
            s.add(eng, lambda e: e.tensor_scalar(out=out, in0=a, scalar1=s1, scalar2=s2, op0=op0, op1=op1), R, W)

    def STT(eng, out, a, sc, b, op0, op1, R, W):
        s.add(eng, lambda e: e.scalar_tensor_tensor(out=out, in0=a, scalar=sc, in1=b, op0=op0, op1=op1), R, W)

    def CP(eng, out, in_, R, W, after=()):
        if eng == "act":
            s.add("act", lambda e: e.copy(out=out, in_=in_), R, W, after=after)
        else:
            s.add(eng, lambda e: e.tensor_copy(out=out, in_=in_), R, W, after=after)

    def MS(eng, ap, val, W):
        s.add(eng, lambda e: e.memset(ap, val), (), W)

    def RCP(out, in_, R, W):
        s.add("dve", lambda e: e.reciprocal(out=out, in_=in_), R, W)

    def DMA(out, in_, R, W):
        s.add("sp", lambda e: e.dma_start(out=out, in_=in_), R, W, dma=True)

    def v3(ap2, k):
        return ap2.rearrange("p (k c) -> p k c", k=k)

    bC = Buf()
    ident_f = RC.alloc(128, F32)
    ident_b = RC.alloc(128, BF16)
    ones_b = RC.alloc(128, BF16)
    tri = RC.alloc(512, F32)
    mask = RC.alloc(1024, F32)
    rope = RC.alloc(2048, F32)
    vecs = [RC.alloc(NV, F32) for _ in range(NL)]
    lnf = RC.alloc(16, F32)
    cvec = RC.alloc(32, F32)
    scT = RC.alloc(32, BF16)
    modTs = [RC.alloc(192, F32) for _ in range(2)]
    gsc = RC.alloc(64, F32)
    epsb = RC.alloc(1, F32)
    fb = RC.alloc(4, F32)
    for tile_, name in ((ident_f, "ident_f"), (ident_b, "ident_b"), (ones_b, "ones_b"), (tri, "tri"), (mask, "mask"), (lnf, "lnf"), (cvec, "cvec")):
        DMA(tile_[:], dr[name], [], [bC])
    DMA(rope[0:64, :], dr["rope"], [], [bC])
    for l in range(NL):
        DMA(vecs[l][:], dr["vecs"][l], [], [bC])
    MS("pool", epsb[:], EPS, [bC])
    AC(scT[:], cvec[:], AF.Silu, [bC], [bC])

    NB = 3
    stg = [RW.alloc(2048, F32) for _ in range(NB)]
    wbf = [RW.alloc(2048, BF16) for _ in range(NB)]
    bStg = [Buf() for _ in range(NB)]
    bWb = [[Buf(), Buf(), Buf()] for _ in range(NB)]
    wctr = [0, 0, 0]

    def split_cast(dst, sv, kc, bsrc, bdst3):
        if kc < 8:
            eng = "act" if wctr[2] % 2 == 0 else "dve"
            wctr[2] += 1
            CP(eng, dst, sv, [bsrc], bdst3)
            return lambda k: bdst3[0]
        ka = int(round(kc * 7 / 16.0))
        kb = int(round(kc * 13 / 16.0))
        prev = Sched.deps_of(bdst3)
        for p, (eng, a_, b_) in enumerate((("act", 0, ka), ("dve", ka, kb), ("pool", kb, kc))):
            if b_ > a_:
                CP(eng, dst[:, a_:b_, :], sv[:, a_:b_, :], [bsrc], [bdst3[p]], after=prev)
        return lambda k: bdst3[0 if k < ka else (1 if k < kb else 2)]

    def wload(src, kc, ncol, mode="cast", parts=128):
        n = kc * ncol
        assert n <= 2048
        if mode == "bf16":
            i = wctr[1] % NB
            wctr[1] += 1
            dst = v3(wbf[i][0:parts, 0:n], kc)
            DMA(dst, src, [], bWb[i])
            return dst, (lambda k, i=i: bWb[i][0])
        i = wctr[0] % NB
        wctr[0] += 1
        sv = v3(stg[i][0:parts, 0:n], kc)
        DMA(sv, src, [], [bStg[i]])
        if mode == "f32":
            return sv, (lambda k, i=i: bStg[i])
        j = wctr[1] % NB
        wctr[1] += 1
        dst = v3(wbf[j][0:parts, 0:n], kc)
        return dst, split_cast(dst, sv, kc, bStg[i], bWb[j])

    def make_ws2(RF, nstg=3):
        stg2 = [RF.alloc(4096, F32) for _ in range(nstg)]
        wb2 = [RF.alloc(4096, BF16) for _ in range(2)]
        bS2 = [Buf() for _ in range(nstg)]
        bW2 = [[Buf(), Buf(), Buf()] for _ in range(2)]
        c2 = [0, 0]

        def wload2(src, kc, ncol):
            n = kc * ncol
            assert n <= 4096
            i = c2[0] % nstg
            c2[0] += 1
            j = c2[1] % 2
            c2[1] += 1
            sv = v3(stg2[i][:, 0:n], kc)
            DMA(sv, src, [], [bS2[i]])
            dst = v3(wb2[j][:, 0:n], kc)
            return dst, split_cast(dst, sv, kc, bS2[i], bW2[j])
        return wload2

    def wsrc(w2d, k0, kc, c0, ncol):
        return w2d[k0 * 128:(k0 + kc) * 128, c0:c0 + ncol].rearrange("(k p) c -> p k c", p=128)

    try:
        bXd = [[Buf() for _ in range(16)] for _ in range(3)]
        xin_t = [RA.alloc(2048, F32) for _ in range(2)]
        xtr_t = [RB.alloc(2048, F32) for _ in range(2)]
        bxin = [Buf(), Buf()]
        bxtr = [Buf(), Buf()]
        for tc in range(12):
            xi, bi = xin_t[tc % 2], bxin[tc % 2]
            xo, bo = xtr_t[tc % 2], bxtr[tc % 2]
            DMA(xi[:], dr["x_in"][tc * 128:(tc + 1) * 128, :], [], [bi])
            for half in range(2):
                u = (tc * 2 + half) % 4
                for kk in range(8):
                    k = half * 8 + kk
                    MM(PS[u][:, kk * 128:(kk + 1) * 128], xi[:, k * 128:(k + 1) * 128], ident_f[:], True, True, [bi, bC], [bPS[2 * u + kk // 4]])
                CP("act" if half else "dve", xo[:, half * 1024:(half + 1) * 1024], PS[u][:, :], [bPS[2 * u], bPS[2 * u + 1]], [bo])
            DMA(xTv[:, :, tc * 128:(tc + 1) * 128], v3(xo[:], 16), [bo], bXd[tc // 4])
        s.barrier()
        stage('prepass')

        def rms_rstd(sq_views, nk, width, denom, psb, out_rstd, Rsq, Wout):
            for k in range(nk):
                MM(bank(psb, width), ones_b[:], sq_views(k), k == 0, k == nk - 1, [bC] + Rsq, [bPS[psb]])
            AC(out_rstd, bank(psb, width), AF.Sqrt, [bPS[psb], bC], Wout, bias=epsb[:], scale=1.0 / denom)
            RCP(out_rstd, out_rstd, Wout, Wout)

        for l in range(nlayers):
            vl = vecs[l]
            modT = modTs[l % 2]
            scv = v3(scT[:], 16)
            if l == 0:
                pv = PS[0][:, 0:192].rearrange("p (j m) -> p m j", j=2)
                RF.reset()
                wl2 = make_ws2(RF)
                for mp in range(48):
                    wt, bk = wl2(wsrc(dr["w_mod"][l], 0, 16, mp * 256, 256), 16, 256)
                    for c in range(2):
                        m = 2 * mp + c
                        for k in range(16):
                            MM(pv[:, m, :], wt[:, k, c * 128:(c + 1) * 128], scv[:, k, :], k == 0, k == 15, [bk(k), bC], [bPS[0]])
                for j in range(2):
                    TT("dve", modT[:, j * 96:(j + 1) * 96], PS[0][:, j * 96:(j + 1) * 96], vl[:, V_BMOD:V_BMOD + 96], ALU.add, [bPS[0], bC], [bC])
            for j in range(2):
                STT("dve", gsc[:, (j * 2) * 16:(j * 2) * 16 + 16], modT[:, j * 96 + 16:j * 96 + 32], 1.0, vl[:, V_LNMIX:V_LNMIX + 16], ALU.add, ALU.mult, [bC], [bC])
                STT("dve", gsc[:, (j * 2 + 1) * 16:(j * 2 + 1) * 16 + 16], modT[:, j * 96 + 64:j * 96 + 80], 1.0, vl[:, V_LNFFN:V_LNFFN + 16], ALU.add, ALU.mult, [bC], [bC])
            TT("dve", fb[0:64, 0:1], vl[0:64, V_FR0:V_FR0 + 1], vl[0:64, V_FB1:V_FB1 + 1], ALU.mult, [bC], [bC])
            TT("dve", fb[0:64, 1:2], vl[0:64, V_FR1:V_FR1 + 1], vl[0:64, V_FB2:V_FB2 + 1], ALU.mult, [bC], [bC])
            s.barrier()
            stage('mod')

            for g in range(2):
                j = g
                L = LP if g == 0 else LS
                nseq = 2 if g == 0 else 1
                nblk = 1 if g == 0 else 2
                Ltot = nseq * L
                tok0 = 0 if g == 0 else TP
                Lk = L if g == 0 else L + PAST
                nf = L // 128
                for r_ in (RA, RB, RIN, RT):
                    r_.reset()
                hT = RA.alloc(16 * Ltot, BF16)
                bH = [Buf() for _ in range(nblk)]
                xblk = RB.alloc(16 * 512, F32)
                bxb = Buf()
                sqt = RT.alloc(16 * 512, BF16)
                bsq = Buf()
                rstd = RT.alloc(512, F32)
                brs = Buf()
                hv = v3(hT[:], 16)
                ntmp = RIN.alloc(2 * 512, F32)
                bnt = [Buf(), Buf()]
                for b in range(nblk):
                    blk = (tok0 // 512) + b
                    DMA(v3(xblk[:], 16), xTv[:, :, blk * 512:(blk + 1) * 512], bXd[blk], [bxb])
                    AC(sqt[:], xblk[:], AF.Square, [bxb], [bsq])
                    rms_rstd(lambda k: sqt[:, k * 512:(k + 1) * 512], 16, 512, float(D), 7, rstd[:], [bsq], [brs])
                    for k in range(16):
                        nt_, bn_ = ntmp[:, (k % 2) * 512:(k % 2) * 512 + 512], bnt[k % 2]
                        TT("dve", nt_, xblk[:, k * 512:(k + 1) * 512], rstd[:], ALU.mult, [bxb, brs], [bn_])
                        AC(hv[:, k, b * 512:(b + 1) * 512], nt_, AF.Identity, [bn_, bC], [bH[b]],
                           bias=modT[:, j * 96 + k:j * 96 + k + 1], scale=gsc[:, (j * 2) * 16 + k:(j * 2) * 16 + k + 1])
                s.barrier()
                stage('stepA')
                RB.reset()
                RIN.reset()
                RT.reset()
                hyT = RIN.alloc(12 * Ltot, BF16)
                qT = RIN.alloc(2 * Ltot, BF16)
                kT = RIN.alloc(2 * Ltot, BF16)
                vT = RIN.alloc(4 * Ltot, BF16)
                sgT = RIN.alloc(4 * Ltot, BF16)
                afT = RIN.alloc(Ltot, BF16)
                abT = RIN.alloc(Ltot, BF16)
                cqnT = RIN.alloc(3 * Ltot, BF16)
                keysT = RIN.alloc(2 * nseq * Lk, BF16)
                krT = RIN.alloc(nseq * Lk, BF16)
                bHY = [Buf() for _ in range(12)]
                bQ, bK, bV, bSG, bAF, bAB, bCQN, bKEYS, bKR = (Buf() for _ in range(9))
                hyv = v3(hyT[:], 12)
                qv, kv_, vv, sgv = v3(qT[:], 2), v3(kT[:], 2), v3(vT[:], 4), v3(sgT[:], 4)
                cqnv = v3(cqnT[:], 3)
                keysv = v3(keysT[:], 2)
                ctmp = RT.alloc(Ltot, F32)
                bct = Buf()
                cqf = RT.alloc(3 * Ltot, F32)
                bcqf = Buf()
                sq3 = RT.alloc(3 * 512, BF16)
                bsq3 = Buf()
                rs2 = RT.alloc(512, F32)
                brs2 = Buf()
                cqfv = v3(cqf[:], 3)
                MS("pool", afT[0:32, :], 1.0, [bAF])
                MS("pool", abT[0:32, :], 1.0, [bAB])
                ckvf = RB.alloc(2 * Ltot, F32)
                bckvf = Buf()
                ckvfv = v3(ckvf[:], 2)
                krf = RB.alloc(Ltot, F32)
                bkrf = Buf()
                otr = RB.alloc(2 * 384, F32)
                krtmp = RB.alloc(Ltot, F32)
                bkrt = Buf()
                krtmp2 = RB.alloc(Ltot, F32)
                bkrt2 = Buf()
                botr = [Buf(), Buf()]

                chunks = []
                for i in range(12):
                    chunks.append(("hy", C_HY + i * 128, 128, i))
                for i in range(2):
                    chunks.append(("q", C_Q + i * 128, 128, i))
                for i in range(2):
                    chunks.append(("k", C_K + i * 128, 128, i))
                for i in range(4):
                    chunks.append(("v", C_V + i * 128, 128, i))
                for i in range(4):
                    chunks.append(("g", C_G + i * 128, 128, i))
                chunks.append(("af", C_AF, 16, 0))
                chunks.append(("ab", C_AB, 16, 0))
                for i in range(3):
                    chunks.append(("cq", C_CQ + i * 128, 128, i))
                for i in range(2):
                    chunks.append(("ckv", C_CKV + i * 128, 128, i))
                chunks.append(("kr", C_KR, 64, 0))
                if g == 1:
                    chunks.append(("krsw", C_KRSW, 64, 0))

                import os as _os
                for ci, (kind, c0, ncol, idx) in enumerate(chunks):
                    if g == 1 and ci >= int(_os.environ.get('KCH', '999')):
                        break
                    wt, bw = wload(wsrc(dr["w_in"][l], 0, 16, c0, ncol), 16, ncol)
                    u = ci % 4
                    for b in range(nblk):
                        for k in range(16):
                            MM(PS[u][0:ncol, b * 512:(b + 1) * 512], wt[:, k, :], hv[:, k, b * 512:(b + 1) * 512], k == 0, k == 15, [bw(k), bH[b]], [bPS[2 * u + b]])
                    pb = [bPS[2 * u + b] for b in range(nblk)]
                    psu = PS[u]
                    if kind == "hy":
                        cw = lambda tap: vl[:, V_CW + tap * 12 + idx:V_CW + tap * 12 + idx + 1]
                        for sq_ in range(nseq):
                            a, e_ = sq_ * L, (sq_ + 1) * L
                            AC(ctmp[:, a:e_], psu[:, a:e_], AF.Identity, pb + [bC], [bct], bias=vl[:, V_CB + idx:V_CB + idx + 1], scale=cw(1))
                            STT("dve", ctmp[:, a + 1:e_], psu[:, a:e_ - 1], cw(0), ctmp[:, a + 1:e_], ALU.mult, ALU.add, pb + [bC, bct], [bct])
                            STT("dve", hyv[:, idx, a:e_ - 1], psu[:, a + 1:e_], cw(2), ctmp[:, a:e_ - 1], ALU.mult, ALU.add, pb + [bC, bct], [bHY[idx]])
                            CP("pool", hyv[:, idx, e_ - 1:e_], ctmp[:, e_ - 1:e_], [bct], [bHY[idx]])
                    elif kind == "q":
                        AC(qv[:, idx, :], psu[:, 0:Ltot], AF.Identity, pb, [bQ], scale=0.125)
                    elif kind == "k":
                        CP("dve", kv_[:, idx, :], psu[:, 0:Ltot], pb, [bK])
                    elif kind == "v":
                        CP("dve", vv[:, idx, :], psu[:, 0:Ltot], pb, [bV])
                    elif kind == "g":
                        AC(sgv[:, idx, :], psu[:, 0:Ltot], AF.Silu, pb, [bSG])
                    elif kind == "af":
                        CP("dve", afT[0:16, :], psu[0:16, 0:Ltot], pb, [bAF])
                    elif kind == "ab":
                        CP("dve", abT[0:16, :], psu[0:16, 0:Ltot], pb, [bAB])
                    elif kind == "cq":
                        CP("dve", cqfv[:, idx, :], psu[:, 0:Ltot], pb, [bcqf])
                        if idx == 2:
                            for b in range(nblk):
                                sl = slice(b * 512, (b + 1) * 512)
                                for k in range(3):
                                    AC(sq3[:, k * 512:(k + 1) * 512], cqfv[:, k, sl], AF.Square, [bcqf], [bsq3])
                                rms_rstd(lambda k: sq3[:, k * 512:(k + 1) * 512], 3, 512, 384.0, 7, rs2[:], [bsq3], [brs2])
                                for k in range(3):
                                    STT("dve", cqnv[:, k, sl], cqfv[:, k, sl], vl[:, V_QN + k:V_QN + k + 1], rs2[:], ALU.mult, ALU.mult, [bcqf, brs2, bC], [bCQN])
                    elif kind == "ckv":
                        CP("dve", ckvfv[:, idx, :], psu[:, 0:Ltot], pb, [bckvf])
                        if idx == 1:
                            for b in range(nblk):
                                sl = slice(b * 512, (b + 1) * 512)
                                for k in range(2):
                                    AC(sq3[:, k * 512:(k + 1) * 512], ckvfv[:, k, sl], AF.Square, [bckvf], [bsq3])
                                rms_rstd(lambda k: sq3[:, k * 512:(k + 1) * 512], 2, 512, 256.0, 7, rs2[:], [bsq3], [brs2])
                                for k in range(2):
                                    STT("dve", ckvfv[:, k, sl], ckvfv[:, k, sl], vl[:, V_KVN + k:V_KVN + k + 1], rs2[:], ALU.mult, ALU.mult, [bckvf, brs2, bC], [bckvf])
                            for sq_ in range(nseq):
                                for k in range(2):
                                    CP("pool", keysv[:, k, sq_ * Lk:sq_ * Lk + L], ckvfv[:, k, sq_ * L:(sq_ + 1) * L], [bckvf], [bKEYS])
                    elif kind == "kr":
                        if g == 0:
                            CP("dve", krf[0:64, 0:Ltot], psu[0:64, 0:Ltot], pb, [bkrf])
                            CP("pool", krT[0:64, 0:Ltot], krf[0:64, 0:Ltot], [bkrf], [bKR])
                        else:
                            TT("dve", krtmp[0:64, :], psu[0:64, 0:Ltot], rope[0:64, 0:LS], ALU.mult, pb + [bC], [bkrt])
                    elif kind == "krsw":
                        TT("dve", krtmp2[0:64, :], psu[0:64, 0:Ltot], rope[0:64, LS:2 * LS], ALU.mult, pb + [bC], [bkrt2])
                        TT("pool", krT[0:64, 0:L], krtmp[0:64, :], krtmp2[0:64, :], ALU.add, [bkrt, bkrt2], [bKR])
                if g == 0:
                    for sq_ in range(2):
                        for tcn in range(2):
                            ot, bo = otr[:, ((sq_ * 2 + tcn) % 2) * 384:((sq_ * 2 + tcn) % 2) * 384 + 384], botr[(sq_ * 2 + tcn) % 2]
                            ts_ = slice(sq_ * L + tcn * 128, sq_ * L + (tcn + 1) * 128)
                            for k in range(2):
                                MM(bank(6, 128, k * 128), ckvfv[:, k, ts_], ident_f[:], True, True, [bckvf, bC], [bPS[6]])
                            MM(bank(6, 64, 256), krf[0:64, ts_], ident_f[0:64, 0:64], True, True, [bkrf, bC], [bPS[6]])
                            CP("act", ot[:, 0:320], bank(6, 320), [bPS[6]], [bo])
                            DMA(dr["o_ckv"][sq_, l, tcn * 128:(tcn + 1) * 128, :], ot[:, 0:256], [bo], [])
                            DMA(dr["o_kr"][sq_, l, tcn * 128:(tcn + 1) * 128, :], ot[:, 256:320], [bo], [])
                elif int(_os.environ.get('KCACHE', '9')) > 0:
                    cst = RB.alloc(2 * 384, F32)
                    bcst = [Buf(), Buf()]
                    kc2 = int(_os.environ.get('KCACHE', '9'))
                    for jc in range(2):
                        if kc2 < 9:
                            cs = cst[:, jc * 384:(jc + 1) * 384]
                            if kc2 >= 1:
                                MS("pool", cs[:, 320:384], 0.0, [bcst[jc]])
                                DMA(cs[:, 0:256], dr["ckv_c"][l, jc * 128:(jc + 1) * 128, :], [], [bcst[jc]])
                            if kc2 >= 2:
                                DMA(cs[:, 256:320], dr["kr_c"][l, jc * 128:(jc + 1) * 128, :], [], [bcst[jc]])
                            if kc2 >= 3:
                                for k in range(3):
                                    MM(bank(6, 128, k * 128), cs[:, k * 128:(k + 1) * 128], ident_f[:], True, True, [bcst[jc], bC], [bPS[6]])
                            if kc2 >= 4:
                                for k in range(2):
                                    CP("act", keysv[:, k, L + jc * 128:L + (jc + 1) * 128], bank(6, 128, k * 128), [bPS[6]], [bKEYS])
                            continue
                        cs = cst[:, jc * 384:(jc + 1) * 384]
                        MS("pool", cs[:, 320:384], 0.0, [bcst[jc]])
                        DMA(cs[:, 0:256], dr["ckv_c"][l, jc * 128:(jc + 1) * 128, :], [], [bcst[jc]])
                        DMA(cs[:, 256:320], dr["kr_c"][l, jc * 128:(jc + 1) * 128, :], [], [bcst[jc]])
                        for k in range(3):
                            MM(bank(6, 128, k * 128), cs[:, k * 128:(k + 1) * 128], ident_f[:], True, True, [bcst[jc], bC], [bPS[6]])
                        for k in range(2):
                            CP("act", keysv[:, k, L + jc * 128:L + (jc + 1) * 128], bank(6, 128, k * 128), [bPS[6]], [bKEYS])
                        CP("act", krT[0:64, L + jc * 128:L + (jc + 1) * 128], bank(6, 128, 256)[0:64, :], [bPS[6]], [bKR])
                s.barrier()
                stage('stepB')
                RA.reset()
                RB.reset()
                RT.reset()
                mixT = RB.alloc(16 * Ltot, BF16)
                mixv = v3(mixT[:], 16)
                bMIX = [Buf() for _ in range(16)]
                if dbg and ("hy_%d_%d" % (l, g)) in dbg_out:
                    for q4 in range(3):
                        dt_ = RT.alloc(4 * Ltot, F32)
                        bd = Buf()
                        CP("dve", v3(dt_[:], 4), hyv[:, q4 * 4:(q4 + 1) * 4, :], bHY, [bd])
                        DMA(dbg_out["hy_%d_%d" % (l, g)][q4 * 512:(q4 + 1) * 512, :].rearrange("(k p) t -> p k t", p=128), v3(dt_[:], 4), [bd], [])
                        s.barrier()
                        RT.reset()

                scale = 192.0 ** -0.5
                qn = [RA.alloc(Ltot, BF16) for _ in range(2)]
                qr = [RA.alloc(Ltot, BF16) for _ in range(2)]
                kn = [RA.alloc(nseq * Lk, BF16) for _ in range(2)]
                vh = [RA.alloc(nseq * Lk, BF16) for _ in range(2)]
                bqn, bqr, bkn, bvh = ([Buf(), Buf()] for _ in range(4))
                pt = [RT.alloc(512, BF16) for _ in range(3)]
                bpt = [Buf() for _ in range(3)]
                rec = [RT.alloc(512, F32) for _ in range(2)]
                brec = [Buf(), Buf()]
                rt1 = RT.alloc(512, F32)
                rt2 = RT.alloc(512, F32)
                brt1, brt2 = Buf(), Buf()
                nkc = Lk // 128
                qblocks = []
                for sq_ in range(nseq):
                    for qb in range(max(1, L // 512)):
                        qn_ = min(L, 512)
                        qblocks.append((sq_, sq_ * L + qb * qn_, qn_))
                ptc = 0
                acc = 0
                pjc = [0]

                def pjn():
                    pjc[0] += 1
                    return 5 + pjc[0] % 3

                def mla_pro(h):
                    hb = h % 2
                    hb = h % 2
                    wq, bwq = wload(wsrc(dr["w_uq"][l], 0, 3, h * 192, 192), 3, 192)
                    if g == 1:
                        wqs, bwqs = wload(wsrc(dr["w_uq"][l], 0, 3, 1536 + h * 64, 64), 3, 64)
                    wkv, bwkv = wload(wsrc(dr["w_ukv"][l], 0, 2, h * 256, 256), 2, 256)
                    for b in range(nblk):
                        sl = slice(b * 512, (b + 1) * 512)
                        pj = pjn()
                        for k in range(3):
                            MM(bank(pj), wq[:, k, 0:128], cqnv[:, k, sl], k == 0, k == 2, [bwq(0), bCQN], [bPS[pj]])
                        CP("act", qn[hb][:, sl], bank(pj), [bPS[pj]], [bqn[hb]])
                        pj = pjn()
                        for k in range(3):
                            MM(bank(pj)[0:64, :], wq[:, k, 128:192], cqnv[:, k, sl], k == 0, k == 2, [bwq(0), bCQN], [bPS[pj]])
                        if g == 0:
                            CP("dve", qr[hb][0:64, sl], bank(pj)[0:64, :], [bPS[pj]], [bqr[hb]])
                        else:
                            TT("dve", rt1[0:64, :], bank(pj)[0:64, :], rope[0:64, b * 512:(b + 1) * 512], ALU.mult, [bPS[pj], bC], [brt1])
                            pj = pjn()
                            for k in range(3):
                                MM(bank(pj)[0:64, :], wqs[:, k, :], cqnv[:, k, sl], k == 0, k == 2, [bwqs(0), bCQN], [bPS[pj]])
                            TT("dve", rt2[0:64, :], bank(pj)[0:64, :], rope[0:64, LS + b * 512:LS + (b + 1) * 512], ALU.mult, [bPS[pj], bC], [brt2])
                            TT("pool", qr[hb][0:64, sl], rt1[0:64, :], rt2[0:64, :], ALU.add, [brt1, brt2], [bqr[hb]])
                    ktot = nseq * Lk
                    for c0 in range(0, ktot, 512):
                        n_ = min(512, ktot - c0)
                        pj = pjn()
                        for k in range(2):
                            MM(bank(pj, n_), wkv[:, k, 0:128], keysv[:, k, c0:c0 + n_], k == 0, k == 1, [bwkv(0), bKEYS], [bPS[pj]])
                        CP("act", kn[hb][:, c0:c0 + n_], bank(pj, n_), [bPS[pj]], [bkn[hb]])
                    for c0 in range(0, ktot, 512):
                        n_ = min(512, ktot - c0)
                        pj = pjn()
                        for jj in range(n_ // 128):
                            for k in range(2):
                                MM(bank(pj, 128, jj * 128), keysv[:, k, c0 + jj * 128:c0 + (jj + 1) * 128], wkv[:, k, 128:256], k == 0, k == 1, [bwkv(0), bKEYS], [bPS[pj]])
                        CP("dve", vh[hb][:, c0:c0 + n_], bank(pj, n_), [bPS[pj]], [bvh[hb]])

                def mla_att(h):
                    nonlocal ptc, acc
                    hb = h % 2
                    for (sq_, q0, qn_) in qblocks:
                        ob, db = 3, 4
                        acc += 1
                        def emit_sc(jk_, ptc0=ptc, sq_=sq_, q0=q0, qn_=qn_, hb=hb):
                            sb_ = (ptc0 + jk_) % 3
                            kc0 = sq_ * Lk + jk_ * 128
                            MM(bank(sb_, qn_), kn[hb][:, kc0:kc0 + 128], qn[hb][:, q0:q0 + qn_], True, False, [bkn[hb], bqn[hb]], [bPS[sb_]])
                            MM(bank(sb_, qn_), krT[0:64, kc0:kc0 + 128], qr[hb][0:64, q0:q0 + qn_], False, True, [bKR, bqr[hb]], [bPS[sb_]])
                            AC(pt[sb_][:, 0:qn_], bank(sb_, qn_), AF.Exp, [bPS[sb_]], [bpt[sb_]], scale=scale)
                        for jk in range(min(2, nkc)):
                            emit_sc(jk)
                        for jk in range(nkc):
                            sb_ = (ptc + jk) % 3
                            kc0 = sq_ * Lk + jk * 128
                            MM(bank(ob, qn_), vh[hb][:, kc0:kc0 + 128], pt[sb_][:, 0:qn_], jk == 0, jk == nkc - 1, [bvh[hb], bpt[sb_]], [bPS[ob]])
                            MM(bank(db, qn_), ones_b[:], pt[sb_][:, 0:qn_], jk == 0, jk == nkc - 1, [bC, bpt[sb_]], [bPS[db]])
                            if jk + 2 < nkc:
                                emit_sc(jk + 2)
                        ptc += nkc
                        rc, brc = rec[acc % 2], brec[acc % 2]
                        RCP(rc[:, 0:qn_], bank(db, qn_), [bPS[db]], [brc])
                        TT("dve", mixv[:, 8 + h, q0:q0 + qn_], bank(ob, qn_), rc[:, 0:qn_], ALU.mult, [bPS[ob], brc], [bMIX[8 + h]])

                mla_pro(0)
                for h in range(8):
                    if h + 1 < 8:
                        mla_pro(h + 1)
                    mla_att(h)
                s.barrier()
                stage('mla')
                RA.reset()
                RT.reset()

                if dbg and ("glain_%d_%d" % (l, g)) in dbg_out:
                    for nm_, t_, nk_, off_ in (("q", qT, 2, 0), ("k", kT, 2, 256), ("v", vT, 4, 512), ("sg", sgT, 4, 1024)):
                        RT.reset()
                        dt_ = RT.alloc(4 * Ltot, F32)
                        bd = Buf()
                        CP("dve", dt_[:, 0:nk_ * Ltot], t_[:], [bQ, bK, bV, bSG], [bd])
                        DMA(dbg_out["glain_%d_%d" % (l, g)][off_:off_ + nk_ * 128, :].rearrange("(k p) t -> p k t", p=128), v3(dt_[:, 0:nk_ * Ltot], nk_), [bd], [])
                        s.barrier()
                    RT.reset()
                    dt_ = RT.alloc(2 * Ltot, F32)
                    bd = Buf()
                    CP("dve", dt_[0:32, 0:Ltot], afT[0:32, :], [bAF], [bd])
                    CP("dve", dt_[0:32, Ltot:2 * Ltot], abT[0:32, :], [bAB], [bd])
                    DMA(dbg_out["glain_%d_%d" % (l, g)][1536:1568, :], dt_[0:32, 0:Ltot], [bd], [])
                    DMA(dbg_out["glain_%d_%d" % (l, g)][1568:1600, :], dt_[0:32, Ltot:2 * Ltot], [bd], [])
                    s.barrier()
                    RT.reset()
                oacc = RA.alloc(4 * Ltot, F32)
                oav = v3(oacc[:], 4)
                bOA = Buf()
                NT = 2
                ltok = [RT.alloc(256, F32) for _ in range(NT)]
                ep = [RT.alloc(256, F32) for _ in range(NT)]
                em = [RT.alloc(256, F32) for _ in range(NT)]
                er = [RT.alloc(256, F32) for _ in range(NT)]
                qd = [RT.alloc(512, BF16) for _ in range(NT)]
                ki = [RT.alloc(256, BF16) for _ in range(NT)]
                ke = [RT.alloc(256, BF16) for _ in range(NT)]
                vt = [RT.alloc(512, BF16) for _ in range(NT)]
                at = [RT.alloc(512, BF16) for _ in range(NT)]
                bl, bep, bem, ber, bqd, bki, bke, bvt, bat = ([Buf() for _ in range(NT)] for _ in range(9))
                for i0 in range(NT):
                    MS("pool", qd[i0][:], 0.0, [bqd[i0]])
                Sst = RT.alloc(256, F32)
                Sbf = RT.alloc(256, BF16)
                bSst, bSbf = Buf(), Buf()
                waug = RA.alloc(512, F32)
                waub = RA.alloc(512, BF16)
                bwa = Buf()
                for dd in range(2):
                    DMA(waug[0:17, dd * 256:(dd + 1) * 256], dr["wa_aug"][l, dd], [], [bwa])
                CP("dve", waub[0:17, :], waug[0:17, :], [bwa], [bwa])
                it = 0
                for sq_ in range(nseq):
                    for dd in range(2):
                        if g == 0:
                            MS("pool", Sst[:], 0.0, [bSst])
                        else:
                            DMA(v3(Sst[:], 2), dr["sf_c" if dd == 0 else "sb_c"][l].rearrange("(hp p) v -> p hp v", p=128), [], [bSst])
                        CP("pool", Sbf[:], Sst[:], [bSst], [bSbf])
                        aT, bA_ = (afT, bAF) if dd == 0 else (abT, bAB)
                        order = range(nf) if dd == 0 else range(nf - 1, -1, -1)
                        for n_ in order:
                            i_ = it % NT
                            it += 1
                            if knob is not None and it > knob[0]:
                                raise _Stop()
                            ts_ = slice(sq_ * L + n_ * 128, sq_ * L + (n_ + 1) * 128)

                            def kc_(n):
                                if knob is not None and it == knob[0] and n >= knob[1]:
                                    raise _Stop()
                            MM(bank(0, 256), aT[0:17, ts_], waub[0:17, dd * 256:(dd + 1) * 256], True, True, [bA_, bwa], [bPS[0]])
                            AC(ltok[i_][:], bank(0, 256), AF.Exp, [bPS[0]], [bl[i_]], scale=-1.0)
                            AC(ltok[i_][:], ltok[i_][:], AF.Ln, [bl[i_]], [bl[i_]], bias=1.0)
                            kc_(1)
                            triI = tri[:, (2 * dd) * 128:(2 * dd + 1) * 128]
                            triE = tri[:, (2 * dd + 1) * 128:(2 * dd + 2) * 128]
                            for hp in range(2):
                                MM(bank(1, 128, hp * 128), ltok[i_][:, hp * 128:(hp + 1) * 128], triI, True, True, [bl[i_], bC], [bPS[1]])
                            MM(bank(1, 256, 256), triE, ltok[i_][:], True, True, [bl[i_], bC], [bPS[1]])
                            kc_(2)
                            AC(ep[i_][:], bank(1, 256), AF.Exp, [bPS[1]], [bep[i_]])
                            AC(em[i_][:], bank(1, 256), AF.Exp, [bPS[1]], [bem[i_]], scale=-1.0)
                            AC(er[i_][:], bank(1, 256, 256), AF.Exp, [bPS[1]], [ber[i_]])
                            for hp in range(2):
                                for h2 in range(2):
                                    r0 = h2 * 64
                                    TT("dve", qd[i_][r0:r0 + 64, (hp * 2 + h2) * 128:(hp * 2 + h2 + 1) * 128], qv[r0:r0 + 64, hp, ts_], ep[i_][r0:r0 + 64, hp * 128:(hp + 1) * 128], ALU.mult, [bQ, bep[i_], bqd[i_]], [bqd[i_]])
                                TT("dve", ki[i_][:, hp * 128:(hp + 1) * 128], kv_[:, hp, ts_], em[i_][:, hp * 128:(hp + 1) * 128], ALU.mult, [bK, bem[i_]], [bki[i_]])
                            kc_(3)
                            for hp in range(2):
                                MM(bank(2, 128, hp * 128), kv_[:, hp, ts_], ident_b[:], True, True, [bK, bC], [bPS[2]])
                            TT("dve", ke[i_][:], bank(2, 256), er[i_][:], ALU.mult, [bPS[2], ber[i_]], [bke[i_]])
                            for hh in range(4):
                                MM(bank(3, 128, hh * 128), vv[:, hh, ts_], ident_b[:], True, True, [bV, bC], [bPS[3]])
                            CP("act", vt[i_][:], bank(3), [bPS[3]], [bvt[i_]])
                            kc_(4)
                            for hp in range(2):
                                MM(bank(4, 256, hp * 256), ki[i_][:, hp * 128:(hp + 1) * 128], qd[i_][:, hp * 256:(hp + 1) * 256], True, True, [bki[i_], bqd[i_]], [bPS[4]])
                            TT("dve", at[i_][:], bank(4), mask[:, dd * 512:(dd + 1) * 512], ALU.mult, [bPS[4], bC], [bat[i_]])
                            for hh in range(4):
                                hp, h2 = hh // 2, hh % 2
                                r0 = h2 * 64
                                MM(bank(5, 128, hh * 128), vt[i_][:, hh * 128:(hh + 1) * 128], at[i_][:, hh * 128:(hh + 1) * 128], True, False, [bvt[i_], bat[i_]], [bPS[5]])
                                MM(bank(5, 128, hh * 128), Sbf[:, hp * 128:(hp + 1) * 128], qd[i_][:, hh * 128:(hh + 1) * 128], False, True, [bSbf, bqd[i_]], [bPS[5]])
                            kc_(5)
                            o4 = bank(5).rearrange("p (h c) -> p h c", h=4)
                            if dd == 0:
                                CP("act", oav[:, :, ts_], o4, [bPS[5]], [bOA])
                            else:
                                TT("dve", oav[:, :, ts_], o4, oav[:, :, ts_], ALU.add, [bPS[5], bOA], [bOA])
                            kc_(6)
                            for hp in range(2):
                                MM(bank(6 + hp, 256), ke[i_][:, hp * 128:(hp + 1) * 128], vt[i_][:, hp * 256:(hp + 1) * 256], True, True, [bke[i_], bvt[i_]], [bPS[6 + hp]])
                            kc_(7)
                            dcol = 127 if dd == 0 else 0
                            for hp in range(2):
                                for h2 in range(2):
                                    r0 = h2 * 64
                                    STT("dve", Sst[r0:r0 + 64, hp * 128:(hp + 1) * 128], Sst[r0:r0 + 64, hp * 128:(hp + 1) * 128],
                                        ep[i_][r0:r0 + 64, hp * 128 + dcol:hp * 128 + dcol + 1], bank(6 + hp, 128, h2 * 128)[r0:r0 + 64, :],
                                        ALU.mult, ALU.add, [bSst, bep[i_], bPS[6 + hp]], [bSst])
                            CP("pool", Sbf[:], Sst[:], [bSst], [bSbf])
                            if _os.environ.get('GLA_BAR'):
                                s.barrier()
                        if g == 0:
                            DMA(dr["o_sf" if dd == 0 else "o_sb"][sq_, l].rearrange("(hp p) v -> p hp v", p=128), v3(Sst[:], 2), [bSst], [])
                if dbg and ("oacc_%d_%d" % (l, g)) in dbg_out:
                    s.barrier()
                    DMA(dbg_out["oacc_%d_%d" % (l, g)].rearrange("(k p) t -> p k t", p=128), oav, [bOA], [])
                    s.barrier()
                gsq = RA.alloc(512, BF16)
                bgsq = Buf()
                grs = RA.alloc(512, F32)
                bgrs = Buf()
                gtm = RA.alloc(512, F32)
                bgtm = Buf()
                for hh in range(4):
                    for b in range(nblk):
                        sl = slice(b * 512, (b + 1) * 512)
                        AC(gsq[:], oav[:, hh, sl], AF.Square, [bOA], [bgsq])
                        rms_rstd(lambda k: gsq[:], 1, 512, 128.0, 0, grs[:], [bgsq], [bgrs])
                        STT("dve", gtm[:], oav[:, hh, sl], vl[:, V_GN:V_GN + 1], grs[:], ALU.mult, ALU.mult, [bOA, bC, bgrs], [bgtm])
                        TT("pool", mixv[:, 4 + hh, sl], gtm[:], sgv[:, hh, sl], ALU.mult, [bgtm, bSG], [bMIX[4 + hh]])
                s.barrier()
                stage('gla')
                RA.reset()
                RT.reset()

                RIN.reset()
                _ = RIN.alloc(12 * Ltot, BF16)
                ksd = RIN.alloc(2 * nf * 512, BF16)
                bKSD = Buf()
                ksdv = ksd[:].rearrange("p (a n c) -> p a n c", a=2, n=nf)
                h1T = RIN.alloc(L, F32)
                h2T = RIN.alloc(L, F32)
                bh1, bh2 = Buf(), Buf()
                zemb = RIN.alloc(L, F32)
                fw12 = RIN.alloc(128, F32)
                bfz = Buf()
                arg = RIN.alloc(512, F32)
                arg2 = RIN.alloc(512, F32)
                barg, barg2 = Buf(), Buf()
                dec_t = RIN.alloc(512, F32)
                bdec = Buf()
                bsb_t = RIN.alloc(512, F32)
                bbsb = Buf()
                sm_t = RIN.alloc(512, F32)
                df_t = RIN.alloc(512, F32)
                bsm, bdf = Buf(), Buf()
                DMA(zemb[0:33, :], dr["zemb%d" % L], [], [bfz])
                DMA(fw12[0:33, 0:64], dr["fw1"][l], [], [bfz])
                DMA(fw12[0:64, 64:128], dr["fw2"][l], [], [bfz])
                ztok = RA.alloc(nseq * nf * 512, BF16)
                ztv = ztok[:].rearrange("p (s n c) -> p s n c", s=nseq, n=nf)
                bZT = Buf()
                p1d = RA.alloc(2 * nseq * nf * 512, BF16)
                p1v = p1d[:].rearrange("p (a s n c) -> p a s n c", a=2, s=nseq, n=nf)
                bP1 = Buf()
                kab = [RT.alloc(1024, F32) for _ in range(2)]
                bkab = [Buf(), Buf()]
                tt_ = [RT.alloc(512, F32) for _ in range(4)]
                btt = [Buf() for _ in range(4)]
                utmp = RT.alloc(512, F32)
                butmp = Buf()
                delta = RT.alloc(512, F32)
                negt = RT.alloc(8, F32)
                DMA(delta[:], dr["delta"], [], [bfz])
                DMA(negt[:, 0:nf], dr["negt%d" % L], [], [bfz])

                def sin_layer(pre_ps, n_, fcol, fbcol, out_ap, Rps, Wout):
                    TS("dve", arg[0:64, 0:n_], pre_ps, vl[0:64, fcol:fcol + 1], fb[0:64, fbcol:fbcol + 1], ALU.mult, ALU.add, Rps + [bC], [barg])
                    TS("dve", arg2[0:64, 0:n_], arg[0:64, 0:n_], 1.0 / TWO_PI, MAGIC, ALU.mult, ALU.add, [barg], [barg2])
                    TS("dve", arg2[0:64, 0:n_], arg2[0:64, 0:n_], MAGIC, -TWO_PI, ALU.subtract, ALU.mult, [barg2], [barg2])
                    TT("dve", arg[0:64, 0:n_], arg[0:64, 0:n_], arg2[0:64, 0:n_], ALU.add, [barg, barg2], [barg])
                    AC(out_ap, arg[0:64, 0:n_], AF.Sin, [barg], Wout)

                for c0 in range(0, L, 512):
                    n_ = min(512, L - c0)
                    MM(PS[0][0:64, 0:n_], fw12[0:33, 0:64], zemb[0:33, c0:c0 + n_], True, True, [bfz], [bPS[0]])
                    sin_layer(PS[0][0:64, 0:n_], n_, V_FR0, 0, h1T[0:64, c0:c0 + n_], [bPS[0]], [bh1])
                for c0 in range(0, L, 512):
                    n_ = min(512, L - c0)
                    MM(PS[0][0:64, 0:n_], fw12[0:64, 64:128], h1T[0:64, c0:c0 + n_], True, True, [bfz, bh1], [bPS[0]])
                    sin_layer(PS[0][0:64, 0:n_], n_, V_FR1, 1, h2T[0:64, c0:c0 + n_], [bPS[0]], [bh2])

                zcur = [hyv[:, cc, :] for cc in range(4)]
                zbuf = [bHY[cc] for cc in range(4)]
                for o in range(2):
                    w3t, bw3 = wload(dr["fw3"][l].rearrange("p (k c) -> p k c", k=1), 1, 2048, mode="f32", parts=64)
                    for lc in range(nf):
                        AC(dec_t[:], delta[:], AF.Exp, [bfz], [bdec], scale=negt[:, lc:lc + 1])
                        for side in range(2):
                            cofs = (side * 2 + o) * 512
                            MM(bank(side), h2T[0:64, lc * 128:(lc + 1) * 128], w3t[0:64, 0, cofs:cofs + 512], True, True, [bh2, bw3(0)], [bPS[side]])
                        CP("act", bsb_t[:], bank(1), [bPS[1]], [bbsb])
                        if lc == 0:
                            MS("pool", bsb_t[0:1, :], 0.0, [bbsb])
                        TT("dve", sm_t[:], bank(0), bsb_t[:], ALU.add, [bPS[0], bbsb], [bsm])
                        TT("dve", df_t[:], bank(0), bsb_t[:], ALU.subtract, [bPS[0], bbsb], [bdf])
                        TT("pool", ksdv[:, 0, lc, :], sm_t[:], dec_t[:], ALU.mult, [bsm, bdec], [bKSD])
                        TT("pool", ksdv[:, 1, lc, :], df_t[:], dec_t[:], ALU.mult, [bdf, bdec], [bKSD])
                    for sq_ in range(nseq):
                        for tcn in range(nf):
                            ts_ = slice(sq_ * L + tcn * 128, sq_ * L + (tcn + 1) * 128)
                            pbk = 2 + (tcn % 2)
                            for cc in range(4):
                                MM(bank(pbk, 128, cc * 128), zcur[cc][:, ts_], ident_b[:], True, True, [zbuf[cc], bC], [bPS[pbk]])
                            CP("act", ztv[:, sq_, tcn, :], bank(pbk), [bPS[pbk]], [bZT])
                    for fb_ in range(nf // 2):
                        Cp, bCp = wload(wsrc(dr["dftC%d" % L], 0, nf, fb_ * 256, 256), nf, 256, mode="bf16")
                        Sp, bSp = wload(wsrc(dr["dftS%d" % L], 0, nf, fb_ * 256, 256), nf, 256, mode="bf16")
                        for f2 in range(2):
                            fc = fb_ * 2 + f2
                            fcs = slice(f2 * 128, (f2 + 1) * 128)
                            kk, bk_ = kab[fc % 2], bkab[fc % 2]
                            for n_ in range(nf):
                                MM(bank(0), Cp[:, n_, fcs], ksdv[:, 0, n_, :], n_ == 0, n_ == nf - 1, [bCp(0), bKSD], [bPS[0]])
                            for n_ in range(nf):
                                MM(bank(1), Sp[:, n_, fcs], ksdv[:, 1, n_, :], n_ == 0, n_ == nf - 1, [bSp(0), bKSD], [bPS[1]])
                            AC(kk[:, 0:512], bank(0), AF.Identity, [bPS[0]], [bk_], scale=1.0 / L)
                            AC(kk[:, 512:1024], bank(1), AF.Identity, [bPS[1]], [bk_], scale=1.0 / L)
                            for sq_ in range(nseq):
                                pa, pb_ = (4, 5) if (fc * nseq + sq_) % 2 == 0 else (6, 7)
                                for n_ in range(nf):
                                    MM(bank(pa), Cp[:, n_, fcs], ztv[:, sq_, n_, :], n_ == 0, n_ == nf - 1, [bCp(0), bZT], [bPS[pa]])
                                for n_ in range(nf):
                                    MM(bank(pb_), Sp[:, n_, fcs], ztv[:, sq_, n_, :], n_ == 0, n_ == nf - 1, [bSp(0), bZT], [bPS[pb_]])
                                TT("dve", tt_[0][:], bank(pa), kk[:, 0:512], ALU.mult, [bPS[pa], bk_], [btt[0]])
                                TT("dve", tt_[1][:], bank(pb_), kk[:, 512:1024], ALU.mult, [bPS[pb_], bk_], [btt[1]])
                                TT("pool", p1v[:, 0, sq_, fc, :], tt_[0][:], tt_[1][:], ALU.subtract, [btt[0], btt[1]], [bP1])
                                TT("dve", tt_[2][:], bank(pa), kk[:, 512:1024], ALU.mult, [bPS[pa], bk_], [btt[2]])
                                TT("dve", tt_[3][:], bank(pb_), kk[:, 0:512], ALU.mult, [bPS[pb_], bk_], [btt[3]])
                                TT("pool", p1v[:, 1, sq_, fc, :], tt_[2][:], tt_[3][:], ALU.add, [btt[2], btt[3]], [bP1])
                    xo = [hyv[:, 4 * (o + 1) + cc, :] for cc in range(4)]
                    xob = [bHY[4 * (o + 1) + cc] for cc in range(4)]
                    it2 = 0
                    for sq_ in range(nseq):
                        for tb in range(L // 256):
                            CTp, bCTp = wload(wsrc(dr["dftCT%d" % L], 0, nf, tb * 256, 256), nf, 256, mode="bf16")
                            STp, bSTp = wload(wsrc(dr["dftST%d" % L], 0, nf, tb * 256, 256), nf, 256, mode="bf16")
                            ts_ = slice(sq_ * L + tb * 256, sq_ * L + (tb + 1) * 256)
                            for cc in range(4):
                                pbk = it2 % 4
                                it2 += 1
                                ccs = slice(cc * 128, (cc + 1) * 128)
                                for n_ in range(nf):
                                    MM(bank(pbk, 256), p1v[:, 0, sq_, n_, ccs], CTp[:, n_, :], n_ == 0, False, [bP1, bCTp(0)], [bPS[pbk]])
                                for n_ in range(nf):
                                    MM(bank(pbk, 256), p1v[:, 1, sq_, n_, ccs], STp[:, n_, :], False, n_ == nf - 1, [bP1, bSTp(0)], [bPS[pbk]])
                                STT("dve", utmp[:, 0:256], zcur[cc][:, ts_], vl[:, V_SKIP + o * 4 + cc:V_SKIP + o * 4 + cc + 1], bank(pbk, 256), ALU.mult, ALU.add, [zbuf[cc], bC, bPS[pbk]], [butmp])
                                if o == 0:
                                    TT("pool", zcur[cc][:, ts_], utmp[:, 0:256], xo[cc][:, ts_], ALU.mult, [butmp, xob[cc]], [zbuf[cc]])
                                else:
                                    TT("pool", mixv[:, cc, ts_], utmp[:, 0:256], xo[cc][:, ts_], ALU.mult, [butmp, xob[cc]], [bMIX[cc]])
                if dbg and ("mix_%d_%d" % (l, g)) in dbg_out:
                    s.barrier()
                    for q4 in range(4):
                        RT.reset()
                        dt_ = RT.alloc(4 * Ltot, F32)
                        bd = Buf()
                        CP("dve", v3(dt_[:], 4), mixv[:, q4 * 4:(q4 + 1) * 4, :], bMIX, [bd])
                        DMA(dbg_out["mix_%d_%d" % (l, g)][q4 * 512:(q4 + 1) * 512, :].rearrange("(k p) t -> p k t", p=128), v3(dt_[:], 4), [bd], [])
                        s.barrier()
                s.barrier()
                stage('hyena')
                RA.reset()
                RT.reset()
                RIN.reset()

                xc = [RT.alloc(512, F32) for _ in range(4)]
                bxc = [Buf() for _ in range(4)]
                wnext = wload(wsrc(dr["w_out"][l], 0, 16, 0, 128), 16, 128)
                for i in range(16):
                    wt, bw = wnext
                    u = i % 4
                    for b in range(nblk):
                        blk = (tok0 // 512) + b
                        n4 = (i % 2) * 2 + b
                        DMA(xc[n4][:], xTv[:, i, blk * 512:(blk + 1) * 512], [bXd[blk][i]], [bxc[n4]])
                    for b in range(nblk):
                        n4 = (i % 2) * 2 + b
                        for k in range(16):
                            MM(bank(2 * u + b), wt[:, k, :], mixv[:, k, b * 512:(b + 1) * 512], k == 0, k == 15, [bw(k), bMIX[k]], [bPS[2 * u + b]])
                        STT("dve", xc[n4][:], bank(2 * u + b), modT[:, j * 96 + 32 + i:j * 96 + 32 + i + 1], xc[n4][:], ALU.mult, ALU.add, [bPS[2 * u + b], bC, bxc[n4]], [bxc[n4]])
                    if i + 1 < 16:
                        wnext = wload(wsrc(dr["w_out"][l], 0, 16, (i + 1) * 128, 128), 16, 128)
                    for b in range(nblk):
                        blk = (tok0 // 512) + b
                        n4 = (i % 2) * 2 + b
                        DMA(xTv[:, i, blk * 512:(blk + 1) * 512], xc[n4][:], [bxc[n4]], [bXd[blk][i]])
                s.barrier()
                stage('outproj')

            last = (l == nlayers - 1)
            RF.reset()
            wl2 = make_ws2(RF, nstg=2)
            h2all = RF.alloc(16 * T, BF16)
            h2a = v3(h2all[:], 16)
            bh2 = [Buf() for _ in range(3)]
            rstd = RF.alloc(512, F32)
            brs = Buf()
            sgt = [RF.alloc(512, F32) for _ in range(2)]
            bsg = [Buf(), Buf()]
            ntmp = RF.alloc(2 * 512, F32)
            bnt = [Buf(), Buf()]
            xc = [RF.alloc(512, F32) for _ in range(3)]
            bxc = [Buf() for _ in range(3)]
            act_off = RF.cur
            xblk = RF.alloc(16 * 512, F32)
            bxb = Buf()
            sqt = RF.alloc(16 * 512, BF16)
            bsq = Buf()
            for blk in range(3):
                j = 0 if blk == 0 else 1
                DMA(v3(xblk[:], 16), xTv[:, :, blk * 512:(blk + 1) * 512], bXd[blk], [bxb])
                AC(sqt[:], xblk[:], AF.Square, [bxb], [bsq])
                rms_rstd(lambda k: sqt[:, k * 512:(k + 1) * 512], 16, 512, float(D), 7, rstd[:], [bsq], [brs])
                for k in range(16):
                    nt_, bn_ = ntmp[:, (k % 2) * 512:(k % 2) * 512 + 512], bnt[k % 2]
                    TT("dve", nt_, xblk[:, k * 512:(k + 1) * 512], rstd[:], ALU.mult, [bxb, brs], [bn_])
                    AC(h2a[:, k, blk * 512:(blk + 1) * 512], nt_, AF.Identity, [bn_, bC], [bh2[blk]],
                       bias=modT[:, j * 96 + 48 + k:j * 96 + 48 + k + 1], scale=gsc[:, (j * 2 + 1) * 16 + k:(j * 2 + 1) * 16 + k + 1])
            s.barrier()
            RF.cur = act_off
            actH = RF.alloc(22 * T, BF16)
            actv = v3(actH[:], 22)
            bact = [[Buf() for _ in range(3)] for _ in range(22)]
            xc6 = xc + [RF.alloc(512, F32) for _ in range(3)]
            bxc6 = bxc + [Buf() for _ in range(3)]
            nup = 0
            for half in range(2):
                for jp in range(11):
                    jf0 = half * 22 + 2 * jp
                    wg, bg = wl2(wsrc(dr["w_ffn_in"][l], 0, 16, jf0 * 128, 256), 16, 256)
                    wu, bu = wl2(wsrc(dr["w_ffn_in"][l], 0, 16, DFF + jf0 * 128, 256), 16, 256)
                    for c in range(2):
                        for blk in range(3):
                            pg = c * 3 + blk
                            sl = slice(blk * 512, (blk + 1) * 512)
                            for k in range(16):
                                MM(bank(pg), wg[:, k, c * 128:(c + 1) * 128], h2a[:, k, sl], k == 0, k == 15, [bg(k), bh2[blk]], [bPS[pg]])
                    for c in range(2):
                        jfl = 2 * jp + c
                        for blk in range(3):
                            pg = c * 3 + blk
                            pu = 6 + (nup % 2)
                            sg_, bsg_ = sgt[nup % 2], bsg[nup % 2]
                            nup += 1
                            sl = slice(blk * 512, (blk + 1) * 512)
                            AC(sg_[:], bank(pg), AF.Silu, [bPS[pg]], [bsg_])
                            for k in range(16):
                                MM(bank(pu), wu[:, k, c * 128:(c + 1) * 128], h2a[:, k, sl], k == 0, k == 15, [bu(k), bh2[blk]], [bPS[pu]])
                            TT("dve", actv[:, jfl, sl], bank(pu), sg_[:], ALU.mult, [bPS[pu], bsg_], [bact[jfl][blk]])
                def out_tiles(ip_):
                    res = []
                    for (k0, kc) in ((0, 16), (16, 6)):
                        wt, bk = wl2(wsrc(dr["w_ffn_out"][l], half * 22 + k0, kc, ip_ * 256, 256), kc, 256)
                        res.append((k0, kc, wt, bk))
                    return res
                tiles = out_tiles(0)
                pvn = bank(7, 192).rearrange("p (j m) -> p m j", j=2)
                for ip in range(8):
                    i0 = 2 * ip
                    for c in range(2):
                        for blk in range(3):
                            n6 = c * 3 + blk
                            DMA(xc6[n6][:], xTv[:, i0 + c, blk * 512:(blk + 1) * 512], [bXd[blk][i0 + c]], [bxc6[n6]])
                    for (k0, kc, wt, bk) in tiles:
                        for c in range(2):
                            for blk in range(3):
                                pb = c * 3 + blk
                                for k in range(kc):
                                    kk = k0 + k
                                    MM(bank(pb), wt[:, k, c * 128:(c + 1) * 128], actv[:, kk, blk * 512:(blk + 1) * 512], kk == 0, kk == 21, [bk(k), bact[kk][blk]], [bPS[pb]])
                    for c in range(2):
                        i = i0 + c
                        for blk in range(3):
                            j = 0 if blk == 0 else 1
                            n6 = c * 3 + blk
                            STT("dve", xc6[n6][:], bank(n6), modT[:, j * 96 + 80 + i:j * 96 + 80 + i + 1], xc6[n6][:], ALU.mult, ALU.add, [bPS[n6], bC, bxc6[n6]], [bxc6[n6]])
                    if l + 1 < nlayers:
                        for t_ in range(3):
                            mp = half * 24 + ip * 3 + t_
                            wtm, bkm = wl2(wsrc(dr["w_mod"][l + 1], 0, 16, mp * 256, 256), 16, 256)
                            for c in range(2):
                                m = 2 * mp + c
                                for k in range(16):
                                    MM(pvn[:, m, :], wtm[:, k, c * 128:(c + 1) * 128], scv[:, k, :], k == 0, k == 15, [bkm(k), bC], [bPS[7]])
                    if ip < 7:
                        tiles = out_tiles(ip + 1)
                    for c in range(2):
                        for blk in range(3):
                            n6 = c * 3 + blk
                            DMA(xTv[:, i0 + c, blk * 512:(blk + 1) * 512], xc6[n6][:], [bxc6[n6]], [bXd[blk][i0 + c]])
                if l + 1 < nlayers:
                    m0 = half * 48
                    for j in range(2):
                        TT("dve", modTs[(l + 1) % 2][:, j * 96 + m0:j * 96 + m0 + 48], bank(7, 48, j * 96 + m0), vecs[l + 1][:, V_BMOD + m0:V_BMOD + m0 + 48], ALU.add, [bPS[7], bC], [bC])
            s.barrier()
            if last:
                RF.cur = act_off
                xblk = RF.alloc(16 * 512, F32)
                bxk = [Buf() for _ in range(16)]
                xv = v3(xblk[:], 16)
                sqt = RF.alloc(16 * 512, BF16)
                bsq = Buf()
                yo = [RF.alloc(2048, F32) for _ in range(2)]
                byo = [Buf(), Buf()]
                for blk in range(3):
                    DMA(xv, xTv[:, :, blk * 512:(blk + 1) * 512], bXd[blk], bxk)
                    AC(sqt[:], xblk[:], AF.Square, bxk, [bsq])
                    rms_rstd(lambda k: sqt[:, k * 512:(k + 1) * 512], 16, 512, float(D), 7, rstd[:], [bsq], [brs])
                    for k in range(16):
                        STT("dve", xv[:, k, :], xv[:, k, :], lnf[:, k:k + 1], rstd[:], ALU.mult, ALU.mult, [bxk[k], bC, brs], [bxk[k]])
                    for tcn in range(4):
                        yt, byt = yo[tcn % 2], byo[tcn % 2]
                        for half in range(2):
                            u = (tcn * 2 + half) % 3
                            for kk in range(8):
                                k = half * 8 + kk
                                MM(PS[u][:, kk * 128:(kk + 1) * 128], xv[:, k, tcn * 128:(tcn + 1) * 128], ident_f[:], True, True, [bxk[k], bC], [bPS[2 * u + kk // 4]])
                            CP("act" if half else "dve", yt[:, half * 1024:(half + 1) * 1024], PS[u][:, :], [bPS[2 * u], bPS[2 * u + 1]], [byt])
                        DMA(dr["y"][blk * 512 + tcn * 128:blk * 512 + (tcn + 1) * 128, :], yt[:], [byt], [])
                s.barrier()
    except _Stop:
        pass
    s.barrier()
    s.emit(nc, st)
    st.close()
    return nc


_NC_CACHE = {}


def prep_inputs(inp, core):
    c = host_consts()
    f32 = lambda a: np.ascontiguousarray(np.asarray(a, dtype=np.float32))
    m = {}
    xp = f32(inp["x_prompt"])[2 * core:2 * core + 2].reshape(TP, D)
    xs = f32(inp["x_sample"])[core]
    m["x_in"] = np.ascontiguousarray(np.concatenate([xp, xs], axis=0))
    cv = np.stack([f32(inp["c_ctx"]), f32(inp["c"])[core]], axis=0)
    m["cvec"] = np.ascontiguousarray(cv.reshape(2, 16, 128).transpose(2, 1, 0).reshape(128, 32))
    m["ckv_c"] = f32(inp["cache_mla_ckv"])[core]
    m["kr_c"] = f32(inp["cache_mla_krope"])[core]
    m["sf_c"] = np.ascontiguousarray(f32(inp["state_gla_fwd"])[core].reshape(NL, 256, 128))
    m["sb_c"] = np.ascontiguousarray(f32(inp["state_gla_bwd"])[core].reshape(NL, 256, 128))
    return m


def prep_shared(inp):
    c = host_consts()
    f32 = lambda a: np.ascontiguousarray(np.asarray(a, dtype=np.float32))
    m = {}
    m["w_mod"] = f32(inp["w_mod"])
    w_in = f32(inp["w_in"])
    m["w_in"] = np.ascontiguousarray(np.concatenate([w_in, w_in[:, :, C_KR + c["perm"]]], axis=2))
    m["w_out"] = f32(inp["w_out"])
    m["w_ffn_in"] = f32(inp["w_ffn_in"])
    m["w_ffn_out"] = f32(inp["w_ffn_out"])
    wuq = f32(inp["mla_w_uq"])
    sw = np.concatenate([wuq[:, :, h * 192 + 128 + c["perm"]] for h in range(8)], axis=2)
    m["w_uq"] = np.ascontiguousarray(np.concatenate([wuq, sw], axis=2))
    m["w_ukv"] = f32(inp["mla_w_ukv"])
    vecs = np.zeros((NL, 128, NV), np.float32)
    pm = lambda a, n: f32(a).reshape(NL, n, 128).transpose(0, 2, 1)
    vecs[:, :, V_BMOD:V_BMOD + 96] = pm(inp["b_mod"], 96)
    vecs[:, :, V_LNMIX:V_LNMIX + 16] = pm(inp["ln_mix"], 16)
    vecs[:, :, V_LNFFN:V_LNFFN + 16] = pm(inp["ln_ffn"], 16)
    vecs[:, :, V_CW:V_CW + 36] = f32(inp["hy_conv_w"]).reshape(NL, 3, 12, 128).transpose(0, 3, 1, 2).reshape(NL, 128, 36)
    vecs[:, :, V_CB:V_CB + 12] = pm(inp["hy_conv_b"], 12)
    vecs[:, :, V_SKIP:V_SKIP + 8] = f32(inp["hy_skip"]).reshape(NL, 2, 4, 128).transpose(0, 3, 1, 2).reshape(NL, 128, 8)
    vecs[:, :, V_GN] = f32(inp["gla_norm"])
    vecs[:, :, V_QN:V_QN + 3] = pm(inp["mla_q_norm"], 3)
    vecs[:, :, V_KVN:V_KVN + 2] = pm(inp["mla_kv_norm"], 2)
    vecs[:, 0:64, V_FB1] = f32(inp["hy_filt_b1"])
    vecs[:, 0:64, V_FB2] = f32(inp["hy_filt_b2"])
    vecs[:, 0:64, V_FR0] = f32(inp["hy_filt_freq"])[:, 0]
    vecs[:, 0:64, V_FR1] = f32(inp["hy_filt_freq"])[:, 1]
    m["vecs"] = vecs
    m["lnf"] = np.ascontiguousarray(f32(inp["ln_final"]).reshape(16, 128).T)
    m["fw1"] = f32(inp["hy_filt_w1"])
    m["fw2"] = f32(inp["hy_filt_w2"])
    m["fw3"] = f32(inp["hy_filt_w3"])
    wa = np.zeros((NL, 2, 17, 256), np.float32)
    wa[:, 0, 0:16] = f32(inp["gla_wa_f"])
    wa[:, 0, 16] = f32(inp["gla_ba_f"])
    wa[:, 1, 0:16] = f32(inp["gla_wa_b"])
    wa[:, 1, 16] = f32(inp["gla_ba_b"])
    m["wa_aug"] = wa
    for name, shape, dt in CONST_DRAM:
        m[name] = c[name]
    return m


def kernel(**inputs):
    if "nc" not in _NC_CACHE:
        _NC_CACHE["nc"] = build()
    nc = _NC_CACHE["nc"]
    shared = prep_shared(inputs)
    in_maps = []
    for core in range(NCORE):
        m = dict(shared)
        m.update(prep_inputs(inputs, core))
        in_maps.append(m)
    res = run_bass_kernel_spmd(nc, in_maps, core_ids=list(range(NCORE)))
    R = res.results
    y_prompt = np.concatenate([r["y"][0:TP].reshape(2, LP, D) for r in R], axis=0)
    y_sample = np.stack([r["y"][TP:T] for r in R], axis=0)
    o_ckv = np.concatenate([r["o_ckv"] for r in R], axis=0)
    o_kr = np.concatenate([r["o_kr"] for r in R], axis=0)
    o_sf = np.concatenate([r["o_sf"].reshape(2, NL, 4, 64, 128) for r in R], axis=0)
    o_sb = np.concatenate([r["o_sb"].reshape(2, NL, 4, 64, 128) for r in R], axis=0)
    f = lambda a: np.ascontiguousarray(a.astype(np.float32))
    return (f(y_prompt), f(y_sample), f(o_ckv), f(o_kr), f(o_sf), f(o_sb))
```
